# Optimizing a Trainium2 kernel written in Bass

```python
import jax, jax.numpy as jnp
from jax import lax
import numpy as np

D_MODEL = 1024
BATCH = 8
SEQ = 2048
DEPTH = 2
DEC_BATCH = 32
DEC_SEQ = 4
PAST_LEN = 16384
PAGE_SIZE = 128

N_A_LAYERS = DEPTH // 2
N_B_LAYERS = DEPTH - N_A_LAYERS
H_A = 16
D_NOPE = 64
D_ROPE = 32
D_V = 64
D_QC = 384
D_C = 256
D_CKV = D_C + D_ROPE
SCALE_A = (D_NOPE + D_ROPE) ** -0.5
H_B = 16
N_KV_B = 4
HD_B = 64
G_B = H_B // N_KV_B
WINDOW = 128
ROT_B = HD_B // 4
SCALE_B = HD_B ** -0.5
D_FF_RAW = -(-8 * D_MODEL // 3)
D_FF = ((D_FF_RAW + 255) // 256) * 256
ROPE_THETA = 500000.0
Q_BLOCK = 128
EPS = 1e-6
NEG = -1e30

kernel_name = 'yoco_mla_swa_sink_decoder_step'


def rms_norm(x, g):
    xf = x.astype(jnp.float32)
    y = xf * lax.rsqrt(jnp.mean(xf * xf, axis=-1, keepdims=True) + EPS)
    return (y * g.astype(jnp.float32)).astype(x.dtype)


def rope_cos_sin(pos, n_rot):
    inv = ROPE_THETA ** (-jnp.arange(0, n_rot, 2, dtype=jnp.float32) / n_rot)
    ang = pos.astype(jnp.float32)[:, None] * inv[None, :]
    return jnp.cos(ang), jnp.sin(ang)


def apply_rope(x, cos, sin):
    r = cos.shape[-1]
    x1 = x[..., :r].astype(jnp.float32)
    x2 = x[..., r:2 * r].astype(jnp.float32)
    c = cos[:, None, :]
    s = sin[:, None, :]
    rot = jnp.concatenate([x1 * c - x2 * s, x2 * c + x1 * s], axis=-1).astype(x.dtype)
    return jnp.concatenate([rot, x[..., 2 * r:]], axis=-1)


def swiglu(hn, w_in, w_out):
    a = hn @ w_in
    return (jax.nn.silu(a[..., :D_FF]) * a[..., D_FF:]) @ w_out


def sink_softmax(s, sink):
    sk = jnp.broadcast_to(sink, s.shape[:-1] + (1,))
    p = jax.nn.softmax(jnp.concatenate([s, sk], axis=-1), axis=-1)
    return p[..., :-1]


def mla_project(hn, pos, w_in, g_qc, w_uq, g_ckv, g_qn, g_qr, g_kr):
    b, s, _ = hn.shape
    a = hn @ w_in
    c_q = rms_norm(a[..., :D_QC], g_qc)
    c_kv = rms_norm(a[..., D_QC:D_QC + D_C], g_ckv)
    k_pe = a[..., D_QC + D_C:][:, :, None, :]
    q = (c_q @ w_uq).reshape(b, s, H_A, D_NOPE + D_ROPE)
    cos, sin = rope_cos_sin(pos, D_ROPE)
    q_nope = rms_norm(q[..., :D_NOPE], g_qn)
    q_pe = apply_rope(rms_norm(q[..., D_NOPE:], g_qr), cos, sin)
    k_pe = apply_rope(rms_norm(k_pe, g_kr), cos, sin)[:, :, 0]
    rows = jnp.concatenate([c_kv, k_pe], axis=-1)
    return q_nope, q_pe, rows


def mla_keys(rows, w_uk, g_kn):
    c = rows[..., :D_C]
    k_nope = jnp.einsum('bsc,chd->bshd', c, w_uk.reshape(D_C, H_A, D_NOPE))
    return rms_norm(k_nope, g_kn), rows[..., D_C:]


def mla_scores(q_nope, q_pe, k_nope, k_pe):
    s = jnp.einsum('bqhd,bshd->bhqs', q_nope, k_nope) + jnp.einsum('bqhr,bsr->bhqs', q_pe, k_pe)
    return s.astype(jnp.float32) * SCALE_A


def mla_prompt_attn(q_nope, q_pe, rows, w_uk, w_uv, g_kn):
    b, s_len = q_nope.shape[:2]
    nb = s_len // Q_BLOCK
    k_nope, k_pe = mla_keys(rows, w_uk, g_kn)
    v = jnp.einsum('bsc,chd->bshd', rows[..., :D_C], w_uv.reshape(D_C, H_A, D_V))
    qn = q_nope.reshape(b, nb, Q_BLOCK, H_A, D_NOPE).transpose(1, 0, 2, 3, 4)
    qp = q_pe.reshape(b, nb, Q_BLOCK, H_A, D_ROPE).transpose(1, 0, 2, 3, 4)
    k_pos = jnp.arange(s_len)

    def block(args):
        i, qn_b, qp_b = args
        sc = mla_scores(qn_b, qp_b, k_nope, k_pe)
        q_pos = i * Q_BLOCK + jnp.arange(Q_BLOCK)
        sc = jnp.where(k_pos[None, :] <= q_pos[:, None], sc, NEG)
        p = jax.nn.softmax(sc, axis=-1).astype(v.dtype)
        return jnp.einsum('bhqs,bshd->bqhd', p, v)

    o = lax.map(block, (jnp.arange(nb), qn, qp))
    return o.transpose(1, 0, 2, 3, 4).reshape(b, s_len, H_A * D_V)


def online_softmax_update(carry, s, vals):
    m, l, acc = carry
    m_new = jnp.maximum(m, jnp.max(s, axis=-1))
    corr = jnp.exp(m - m_new)
    p = jnp.exp(s - m_new[..., None])
    return (m_new, l * corr + jnp.sum(p, axis=-1),
            acc * corr[..., None] + jnp.einsum('bhts,bsc->bhtc', p, vals))


def mla_paged_attn(q_nope, q_pe, new_rows, cache_l, page_table, w_uk, w_uv, g_kn):
    bd, t = q_nope.shape[:2]
    f32 = jnp.float32
    init = (jnp.full((bd, H_A, t), NEG, f32), jnp.zeros((bd, H_A, t), f32),
            jnp.zeros((bd, H_A, t, D_C), f32))

    def page_step(carry, page_ids):
        rows = cache_l[page_ids]
        k_nope, k_pe = mla_keys(rows, w_uk, g_kn)
        sc = mla_scores(q_nope, q_pe, k_nope, k_pe)
        return online_softmax_update(carry, sc, rows[..., :D_C].astype(f32)), None

    carry, _ = lax.scan(page_step, init, page_table.T)
    k_nope, k_pe = mla_keys(new_rows, w_uk, g_kn)
    sc = mla_scores(q_nope, q_pe, k_nope, k_pe)
    sc = jnp.where(jnp.tril(jnp.ones((t, t), dtype=bool)), sc, NEG)
    _, l, acc = online_softmax_update(carry, sc, new_rows[..., :D_C].astype(f32))
    o_lat = acc / l[..., None]
    o = jnp.einsum('bhtc,chd->bthd', o_lat, w_uv.reshape(D_C, H_A, D_V).astype(f32))
    return o.astype(q_nope.dtype).reshape(bd, t, H_A * D_V)


def shared_kv(h, pos, g_kv, w_kv, g_k):
    b, s, _ = h.shape
    kv = rms_norm(h, g_kv) @ w_kv
    k = kv[..., :N_KV_B * HD_B].reshape(b, s, N_KV_B, HD_B)
    v = kv[..., N_KV_B * HD_B:].reshape(b, s, N_KV_B, HD_B)
    cos, sin = rope_cos_sin(pos, ROT_B)
    return apply_rope(rms_norm(k, g_k), cos, sin), v


def swa_query(hn, pos, w_q, g_q):
    b, s, _ = hn.shape
    q = rms_norm((hn @ w_q).reshape(b, s, H_B, HD_B), g_q)
    cos, sin = rope_cos_sin(pos, ROT_B)
    return apply_rope(q, cos, sin)


def swa_banded_attn(q, k, v, sink):
    b, s_len = q.shape[:2]
    nb = s_len // WINDOW
    qb = q.reshape(b, nb, WINDOW, N_KV_B, G_B, HD_B)

    def band(z):
        zp = jnp.concatenate([jnp.zeros_like(z[:, :WINDOW]), z[:, :s_len - WINDOW]], axis=1)
        return jnp.concatenate([zp.reshape(b, nb, WINDOW, N_KV_B, HD_B),
                                z.reshape(b, nb, WINDOW, N_KV_B, HD_B)], axis=2)

    kb, vb = band(k), band(v)
    sc = jnp.einsum('bnqkgd,bnskd->bnkgqs', qb, kb).astype(jnp.float32) * SCALE_B
    blk = jnp.arange(nb)[:, None, None]
    q_rel = jnp.arange(WINDOW)[None, :, None] + WINDOW
    k_rel = jnp.arange(2 * WINDOW)[None, None, :]
    diff = q_rel - k_rel
    valid = (diff >= 0) & (diff < WINDOW) & (blk * WINDOW - WINDOW + k_rel >= 0)
    sc = jnp.where(valid[None, :, None, None], sc, NEG)
    p = sink_softmax(sc, sink.astype(jnp.float32).reshape(N_KV_B, G_B)[None, None, :, :, None, None])
    o = jnp.einsum('bnkgqs,bnskd->bnqkgd', p.astype(v.dtype), vb)
    return o.reshape(b, s_len, H_B * HD_B)


def swa_explicit_attn(q, q_pos, k, v, k_pos, sink):
    bd, t = q.shape[:2]
    qg = q.reshape(bd, t, N_KV_B, G_B, HD_B)
    sc = jnp.einsum('btkgd,bskd->bkgts', qg, k).astype(jnp.float32) * SCALE_B
    diff = q_pos[:, None] - k_pos[None, :]
    sc = jnp.where((diff >= 0) & (diff < WINDOW), sc, NEG)
    p = sink_softmax(sc, sink.astype(jnp.float32).reshape(N_KV_B, G_B)[None, :, :, None, None])
    o = jnp.einsum('bkgts,bskd->btkgd', p.astype(v.dtype), v)
    return o.reshape(bd, t, H_B * HD_B)


def setup_inputs(seed: int = 0) -> dict:
    key = jax.random.key(seed)
    ks = iter(jax.random.split(key, 48))
    f32 = jnp.float32

    def w(shape, fan_in):
        return jax.random.normal(next(ks), shape, f32) * fan_in ** -0.5

    def gain(shape):
        return 1.0 + 0.02 * jax.random.normal(next(ks), shape, f32)

    def act(shape):
        return jax.random.normal(next(ks), shape, f32)

    n_pages = PAST_LEN // PAGE_SIZE
    n_used = DEC_BATCH * n_pages
    n_pool = n_used + (n_used + 3) // 4
    w_buf = min(WINDOW, PAST_LEN)
    perm = jax.random.permutation(next(ks), n_pool)
    page_table = perm[:n_used].reshape(DEC_BATCH, n_pages).astype(jnp.int32)
    return {
        'x_prompt': act((BATCH, SEQ, D_MODEL)),
        'x_sample': act((DEC_BATCH, DEC_SEQ, D_MODEL)),
        'cache_mla': act((N_A_LAYERS, n_pool, PAGE_SIZE, D_CKV)),
        'state_win_k': act((DEC_BATCH, w_buf, N_KV_B, HD_B)),
        'state_win_v': act((DEC_BATCH, w_buf, N_KV_B, HD_B)),
        'page_table': page_table,
        'norm_attn': gain((DEPTH, D_MODEL)),
        'norm_ffn': gain((DEPTH, D_MODEL)),
        'w_a_in': w((N_A_LAYERS, D_MODEL, D_QC + D_C + D_ROPE), D_MODEL),
        'g_qc': gain((N_A_LAYERS, D_QC)),
        'w_uq': w((N_A_LAYERS, D_QC, H_A * (D_NOPE + D_ROPE)), D_QC),
        'g_ckv': gain((N_A_LAYERS, D_C)),
        'w_uk': w((N_A_LAYERS, D_C, H_A * D_NOPE), D_C),
        'w_uv': w((N_A_LAYERS, D_C, H_A * D_V), D_C),
        'g_qn_a': gain((N_A_LAYERS, D_NOPE)),
        'g_qr_a': gain((N_A_LAYERS, D_ROPE)),
        'g_kn_a': gain((N_A_LAYERS, D_NOPE)),
        'g_kr_a': gain((N_A_LAYERS, D_ROPE)),
        'w_a_out': w((N_A_LAYERS, H_A * D_V, D_MODEL), H_A * D_V),
        'g_kv_shared': gain((D_MODEL,)),
        'w_kv_shared': w((D_MODEL, 2 * N_KV_B * HD_B), D_MODEL),
        'g_k_b': gain((HD_B,)),
        'w_q_b': w((N_B_LAYERS, D_MODEL, H_B * HD_B), D_MODEL),
        'g_q_b': gain((N_B_LAYERS, HD_B)),
        'sinks': 0.5 * act((N_B_LAYERS, H_B)),
        'w_b_out': w((N_B_LAYERS, H_B * HD_B, D_MODEL), H_B * HD_B),
        'w_ffn_in': w((DEPTH, D_MODEL, 2 * D_FF), D_MODEL),
        'w_ffn_out': w((DEPTH, D_FF, D_MODEL), D_FF),
    }


def reference(x_prompt, x_sample, cache_mla, state_win_k, state_win_v, page_table,
              norm_attn, norm_ffn, w_a_in, g_qc, w_uq, g_ckv, w_uk, w_uv,
              g_qn_a, g_qr_a, g_kn_a, g_kr_a, w_a_out,
              g_kv_shared, w_kv_shared, g_k_b, w_q_b, g_q_b, sinks, w_b_out,
              w_ffn_in, w_ffn_out):
    seq = x_prompt.shape[1]
    dec_seq = x_sample.shape[1]
    past_len = page_table.shape[1] * PAGE_SIZE
    w_buf = state_win_k.shape[1]
    pos_p = jnp.arange(seq)
    pos_s = past_len + jnp.arange(dec_seq)

    def trunk(x, pos, mla_attend, swa_attend):
        h = x
        rows_all = []
        k_sh = None
        v_sh = None
        for l in range(DEPTH):
            hn = rms_norm(h, norm_attn[l])
            if l < N_A_LAYERS:
                q_nope, q_pe, rows = mla_project(hn, pos, w_a_in[l], g_qc[l], w_uq[l], g_ckv[l],
                                                 g_qn_a[l], g_qr_a[l], g_kr_a[l])
                h = h + mla_attend(l, q_nope, q_pe, rows) @ w_a_out[l]
                rows_all.append(rows)
            else:
                if l == N_A_LAYERS:
                    k_sh, v_sh = shared_kv(h, pos, g_kv_shared, w_kv_shared, g_k_b)
                lb = l - N_A_LAYERS
                q = swa_query(hn, pos, w_q_b[lb], g_q_b[lb])
                h = h + swa_attend(q, k_sh, v_sh, sinks[lb]) @ w_b_out[lb]
            h = h + swiglu(rms_norm(h, norm_ffn[l]), w_ffn_in[l], w_ffn_out[l])
        return h, jnp.stack(rows_all), k_sh, v_sh

    def mla_attend_prompt(la, q_nope, q_pe, rows):
        return mla_prompt_attn(q_nope, q_pe, rows, w_uk[la], w_uv[la], g_kn_a[la])

    def mla_attend_sample(la, q_nope, q_pe, rows):
        return mla_paged_attn(q_nope, q_pe, rows, cache_mla[la], page_table,
                              w_uk[la], w_uv[la], g_kn_a[la])

    def swa_attend_sample(q, k, v, sink):
        k_all = jnp.concatenate([state_win_k, k], axis=1)
        v_all = jnp.concatenate([state_win_v, v], axis=1)
        k_pos = jnp.concatenate([past_len - w_buf + jnp.arange(w_buf), pos_s])
        return swa_explicit_attn(q, pos_s, k_all, v_all, k_pos, sink)

    y_prompt, rows_p, k_p, v_p = trunk(x_prompt, pos_p, mla_attend_prompt, swa_banded_attn)
    y_sample, rows_s, k_s, v_s = trunk(x_sample, pos_s, mla_attend_sample, swa_attend_sample)

    w_p = min(WINDOW, seq)
    win_k_p = k_p[:, seq - w_p:]
    win_v_p = v_p[:, seq - w_p:]
    win_k_s = jnp.concatenate([state_win_k, k_s], axis=1)[:, -w_buf:]
    win_v_s = jnp.concatenate([state_win_v, v_s], axis=1)[:, -w_buf:]
    return (y_prompt, y_sample, rows_p, rows_s, win_k_p, win_v_p, win_k_s, win_v_s)
```

```python
import numpy as np
import concourse.bass as bass
import concourse.mybir as mybir
from concourse.bass_utils import run_bass_kernel_spmd

F32 = mybir.dt.float32
BF16 = mybir.dt.bfloat16
I32 = mybir.dt.int32
AF = mybir.ActivationFunctionType
ALU = mybir.AluOpType
AX = mybir.AxisListType

ENGS = ["pe", "act", "dve", "pool", "sp"]

D = 1024
SEQ = 2048
NS = 16
NT = SEQ + NS
H = 16
D_NOPE, D_ROPE, D_V = 64, 32, 64
D_QC, D_C = 384, 256
D_CKV = D_C + D_ROPE
SCALE_A = float((D_NOPE + D_ROPE) ** -0.5)
N_KV, HD = 4, 64
SCALE_B = float(HD ** -0.5)
D_FF = 2816
EPS = 1e-6
THETA = 500000.0
PAST = 16384
NPOOL = 5120
NEG = -1e30
FILLER = 0
CHUNKS = [(0, 512), (512, 512), (1024, 512), (1536, 512), (2048, 16)]
NV = 69
V_GA, V_GFA, V_GKV, V_GB, V_GFB, V_GQC, V_GCKV, V_GQ96, V_GK96, V_GKB, V_GQB, V_SINKC, V_SINKR = 0, 8, 16, 24, 32, 40, 43, 45, 46, 47, 48, 49, 53
CB_ONES, CB_B96, CB_BS, CB_MD, CB_MP, CB_W = 0, 128, 224, 736, 1248, 1760
CF_ID, CF_P96, CF_P16, CF_W = 0, 128, 224, 240


class Buf:
    __slots__ = ("name", "t", "last_w", "readers", "dsem", "dcnt", "excl")

    def __init__(self, name, t, excl=False, init=()):
        self.name = name
        self.t = t
        self.last_w = []
        self.readers = list(init)
        self.dsem = None
        self.dcnt = 0
        self.excl = excl

    def __getitem__(self, idx):
        return self.t[idx]


def _compact(evs):
    d = {}
    for k, v in evs:
        if d.get(k, 0) < v:
            d[k] = v
    return list(d.items())


class FW:
    def __init__(self, nc):
        self.nc = nc
        self.streams = {e: [] for e in ENGS}
        self.esem = {}
        self.ecnt = {e: 0 for e in ENGS}
        self.seen = {e: {} for e in ENGS}
        self.sems = {}
        self.dma_bufs = []
        self._ctx = []
        self.free_events = []
        self.free_sems = []
        for e in ["pe", "act", "dve", "pool"]:
            self.esem[e] = self._newsem("e_" + e)

    def _newsem(self, name):
        cm = self.nc.semaphore(name)
        h = cm.__enter__()
        self._ctx.append((cm, None))
        self.sems[name] = h
        return name

    def mark(self):
        return len(self._ctx)

    def release(self, mark):
        ev = list(self.free_events)
        while len(self._ctx) > mark:
            cm, b = self._ctx.pop()
            if b is not None:
                ev.extend(b.last_w)
                ev.extend(b.readers)
                if b.dsem is not None:
                    ev.append((b.dsem, b.dcnt))
                cm.__exit__(None, None, None)
            else:
                self._keep.append((cm, b))
        self.free_events = _compact(ev)

    _keep = []

    def sbuf(self, name, shape, dtype):
        self._uid = getattr(self, "_uid", 0) + 1
        name = "s%d_%s" % (self._uid, name)
        cm = self.nc.sbuf_tensor(name, list(shape), dtype)
        t = cm.__enter__()
        b = Buf(name, t, init=self.free_events)
        self._ctx.append((cm, b))
        return b

    def psum(self, name, shape, dtype=F32):
        cm = self.nc.psum_tensor(name, list(shape), dtype)
        t = cm.__enter__()
        b = Buf(name, t, excl=True)
        self._ctx.append((cm, b))
        return b

    def _deps(self, reads, writes):
        ev = []
        for b in reads:
            ev.extend(b.last_w)
            if b.excl:
                ev.extend(b.readers)
        for b in writes:
            ev.extend(b.last_w)
            ev.extend(b.readers)
        return ev

    def _waits(self, eng, ev):
        need = {}
        for (k, v) in ev:
            if need.get(k, 0) < v:
                need[k] = v
        seen = self.seen[eng]
        out = []
        for k, v in need.items():
            if seen.get(k, 0) >= v:
                continue
            seen[k] = v
            out.append((k, v))
        return out

    def _record(self, reads, writes, event, nowaw=False):
        for b in reads:
            if b.excl:
                b.last_w = [event]
                b.readers = []
            else:
                b.readers.append(event)
                if len(b.readers) > 48:
                    b.readers = _compact(b.readers)
        for b in writes:
            if nowaw:
                b.last_w.append(event)
                b.last_w = _compact(b.last_w)
            else:
                b.last_w = [event]
            b.readers = []

    def op(self, eng, fn, reads=(), writes=()):
        ev = self._deps(reads, writes)
        waits = self._waits(eng, ev)
        self.ecnt[eng] += 1
        val = self.ecnt[eng]
        semname = self.esem[eng]
        if eng == "pe":
            self.seen[eng][semname] = val
        sems = self.sems

        def thunk(e, fn=fn, waits=waits, semname=semname):
            for (k, v) in waits:
                e.wait_ge(sems[k], v)
            fn(e).then_inc(sems[semname], 1)
        self.streams[eng].append(thunk)
        self._record(reads, writes, (semname, val))

    def dma(self, q, fn, reads=(), writes=(), own=None):
        if own is None:
            own = writes[0] if writes else reads[0]
        if own.dsem is None:
            cm = self.nc.semaphore("d_" + own.name)
            h = cm.__enter__()
            self._keep.append((cm, None))
            self.sems["d_" + own.name] = h
            own.dsem = "d_" + own.name
            self.dma_bufs.append(own)
        ev = []
        for b in reads:
            ev.extend(b.last_w)
        for b in writes:
            ev.extend([e for e in b.last_w if e[0] != own.dsem])
            ev.extend(b.readers)
        waits = self._waits(q, ev)
        own.dcnt += 16
        val = own.dcnt
        semname = own.dsem
        sems = self.sems

        def thunk(e, fn=fn, waits=waits, semname=semname):
            for (k, v) in waits:
                e.wait_ge(sems[k], v)
            fn(e).then_inc(sems[semname], 16)
        self.streams[q].append(thunk)
        self._record(reads, writes, (semname, val), nowaw=True)

    def finish(self):
        finals = [(b.dsem, b.dcnt) for b in self.dma_bufs]
        sems = self.sems
        nc = self.nc
        streams = self.streams
        with nc.Block() as block:
            @block.sync
            def _(e):
                for th in streams["sp"]:
                    th(e)
                for (k, v) in finals:
                    e.wait_ge(sems[k], v)

            @block.tensor
            def _(e):
                for th in streams["pe"]:
                    th(e)

            @block.scalar
            def _(e):
                for th in streams["act"]:
                    th(e)

            @block.vector
            def _(e):
                for th in streams["dve"]:
                    th(e)

            @block.gpsimd
            def _(e):
                for th in streams["pool"]:
                    th(e)
        while self._ctx:
            cm, b = self._ctx.pop()
            cm.__exit__(None, None, None)
        for cm, b in reversed(self._keep):
            cm.__exit__(None, None, None)
        FW._keep = []


class Ring:
    def __init__(self, items):
        self.items = items
        self.i = 0

    def next(self):
        b = self.items[self.i % len(self.items)]
        self.i += 1
        return b


def build_program(stop_after=None, npool=NPOOL):
    nc = bass.Bass("TRN2", target_bir_lowering=False)
    FW._keep = []
    f = FW(nc)

    def din(name, shape, dt=F32):
        return nc.dram_tensor(name, list(shape), dt, kind="ExternalInput").ap()

    def dout(name, shape, dt=F32):
        return nc.dram_tensor(name, list(shape), dt, kind="ExternalOutput").ap()

    x_p = din("x_p", [SEQ, D]); x_s = din("x_s", [NS, D])
    cache = din("cache", [npool * 128, D_CKV])
    ptab = din("ptab", [1, 512], I32)
    swk = din("swk", [4, 128, 256]); swv = din("swv", [4, 128, 256])
    w_a_in = din("w_a_in", [D, 672]); w_uq = din("w_uq", [D_QC, 1536])
    w_uk = din("w_uk", [D_C, 1024]); w_uv = din("w_uv", [D_C, 1024])
    w_a_out = din("w_a_out", [1024, D]); w_kv = din("w_kv", [D, 512])
    w_q_b = din("w_q_b", [D, 1024]); w_b_out = din("w_b_out", [1024, D])
    w_ffn_in = din("w_ffn_in", [2, D, 2 * D_FF]); w_ffn_out = din("w_ffn_out", [2, D_FF, D])
    vecs_d = din("vecs", [128, NV]); cf_d = din("cf32", [128, CF_W]); cb_d = din("cb32", [128, CB_W])
    tabA_d = din("tabA", [2, 96, NT]); tabB_d = din("tabB", [2, 16, NT])
    maskS_d = din("maskS", [64, 4 * 16]); maskB_d = din("maskB", [16, 4 * 144])

    y_p = dout("y_p", [SEQ, D]); y_s = dout("y_s", [NS, D])
    rows_p = dout("rows_p", [SEQ, D_CKV]); rows_s = dout("rows_s", [NS, D_CKV])
    wk_p = dout("wk_p", [128, 256]); wv_p = dout("wv_p", [128, 256])
    wk_s = dout("wk_s", [4, 128, 256]); wv_s = dout("wv_s", [4, 128, 256])

    def MM(pb, out, lhsT, rhs, start, stop, rd):
        f.op("pe", lambda e: e.matmul(out, lhsT=lhsT, rhs=rhs, start=start, stop=stop), reads=rd, writes=[pb])

    def TR(pb, out, in_, ident, rd):
        f.op("pe", lambda e: e.transpose(out=out, in_=in_, identity=ident), reads=rd, writes=[pb])

    def ACT(out, in_, func, rd, wr, **kw):
        f.op("act", lambda e: e.activation(out=out, in_=in_, func=func, **kw), reads=rd, writes=wr)

    def CP(eng, out, in_, rd, wr):
        if eng == "act":
            f.op("act", lambda e: e.copy(out=out, in_=in_), reads=rd, writes=wr)
        else:
            f.op(eng, lambda e: e.tensor_copy(out=out, in_=in_), reads=rd, writes=wr)

    def TT(eng, out, in0, in1, op, rd, wr):
        f.op(eng, lambda e: e.tensor_tensor(out=out, in0=in0, in1=in1, op=op), reads=rd, writes=wr)

    def STT(eng, out, in0, scalar, in1, op0, op1, rd, wr):
        f.op(eng, lambda e: e.scalar_tensor_tensor(out=out, in0=in0, scalar=scalar, in1=in1, op0=op0, op1=op1), reads=rd, writes=wr)

    def TS(eng, out, in0, s1, op0, rd, wr):
        f.op(eng, lambda e: e.tensor_scalar(out=out, in0=in0, scalar1=s1, scalar2=None, op0=op0), reads=rd, writes=wr)

    def MSET(eng, ap, val, wr):
        f.op(eng, lambda e: e.memset(ap, val), writes=wr)

    def LD(q, out, in_, wr, rd=()):
        f.dma(q, lambda e: e.dma_start(out=out, in_=in_), reads=list(rd), writes=list(wr))

    def ST(q, out, in_, rd):
        f.dma(q, lambda e: e.dma_start(out=out, in_=in_), reads=list(rd), writes=[])

    PS = [f.psum(f"ps{i}", [128, 512], F32) for i in range(8)]

    def bfv(pb):
        return pb.t[:, :].bitcast(BF16)

    hT = f.sbuf("hT", [128, 8, NT], F32)
    vecs = f.sbuf("vecs", [128, NV], F32)
    cf = f.sbuf("cf", [128, CF_W], F32)
    cb = f.sbuf("cb", [128, CB_W], BF16)
    idb = f.sbuf("idb", [128, 128], BF16)
    LD("sp", vecs[:, :], vecs_d[:, :], [vecs])
    LD("sp", cf[:, :], cf_d[:, :], [cf])
    LD("pool", cb[:, :], cb_d[:, :], [cb])
    CP("pool", idb[:, :], cf[:, CF_ID:CF_ID + 128], [cf], [idb])
    identf = cf.t[:, CF_ID:CF_ID + 128]
    ones_b = cb.t[:, CB_ONES:CB_ONES + 128]

    def vcol(c, p0=0, p1=128):
        return vecs.t[p0:p1, c:c + 1]

    def rstd_chain(pb, M, N, rs, scale):
        ACT(rs.t[0:M, 0:N], pb.t[0:M, 0:N], AF.Ln, [pb], [rs], scale=scale, bias=EPS)
        ACT(rs.t[0:M, 0:N], rs.t[0:M, 0:N], AF.Exp, [rs], [rs], scale=-0.5)

    def load_w(dst, src2d, kchunks, c0, ncols, prows=128):
        for j in range(kchunks):
            LD("pool", dst.t[0:prows, j, 0:ncols], src2d[j * prows:(j + 1) * prows, c0:c0 + ncols], [dst])

    def emit_output():
        ys_r = Ring([f.sbuf(f"ys{i}", [128, D], F32) for i in range(2)])
        for i in range(17):
            ys = ys_r.next()
            rows = 128 if i < 16 else NS
            for half in range(2):
                pb = PS[(2 * i + half) % 8]
                for q in range(4):
                    j = half * 4 + q
                    TR(pb, pb.t[0:rows, q * 128:(q + 1) * 128], hT.t[:, j, i * 128:i * 128 + rows], identf, [hT, cf])
                CP("act" if half == 0 else "dve", ys.t[0:rows, half * 512:(half + 1) * 512], pb.t[0:rows, 0:512], [pb], [ys])
            if i < 16:
                ST("sp", y_p[i * 128:(i + 1) * 128, :], ys.t[0:rows, :], [ys])
            else:
                ST("sp", y_s[:, :], ys.t[0:rows, :], [ys])


    m0 = f.mark()
    xs_ring = Ring([f.sbuf(f"xs{i}", [128, D], F32) for i in range(2)])
    for i in range(17):
        xs = xs_ring.next()
        rows = 128 if i < 16 else NS
        src = x_p[i * 128:(i + 1) * 128, :] if i < 16 else x_s[:, :]
        LD("sp", xs.t[0:rows, :], src, [xs])
        for half in range(2):
            pb = PS[(2 * i + half) % 8]
            for q in range(4):
                j = half * 4 + q
                TR(pb, pb.t[:, q * 128:q * 128 + rows], xs.t[0:rows, j * 128:(j + 1) * 128], identf[0:rows, 0:rows], [xs, cf])
            srcv = pb.t[:, :].rearrange("p (q c) -> p q c", q=4)[:, :, 0:rows]
            CP("act" if half == 0 else "dve", hT.t[:, half * 4:half * 4 + 4, i * 128:i * 128 + rows], srcv, [pb], [hT])
    f.release(m0)

    mA = f.mark()
    cqT = f.sbuf("cqT", [128, 3, NT], BF16)
    ckvT = f.sbuf("ckvT", [128, 2, NT], BF16)
    kpe_b = f.sbuf("kpe_b", [96, NT], BF16)
    rown_b = f.sbuf("rown_b", [NS, 256], BF16)
    qs_all = f.sbuf("qs_all", [96, H, NS], BF16)

    mA1 = f.mark()
    wain = f.sbuf("wain", [128, 8, 672], BF16)
    load_w(wain, w_a_in, 8, 0, 672)
    sq8 = f.sbuf("sq8", [128, 8, 512], BF16)
    xn = f.sbuf("xn", [128, 8, 512], BF16)
    rs_r = Ring([f.sbuf(f"rsA{i}", [128, 512], F32) for i in range(2)])
    ckv_f = f.sbuf("ckv_f", [128, 2, 512], F32)
    kpn = f.sbuf("kpn", [96, 512], F32)
    kpe_f = f.sbuf("kpe_f", [96, 512], F32)
    t1 = f.sbuf("t1A", [96, 512], F32)
    tabc = Ring([f.sbuf(f"tabcA{i}", [96, 2, 512], F32) for i in range(2)])
    rstage = Ring([f.sbuf(f"rstage{i}", [128, D_CKV], F32) for i in range(2)])
    for (c0, N) in CHUNKS:
        tb = tabc.next()
        LD("sp", tb.t[64:96, 0, 0:N], tabA_d[0, 64:96, c0:c0 + N], [tb])
        LD("sp", tb.t[64:96, 1, 0:N], tabA_d[1, 64:96, c0:c0 + N], [tb])
        ACT(sq8.t[:, :, 0:N], hT.t[:, :, c0:c0 + N], AF.Square, [hT], [sq8])
        pss = PS[6]
        for j in range(8):
            MM(pss, pss.t[:, 0:N], ones_b, sq8.t[:, j, 0:N], j == 0, j == 7, [cb, sq8])
        rs = rs_r.next()
        rstd_chain(pss, 128, N, rs, 1.0 / D)
        for j in range(8):
            STT("dve", xn.t[:, j, 0:N], hT.t[:, j, c0:c0 + N], vcol(V_GA + j), rs.t[:, 0:N], ALU.mult, ALU.mult, [hT, vecs, rs], [xn])
        mts = [(0, 128), (128, 128), (256, 128), (384, 128), (512, 128), (576, 96)]
        for mi, (mc, M) in enumerate(mts):
            pb = PS[mi]
            for j in range(8):
                MM(pb, pb.t[0:M, 0:N], wain.t[:, j, mc:mc + M], xn.t[:, j, 0:N], j == 0, j == 7, [wain, xn])
        for m in range(3):
            ACT(sq8.t[:, m, 0:N], PS[m].t[:, 0:N], AF.Square, [PS[m]], [sq8])
        for m in range(3):
            MM(pss, pss.t[:, 0:N], ones_b, sq8.t[:, m, 0:N], m == 0, m == 2, [cb, sq8])
        rs = rs_r.next()
        rstd_chain(pss, 128, N, rs, 1.0 / D_QC)
        for m in range(3):
            STT("dve", cqT.t[:, m, c0:c0 + N], PS[m].t[:, 0:N], vcol(V_GQC + m), rs.t[:, 0:N], ALU.mult, ALU.mult, [PS[m], vecs, rs], [cqT])
        for m in range(2):
            ACT(sq8.t[:, 3 + m, 0:N], PS[3 + m].t[:, 0:N], AF.Square, [PS[3 + m]], [sq8])
        for m in range(2):
            MM(pss, pss.t[:, 0:N], ones_b, sq8.t[:, 3 + m, 0:N], m == 0, m == 1, [cb, sq8])
        rs = rs_r.next()
        rstd_chain(pss, 128, N, rs, 1.0 / D_C)
        for m in range(2):
            STT("dve", ckv_f.t[:, m, 0:N], PS[3 + m].t[:, 0:N], vcol(V_GCKV + m), rs.t[:, 0:N], ALU.mult, ALU.mult, [PS[3 + m], vecs, rs], [ckv_f])
        CP("pool", ckvT.t[:, :, c0:c0 + N], ckv_f.t[:, :, 0:N], [ckv_f], [ckvT])
        ACT(sq8.t[0:96, 5, 0:N], PS[5].t[0:96, 0:N], AF.Square, [PS[5]], [sq8])
        MM(pss, pss.t[0:96, 0:N], cb.t[0:96, CB_B96:CB_B96 + 96], sq8.t[0:96, 5, 0:N], True, True, [cb, sq8])
        rs = rs_r.next()
        rstd_chain(pss, 96, N, rs, 1.0)
        STT("dve", kpn.t[0:96, 0:N], PS[5].t[0:96, 0:N], vcol(V_GK96, 0, 96), rs.t[0:96, 0:N], ALU.mult, ALU.mult, [PS[5], vecs, rs], [kpn])
        pr = PS[7]
        MM(pr, pr.t[0:96, 0:N], cf.t[0:96, CF_P96:CF_P96 + 96], kpn.t[0:96, 0:N], True, True, [cf, kpn])
        TT("pool", t1.t[64:96, 0:N], kpn.t[64:96, 0:N], tb.t[64:96, 0, 0:N], ALU.mult, [kpn, tb], [t1])
        TT("dve", kpe_f.t[64:96, 0:N], pr.t[64:96, 0:N], tb.t[64:96, 1, 0:N], ALU.mult, [pr, tb], [kpe_f])
        TT("dve", kpe_f.t[64:96, 0:N], kpe_f.t[64:96, 0:N], t1.t[64:96, 0:N], ALU.add, [kpe_f, t1], [kpe_f])
        CP("pool", kpe_b.t[64:96, c0:c0 + N], kpe_f.t[64:96, 0:N], [kpe_f], [kpe_b])
        ntile = (N + 127) // 128
        for ti in range(ntile):
            r = min(128, N - ti * 128)
            pbT = PS[7]
            for m in range(2):
                TR(pbT, pbT.t[0:r, m * 128:(m + 1) * 128], ckv_f.t[:, m, ti * 128:ti * 128 + r], identf, [ckv_f, cf])
            TR(pbT, pbT.t[0:r, 256:288], kpe_f.t[64:96, ti * 128:ti * 128 + r], cf.t[64:96, CF_ID + 64:CF_ID + 96], [kpe_f, cf])
            stg = rstage.next()
            CP("act", stg.t[0:r, :], pbT.t[0:r, 0:D_CKV], [pbT], [stg])
            if c0 < SEQ:
                ST("sp", rows_p[c0 + ti * 128:c0 + ti * 128 + r, :], stg.t[0:r, :], [stg])
            else:
                ST("sp", rows_s[:, :], stg.t[0:r, :], [stg])
                CP("pool", rown_b.t[0:NS, :], stg.t[0:NS, 0:256], [stg], [rown_b])
    f.release(mA1)
    if stop_after == "A1":
        f.release(mA)
        f.finish()
        return nc

    mA2 = f.mark()
    G = 2
    NG = H // G
    wq_r = Ring([f.sbuf(f"wq_g{i}", [128, 3, G * 96], BF16) for i in range(2)])
    wk_r = Ring([f.sbuf(f"wk_g{i}", [128, 2, G * 64], BF16) for i in range(2)])
    wv_r = Ring([f.sbuf(f"wv_g{i}", [128, 2, G * 64], BF16) for i in range(2)])
    wo_r = Ring([f.sbuf(f"wo_g{i}", [128, D], BF16) for i in range(2)])
    qT_g = f.sbuf("qT_g", [128, G, NT], BF16)
    KT_g = f.sbuf("KT_g", [128, G, NT], BF16)
    V_g = f.sbuf("V_g", [128, 16, G, 128], BF16)
    OT_r = Ring([f.sbuf(f"OT_g{i}", [128, SEQ], BF16) for i in range(2)])
    sq_r = Ring([f.sbuf(f"sqB{i}", [128, 512], BF16) for i in range(4)])
    rs_r = Ring([f.sbuf(f"rsB{i}", [96, 512], F32) for i in range(4)])
    qn_r = Ring([f.sbuf(f"qnB{i}", [96, 512], F32) for i in range(2)])
    t1_r = Ring([f.sbuf(f"t1B{i}", [96, 512], F32) for i in range(2)])
    t2_r = Ring([f.sbuf(f"t2B{i}", [96, 512], F32) for i in range(2)])
    pT_r = Ring([f.sbuf(f"pT{i}", [128, 512], BF16) for i in range(4)])
    rl_r = Ring([f.sbuf(f"rl{i}", [128, 512], F32) for i in range(2)])
    tabc = Ring([f.sbuf(f"tabcB{i}", [96, 2, 512], F32) for i in range(2)])
    MSET("pool", qT_g.t[:, :, :], 0.0, [qT_g])
    MSET("pool", KT_g.t[:, :, :], 0.0, [KT_g])
    MSET("pool", V_g.t[:, :, 0, 64:128], 1.0, [V_g])
    MSET("pool", V_g.t[:, :, 1, 0:64], 1.0, [V_g])
    ringP = Ring(PS[5:8])
    ringS = Ring(PS[0:3])
    ringO = Ring(PS[3:5])
    ringA = Ring(PS[0:8])
    maskD = cb.t[:, CB_MD:CB_MD + 128]
    B96 = cb.t[0:96, CB_B96:CB_B96 + 96]
    B64 = cb.t[0:64, CB_B96:CB_B96 + 64]
    P96 = cf.t[0:96, CF_P96:CF_P96 + 96]
    for g in range(NG):
        wq = wq_r.next(); wk = wk_r.next(); wv = wv_r.next(); wo = wo_r.next()
        OT_g = OT_r.next()
        if g % 2 == 0:
            pend = []
        pend.append((wo, OT_g))
        load_w(wq, w_uq, 3, g * G * 96, G * 96)
        load_w(wk, w_uk, 2, g * G * 64, G * 64)
        load_w(wv, w_uv, 2, g * G * 64, G * 64)
        LD("pool", wo.t[:, :], w_a_out[g * 128:(g + 1) * 128, :], [wo])
        for ci, (c0, N) in enumerate(CHUNKS):
            tb = tabc.next()
            LD("sp", tb.t[64:96, 0, 0:N], tabA_d[0, 64:96, c0:c0 + N], [tb])
            LD("sp", tb.t[64:96, 1, 0:N], tabA_d[1, 64:96, c0:c0 + N], [tb])
            chs = []
            for hl in range(G):
                bq = PS[hl]
                for m in range(3):
                    MM(bq, bq.t[0:96, 0:N], wq.t[:, m, hl * 96:(hl + 1) * 96], cqT.t[:, m, c0:c0 + N], m == 0, m == 2, [wq, cqT])
                chs.append(dict(q=True, hl=hl, b=bq, M=96))
            for hl in range(G):
                bk = PS[2 + hl]
                for m in range(2):
                    MM(bk, bk.t[0:64, 0:N], wk.t[:, m, hl * 64:(hl + 1) * 64], ckvT.t[:, m, c0:c0 + N], m == 0, m == 1, [wk, ckvT])
                chs.append(dict(q=False, hl=hl, b=bk, M=64))
            for c in chs:
                M = c["M"]
                c["sq"] = sq_r.next()
                ACT(c["sq"].t[0:M, 0:N], c["b"].t[0:M, 0:N], AF.Square, [c["b"]], [c["sq"]])
            for ic, c in enumerate(chs):
                M = c["M"]
                c["bs"] = PS[4 + ic]
                MM(c["bs"], c["bs"].t[0:M, 0:N], cb.t[0:M, CB_B96:CB_B96 + M], c["sq"].t[0:M, 0:N], True, True, [cb, c["sq"]])
            for c in chs:
                c["rs"] = rs_r.next()
                rstd_chain(c["bs"], c["M"], N, c["rs"], 1.0)
            if c0 < SEQ:
                for ti in range(ci * 4, ci * 4 + 4):
                    bv = PS[4 + ti % 4]
                    for m in range(2):
                        MM(bv, bv.t[:, 0:G * 64], ckvT.t[:, m, ti * 128:(ti + 1) * 128], wv.t[:, m, 0:G * 64], m == 0, m == 1, [ckvT, wv])
                    CP("dve" if ti % 2 else "act", V_g.t[:, ti, 0, 0:64], bv.t[:, 0:64], [bv], [V_g])
                    CP("act" if ti % 2 else "dve", V_g.t[:, ti, 1, 64:128], bv.t[:, 64:128], [bv], [V_g])
            for c in chs:
                hl = c["hl"]
                if c["q"]:
                    c["qn"] = qn_r.next()
                    STT("dve", c["qn"].t[0:96, 0:N], c["b"].t[0:96, 0:N], vcol(V_GQ96, 0, 96), c["rs"].t[0:96, 0:N], ALU.mult, ALU.mult, [c["b"], vecs, c["rs"]], [c["qn"]])
                    STT("dve", qT_g.t[0:64, hl, c0:c0 + N], c["b"].t[0:64, 0:N], vcol(V_GQ96, 0, 64), c["rs"].t[0:64, 0:N], ALU.mult, ALU.mult, [c["b"], vecs, c["rs"]], [qT_g])
                else:
                    STT("dve", KT_g.t[0:64, hl, c0:c0 + N], c["b"].t[0:64, 0:N], vcol(V_GK96, 0, 64), c["rs"].t[0:64, 0:N], ALU.mult, ALU.mult, [c["b"], vecs, c["rs"]], [KT_g])
                    CP("pool", KT_g.t[64:96, hl, c0:c0 + N], kpe_b.t[64:96, c0:c0 + N], [kpe_b], [KT_g])
            for c in chs:
                if c["q"]:
                    c["br"] = PS[4 + c["hl"]]
                    MM(c["br"], c["br"].t[0:96, 0:N], P96, c["qn"].t[0:96, 0:N], True, True, [cf, c["qn"]])
            for c in chs:
                if c["q"]:
                    hl = c["hl"]; qn = c["qn"]; br = c["br"]
                    t1 = t1_r.next(); t2 = t2_r.next()
                    TT("pool", t1.t[64:96, 0:N], qn.t[64:96, 0:N], tb.t[64:96, 0, 0:N], ALU.mult, [qn, tb], [t1])
                    TT("dve", t2.t[64:96, 0:N], br.t[64:96, 0:N], tb.t[64:96, 1, 0:N], ALU.mult, [br, tb], [t2])
                    TT("dve", qT_g.t[64:96, hl, c0:c0 + N], t1.t[64:96, 0:N], t2.t[64:96, 0:N], ALU.add, [t1, t2], [qT_g])
            if c0 >= SEQ:
                for hl in range(G):
                    CP("pool", qs_all.t[0:96, g * G + hl, :], qT_g.t[0:96, hl, SEQ:NT], [qT_g], [qs_all])
        iters = []
        for hl in range(G):
            for qc in range(4):
                nkt = 4 * qc + 4
                for kt in range(nkt):
                    iters.append(dict(hl=hl, qc=qc, kt=kt, nkt=nkt))

        def emitS(it):
            hl, qc, kt = it["hl"], it["qc"], it["kt"]
            n0 = max(qc * 512, kt * 128)
            W = qc * 512 + 512 - n0
            bs_ = ringS.next()
            MM(bs_, bs_.t[:, 0:W], KT_g.t[:, hl, kt * 128:(kt + 1) * 128], qT_g.t[:, hl, n0:n0 + W], True, True, [KT_g, qT_g])
            pT = pT_r.next()
            ACT(pT.t[:, 0:W], bs_.t[:, 0:W], AF.Exp, [bs_], [pT], scale=SCALE_A)
            if kt * 128 >= qc * 512:
                TT("pool", pT.t[:, 0:128], pT.t[:, 0:128], maskD, ALU.mult, [pT, cb], [pT])
            it["pT"] = pT; it["W"] = W; it["o0"] = n0 - qc * 512

        cur = {}

        def emitPV(it):
            hl, qc, kt, nkt = it["hl"], it["qc"], it["kt"], it["nkt"]
            if kt == 0:
                cur["bo"] = ringO.next()
            bo = cur["bo"]
            MM(bo, bo.t[:, it["o0"]:512], V_g.t[:, kt, hl, :], it["pT"].t[:, 0:it["W"]], kt == 0, kt == nkt - 1, [V_g, it["pT"]])
            if kt == nkt - 1:
                rl = rl_r.next()
                oL, oH = (0, 64) if hl == 0 else (64, 128)
                lL, lH = (64, 128) if hl == 0 else (0, 64)
                f.op("dve", lambda e, rl=rl, bo=bo, lL=lL, lH=lH: e.reciprocal(out=rl.t[lL:lH, :], in_=bo.t[lL:lH, :]), reads=[bo], writes=[rl])
                TT("dve", OT_g.t[oL:oH, qc * 512:(qc + 1) * 512], bo.t[oL:oH, :], rl.t[lL:lH, :], ALU.mult, [bo, rl], [OT_g])
        DEPTH = 2
        for idx in range(len(iters) + DEPTH):
            if idx < len(iters):
                emitS(iters[idx])
            if FILLER:
                MM(PS[7], PS[7].t[:, 0:FILLER], KT_g.t[0:96, 0, 0:128], qT_g.t[0:96, 0, 0:FILLER], True, True, [KT_g, qT_g])
            if idx - DEPTH >= 0:
                emitPV(iters[idx - DEPTH])
        if g % 2 == 1:
            for qc in range(4):
                c0 = qc * 512
                for j in range(8):
                    pb = ringP.next()
                    for ip, (wo_, OT_) in enumerate(pend):
                        MM(pb, pb.t[:, 0:512], wo_.t[:, j * 128:(j + 1) * 128], OT_.t[:, c0:c0 + 512], ip == 0, ip == len(pend) - 1, [wo_, OT_])
                    TT("dve", hT.t[:, j, c0:c0 + 512], pb.t[:, 0:512], hT.t[:, j, c0:c0 + 512], ALU.add, [pb, hT], [hT])
    f.release(mA2)
    if stop_after == "A2":
        f.release(mA)
        emit_output()
        f.finish()
        return nc

    mA3 = f.mark()
    wuk = f.sbuf("wuk", [128, 2, 1024], BF16)
    wuv = f.sbuf("wuv", [128, 2, 1024], BF16)
    load_w(wuk, w_uk, 2, 0, 1024)
    load_w(wuv, w_uv, 2, 0, 1024)
    OTs = f.sbuf("OTs", [64, H, NS], BF16)
    wukT = f.sbuf("wukT", [64, H, 256], BF16)
    qabsT = f.sbuf("qabsT", [128, 2, 4, 64], BF16)
    qpeT = f.sbuf("qpeT", [96, 4, 64], BF16)
    maskS = f.sbuf("maskS", [64, 4, 16], F32)
    LD("sp", maskS.t[:, :, :], maskS_d.rearrange("p (b k) -> p b k", b=4), [maskS])
    pti = f.sbuf("pti", [128, 512], I32)
    ptf = f.sbuf("ptf", [128, 512], F32)
    iot = f.sbuf("iot", [128, 1], I32)
    iof = f.sbuf("iof", [128, 1], F32)
    ridx = f.sbuf("ridx", [128, 512], I32)
    LD("sp", pti.t[:, :], ptab[0:1, :].partition_broadcast(128), [pti])
    f.op("pool", lambda e: e.iota(iot.t[:, :], pattern=[[0, 1]], base=0, channel_multiplier=1), writes=[iot])
    CP("dve", iof.t[:, :], iot.t[:, :], [iot], [iof])
    CP("dve", ptf.t[:, :], pti.t[:, :], [pti], [ptf])
    f.op("dve", lambda e: e.tensor_scalar(out=ptf.t[:, :], in0=ptf.t[:, :], scalar1=128.0, scalar2=iof.t[:, 0:1], op0=ALU.mult, op1=ALU.add), reads=[ptf, iof], writes=[ptf])
    CP("dve", ridx.t[:, :], ptf.t[:, :], [ptf], [ridx])
    for hb in range(4):
        pb = PS[hb]
        for hq in range(4):
            for m in range(2):
                slot = hq * 2 + m
                TR(pb, bfv(pb)[0:64, slot * 128:(slot + 1) * 128], wuk.t[:, m, (hb * 4 + hq) * 64:(hb * 4 + hq + 1) * 64], idb.t[:, :], [wuk, idb])
        CP("dve" if hb % 2 else "act", wukT.t[0:64, hb * 4:(hb + 1) * 4, :], bfv(pb)[0:64, 0:1024].rearrange("p (h l) -> p h l", h=4), [pb], [wukT])
    qsg = f.sbuf("qsg", [64, H, NS], BF16)
    TS("dve", qsg.t[0:64, :, :], qs_all.t[0:64, :, :], vcol(V_GK96, 0, 64), ALU.mult, [qs_all, vecs], [qsg])
    pb = PS[4]
    for m in range(2):
        for hh in range(H):
            col = (m * H + hh) * NS
            MM(pb, pb.t[:, col:col + NS], wukT.t[0:64, hh, m * 128:(m + 1) * 128], qsg.t[0:64, hh, :], True, True, [wukT, qsg])
    for m in range(2):
        CP("dve", qabsT.t[:, m, :, :].rearrange("p b (h t) -> p b h t", h=H),
           pb.t[:, m * 256:(m + 1) * 256].rearrange("p (h b t) -> p b h t", h=H, b=4), [pb], [qabsT])
    CP("pool", qpeT.t[64:96, :, :].rearrange("p b (h t) -> p b h t", h=H),
       qs_all.t[64:96, :, :].rearrange("p h (b t) -> p b h t", b=4), [qs_all], [qpeT])
    BS = cb.t[:, CB_BS:CB_BS + 512].rearrange("p (j c) -> p j c", j=8)
    mA3b = f.mark()
    rowsb_r = Ring([f.sbuf(f"rowsb{i}", [128, 4, D_CKV], BF16) for i in range(8)])
    cT_r = Ring([f.sbuf(f"cT{i}", [128, 2, 512], BF16) for i in range(3)])
    kpT_r = Ring([f.sbuf(f"kpT{i}", [96, 512], BF16) for i in range(3)])
    sqk_r = Ring([f.sbuf(f"sqk{i}", [128, 512], BF16) for i in range(3)])
    rs_r3 = Ring([f.sbuf(f"rsS{i}", [64, 512], F32) for i in range(2)])
    sc_r3 = Ring([f.sbuf(f"scS{i}", [64, 512], F32) for i in range(2)])
    pS_r3 = Ring([f.sbuf(f"pS{i}", [64, 512], BF16) for i in range(2)])
    pTs_r3 = Ring([f.sbuf(f"pTs{i}", [128, 4, 64], BF16) for i in range(2)])
    tmp_r3 = Ring([f.sbuf(f"tmpS{i}", [64, 8], F32) for i in range(2)])
    m_run = f.sbuf("m_run", [64, 1], F32)
    l_run = f.sbuf("l_run", [64, 2], F32)
    acc = f.sbuf("accS", [64, 256], F32)
    accn = f.sbuf("accn", [64, 256], BF16)
    olT = f.sbuf("olT", [128, 2, 64], BF16)
    bX, bY, bK0, bK1, bSS, bN, bP, bT2 = PS
    ringK = Ring([bK0, bK1])

    units = []
    for b in range(4):
        for gi in range(32):
            units.append(dict(b=b, gi=gi, N=512, first=(gi == 0), last=False))
        units.append(dict(b=b, gi=None, N=NS, first=False, last=True))

    def stageG(u):
        if u["gi"] is None:
            return
        rb = rowsb_r.next()
        u["rb"] = rb
        for pg in range(4):
            col = u["b"] * 128 + u["gi"] * 4 + pg
            f.dma("pool", lambda e, rb=rb, pg=pg, col=col: e.indirect_dma_start(
                out=rb.t[:, pg, :], out_offset=None, in_=cache[:, :],
                in_offset=bass.IndirectOffsetOnAxis(ap=ridx.t[:, col:col + 1], axis=0)), reads=[ridx], writes=[rb])

    def stageA_tr(u, pg):
        if u["gi"] is None:
            return
        rb = u["rb"]
        for m in range(2):
            slot = m * 4 + pg
            TR(bX, bfv(bX)[:, slot * 128:(slot + 1) * 128], rb.t[:, pg, m * 128:(m + 1) * 128], idb.t[:, :], [rb, idb])
        TR(bY, bfv(bY)[0:96, pg * 128:(pg + 1) * 128], rb.t[:, pg, 192:288], idb.t[:, :], [rb, idb])

    def stageA_cp(u):
        if u["gi"] is None:
            u["cT"] = lambda m: ckvT.t[:, m, SEQ:NT]
            u["kpT"] = kpe_b.t[64:96, SEQ:NT]
            u["nat"] = [(rown_b.t[0:NS, :], NS, 0)]
            u["cbufs"] = [ckvT, kpe_b, rown_b]
            u["mask"] = maskS.t[:, u["b"], :]
            return
        rb = u["rb"]
        cT = cT_r.next(); kpT = kpT_r.next()
        CP("dve", cT.t[:, :, :].rearrange("p m n -> p (m n)"), bfv(bX)[:, 0:1024], [bX], [cT])
        CP("dve", kpT.t[64:96, :], bfv(bY)[64:96, 0:512], [bY], [kpT])
        u["cT"] = lambda m, cT=cT: cT.t[:, m, :]
        u["kpT"] = kpT.t[64:96, :]
        u["nat"] = [(rb.t[:, pg, 0:256], 128, pg * 128) for pg in range(4)]
        u["cbufs"] = [cT, kpT, rb]
        u["mask"] = None

    def stageA(u):
        for pg in range(4):
            stageA_tr(u, pg)
        stageA_cp(u)

    def stageB_step(u, k):
        N = u["N"]; cT_ap = u["cT"]; cbufs = u["cbufs"]; b = u["b"]

        def kr(jc):
            bk = ringK.next()
            for m in range(2):
                MM(bk, bk.t[:, 0:N], wuk.t[:, m, jc * 128:(jc + 1) * 128], cT_ap(m), m == 0, m == 1, [wuk] + cbufs)
            sqk = sqk_r.next()
            ACT(sqk.t[:, 0:N], bk.t[:, 0:N], AF.Square, [bk], [sqk])
            u.setdefault("sq", {})[jc] = sqk

        def ss(jc):
            sqk = u["sq"][jc]
            MM(bSS, bSS.t[0:64, 0:N], BS[:, jc, :], sqk.t[:, 0:N], jc == 0, jc == 7, [cb, sqk])
        if k == 0:
            kr(0); kr(1)
        elif k < 7:
            ss(k - 1); kr(k + 1)
            if k == 1:
                for m in range(2):
                    MM(bN, bN.t[0:64, 0:N], qabsT.t[:, m, b, :], cT_ap(m), m == 0, m == 1, [qabsT] + cbufs)
                MM(bP, bP.t[0:64, 0:N], qpeT.t[64:96, b, :], u["kpT"], True, True, [qpeT] + cbufs)
        else:
            ss(6); ss(7)

    def stageC1(u):
        N = u["N"]
        rs = rs_r3.next(); sc = sc_r3.next()
        rstd_chain(bSS, 64, N, rs, 1.0)
        TT("dve", sc.t[0:64, 0:N], bN.t[0:64, 0:N], rs.t[0:64, 0:N], ALU.mult, [bN, rs], [sc])
        TT("dve", sc.t[0:64, 0:N], bP.t[0:64, 0:N], sc.t[0:64, 0:N], ALU.add, [bP, sc], [sc])
        if u["mask"] is not None:
            TT("dve", sc.t[0:64, 0:N], sc.t[0:64, 0:N], u["mask"], ALU.add, [sc, maskS], [sc])
        u["sc"] = sc

    def stageC2a(u):
        N = u["N"]; sc = u["sc"]; nat = u["nat"]; cbufs = u["cbufs"]; b = u["b"]
        if u["first"]:
            MSET("pool", m_run.t[:, :], NEG, [m_run])
            MSET("pool", l_run.t[:, 0:1], 0.0, [l_run])
            MSET("pool", acc.t[:, :], 0.0, [acc])
        tmp = tmp_r3.next(); pS = pS_r3.next(); pTs = pTs_r3.next()

        def tc(i):
            return tmp.t[0:64, i:i + 1]
        MSET("dve", tc(4), 0.0, [tmp])
        f.op("dve", lambda e: e.tensor_reduce(out=tc(0), in_=sc.t[0:64, 0:N], axis=AX.X, op=ALU.max), reads=[sc], writes=[tmp])
        TT("dve", tc(1), m_run.t[:, 0:1], tc(0), ALU.max, [m_run, tmp], [tmp])
        TS("dve", tc(2), tc(1), -SCALE_A, ALU.mult, [tmp], [tmp])
        u["tmp"] = tmp; u["pS"] = pS; u["pTs"] = pTs

    def stageC2a2(u):
        N = u["N"]; sc = u["sc"]
        tmp = u["tmp"]; pS = u["pS"]

        def tc(i):
            return tmp.t[0:64, i:i + 1]
        ACT(tc(3), m_run.t[:, 0:1], AF.Exp, [m_run, tmp], [tmp], scale=SCALE_A, bias=tc(2))
        ACT(pS.t[0:64, 0:N], sc.t[0:64, 0:N], AF.Exp, [sc, tmp], [pS, tmp], scale=SCALE_A, bias=tc(2), accum_out=tc(4))
        STT("dve", l_run.t[:, 0:1], l_run.t[:, 0:1], tc(3), tc(4), ALU.mult, ALU.add, [l_run, tmp], [l_run])
        CP("dve", m_run.t[:, 0:1], tc(1), [tmp], [m_run])

    def stageC2b(u):
        N = u["N"]; nat = u["nat"]; cbufs = u["cbufs"]; b = u["b"]
        tmp = u["tmp"]; pS = u["pS"]; pTs = u["pTs"]

        def tc(i):
            return tmp.t[0:64, i:i + 1]
        npg = len(nat)
        for pg, (rows_ap, K, col0) in enumerate(nat):
            TR(bT2, bfv(bT2)[0:K, pg * 64:(pg + 1) * 64], pS.t[0:64, col0:col0 + K], idb.t[0:64, 0:64], [pS, idb])
        Kmax = max(K for (_, K, _) in nat)
        CP("dve", pTs.t[0:Kmax, 0:npg, :], bfv(bT2)[0:Kmax, 0:npg * 64].rearrange("p (g c) -> p g c", g=npg), [bT2], [pTs])

    def stageC2c(u):
        N = u["N"]; nat = u["nat"]; cbufs = u["cbufs"]; b = u["b"]
        tmp = u["tmp"]; pS = u["pS"]; pTs = u["pTs"]

        def tc(i):
            return tmp.t[0:64, i:i + 1]
        npg = len(nat)
        for pg, (rows_ap, K, col0) in enumerate(nat):
            MM(bT2, bT2.t[0:64, 256:512], pTs.t[0:K, pg, :], rows_ap, pg == 0, pg == npg - 1, [pTs] + cbufs)
        STT("dve", acc.t[:, :], acc.t[:, :], tc(3), bT2.t[0:64, 256:512], ALU.mult, ALU.add, [acc, tmp, bT2], [acc])
        if u["last"]:
            f.op("dve", lambda e: e.reciprocal(out=l_run.t[:, 1:2], in_=l_run.t[:, 0:1]), reads=[l_run], writes=[l_run])
            TS("dve", accn.t[:, :], acc.t[:, :], l_run.t[:, 1:2], ALU.mult, [acc, l_run], [accn])
            for m in range(2):
                TR(bT2, bfv(bT2)[:, m * 64:(m + 1) * 64], accn.t[0:64, m * 128:(m + 1) * 128], idb.t[0:64, 0:64], [accn, idb])
            CP("act", olT.t[:, :, :], bfv(bT2)[:, 0:128].rearrange("p (m c) -> p m c", m=2), [bT2], [olT])
            for hh in range(H):
                for m in range(2):
                    MM(bT2, bT2.t[0:64, 256 + hh * 4:256 + (hh + 1) * 4], wuv.t[:, m, hh * 64:(hh + 1) * 64], olT.t[:, m, hh * 4:(hh + 1) * 4], m == 0, m == 1, [wuv, olT])
            CP("dve", OTs.t[0:64, :, b * 4:(b + 1) * 4], bT2.t[0:64, 256:320].rearrange("p (h t) -> p h t", h=H), [bT2], [OTs])

    nu = len(units)
    for i in range(min(4, nu)):
        stageG(units[i])
    stageA(units[0]); stageA(units[1])
    for k in range(8):
        stageB_step(units[0], k)
    stageC1(units[0])
    for i in range(nu):
        u0 = units[i]
        u1 = units[i + 1] if i + 1 < nu else None
        u2 = units[i + 2] if i + 2 < nu else None
        if i + 4 < nu:
            stageG(units[i + 4])
        for k in range(8):
            if u1 is not None:
                stageB_step(u1, k)
            if k < 4 and u2 is not None:
                stageA_tr(u2, k)
            if k == 0:
                stageC2a(u0)
            if k == 3 and u2 is not None:
                stageA_cp(u2)
            if k == 4:
                stageC2a2(u0)
            if k == 6:
                stageC2b(u0)
            if k == 7:
                stageC2c(u0)
        if u1 is not None:
            stageC1(u1)
    f.release(mA3b)
    wao = f.sbuf("wao", [64, H, D], BF16)
    for hh in range(H):
        LD("pool", wao.t[0:64, hh, :], w_a_out[hh * 64:(hh + 1) * 64, :], [wao])
    for j in range(8):
        pb = ringK.next()
        for hh in range(H):
            MM(pb, pb.t[:, 0:NS], wao.t[0:64, hh, j * 128:(j + 1) * 128], OTs.t[0:64, hh, :], hh == 0, hh == H - 1, [wao, OTs])
        TT("dve", hT.t[:, j, SEQ:NT], pb.t[:, 0:NS], hT.t[:, j, SEQ:NT], ALU.add, [pb, hT], [hT])
    f.release(mA3)
    f.release(mA)
    if stop_after == "A3":
        emit_output()
        f.finish()
        return nc

    def ffn(l):
        mF = f.mark()
        xnT = f.sbuf("xnT", [128, 8, NT], BF16)
        sq8 = f.sbuf("sq8F", [128, 8, 512], BF16)
        rs = f.sbuf("rsF", [128, 512], F32)
        gcol = V_GFA if l == 0 else V_GFB
        for (c0, N) in CHUNKS:
            ACT(sq8.t[:, :, 0:N], hT.t[:, :, c0:c0 + N], AF.Square, [hT], [sq8])
            pss = PS[7]
            for j in range(8):
                MM(pss, pss.t[:, 0:N], ones_b, sq8.t[:, j, 0:N], j == 0, j == 7, [cb, sq8])
            rstd_chain(pss, 128, N, rs, 1.0 / D)
            for j in range(8):
                STT("dve", xnT.t[:, j, c0:c0 + N], hT.t[:, j, c0:c0 + N], vcol(gcol + j), rs.t[:, 0:N], ALU.mult, ALU.mult, [hT, vecs, rs], [xnT])
        wg_r = Ring([f.sbuf(f"wg{i}", [128, 8, 512], BF16) for i in range(2)])
        wu_r = Ring([f.sbuf(f"wu{i}", [128, 8, 512], BF16) for i in range(2)])
        wo_r2 = Ring([f.sbuf(f"wo{i}", [128, 4, D], BF16) for i in range(2)])
        uT_r = Ring([f.sbuf(f"uT{i}", [128, 4, 512], BF16) for i in range(2)])
        sg_r = Ring([f.sbuf(f"sg{i}", [128, 512], F32) for i in range(2)])
        ring = Ring(PS[0:8])
        nblk = (D_FF + 511) // 512
        for hb in range(nblk):
            h0 = hb * 512
            HW = min(512, D_FF - h0)
            nhc = HW // 128
            wg = wg_r.next(); wu = wu_r.next(); wo = wo_r2.next()
            load_w(wg, w_ffn_in[l], 8, h0, HW)
            load_w(wu, w_ffn_in[l], 8, D_FF + h0, HW)
            for hc in range(nhc):
                LD("pool", wo.t[:, hc, :], w_ffn_out[l, h0 + hc * 128:h0 + (hc + 1) * 128, :], [wo])
            for (c0, N) in CHUNKS:
                uT = uT_r.next()
                for hc in range(nhc):
                    pg_ = ring.next()
                    for j in range(8):
                        MM(pg_, pg_.t[:, 0:N], wg.t[:, j, hc * 128:(hc + 1) * 128], xnT.t[:, j, c0:c0 + N], j == 0, j == 7, [wg, xnT])
                    pu_ = ring.next()
                    for j in range(8):
                        MM(pu_, pu_.t[:, 0:N], wu.t[:, j, hc * 128:(hc + 1) * 128], xnT.t[:, j, c0:c0 + N], j == 0, j == 7, [wu, xnT])
                    sg = sg_r.next()
                    ACT(sg.t[:, 0:N], pg_.t[:, 0:N], AF.Silu, [pg_], [sg])
                    TT("dve", uT.t[:, hc, 0:N], pu_.t[:, 0:N], sg.t[:, 0:N], ALU.mult, [pu_, sg], [uT])
                for j in range(8):
                    po = ring.next()
                    for hc in range(nhc):
                        MM(po, po.t[:, 0:N], wo.t[:, hc, j * 128:(j + 1) * 128], uT.t[:, hc, 0:N], hc == 0, hc == nhc - 1, [wo, uT])
                    TT("dve", hT.t[:, j, c0:c0 + N], po.t[:, 0:N], hT.t[:, j, c0:c0 + N], ALU.add, [po, hT], [hT])
        f.release(mF)

    ffn(0)
    if stop_after == "FA":
        emit_output()
        f.finish()
        return nc

    mB = f.mark()
    wkv = f.sbuf("wkv", [128, 8, 512], BF16)
    load_w(wkv, w_kv, 8, 0, 512)
    wq_r = Ring([f.sbuf("wqh0", [128, 8, 512], BF16)])
    wbo_r = Ring([f.sbuf("wboh0", [64, 8, D], BF16)])
    esk = f.sbuf("esk", [128, H], F32)
    ACT(esk.t[:, :], vecs.t[:, V_SINKR:V_SINKR + H], AF.Exp, [vecs], [esk])
    maskB = f.sbuf("maskB", [16, 4, 144], F32)
    LD("sp", maskB.t[:, :, :], maskB_d.rearrange("p (b k) -> p b k", b=4), [maskB])
    KB_c = f.sbuf("KB_c", [64, 4, 640], BF16)
    VB_c = f.sbuf("VB_c", [128, 5, 4, 128], BF16)
    MSET("pool", VB_c.t[:, :, :, 64:128], 1.0, [VB_c])
    QB_c = f.sbuf("QB_c", [64, 4, 8, 128], BF16)
    OB_c = f.sbuf("OB_c", [64, 8, 512], BF16)
    QBs = f.sbuf("QBs", [64, 4, 8, 4], BF16)
    OBs = f.sbuf("OBs", [64, 8, NS], BF16)
    kn_s = f.sbuf("kn_s", [64, 4, NS], F32)
    kn_sb = f.sbuf("kn_sb", [64, 4, NS], BF16)
    VBs_f = f.sbuf("VBs_f", [NS, 256], F32)
    VBs_b = f.sbuf("VBs_b", [NS, 256], BF16)
    sqx = f.sbuf("sqxB", [128, 8, 512], BF16)
    rs0 = f.sbuf("rs0B", [128, 512], F32)
    xkv = f.sbuf("xkv", [128, 8, 512], BF16)
    sq_r = Ring([f.sbuf(f"sqC{i}", [64, 512], BF16) for i in range(4)])
    rs_r = Ring([f.sbuf(f"rsC{i}", [64, 512], F32) for i in range(4)])
    kn_r = Ring([f.sbuf(f"knC{i}", [64, 512], F32) for i in range(4)])
    t1_r = Ring([f.sbuf(f"t1C{i}", [16, 512], F32) for i in range(2)])
    t2_r = Ring([f.sbuf(f"t2C{i}", [16, 512], F32) for i in range(2)])
    pT_r = Ring([f.sbuf(f"pTC{i}", [128, 512], BF16) for i in range(4)])
    lt_r = Ring([f.sbuf(f"ltC{i}", [128, 512], F32) for i in range(1)])
    tabc = Ring([f.sbuf(f"tabcC{i}", [16, 2, 512], F32) for i in range(2)])
    kvst = f.sbuf("kvst", [128, 2, 256], F32)
    ringP = Ring(PS[5:8])
    ringS = Ring(PS[0:3])
    ringO = Ring(PS[3:5])
    B64 = cb.t[0:64, CB_B96:CB_B96 + 64]
    P16 = cf.t[0:16, CF_P16:CF_P16 + 16]
    maskD4 = cb.t[:, CB_MD:CB_MD + 512]
    maskP4 = cb.t[:, CB_MP:CB_MP + 512]

    def proj_batch(specs, N, tb):
        n = len(specs)
        for i, sp in enumerate(specs):
            for j in range(8):
                MM(PS[i], PS[i].t[0:64, 0:N], sp["lhsT"](j), sp["rhs"](j), j == 0, j == 7, sp["rd"])
        sqs, rss, kns = [], [], []
        for i in range(n):
            sq = sq_r.next(); sqs.append(sq)
            ACT(sq.t[0:64, 0:N], PS[i].t[0:64, 0:N], AF.Square, [PS[i]], [sq])
        for i in range(n):
            MM(PS[4 + i], PS[4 + i].t[0:64, 0:N], B64, sqs[i].t[0:64, 0:N], True, True, [cb, sqs[i]])
        for i in range(n):
            rs = rs_r.next(); rss.append(rs)
            rstd_chain(PS[4 + i], 64, N, rs, 1.0)
        for i, sp in enumerate(specs):
            kn = kn_r.next(); kns.append(kn)
            if sp.get("dst") is not None:
                dap, dbuf = sp["dst"]
                vw = lambda ap: ap.rearrange("p (t q) -> p t q", t=4)
                STT("dve", dap(0, 64), vw(PS[i].t[0:64, 0:N]), vcol(sp["gcol"], 0, 64), vw(rss[i].t[0:64, 0:N]), ALU.mult, ALU.mult, [PS[i], vecs, rss[i]], [dbuf])
                STT("dve", kn.t[0:16, 0:N], PS[i].t[0:16, 0:N], vcol(sp["gcol"], 0, 16), rss[i].t[0:16, 0:N], ALU.mult, ALU.mult, [PS[i], vecs, rss[i]], [kn])
            else:
                STT("dve", kn.t[0:64, 0:N], PS[i].t[0:64, 0:N], vcol(sp["gcol"], 0, 64), rss[i].t[0:64, 0:N], ALU.mult, ALU.mult, [PS[i], vecs, rss[i]], [kn])
        for i in range(n):
            MM(PS[4 + i], PS[4 + i].t[0:16, 0:N], P16, kns[i].t[0:16, 0:N], True, True, [cf, kns[i]])
        for i, sp in enumerate(specs):
            kn = kns[i]
            t1 = t1_r.next(); t2 = t2_r.next()
            TT("pool", t1.t[0:16, 0:N], kn.t[0:16, 0:N], tb.t[0:16, 0, 0:N], ALU.mult, [kn, tb], [t1])
            TT("dve", t2.t[0:16, 0:N], PS[4 + i].t[0:16, 0:N], tb.t[0:16, 1, 0:N], ALU.mult, [PS[4 + i], tb], [t2])
            if sp.get("dst") is not None:
                dap, dbuf = sp["dst"]
                vw = lambda ap: ap.rearrange("p (t q) -> p t q", t=4)
                TT("dve", dap(0, 16), vw(t1.t[0:16, 0:N]), vw(t2.t[0:16, 0:N]), ALU.add, [t1, t2], [dbuf])
            else:
                TT("dve", kn.t[0:16, 0:N], t1.t[0:16, 0:N], t2.t[0:16, 0:N], ALU.add, [t1, t2], [kn])
                sp["consume"](kn)

    for ci, (c0, N) in enumerate(CHUNKS):
        samp = c0 >= SEQ
        tb = tabc.next()
        LD("sp", tb.t[0:16, 0, 0:N], tabB_d[0, :, c0:c0 + N], [tb])
        LD("sp", tb.t[0:16, 1, 0:N], tabB_d[1, :, c0:c0 + N], [tb])
        ACT(sqx.t[:, :, 0:N], hT.t[:, :, c0:c0 + N], AF.Square, [hT], [sqx])
        pss = ringP.next()
        for j in range(8):
            MM(pss, pss.t[:, 0:N], ones_b, sqx.t[:, j, 0:N], j == 0, j == 7, [cb, sqx])
        rstd_chain(pss, 128, N, rs0, 1.0 / D)
        xnB = sqx
        for j in range(8):
            STT("dve", xkv.t[:, j, 0:N], hT.t[:, j, c0:c0 + N], vcol(V_GKV + j), rs0.t[:, 0:N], ALU.mult, ALU.mult, [hT, vecs, rs0], [xkv])
            STT("dve", xnB.t[:, j, 0:N], hT.t[:, j, c0:c0 + N], vcol(V_GB + j), rs0.t[:, 0:N], ALU.mult, ALU.mult, [hT, vecs, rs0], [xnB])
        kn_keep = {}

        def k_consume(kvh):
            def fn(kn):
                if not samp:
                    CP("pool", KB_c.t[0:64, kvh, 128:128 + N], kn.t[0:64, 0:N], [kn], [KB_c])
                    kn_keep[kvh] = kn
                else:
                    CP("pool", kn_s.t[0:64, kvh, :], kn.t[0:64, 0:NS], [kn], [kn_s])
                    CP("pool", kn_sb.t[0:64, kvh, :], kn.t[0:64, 0:NS], [kn], [kn_sb])
            return fn
        proj_batch([dict(lhsT=(lambda j, kvh=kvh: wkv.t[:, j, kvh * 64:(kvh + 1) * 64]), rhs=(lambda j: xkv.t[:, j, 0:N]),
                         rd=[wkv, xkv], gcol=V_GKB, consume=k_consume(kvh)) for kvh in range(N_KV)], N, tb)
        if ci == 3:
            for kvh in range(N_KV):
                pbT = ringP.next()
                TR(pbT, pbT.t[:, 0:64], kn_keep[kvh].t[0:64, 384:512], cf.t[0:64, CF_ID:CF_ID + 64], [kn_keep[kvh], cf])
                CP("act", kvst.t[:, 0, kvh * 64:(kvh + 1) * 64], pbT.t[:, 0:64], [pbT], [kvst])
            ST("sp", wk_p[:, :], kvst.t[:, 0, :], [kvst])
        if not samp:
            for ti in range(4):
                bv = ringP.next()
                for j in range(8):
                    MM(bv, bv.t[:, 0:256], xkv.t[:, j, ti * 128:(ti + 1) * 128], wkv.t[:, j, 256:512], j == 0, j == 7, [xkv, wkv])
                CP("act", VB_c.t[:, 1 + ti, :, 0:64], bv.t[:, 0:256].rearrange("p (k d) -> p k d", k=4), [bv], [VB_c])
                if ci == 3 and ti == 3:
                    CP("dve", kvst.t[:, 1, :], bv.t[:, 0:256], [bv], [kvst])
                    ST("sp", wv_p[:, :], kvst.t[:, 1, :], [kvst])
        else:
            bv = ringP.next()
            for j in range(8):
                MM(bv, bv.t[0:NS, 0:256], xkv.t[:, j, 0:NS], wkv.t[:, j, 256:512], j == 0, j == 7, [xkv, wkv])
            CP("act", VBs_f.t[:, :], bv.t[0:NS, 0:256], [bv], [VBs_f])
            CP("dve", VBs_b.t[:, :], bv.t[0:NS, 0:256], [bv], [VBs_b])
            pbT = ringP.next()
            for kvh in range(N_KV):
                TR(pbT, pbT.t[0:NS, kvh * 64:(kvh + 1) * 64], kn_s.t[0:64, kvh, :], cf.t[0:64, CF_ID:CF_ID + 64], [kn_s, cf])
            ktok = f.sbuf("ktok", [NS, 256], F32)
            CP("act", ktok.t[:, :], pbT.t[0:NS, 0:256], [pbT], [ktok])
            kw_all = f.sbuf("kw_all", [128, 4, 256], F32)
            vw_all = f.sbuf("vw_all", [128, 4, 256], F32)
            vwb_all = f.sbuf("vwb_all", [128, 4, 256], BF16)
            for b in range(4):
                LD("sp", kw_all.t[:, b, :], swk[b, :, :], [kw_all])
                LD("sp", vw_all.t[:, b, :], swv[b, :, :], [vw_all])
            CP("pool", vwb_all.t[:, :, :], vw_all.t[:, :, :], [vw_all], [vwb_all])
            for b in range(4):
                ST("sp", wk_s[b, 0:124, :], kw_all.t[4:128, b, :], [kw_all])
                ST("sp", wv_s[b, 0:124, :], vw_all.t[4:128, b, :], [vw_all])
                ST("sp", wk_s[b, 124:128, :], ktok.t[b * 4:(b + 1) * 4, :], [ktok])
                ST("sp", wv_s[b, 124:128, :], VBs_f.t[b * 4:(b + 1) * 4, :], [VBs_f])
            Kcat = f.sbuf("Kcat", [64, 144], BF16)
            scb = f.sbuf("scb", [16, 144], F32)
            pb_ = f.sbuf("pbB", [16, 144], BF16)
            pTw = f.sbuf("pTw", [128, 32], BF16)
            st2 = f.sbuf("st2", [16, 8], F32)
            onb = f.sbuf("onb", [16, 64], BF16)

            def s2(i):
                return st2.t[0:16, i:i + 1]
        for half in range(2):
            wqh = wq_r.next(); wboh = wbo_r.next()
            load_w(wqh, w_q_b, 8, half * 512, 512)
            for h8 in range(8):
                LD("pool", wboh.t[0:64, h8, :], w_b_out[(half * 8 + h8) * 64:(half * 8 + h8 + 1) * 64, :], [wboh])
            def q_consume(h8):
                def fn(kn):
                    if not samp:
                        CP("pool", QB_c.t[0:64, :, h8, :], kn.t[0:64, 0:512].rearrange("p (t q) -> p t q", t=4), [kn], [QB_c])
                    else:
                        CP("pool", QBs.t[0:64, :, h8, :], kn.t[0:64, 0:NS].rearrange("p (b t) -> p b t", b=4), [kn], [QBs])
                return fn
            def q_dst(h8):
                if samp:
                    return None
                return ((lambda p0, p1, h8=h8: QB_c.t[p0:p1, :, h8, :]), QB_c)
            for qb in range(2):
                proj_batch([dict(lhsT=(lambda j, h8=h8: wqh.t[:, j, h8 * 64:(h8 + 1) * 64]), rhs=(lambda j: xnB.t[:, j, 0:N]),
                                 rd=[wqh, xnB], gcol=V_GQB, consume=q_consume(h8), dst=q_dst(h8)) for h8 in range(qb * 4, qb * 4 + 4)], N, tb)
            if not samp:
                iters = []
                for kk in range(2):
                    for ti in range(4):
                        gi = ci * 4 + ti
                        kts = [kt for kt in (gi - 1, gi) if kt >= 0]
                        for n_, kt in enumerate(kts):
                            iters.append(dict(kk=kk, ti=ti, gi=gi, kt=kt, first=(n_ == 0), last=(n_ == len(kts) - 1), n=len(iters)))

                def emitS(it):
                    kk, ti, kt, gi = it["kk"], it["ti"], it["kt"], it["gi"]
                    kvh = half * 2 + kk
                    slot = kt - (ci * 4 - 1)
                    bs_ = ringS.next()
                    MM(bs_, bs_.t[:, 0:512], KB_c.t[0:64, kvh, slot * 128:(slot + 1) * 128],
                       QB_c.t[0:64, ti, kk * 4:(kk + 1) * 4, :].rearrange("p h q -> p (h q)"), True, True, [KB_c, QB_c])
                    pT = pT_r.next()
                    ACT(pT.t[:, :], bs_.t[:, :], AF.Exp, [bs_], [pT], scale=SCALE_B)
                    TT("pool" if it["n"] % 4 == 3 else "dve", pT.t[:, :], pT.t[:, :], maskD4 if kt == gi else maskP4, ALU.mult, [pT, cb], [pT])
                    it["pT"] = pT; it["slot"] = slot; it["kvh"] = kvh
                curB = {}

                def emitPV(it):
                    kk, ti, kvh = it["kk"], it["ti"], it["kvh"]
                    if it["first"]:
                        curB["bo"] = ringO.next()
                    bo = curB["bo"]
                    MM(bo, bo.t[:, 0:512], VB_c.t[:, it["slot"], kvh, :], it["pT"].t[:, :], it["first"], it["last"], [VB_c, it["pT"]])
                    if it["last"]:
                        lt = lt_r.next()
                        TT("dve", lt.t[64:128, :].rearrange("p (h q) -> p h q", h=4), bo.t[64:128, :].rearrange("p (h q) -> p h q", h=4),
                           esk.t[64:128, kvh * 4:(kvh + 1) * 4].unsqueeze(2).broadcast_to([64, 4, 128]), ALU.add, [bo, esk], [lt])
                        ACT(lt.t[64:128, :], lt.t[64:128, :], AF.Ln, [lt], [lt])
                        ACT(lt.t[64:128, :], lt.t[64:128, :], AF.Exp, [lt], [lt], scale=-1.0)
                        TT("dve", OB_c.t[0:64, kk * 4:(kk + 1) * 4, ti * 128:(ti + 1) * 128], bo.t[0:64, :].rearrange("p (h q) -> p h q", h=4),
                           lt.t[64:128, :].rearrange("p (h q) -> p h q", h=4), ALU.mult, [bo, lt], [OB_c])
                DEPTH = 2
                for idx in range(len(iters) + DEPTH):
                    if idx < len(iters):
                        emitS(iters[idx])
                    if idx - DEPTH >= 0:
                        emitPV(iters[idx - DEPTH])
                for j in range(8):
                    pb = ringP.next()
                    for h8 in range(8):
                        MM(pb, pb.t[:, 0:512], wboh.t[0:64, h8, j * 128:(j + 1) * 128], OB_c.t[0:64, h8, :], h8 == 0, h8 == 7, [wboh, OB_c])
                    TT("dve", hT.t[:, j, c0:c0 + 512], pb.t[:, 0:512], hT.t[:, j, c0:c0 + 512], ALU.add, [pb, hT], [hT])
            else:
                for b in range(4):
                    for kk in range(2):
                        kvh = half * 2 + kk
                        pbK = ringP.next()
                        TR(pbK, pbK.t[0:64, 0:128], kw_all.t[:, b, kvh * 64:(kvh + 1) * 64], identf, [kw_all, cf])
                        CP("act", Kcat.t[0:64, 0:128], pbK.t[0:64, 0:128], [pbK], [Kcat])
                        CP("pool", Kcat.t[0:64, 128:144], kn_sb.t[0:64, kvh, :], [kn_sb], [Kcat])
                        bs_ = ringS.next()
                        MM(bs_, bs_.t[0:16, 0:144], QBs.t[0:64, b, kk * 4:(kk + 1) * 4, :].rearrange("p h t -> p (h t)"), Kcat.t[0:64, 0:144], True, True, [QBs, Kcat])
                        TT("dve", scb.t[:, :], bs_.t[0:16, 0:144], maskB.t[:, b, :], ALU.add, [bs_, maskB], [scb])
                        f.op("dve", lambda e: e.tensor_reduce(out=s2(0), in_=scb.t[:, :], axis=AX.X, op=ALU.max), reads=[scb], writes=[st2])
                        TS("dve", s2(1), s2(0), -SCALE_B, ALU.mult, [st2], [st2])
                        MSET("pool", s2(2), 0.0, [st2])
                        ACT(pb_.t[:, :], scb.t[:, :], AF.Exp, [scb, st2], [pb_, st2], scale=SCALE_B, bias=s2(1), accum_out=s2(2))
                        ACT(s2(3), vecs.t[0:16, V_SINKC + kvh:V_SINKC + kvh + 1], AF.Exp, [vecs, st2], [st2], scale=1.0, bias=s2(1))
                        TT("dve", s2(4), s2(2), s2(3), ALU.add, [st2], [st2])
                        f.op("dve", lambda e: e.reciprocal(out=s2(5), in_=s2(4)), reads=[st2], writes=[st2])
                        pbP = ringP.next()
                        TR(pbP, bfv(pbP)[:, 0:16], pb_.t[0:16, 0:128], idb.t[0:16, 0:16], [pb_, idb])
                        TR(pbP, bfv(pbP)[0:16, 16:32], pb_.t[0:16, 128:144], idb.t[0:16, 0:16], [pb_, idb])
                        CP("act", pTw.t[:, 0:32], bfv(pbP)[:, 0:32], [pbP], [pTw])
                        bo = ringO.next()
                        MM(bo, bo.t[0:16, 0:64], pTw.t[:, 0:16], vwb_all.t[:, b, kvh * 64:(kvh + 1) * 64], True, False, [pTw, vwb_all])
                        MM(bo, bo.t[0:16, 0:64], pTw.t[0:16, 16:32], VBs_b.t[0:NS, kvh * 64:(kvh + 1) * 64], False, True, [pTw, VBs_b])
                        TS("dve", onb.t[:, :], bo.t[0:16, 0:64], s2(5), ALU.mult, [bo, st2], [onb])
                        pbO = ringP.next()
                        TR(pbO, bfv(pbO)[0:64, 0:16], onb.t[0:16, 0:64], idb.t[0:16, 0:16], [onb, idb])
                        CP("act", OBs.t[0:64, kk * 4:(kk + 1) * 4, b * 4:(b + 1) * 4], bfv(pbO)[0:64, 0:16].rearrange("p (h t) -> p h t", h=4), [pbO], [OBs])
                for j in range(8):
                    pb = ringP.next()
                    for h8 in range(8):
                        MM(pb, pb.t[:, 0:NS], wboh.t[0:64, h8, j * 128:(j + 1) * 128], OBs.t[0:64, h8, :], h8 == 0, h8 == 7, [wboh, OBs])
                    TT("dve", hT.t[:, j, SEQ:NT], pb.t[:, 0:NS], hT.t[:, j, SEQ:NT], ALU.add, [pb, hT], [hT])
        if ci < 3:
            CP("pool", KB_c.t[0:64, :, 0:128], KB_c.t[0:64, :, 512:640], [KB_c], [KB_c])
            CP("pool", VB_c.t[:, 0, :, 0:64], VB_c.t[:, 4, :, 0:64], [VB_c], [VB_c])
    f.release(mB)
    if stop_after == "B":
        emit_output()
        f.finish()
        return nc

    ffn(1)

    emit_output()
    f.finish()
    return nc


def _rope_tab(n_rot, pos):
    inv = np.power(np.float32(THETA), (-np.arange(0, n_rot, 2, dtype=np.float32) / np.float32(n_rot)).astype(np.float32)).astype(np.float32)
    ang = (pos.astype(np.float32)[:, None] * inv[None, :]).astype(np.float32)
    return np.cos(ang.astype(np.float64)).astype(np.float32), np.sin(ang.astype(np.float64)).astype(np.float32)


def _constants():
    pos = np.concatenate([np.arange(SEQ), np.tile(PAST + np.arange(4), 4)]).astype(np.int64)
    cA, sA = _rope_tab(D_ROPE, pos)
    tabA = np.zeros((2, 96, NT), np.float32)
    tabA[0, :64] = 1.0
    for d in range(32):
        tabA[0, 64 + d] = cA[:, d % 16]
        tabA[1, 64 + d] = sA[:, d % 16]
    cB, sB = _rope_tab(16, pos)
    tabB = np.zeros((2, 16, NT), np.float32)
    for d in range(16):
        tabB[0, d] = cB[:, d % 8]
        tabB[1, d] = sB[:, d % 8]
    cf = np.zeros((128, CF_W), np.float32)
    cf[:, CF_ID:CF_ID + 128] = np.eye(128, dtype=np.float32)
    for d in range(16):
        cf[64 + d + 16, CF_P96 + 64 + d] = -1.0
        cf[64 + d, CF_P96 + 64 + d + 16] = 1.0
    for d in range(8):
        cf[d + 8, CF_P16 + d] = -1.0
        cf[d, CF_P16 + d + 8] = 1.0
    cb = np.zeros((128, CB_W), np.float32)
    cb[:, CB_ONES:CB_ONES + 128] = 1.0
    cb[0:64, CB_B96:CB_B96 + 64] = 1.0 / 64
    cb[64:96, CB_B96 + 64:CB_B96 + 96] = 1.0 / 32
    for jc in range(8):
        for p in range(128):
            hh = 2 * jc + p // 64
            cb[p, CB_BS + jc * 64 + hh * 4:CB_BS + jc * 64 + hh * 4 + 4] = 1.0 / 64
    pp = np.arange(128)[:, None]; cc = np.arange(128)[None, :]
    mD = (cc >= pp).astype(np.float32); mP = (cc < pp).astype(np.float32)
    cb[:, CB_MD:CB_MD + 512] = np.tile(mD, (1, 4))
    cb[:, CB_MP:CB_MP + 512] = np.tile(mP, (1, 4))
    maskS = np.full((64, 4, 16), NEG, np.float32)
    for hh in range(H):
        for t in range(4):
            for b in range(4):
                for t2 in range(t + 1):
                    maskS[hh * 4 + t, b, b * 4 + t2] = 0.0
    maskB = np.full((16, 4, 144), NEG, np.float32)
    for hq in range(4):
        for t in range(4):
            for b in range(4):
                maskB[hq * 4 + t, b, t + 1:128] = 0.0
                for t2 in range(t + 1):
                    maskB[hq * 4 + t, b, 128 + b * 4 + t2] = 0.0
    return dict(tabA=tabA, tabB=tabB, cf32=cf, cb32=cb, maskS=maskS.reshape(64, 64), maskB=maskB.reshape(16, 576))


def _vecs(inp):
    v = np.zeros((128, NV), np.float32)

    def colmaj(g, c0):
        n = g.shape[0] // 128
        v[:, c0:c0 + n] = g.reshape(n, 128).T
    colmaj(inp["norm_attn"][0], V_GA); colmaj(inp["norm_ffn"][0], V_GFA); colmaj(inp["g_kv_shared"], V_GKV)
    colmaj(inp["norm_attn"][1], V_GB); colmaj(inp["norm_ffn"][1], V_GFB)
    colmaj(inp["g_qc"][0], V_GQC); colmaj(inp["g_ckv"][0], V_GCKV)
    v[0:64, V_GQ96] = inp["g_qn_a"][0]; v[64:96, V_GQ96] = inp["g_qr_a"][0]
    v[0:64, V_GK96] = inp["g_kn_a"][0]; v[64:96, V_GK96] = inp["g_kr_a"][0]
    v[0:64, V_GKB] = inp["g_k_b"]; v[0:64, V_GQB] = inp["g_q_b"][0]
    sk = inp["sinks"][0]
    for kvh in range(4):
        for hq in range(4):
            v[hq * 4:hq * 4 + 4, V_SINKC + kvh] = sk[kvh * 4 + hq]
    v[:, V_SINKR:V_SINKR + H] = sk[None, :]
    return v


_PROG = {}


def kernel(_ncores=8, _stop_after=None, **inp):
    inp = {k: np.asarray(v) for k, v in inp.items()}
    key = _stop_after
    if key not in _PROG:
        _PROG[key] = build_program(_stop_after)
    nc = _PROG[key]
    consts = _constants()
    vecs = _vecs(inp)
    cache2d = np.ascontiguousarray(inp["cache_mla"][0].reshape(NPOOL * 128, D_CKV))
    shared = dict(
        cache=cache2d,
        w_a_in=np.ascontiguousarray(inp["w_a_in"][0]), w_uq=np.ascontiguousarray(inp["w_uq"][0]),
        w_uk=np.ascontiguousarray(inp["w_uk"][0]), w_uv=np.ascontiguousarray(inp["w_uv"][0]),
        w_a_out=np.ascontiguousarray(inp["w_a_out"][0]), w_kv=np.ascontiguousarray(inp["w_kv_shared"]),
        w_q_b=np.ascontiguousarray(inp["w_q_b"][0]), w_b_out=np.ascontiguousarray(inp["w_b_out"][0]),
        w_ffn_in=np.ascontiguousarray(inp["w_ffn_in"]), w_ffn_out=np.ascontiguousarray(inp["w_ffn_out"]),
        vecs=vecs, **consts)
    in_maps = []
    for c in range(_ncores):
        m = dict(shared)
        m["x_p"] = np.ascontiguousarray(inp["x_prompt"][c])
        m["x_s"] = np.ascontiguousarray(inp["x_sample"][4 * c:4 * c + 4].reshape(NS, D))
        m["ptab"] = np.ascontiguousarray(inp["page_table"][4 * c:4 * c + 4].reshape(1, 512).astype(np.int32))
        m["swk"] = np.ascontiguousarray(inp["state_win_k"][4 * c:4 * c + 4].reshape(4, 128, 256))
        m["swv"] = np.ascontiguousarray(inp["state_win_v"][4 * c:4 * c + 4].reshape(4, 128, 256))
        in_maps.append(m)
    res = run_bass_kernel_spmd(nc, in_maps, core_ids=list(range(_ncores)))
    R = res.results
    n = _ncores
    y_prompt = np.stack([R[c]["y_p"] for c in range(n)])
    y_sample = np.concatenate([R[c]["y_s"].reshape(4, 4, D) for c in range(n)])
    rows_pr = np.stack([R[c]["rows_p"] for c in range(n)])[None]
    rows_sa = np.concatenate([R[c]["rows_s"].reshape(4, 4, D_CKV) for c in range(n)])[None]
    wkp = np.stack([R[c]["wk_p"].reshape(128, 4, 64) for c in range(n)])
    wvp = np.stack([R[c]["wv_p"].reshape(128, 4, 64) for c in range(n)])
    wks = np.concatenate([R[c]["wk_s"].reshape(4, 128, 4, 64) for c in range(n)])
    wvs = np.concatenate([R[c]["wv_s"].reshape(4, 128, 4, 64) for c in range(n)])
    f32 = np.float32
    return (y_prompt.astype(f32), y_sample.astype(f32), rows_pr.astype(f32), rows_sa.astype(f32),
            wkp.astype(f32), wvp.astype(f32), wks.astype(f32), wvs.astype(f32))
```

```python
import numpy as np
import concourse.bass as bass
import concourse.mybir as mybir
from concourse.bass_utils import run_bass_kernel_spmd

F32 = mybir.dt.float32
BF16 = mybir.dt.bfloat16
I32 = mybir.dt.int32
AF = mybir.ActivationFunctionType
ALU = mybir.AluOpType
AX = mybir.AxisListType

ENGS = ["pe", "act", "dve", "pool", "sp"]

D = 1024
SEQ = 2048
NS = 16
NT = SEQ + NS
H = 16
D_NOPE, D_ROPE, D_V = 64, 32, 64
D_QC, D_C = 384, 256
D_CKV = D_C + D_ROPE
SCALE_A = float((D_NOPE + D_ROPE) ** -0.5)
N_KV, HD = 4, 64
SCALE_B = float(HD ** -0.5)
D_FF = 2816
EPS = 1e-6
THETA = 500000.0
PAST = 16384
NPOOL = 5120
NEG = -1e30
FILLER = 0
CHUNKS = [(0, 512), (512, 512), (1024, 512), (1536, 512), (2048, 16)]
NV = 69
V_GA, V_GFA, V_GKV, V_GB, V_GFB, V_GQC, V_GCKV, V_GQ96, V_GK96, V_GKB, V_GQB, V_SINKC, V_SINKR = 0, 8, 16, 24, 32, 40, 43, 45, 46, 47, 48, 49, 53
CB_ONES, CB_B96, CB_BS, CB_MD, CB_MP, CB_W = 0, 128, 224, 736, 1248, 1760
CF_ID, CF_P96, CF_P16, CF_W = 0, 128, 224, 240


class Buf:
    __slots__ = ("name", "t", "last_w", "readers", "dsem", "dcnt", "excl")

    def __init__(self, name, t, excl=False, init=()):
        self.name = name
        self.t = t
        self.last_w = []
        self.readers = list(init)
        self.dsem = None
        self.dcnt = 0
        self.excl = excl

    def __getitem__(self, idx):
        return self.t[idx]


def _compact(evs):
    d = {}
    for k, v in evs:
        if d.get(k, 0) < v:
            d[k] = v
    return list(d.items())


class FW:
    def __init__(self, nc):
        self.nc = nc
        self.streams = {e: [] for e in ENGS}
        self.esem = {}
        self.ecnt = {e: 0 for e in ENGS}
        self.seen = {e: {} for e in ENGS}
        self.sems = {}
        self.dma_bufs = []
        self._ctx = []
        self.free_events = []
        self.free_sems = []
        for e in ["pe", "act", "dve", "pool"]:
            self.esem[e] = self._newsem("e_" + e)

    def _newsem(self, name):
        cm = self.nc.semaphore(name)
        h = cm.__enter__()
        self._ctx.append((cm, None))
        self.sems[name] = h
        return name

    def mark(self):
        return len(self._ctx)

    def release(self, mark):
        ev = list(self.free_events)
        while len(self._ctx) > mark:
            cm, b = self._ctx.pop()
            if b is not None:
                ev.extend(b.last_w)
                ev.extend(b.readers)
                if b.dsem is not None:
                    ev.append((b.dsem, b.dcnt))
                cm.__exit__(None, None, None)
            else:
                self._keep.append((cm, b))
        self.free_events = _compact(ev)

    _keep = []

    def sbuf(self, name, shape, dtype):
        self._uid = getattr(self, "_uid", 0) + 1
        name = "s%d_%s" % (self._uid, name)
        cm = self.nc.sbuf_tensor(name, list(shape), dtype)
        t = cm.__enter__()
        b = Buf(name, t, init=self.free_events)
        self._ctx.append((cm, b))
        return b

    def psum(self, name, shape, dtype=F32):
        cm = self.nc.psum_tensor(name, list(shape), dtype)
        t = cm.__enter__()
        b = Buf(name, t, excl=True)
        self._ctx.append((cm, b))
        return b

    def _deps(self, reads, writes):
        ev = []
        for b in reads:
            ev.extend(b.last_w)
            if b.excl:
                ev.extend(b.readers)
        for b in writes:
            ev.extend(b.last_w)
            ev.extend(b.readers)
        return ev

    def _waits(self, eng, ev):
        need = {}
        for (k, v) in ev:
            if need.get(k, 0) < v:
                need[k] = v
        seen = self.seen[eng]
        out = []
        for k, v in need.items():
            if seen.get(k, 0) >= v:
                continue
            seen[k] = v
            out.append((k, v))
        return out

    def _record(self, reads, writes, event, nowaw=False):
        for b in reads:
            if b.excl:
                b.last_w = [event]
                b.readers = []
            else:
                b.readers.append(event)
                if len(b.readers) > 48:
                    b.readers = _compact(b.readers)
        for b in writes:
            if nowaw:
                b.last_w.append(event)
                b.last_w = _compact(b.last_w)
            else:
                b.last_w = [event]
            b.readers = []

    def op(self, eng, fn, reads=(), writes=()):
        ev = self._deps(reads, writes)
        waits = self._waits(eng, ev)
        self.ecnt[eng] += 1
        val = self.ecnt[eng]
        semname = self.esem[eng]
        if eng == "pe":
            self.seen[eng][semname] = val
        sems = self.sems

        def thunk(e, fn=fn, waits=waits, semname=semname):
            for (k, v) in waits:
                e.wait_ge(sems[k], v)
            fn(e).then_inc(sems[semname], 1)
        self.streams[eng].append(thunk)
        self._record(reads, writes, (semname, val))

    def dma(self, q, fn, reads=(), writes=(), own=None):
        if own is None:
            own = writes[0] if writes else reads[0]
        if own.dsem is None:
            cm = self.nc.semaphore("d_" + own.name)
            h = cm.__enter__()
            self._keep.append((cm, None))
            self.sems["d_" + own.name] = h
            own.dsem = "d_" + own.name
            self.dma_bufs.append(own)
        ev = []
        for b in reads:
            ev.extend(b.last_w)
        for b in writes:
            ev.extend([e for e in b.last_w if e[0] != own.dsem])
            ev.extend(b.readers)
        waits = self._waits(q, ev)
        own.dcnt += 16
        val = own.dcnt
        semname = own.dsem
        sems = self.sems

        def thunk(e, fn=fn, waits=waits, semname=semname):
            for (k, v) in waits:
                e.wait_ge(sems[k], v)
            fn(e).then_inc(sems[semname], 16)
        self.streams[q].append(thunk)
        self._record(reads, writes, (semname, val), nowaw=True)

    def finish(self):
        finals = [(b.dsem, b.dcnt) for b in self.dma_bufs]
        sems = self.sems
        nc = self.nc
        streams = self.streams
        with nc.Block() as block:
            @block.sync
            def _(e):
                for th in streams["sp"]:
                    th(e)
                for (k, v) in finals:
                    e.wait_ge(sems[k], v)

            @block.tensor
            def _(e):
                for th in streams["pe"]:
                    th(e)

            @block.scalar
            def _(e):
                for th in streams["act"]:
                    th(e)

            @block.vector
            def _(e):
                for th in streams["dve"]:
                    th(e)

            @block.gpsimd
            def _(e):
                for th in streams["pool"]:
                    th(e)
        while self._ctx:
            cm, b = self._ctx.pop()
            cm.__exit__(None, None, None)
        for cm, b in reversed(self._keep):
            cm.__exit__(None, None, None)
        FW._keep = []


class Ring:
    def __init__(self, items):
        self.items = items
        self.i = 0

    def next(self):
        b = self.items[self.i % len(self.items)]
        self.i += 1
        return b


def build_program(stop_after=None, npool=NPOOL):
    nc = bass.Bass("TRN2", target_bir_lowering=False)
    FW._keep = []
    f = FW(nc)

    def din(name, shape, dt=F32):
        return nc.dram_tensor(name, list(shape), dt, kind="ExternalInput").ap()

    def dout(name, shape, dt=F32):
        return nc.dram_tensor(name, list(shape), dt, kind="ExternalOutput").ap()

    x_p = din("x_p", [SEQ, D]); x_s = din("x_s", [NS, D])
    cache = din("cache", [npool * 128, D_CKV])
    ptab = din("ptab", [1, 512], I32)
    swk = din("swk", [4, 128, 256]); swv = din("swv", [4, 128, 256])
    w_a_in = din("w_a_in", [D, 672]); w_uq = din("w_uq", [D_QC, 1536])
    w_uk = din("w_uk", [D_C, 1024]); w_uv = din("w_uv", [D_C, 1024])
    w_a_out = din("w_a_out", [1024, D]); w_kv = din("w_kv", [D, 512])
    w_q_b = din("w_q_b", [D, 1024]); w_b_out = din("w_b_out", [1024, D])
    w_ffn_in = din("w_ffn_in", [2, D, 2 * D_FF]); w_ffn_out = din("w_ffn_out", [2, D_FF, D])
    vecs_d = din("vecs", [128, NV]); cf_d = din("cf32", [128, CF_W]); cb_d = din("cb32", [128, CB_W])
    tabA_d = din("tabA", [2, 96, NT]); tabB_d = din("tabB", [2, 16, NT])
    maskS_d = din("maskS", [64, 4 * 16]); maskB_d = din("maskB", [16, 4 * 144])

    y_p = dout("y_p", [SEQ, D]); y_s = dout("y_s", [NS, D])
    rows_p = dout("rows_p", [SEQ, D_CKV]); rows_s = dout("rows_s", [NS, D_CKV])
    wk_p = dout("wk_p", [128, 256]); wv_p = dout("wv_p", [128, 256])
    wk_s = dout("wk_s", [4, 128, 256]); wv_s = dout("wv_s", [4, 128, 256])

    def MM(pb, out, lhsT, rhs, start, stop, rd, sgc=False):
        if sgc:
            f.op("pe", lambda e: e.matmul(out, lhsT=lhsT, rhs=rhs, start=start, stop=stop, skip_group_check=True), reads=rd, writes=[pb])
        else:
            f.op("pe", lambda e: e.matmul(out, lhsT=lhsT, rhs=rhs, start=start, stop=stop), reads=rd, writes=[pb])

    def TR(pb, out, in_, ident, rd):
        f.op("pe", lambda e: e.transpose(out=out, in_=in_, identity=ident), reads=rd, writes=[pb])

    def ACT(out, in_, func, rd, wr, **kw):
        f.op("act", lambda e: e.activation(out=out, in_=in_, func=func, **kw), reads=rd, writes=wr)

    def CP(eng, out, in_, rd, wr):
        if eng == "act":
            f.op("act", lambda e: e.copy(out=out, in_=in_), reads=rd, writes=wr)
        else:
            f.op(eng, lambda e: e.tensor_copy(out=out, in_=in_), reads=rd, writes=wr)

    def TT(eng, out, in0, in1, op, rd, wr):
        f.op(eng, lambda e: e.tensor_tensor(out=out, in0=in0, in1=in1, op=op), reads=rd, writes=wr)

    def STT(eng, out, in0, scalar, in1, op0, op1, rd, wr):
        f.op(eng, lambda e: e.scalar_tensor_tensor(out=out, in0=in0, scalar=scalar, in1=in1, op0=op0, op1=op1), reads=rd, writes=wr)

    def TS(eng, out, in0, s1, op0, rd, wr):
        f.op(eng, lambda e: e.tensor_scalar(out=out, in0=in0, scalar1=s1, scalar2=None, op0=op0), reads=rd, writes=wr)

    def MSET(eng, ap, val, wr):
        f.op(eng, lambda e: e.memset(ap, val), writes=wr)

    def LD(q, out, in_, wr, rd=()):
        f.dma(q, lambda e: e.dma_start(out=out, in_=in_), reads=list(rd), writes=list(wr))

    def ST(q, out, in_, rd):
        f.dma(q, lambda e: e.dma_start(out=out, in_=in_), reads=list(rd), writes=[])

    PS = [f.psum(f"ps{i}", [128, 512], F32) for i in range(8)]

    def bfv(pb):
        return pb.t[:, :].bitcast(BF16)

    hT = f.sbuf("hT", [128, 8, NT], F32)
    vecs = f.sbuf("vecs", [128, NV], F32)
    cf = f.sbuf("cf", [128, CF_W], F32)
    cb = f.sbuf("cb", [128, CB_W], BF16)
    idb = f.sbuf("idb", [128, 128], BF16)
    LD("sp", vecs[:, :], vecs_d[:, :], [vecs])
    LD("sp", cf[:, :], cf_d[:, :], [cf])
    LD("pool", cb[:, :], cb_d[:, :], [cb])
    CP("pool", idb[:, :], cf[:, CF_ID:CF_ID + 128], [cf], [idb])
    identf = cf.t[:, CF_ID:CF_ID + 128]
    ones_b = cb.t[:, CB_ONES:CB_ONES + 128]

    def vcol(c, p0=0, p1=128):
        return vecs.t[p0:p1, c:c + 1]

    def rstd_chain(pb, M, N, rs, scale):
        ACT(rs.t[0:M, 0:N], pb.t[0:M, 0:N], AF.Ln, [pb], [rs], scale=scale, bias=EPS)
        ACT(rs.t[0:M, 0:N], rs.t[0:M, 0:N], AF.Exp, [rs], [rs], scale=-0.5)

    def load_w(dst, src2d, kchunks, c0, ncols, prows=128):
        for j in range(kchunks):
            LD("pool", dst.t[0:prows, j, 0:ncols], src2d[j * prows:(j + 1) * prows, c0:c0 + ncols], [dst])

    def emit_output():
        ys_r = Ring([f.sbuf(f"ys{i}", [128, D], F32) for i in range(2)])
        for i in range(17):
            ys = ys_r.next()
            rows = 128 if i < 16 else NS
            for half in range(2):
                pb = PS[(2 * i + half) % 8]
                for q in range(4):
                    j = half * 4 + q
                    TR(pb, pb.t[0:rows, q * 128:(q + 1) * 128], hT.t[:, j, i * 128:i * 128 + rows], identf, [hT, cf])
                CP("act" if half == 0 else "dve", ys.t[0:rows, half * 512:(half + 1) * 512], pb.t[0:rows, 0:512], [pb], [ys])
            if i < 16:
                ST("sp", y_p[i * 128:(i + 1) * 128, :], ys.t[0:rows, :], [ys])
            else:
                ST("sp", y_s[:, :], ys.t[0:rows, :], [ys])


    m0 = f.mark()
    xs_ring = Ring([f.sbuf(f"xs{i}", [128, D], F32) for i in range(2)])
    for i in range(17):
        xs = xs_ring.next()
        rows = 128 if i < 16 else NS
        src = x_p[i * 128:(i + 1) * 128, :] if i < 16 else x_s[:, :]
        LD("sp", xs.t[0:rows, :], src, [xs])
        for half in range(2):
            pb = PS[(2 * i + half) % 8]
            for q in range(4):
                j = half * 4 + q
                TR(pb, pb.t[:, q * 128:q * 128 + rows], xs.t[0:rows, j * 128:(j + 1) * 128], identf[0:rows, 0:rows], [xs, cf])
            srcv = pb.t[:, :].rearrange("p (q c) -> p q c", q=4)[:, :, 0:rows]
            CP("act" if half == 0 else "dve", hT.t[:, half * 4:half * 4 + 4, i * 128:i * 128 + rows], srcv, [pb], [hT])
    f.release(m0)

    mA = f.mark()
    cqT = f.sbuf("cqT", [128, 3, NT], BF16)
    ckvT = f.sbuf("ckvT", [128, 2, NT], BF16)
    kpe_b = f.sbuf("kpe_b", [96, NT], BF16)
    rown_b = f.sbuf("rown_b", [NS, 256], BF16)
    qs_all = f.sbuf("qs_all", [96, H, NS], BF16)

    mA1 = f.mark()
    wain = f.sbuf("wain", [128, 8, 672], BF16)
    load_w(wain, w_a_in, 8, 0, 672)
    sq8 = f.sbuf("sq8", [128, 8, 512], BF16)
    xn = f.sbuf("xn", [128, 8, 512], BF16)
    rs_r = Ring([f.sbuf(f"rsA{i}", [128, 512], F32) for i in range(2)])
    ckv_f = f.sbuf("ckv_f", [128, 2, 512], F32)
    kpn = f.sbuf("kpn", [96, 512], F32)
    kpe_f = f.sbuf("kpe_f", [96, 512], F32)
    t1 = f.sbuf("t1A", [96, 512], F32)
    tabc = Ring([f.sbuf(f"tabcA{i}", [96, 2, 512], F32) for i in range(2)])
    rstage = Ring([f.sbuf(f"rstage{i}", [128, D_CKV], F32) for i in range(2)])
    for (c0, N) in CHUNKS:
        tb = tabc.next()
        LD("sp", tb.t[64:96, 0, 0:N], tabA_d[0, 64:96, c0:c0 + N], [tb])
        LD("sp", tb.t[64:96, 1, 0:N], tabA_d[1, 64:96, c0:c0 + N], [tb])
        ACT(sq8.t[:, :, 0:N], hT.t[:, :, c0:c0 + N], AF.Square, [hT], [sq8])
        pss = PS[6]
        for j in range(8):
            MM(pss, pss.t[:, 0:N], ones_b, sq8.t[:, j, 0:N], j == 0, j == 7, [cb, sq8])
        rs = rs_r.next()
        rstd_chain(pss, 128, N, rs, 1.0 / D)
        for j in range(8):
            STT("dve", xn.t[:, j, 0:N], hT.t[:, j, c0:c0 + N], vcol(V_GA + j), rs.t[:, 0:N], ALU.mult, ALU.mult, [hT, vecs, rs], [xn])
        mts = [(0, 128), (128, 128), (256, 128), (384, 128), (512, 128), (576, 96)]
        for mi, (mc, M) in enumerate(mts):
            pb = PS[mi]
            for j in range(8):
                MM(pb, pb.t[0:M, 0:N], wain.t[:, j, mc:mc + M], xn.t[:, j, 0:N], j == 0, j == 7, [wain, xn])
        for m in range(3):
            ACT(sq8.t[:, m, 0:N], PS[m].t[:, 0:N], AF.Square, [PS[m]], [sq8])
        for m in range(3):
            MM(pss, pss.t[:, 0:N], ones_b, sq8.t[:, m, 0:N], m == 0, m == 2, [cb, sq8])
        rs = rs_r.next()
        rstd_chain(pss, 128, N, rs, 1.0 / D_QC)
        for m in range(3):
            STT("dve", cqT.t[:, m, c0:c0 + N], PS[m].t[:, 0:N], vcol(V_GQC + m), rs.t[:, 0:N], ALU.mult, ALU.mult, [PS[m], vecs, rs], [cqT])
        for m in range(2):
            ACT(sq8.t[:, 3 + m, 0:N], PS[3 + m].t[:, 0:N], AF.Square, [PS[3 + m]], [sq8])
        for m in range(2):
            MM(pss, pss.t[:, 0:N], ones_b, sq8.t[:, 3 + m, 0:N], m == 0, m == 1, [cb, sq8])
        rs = rs_r.next()
        rstd_chain(pss, 128, N, rs, 1.0 / D_C)
        for m in range(2):
            STT("dve", ckv_f.t[:, m, 0:N], PS[3 + m].t[:, 0:N], vcol(V_GCKV + m), rs.t[:, 0:N], ALU.mult, ALU.mult, [PS[3 + m], vecs, rs], [ckv_f])
        CP("pool", ckvT.t[:, :, c0:c0 + N], ckv_f.t[:, :, 0:N], [ckv_f], [ckvT])
        ACT(sq8.t[0:96, 5, 0:N], PS[5].t[0:96, 0:N], AF.Square, [PS[5]], [sq8])
        MM(pss, pss.t[0:96, 0:N], cb.t[0:96, CB_B96:CB_B96 + 96], sq8.t[0:96, 5, 0:N], True, True, [cb, sq8])
        rs = rs_r.next()
        rstd_chain(pss, 96, N, rs, 1.0)
        STT("dve", kpn.t[0:96, 0:N], PS[5].t[0:96, 0:N], vcol(V_GK96, 0, 96), rs.t[0:96, 0:N], ALU.mult, ALU.mult, [PS[5], vecs, rs], [kpn])
        pr = PS[7]
        MM(pr, pr.t[0:96, 0:N], cf.t[0:96, CF_P96:CF_P96 + 96], kpn.t[0:96, 0:N], True, True, [cf, kpn])
        TT("pool", t1.t[64:96, 0:N], kpn.t[64:96, 0:N], tb.t[64:96, 0, 0:N], ALU.mult, [kpn, tb], [t1])
        TT("dve", kpe_f.t[64:96, 0:N], pr.t[64:96, 0:N], tb.t[64:96, 1, 0:N], ALU.mult, [pr, tb], [kpe_f])
        TT("dve", kpe_f.t[64:96, 0:N], kpe_f.t[64:96, 0:N], t1.t[64:96, 0:N], ALU.add, [kpe_f, t1], [kpe_f])
        CP("pool", kpe_b.t[64:96, c0:c0 + N], kpe_f.t[64:96, 0:N], [kpe_f], [kpe_b])
        ntile = (N + 127) // 128
        for ti in range(ntile):
            r = min(128, N - ti * 128)
            pbT = PS[7]
            for m in range(2):
                TR(pbT, pbT.t[0:r, m * 128:(m + 1) * 128], ckv_f.t[:, m, ti * 128:ti * 128 + r], identf, [ckv_f, cf])
            TR(pbT, pbT.t[0:r, 256:288], kpe_f.t[64:96, ti * 128:ti * 128 + r], cf.t[64:96, CF_ID + 64:CF_ID + 96], [kpe_f, cf])
            stg = rstage.next()
            CP("act", stg.t[0:r, :], pbT.t[0:r, 0:D_CKV], [pbT], [stg])
            if c0 < SEQ:
                ST("sp", rows_p[c0 + ti * 128:c0 + ti * 128 + r, :], stg.t[0:r, :], [stg])
            else:
                ST("sp", rows_s[:, :], stg.t[0:r, :], [stg])
                CP("pool", rown_b.t[0:NS, :], stg.t[0:NS, 0:256], [stg], [rown_b])
    f.release(mA1)
    if stop_after == "A1":
        f.release(mA)
        f.finish()
        return nc

    mA2 = f.mark()
    G = 2
    NG = H // G
    wq_r = Ring([f.sbuf(f"wq_g{i}", [128, 3, G * 96], BF16) for i in range(2)])
    wk_r = Ring([f.sbuf(f"wk_g{i}", [128, 2, G * 64], BF16) for i in range(2)])
    wv_r = Ring([f.sbuf(f"wv_g{i}", [128, 2, G * 64], BF16) for i in range(2)])
    wo_r = Ring([f.sbuf(f"wo_g{i}", [128, D], BF16) for i in range(2)])
    qT_g = f.sbuf("qT_g", [128, G, NT], BF16)
    KT_g = f.sbuf("KT_g", [128, G, NT], BF16)
    V_g = f.sbuf("V_g", [128, 16, G, 128], BF16)
    OT_r = Ring([f.sbuf(f"OT_g{i}", [128, SEQ], BF16) for i in range(2)])
    sq_r = Ring([f.sbuf(f"sqB{i}", [128, 512], BF16) for i in range(4)])
    rs_r = Ring([f.sbuf(f"rsB{i}", [96, 512], F32) for i in range(4)])
    qn_r = Ring([f.sbuf(f"qnB{i}", [96, 512], F32) for i in range(2)])
    t1_r = Ring([f.sbuf(f"t1B{i}", [96, 512], F32) for i in range(2)])
    t2_r = Ring([f.sbuf(f"t2B{i}", [96, 512], F32) for i in range(2)])
    pT_r = Ring([f.sbuf(f"pT{i}", [128, 512], BF16) for i in range(4)])
    rl_r = Ring([f.sbuf(f"rl{i}", [128, 512], F32) for i in range(2)])
    tabc = Ring([f.sbuf(f"tabcB{i}", [96, 2, 512], F32) for i in range(2)])
    MSET("pool", qT_g.t[:, :, :], 0.0, [qT_g])
    MSET("pool", KT_g.t[:, :, :], 0.0, [KT_g])
    MSET("pool", V_g.t[:, :, 0, 64:128], 1.0, [V_g])
    MSET("pool", V_g.t[:, :, 1, 0:64], 1.0, [V_g])
    ringP = Ring(PS[5:8])
    ringS = Ring(PS[0:3])
    ringO = Ring(PS[3:5])
    ringA = Ring(PS[0:8])
    maskD = cb.t[:, CB_MD:CB_MD + 128]
    B96 = cb.t[0:96, CB_B96:CB_B96 + 96]
    B64 = cb.t[0:64, CB_B96:CB_B96 + 64]
    P96 = cf.t[0:96, CF_P96:CF_P96 + 96]
    for g in range(NG):
        wq = wq_r.next(); wk = wk_r.next(); wv = wv_r.next(); wo = wo_r.next()
        OT_g = OT_r.next()
        if g % 2 == 0:
            pend = []
        pend.append((wo, OT_g))
        load_w(wq, w_uq, 3, g * G * 96, G * 96)
        load_w(wk, w_uk, 2, g * G * 64, G * 64)
        load_w(wv, w_uv, 2, g * G * 64, G * 64)
        LD("pool", wo.t[:, :], w_a_out[g * 128:(g + 1) * 128, :], [wo])
        for ci, (c0, N) in enumerate(CHUNKS):
            tb = tabc.next()
            LD("sp", tb.t[64:96, 0, 0:N], tabA_d[0, 64:96, c0:c0 + N], [tb])
            LD("sp", tb.t[64:96, 1, 0:N], tabA_d[1, 64:96, c0:c0 + N], [tb])
            chs = []
            for hl in range(G):
                bq = PS[hl]
                for m in range(3):
                    MM(bq, bq.t[0:96, 0:N], wq.t[:, m, hl * 96:(hl + 1) * 96], cqT.t[:, m, c0:c0 + N], m == 0, m == 2, [wq, cqT])
                chs.append(dict(q=True, hl=hl, b=bq, M=96))
            for hl in range(G):
                bk = PS[2 + hl]
                for m in range(2):
                    MM(bk, bk.t[0:64, 0:N], wk.t[:, m, hl * 64:(hl + 1) * 64], ckvT.t[:, m, c0:c0 + N], m == 0, m == 1, [wk, ckvT])
                chs.append(dict(q=False, hl=hl, b=bk, M=64))
            for c in chs:
                M = c["M"]
                c["sq"] = sq_r.next()
                ACT(c["sq"].t[0:M, 0:N], c["b"].t[0:M, 0:N], AF.Square, [c["b"]], [c["sq"]])
            for ic, c in enumerate(chs):
                M = c["M"]
                c["bs"] = PS[4 + ic]
                MM(c["bs"], c["bs"].t[0:M, 0:N], cb.t[0:M, CB_B96:CB_B96 + M], c["sq"].t[0:M, 0:N], True, True, [cb, c["sq"]])
            for c in chs:
                c["rs"] = rs_r.next()
                rstd_chain(c["bs"], c["M"], N, c["rs"], 1.0)
            if c0 < SEQ:
                for ti in range(ci * 4, ci * 4 + 4):
                    bv = PS[4 + ti % 4]
                    for m in range(2):
                        MM(bv, bv.t[:, 0:G * 64], ckvT.t[:, m, ti * 128:(ti + 1) * 128], wv.t[:, m, 0:G * 64], m == 0, m == 1, [ckvT, wv])
                    CP("dve" if ti % 2 else "act", V_g.t[:, ti, 0, 0:64], bv.t[:, 0:64], [bv], [V_g])
                    CP("act" if ti % 2 else "dve", V_g.t[:, ti, 1, 64:128], bv.t[:, 64:128], [bv], [V_g])
            for c in chs:
                hl = c["hl"]
                if c["q"]:
                    c["qn"] = qn_r.next()
                    STT("dve", c["qn"].t[0:96, 0:N], c["b"].t[0:96, 0:N], vcol(V_GQ96, 0, 96), c["rs"].t[0:96, 0:N], ALU.mult, ALU.mult, [c["b"], vecs, c["rs"]], [c["qn"]])
                    STT("dve", qT_g.t[0:64, hl, c0:c0 + N], c["b"].t[0:64, 0:N], vcol(V_GQ96, 0, 64), c["rs"].t[0:64, 0:N], ALU.mult, ALU.mult, [c["b"], vecs, c["rs"]], [qT_g])
                else:
                    STT("dve", KT_g.t[0:64, hl, c0:c0 + N], c["b"].t[0:64, 0:N], vcol(V_GK96, 0, 64), c["rs"].t[0:64, 0:N], ALU.mult, ALU.mult, [c["b"], vecs, c["rs"]], [KT_g])
                    CP("pool", KT_g.t[64:96, hl, c0:c0 + N], kpe_b.t[64:96, c0:c0 + N], [kpe_b], [KT_g])
            for c in chs:
                if c["q"]:
                    c["br"] = PS[4 + c["hl"]]
                    MM(c["br"], c["br"].t[0:96, 0:N], P96, c["qn"].t[0:96, 0:N], True, True, [cf, c["qn"]])
            for c in chs:
                if c["q"]:
                    hl = c["hl"]; qn = c["qn"]; br = c["br"]
                    t1 = t1_r.next(); t2 = t2_r.next()
                    TT("pool", t1.t[64:96, 0:N], qn.t[64:96, 0:N], tb.t[64:96, 0, 0:N], ALU.mult, [qn, tb], [t1])
                    TT("dve", t2.t[64:96, 0:N], br.t[64:96, 0:N], tb.t[64:96, 1, 0:N], ALU.mult, [br, tb], [t2])
                    TT("dve", qT_g.t[64:96, hl, c0:c0 + N], t1.t[64:96, 0:N], t2.t[64:96, 0:N], ALU.add, [t1, t2], [qT_g])
            if c0 >= SEQ:
                for hl in range(G):
                    CP("pool", qs_all.t[0:96, g * G + hl, :], qT_g.t[0:96, hl, SEQ:NT], [qT_g], [qs_all])
        iters = []
        for hl in range(G):
            for qc in range(4):
                nkt = 4 * qc + 4
                for kt in range(nkt):
                    iters.append(dict(hl=hl, qc=qc, kt=kt, nkt=nkt))

        def emitS(it):
            hl, qc, kt = it["hl"], it["qc"], it["kt"]
            n0 = max(qc * 512, kt * 128)
            W = qc * 512 + 512 - n0
            bs_ = ringS.next()
            MM(bs_, bs_.t[:, 0:W], KT_g.t[:, hl, kt * 128:(kt + 1) * 128], qT_g.t[:, hl, n0:n0 + W], True, True, [KT_g, qT_g])
            pT = pT_r.next()
            ACT(pT.t[:, 0:W], bs_.t[:, 0:W], AF.Exp, [bs_], [pT], scale=SCALE_A)
            if kt * 128 >= qc * 512:
                TT("pool", pT.t[:, 0:128], pT.t[:, 0:128], maskD, ALU.mult, [pT, cb], [pT])
            it["pT"] = pT; it["W"] = W; it["o0"] = n0 - qc * 512

        cur = {}

        def emitPV(it):
            hl, qc, kt, nkt = it["hl"], it["qc"], it["kt"], it["nkt"]
            if kt == 0:
                cur["bo"] = ringO.next()
            bo = cur["bo"]
            MM(bo, bo.t[:, it["o0"]:512], V_g.t[:, kt, hl, :], it["pT"].t[:, 0:it["W"]], kt == 0, kt == nkt - 1, [V_g, it["pT"]])
            if kt == nkt - 1:
                rl = rl_r.next()
                oL, oH = (0, 64) if hl == 0 else (64, 128)
                lL, lH = (64, 128) if hl == 0 else (0, 64)
                f.op("dve", lambda e, rl=rl, bo=bo, lL=lL, lH=lH: e.reciprocal(out=rl.t[lL:lH, :], in_=bo.t[lL:lH, :]), reads=[bo], writes=[rl])
                TT("dve", OT_g.t[oL:oH, qc * 512:(qc + 1) * 512], bo.t[oL:oH, :], rl.t[lL:lH, :], ALU.mult, [bo, rl], [OT_g])
        DEPTH = 2
        for idx in range(len(iters) + DEPTH):
            if idx < len(iters):
                emitS(iters[idx])
            if FILLER:
                MM(PS[7], PS[7].t[:, 0:FILLER], KT_g.t[0:96, 0, 0:128], qT_g.t[0:96, 0, 0:FILLER], True, True, [KT_g, qT_g])
            if idx - DEPTH >= 0:
                emitPV(iters[idx - DEPTH])
        if g % 2 == 1:
            for qc in range(4):
                c0 = qc * 512
                for j in range(8):
                    pb = ringP.next()
                    for ip, (wo_, OT_) in enumerate(pend):
                        MM(pb, pb.t[:, 0:512], wo_.t[:, j * 128:(j + 1) * 128], OT_.t[:, c0:c0 + 512], ip == 0, ip == len(pend) - 1, [wo_, OT_])
                    TT("dve", hT.t[:, j, c0:c0 + 512], pb.t[:, 0:512], hT.t[:, j, c0:c0 + 512], ALU.add, [pb, hT], [hT])
    f.release(mA2)
    if stop_after == "A2":
        f.release(mA)
        emit_output()
        f.finish()
        return nc

    mA3 = f.mark()
    wuk = f.sbuf("wuk", [128, 2, 1024], BF16)
    wuv = f.sbuf("wuv", [128, 2, 1024], BF16)
    load_w(wuk, w_uk, 2, 0, 1024)
    load_w(wuv, w_uv, 2, 0, 1024)
    OTs = f.sbuf("OTs", [64, H, NS], BF16)
    wukT = f.sbuf("wukT", [64, H, 256], BF16)
    qabsT = f.sbuf("qabsT", [128, 2, 4, 64], BF16)
    qpeT = f.sbuf("qpeT", [96, 4, 64], BF16)
    maskS = f.sbuf("maskS", [64, 4, 16], F32)
    LD("sp", maskS.t[:, :, :], maskS_d.rearrange("p (b k) -> p b k", b=4), [maskS])
    pti = f.sbuf("pti", [128, 512], I32)
    ptf = f.sbuf("ptf", [128, 512], F32)
    iot = f.sbuf("iot", [128, 1], I32)
    iof = f.sbuf("iof", [128, 1], F32)
    ridx = f.sbuf("ridx", [128, 512], I32)
    LD("sp", pti.t[:, :], ptab[0:1, :].partition_broadcast(128), [pti])
    f.op("pool", lambda e: e.iota(iot.t[:, :], pattern=[[0, 1]], base=0, channel_multiplier=1), writes=[iot])
    CP("dve", iof.t[:, :], iot.t[:, :], [iot], [iof])
    CP("dve", ptf.t[:, :], pti.t[:, :], [pti], [ptf])
    f.op("dve", lambda e: e.tensor_scalar(out=ptf.t[:, :], in0=ptf.t[:, :], scalar1=128.0, scalar2=iof.t[:, 0:1], op0=ALU.mult, op1=ALU.add), reads=[ptf, iof], writes=[ptf])
    CP("dve", ridx.t[:, :], ptf.t[:, :], [ptf], [ridx])
    for hb in range(4):
        pb = PS[hb]
        for hq in range(4):
            for m in range(2):
                slot = hq * 2 + m
                TR(pb, bfv(pb)[0:64, slot * 128:(slot + 1) * 128], wuk.t[:, m, (hb * 4 + hq) * 64:(hb * 4 + hq + 1) * 64], idb.t[:, :], [wuk, idb])
        CP("dve" if hb % 2 else "act", wukT.t[0:64, hb * 4:(hb + 1) * 4, :], bfv(pb)[0:64, 0:1024].rearrange("p (h l) -> p h l", h=4), [pb], [wukT])
    qsg = f.sbuf("qsg", [64, H, NS], BF16)
    TS("dve", qsg.t[0:64, :, :], qs_all.t[0:64, :, :], vcol(V_GK96, 0, 64), ALU.mult, [qs_all, vecs], [qsg])
    pb = PS[4]
    for m in range(2):
        for hh in range(H):
            col = (m * H + hh) * NS
            MM(pb, pb.t[:, col:col + NS], wukT.t[0:64, hh, m * 128:(m + 1) * 128], qsg.t[0:64, hh, :], True, True, [wukT, qsg])
    for m in range(2):
        CP("dve", qabsT.t[:, m, :, :].rearrange("p b (h t) -> p b h t", h=H),
           pb.t[:, m * 256:(m + 1) * 256].rearrange("p (h b t) -> p b h t", h=H, b=4), [pb], [qabsT])
    CP("pool", qpeT.t[64:96, :, :].rearrange("p b (h t) -> p b h t", h=H),
       qs_all.t[64:96, :, :].rearrange("p h (b t) -> p b h t", b=4), [qs_all], [qpeT])
    BS = cb.t[:, CB_BS:CB_BS + 512].rearrange("p (j c) -> p j c", j=8)
    mA3b = f.mark()
    rowsb_r = Ring([f.sbuf(f"rowsb{i}", [128, 4, D_CKV], BF16) for i in range(8)])
    cT_r = Ring([f.sbuf(f"cT{i}", [128, 2, 512], BF16) for i in range(3)])
    kpT_r = Ring([f.sbuf(f"kpT{i}", [96, 512], BF16) for i in range(3)])
    sqk_r = Ring([f.sbuf(f"sqk{i}", [128, 512], BF16) for i in range(3)])
    rs_r3 = Ring([f.sbuf(f"rsS{i}", [64, 512], F32) for i in range(2)])
    sc_r3 = Ring([f.sbuf(f"scS{i}", [64, 512], F32) for i in range(2)])
    pS_r3 = Ring([f.sbuf(f"pS{i}", [64, 512], BF16) for i in range(2)])
    pTs_r3 = Ring([f.sbuf(f"pTs{i}", [128, 4, 64], BF16) for i in range(2)])
    tmp_r3 = Ring([f.sbuf(f"tmpS{i}", [64, 8], F32) for i in range(2)])
    m_run = f.sbuf("m_run", [64, 1], F32)
    l_run = f.sbuf("l_run", [64, 2], F32)
    acc = f.sbuf("accS", [64, 256], F32)
    accn = f.sbuf("accn", [64, 256], BF16)
    olT = f.sbuf("olT", [128, 2, 64], BF16)
    bX, bY, bK0, bK1, bSS, bN, bP, bT2 = PS
    ringK = Ring([bK0, bK1])

    units = []
    for b in range(4):
        for gi in range(32):
            units.append(dict(b=b, gi=gi, N=512, first=(gi == 0), last=False))
        units.append(dict(b=b, gi=None, N=NS, first=False, last=True))

    def stageG(u):
        if u["gi"] is None:
            return
        rb = rowsb_r.next()
        u["rb"] = rb
        for pg in range(4):
            col = u["b"] * 128 + u["gi"] * 4 + pg
            f.dma("pool", lambda e, rb=rb, pg=pg, col=col: e.indirect_dma_start(
                out=rb.t[:, pg, :], out_offset=None, in_=cache[:, :],
                in_offset=bass.IndirectOffsetOnAxis(ap=ridx.t[:, col:col + 1], axis=0)), reads=[ridx], writes=[rb])

    def stageA_tr(u, pg):
        if u["gi"] is None:
            return
        rb = u["rb"]
        for m in range(2):
            slot = m * 4 + pg
            TR(bX, bfv(bX)[:, slot * 128:(slot + 1) * 128], rb.t[:, pg, m * 128:(m + 1) * 128], idb.t[:, :], [rb, idb])
        TR(bY, bfv(bY)[0:96, pg * 128:(pg + 1) * 128], rb.t[:, pg, 192:288], idb.t[:, :], [rb, idb])

    def stageA_cp(u):
        if u["gi"] is None:
            u["cT"] = lambda m: ckvT.t[:, m, SEQ:NT]
            u["kpT"] = kpe_b.t[64:96, SEQ:NT]
            u["nat"] = [(rown_b.t[0:NS, :], NS, 0)]
            u["cbufs"] = [ckvT, kpe_b, rown_b]
            u["mask"] = maskS.t[:, u["b"], :]
            return
        rb = u["rb"]
        cT = cT_r.next(); kpT = kpT_r.next()
        CP("dve", cT.t[:, :, :].rearrange("p m n -> p (m n)"), bfv(bX)[:, 0:1024], [bX], [cT])
        CP("dve", kpT.t[64:96, :], bfv(bY)[64:96, 0:512], [bY], [kpT])
        u["cT"] = lambda m, cT=cT: cT.t[:, m, :]
        u["kpT"] = kpT.t[64:96, :]
        u["nat"] = [(rb.t[:, pg, 0:256], 128, pg * 128) for pg in range(4)]
        u["cbufs"] = [cT, kpT, rb]
        u["mask"] = None

    def stageA(u):
        for pg in range(4):
            stageA_tr(u, pg)
        stageA_cp(u)

    def stageB_step(u, k):
        N = u["N"]; cT_ap = u["cT"]; cbufs = u["cbufs"]; b = u["b"]

        def kr(jc):
            bk = ringK.next()
            for m in range(2):
                MM(bk, bk.t[:, 0:N], wuk.t[:, m, jc * 128:(jc + 1) * 128], cT_ap(m), m == 0, m == 1, [wuk] + cbufs)
            sqk = sqk_r.next()
            ACT(sqk.t[:, 0:N], bk.t[:, 0:N], AF.Square, [bk], [sqk])
            u.setdefault("sq", {})[jc] = sqk

        def ss(jc):
            sqk = u["sq"][jc]
            MM(bSS, bSS.t[0:64, 0:N], BS[:, jc, :], sqk.t[:, 0:N], jc == 0, jc == 7, [cb, sqk])
        if k == 0:
            kr(0); kr(1)
        elif k < 7:
            ss(k - 1); kr(k + 1)
            if k == 1:
                for m in range(2):
                    MM(bN, bN.t[0:64, 0:N], qabsT.t[:, m, b, :], cT_ap(m), m == 0, m == 1, [qabsT] + cbufs)
                MM(bP, bP.t[0:64, 0:N], qpeT.t[64:96, b, :], u["kpT"], True, True, [qpeT] + cbufs)
        else:
            ss(6); ss(7)

    def stageC1(u):
        N = u["N"]
        rs = rs_r3.next(); sc = sc_r3.next()
        rstd_chain(bSS, 64, N, rs, 1.0)
        TT("dve", sc.t[0:64, 0:N], bN.t[0:64, 0:N], rs.t[0:64, 0:N], ALU.mult, [bN, rs], [sc])
        TT("dve", sc.t[0:64, 0:N], bP.t[0:64, 0:N], sc.t[0:64, 0:N], ALU.add, [bP, sc], [sc])
        if u["mask"] is not None:
            TT("dve", sc.t[0:64, 0:N], sc.t[0:64, 0:N], u["mask"], ALU.add, [sc, maskS], [sc])
        u["sc"] = sc

    def stageC2a(u):
        N = u["N"]; sc = u["sc"]; nat = u["nat"]; cbufs = u["cbufs"]; b = u["b"]
        if u["first"]:
            MSET("pool", m_run.t[:, :], NEG, [m_run])
            MSET("pool", l_run.t[:, 0:1], 0.0, [l_run])
            MSET("pool", acc.t[:, :], 0.0, [acc])
        tmp = tmp_r3.next(); pS = pS_r3.next(); pTs = pTs_r3.next()

        def tc(i):
            return tmp.t[0:64, i:i + 1]
        MSET("dve", tc(4), 0.0, [tmp])
        f.op("dve", lambda e: e.tensor_reduce(out=tc(0), in_=sc.t[0:64, 0:N], axis=AX.X, op=ALU.max), reads=[sc], writes=[tmp])
        TT("dve", tc(1), m_run.t[:, 0:1], tc(0), ALU.max, [m_run, tmp], [tmp])
        TS("dve", tc(2), tc(1), -SCALE_A, ALU.mult, [tmp], [tmp])
        u["tmp"] = tmp; u["pS"] = pS; u["pTs"] = pTs

    def stageC2a2(u):
        N = u["N"]; sc = u["sc"]
        tmp = u["tmp"]; pS = u["pS"]

        def tc(i):
            return tmp.t[0:64, i:i + 1]
        ACT(tc(3), m_run.t[:, 0:1], AF.Exp, [m_run, tmp], [tmp], scale=SCALE_A, bias=tc(2))
        ACT(pS.t[0:64, 0:N], sc.t[0:64, 0:N], AF.Exp, [sc, tmp], [pS, tmp], scale=SCALE_A, bias=tc(2), accum_out=tc(4))
        STT("dve", l_run.t[:, 0:1], l_run.t[:, 0:1], tc(3), tc(4), ALU.mult, ALU.add, [l_run, tmp], [l_run])
        CP("dve", m_run.t[:, 0:1], tc(1), [tmp], [m_run])

    def stageC2b(u):
        N = u["N"]; nat = u["nat"]; cbufs = u["cbufs"]; b = u["b"]
        tmp = u["tmp"]; pS = u["pS"]; pTs = u["pTs"]

        def tc(i):
            return tmp.t[0:64, i:i + 1]
        npg = len(nat)
        for pg, (rows_ap, K, col0) in enumerate(nat):
            TR(bT2, bfv(bT2)[0:K, pg * 64:(pg + 1) * 64], pS.t[0:64, col0:col0 + K], idb.t[0:64, 0:64], [pS, idb])
        Kmax = max(K for (_, K, _) in nat)
        CP("dve", pTs.t[0:Kmax, 0:npg, :], bfv(bT2)[0:Kmax, 0:npg * 64].rearrange("p (g c) -> p g c", g=npg), [bT2], [pTs])

    def stageC2c(u):
        N = u["N"]; nat = u["nat"]; cbufs = u["cbufs"]; b = u["b"]
        tmp = u["tmp"]; pS = u["pS"]; pTs = u["pTs"]

        def tc(i):
            return tmp.t[0:64, i:i + 1]
        npg = len(nat)
        for pg, (rows_ap, K, col0) in enumerate(nat):
            MM(bT2, bT2.t[0:64, 256:512], pTs.t[0:K, pg, :], rows_ap, pg == 0, pg == npg - 1, [pTs] + cbufs)
        STT("dve", acc.t[:, :], acc.t[:, :], tc(3), bT2.t[0:64, 256:512], ALU.mult, ALU.add, [acc, tmp, bT2], [acc])
        if u["last"]:
            f.op("dve", lambda e: e.reciprocal(out=l_run.t[:, 1:2], in_=l_run.t[:, 0:1]), reads=[l_run], writes=[l_run])
            TS("dve", accn.t[:, :], acc.t[:, :], l_run.t[:, 1:2], ALU.mult, [acc, l_run], [accn])
            for m in range(2):
                TR(bT2, bfv(bT2)[:, m * 64:(m + 1) * 64], accn.t[0:64, m * 128:(m + 1) * 128], idb.t[0:64, 0:64], [accn, idb])
            CP("act", olT.t[:, :, :], bfv(bT2)[:, 0:128].rearrange("p (m c) -> p m c", m=2), [bT2], [olT])
            for hh in range(H):
                for m in range(2):
                    MM(bT2, bT2.t[0:64, 256 + hh * 4:256 + (hh + 1) * 4], wuv.t[:, m, hh * 64:(hh + 1) * 64], olT.t[:, m, hh * 4:(hh + 1) * 4], m == 0, m == 1, [wuv, olT])
            CP("dve", OTs.t[0:64, :, b * 4:(b + 1) * 4], bT2.t[0:64, 256:320].rearrange("p (h t) -> p h t", h=H), [bT2], [OTs])

    nu = len(units)
    for i in range(min(4, nu)):
        stageG(units[i])
    stageA(units[0]); stageA(units[1])
    for k in range(8):
        stageB_step(units[0], k)
    stageC1(units[0])
    for i in range(nu):
        u0 = units[i]
        u1 = units[i + 1] if i + 1 < nu else None
        u2 = units[i + 2] if i + 2 < nu else None
        if i + 4 < nu:
            stageG(units[i + 4])
        for k in range(8):
            if u1 is not None:
                stageB_step(u1, k)
            if k < 4 and u2 is not None:
                stageA_tr(u2, k)
            if k == 0:
                stageC2a(u0)
            if k == 3 and u2 is not None:
                stageA_cp(u2)
            if k == 4:
                stageC2a2(u0)
            if k == 6:
                stageC2b(u0)
            if k == 7:
                stageC2c(u0)
        if u1 is not None:
            stageC1(u1)
    f.release(mA3b)
    wao = f.sbuf("wao", [64, H, D], BF16)
    for hh in range(H):
        LD("pool", wao.t[0:64, hh, :], w_a_out[hh * 64:(hh + 1) * 64, :], [wao])
    for j in range(8):
        pb = ringK.next()
        for hh in range(H):
            MM(pb, pb.t[:, 0:NS], wao.t[0:64, hh, j * 128:(j + 1) * 128], OTs.t[0:64, hh, :], hh == 0, hh == H - 1, [wao, OTs])
        TT("dve", hT.t[:, j, SEQ:NT], pb.t[:, 0:NS], hT.t[:, j, SEQ:NT], ALU.add, [pb, hT], [hT])
    f.release(mA3)
    f.release(mA)
    if stop_after == "A3":
        emit_output()
        f.finish()
        return nc

    def ffn(l):
        mF = f.mark()
        xnT = f.sbuf("xnT", [128, 8, NT], BF16)
        sq8 = f.sbuf("sq8F", [128, 8, 512], BF16)
        rs = f.sbuf("rsF", [128, 512], F32)
        gcol = V_GFA if l == 0 else V_GFB
        for (c0, N) in CHUNKS:
            ACT(sq8.t[:, :, 0:N], hT.t[:, :, c0:c0 + N], AF.Square, [hT], [sq8])
            pss = PS[7]
            for j in range(8):
                MM(pss, pss.t[:, 0:N], ones_b, sq8.t[:, j, 0:N], j == 0, j == 7, [cb, sq8])
            rstd_chain(pss, 128, N, rs, 1.0 / D)
            for j in range(8):
                STT("dve", xnT.t[:, j, c0:c0 + N], hT.t[:, j, c0:c0 + N], vcol(gcol + j), rs.t[:, 0:N], ALU.mult, ALU.mult, [hT, vecs, rs], [xnT])
        wg_r = Ring([f.sbuf(f"wg{i}", [128, 8, 512], BF16) for i in range(2)])
        wu_r = Ring([f.sbuf(f"wu{i}", [128, 8, 512], BF16) for i in range(2)])
        wo_r2 = Ring([f.sbuf(f"wo{i}", [128, 4, D], BF16) for i in range(2)])
        uT_r = Ring([f.sbuf(f"uT{i}", [128, 4, 512], BF16) for i in range(2)])
        sg_r = Ring([f.sbuf(f"sg{i}", [128, 512], F32) for i in range(2)])
        ring = Ring(PS[0:8])
        nblk = (D_FF + 511) // 512
        for hb in range(nblk):
            h0 = hb * 512
            HW = min(512, D_FF - h0)
            nhc = HW // 128
            wg = wg_r.next(); wu = wu_r.next(); wo = wo_r2.next()
            load_w(wg, w_ffn_in[l], 8, h0, HW)
            load_w(wu, w_ffn_in[l], 8, D_FF + h0, HW)
            for hc in range(nhc):
                LD("pool", wo.t[:, hc, :], w_ffn_out[l, h0 + hc * 128:h0 + (hc + 1) * 128, :], [wo])
            for (c0, N) in CHUNKS:
                uT = uT_r.next()
                for hc in range(nhc):
                    pg_ = ring.next()
                    for j in range(8):
                        MM(pg_, pg_.t[:, 0:N], wg.t[:, j, hc * 128:(hc + 1) * 128], xnT.t[:, j, c0:c0 + N], j == 0, j == 7, [wg, xnT])
                    pu_ = ring.next()
                    for j in range(8):
                        MM(pu_, pu_.t[:, 0:N], wu.t[:, j, hc * 128:(hc + 1) * 128], xnT.t[:, j, c0:c0 + N], j == 0, j == 7, [wu, xnT])
                    sg = sg_r.next()
                    ACT(sg.t[:, 0:N], pg_.t[:, 0:N], AF.Silu, [pg_], [sg])
                    TT("dve", uT.t[:, hc, 0:N], pu_.t[:, 0:N], sg.t[:, 0:N], ALU.mult, [pu_, sg], [uT])
                for j in range(8):
                    po = ring.next()
                    for hc in range(nhc):
                        MM(po, po.t[:, 0:N], wo.t[:, hc, j * 128:(j + 1) * 128], uT.t[:, hc, 0:N], hc == 0, hc == nhc - 1, [wo, uT])
                    TT("dve", hT.t[:, j, c0:c0 + N], po.t[:, 0:N], hT.t[:, j, c0:c0 + N], ALU.add, [po, hT], [hT])
        f.release(mF)

    ffn(0)
    if stop_after == "FA":
        emit_output()
        f.finish()
        return nc

    mB = f.mark()
    wkv = f.sbuf("wkv", [128, 8, 512], BF16)
    load_w(wkv, w_kv, 8, 0, 512)
    wq_r = Ring([f.sbuf("wqh0", [128, 8, 512], BF16)])
    wbo_r = Ring([f.sbuf("wboh0", [128, 4, D], BF16)])
    esk = f.sbuf("esk", [128, H], F32)
    ACT(esk.t[:, :], vecs.t[:, V_SINKR:V_SINKR + H], AF.Exp, [vecs], [esk])
    maskB = f.sbuf("maskB", [16, 4, 144], F32)
    LD("sp", maskB.t[:, :, :], maskB_d.rearrange("p (b k) -> p b k", b=4), [maskB])
    KB_c = f.sbuf("KB_c", [64, 4, 640], BF16)
    VB_c = f.sbuf("VB_c", [128, 5, 4, 192], BF16)
    MSET("pool", VB_c.t[:, :, :, 0:64], 1.0, [VB_c])
    MSET("pool", VB_c.t[:, :, :, 128:192], 1.0, [VB_c])
    QB_c = f.sbuf("QB_c", [64, 4, 8, 128], BF16)
    OB_c = f.sbuf("OB_c", [128, 4, 512], BF16)
    QBs = f.sbuf("QBs", [64, 4, 8, 4], BF16)
    OBs = f.sbuf("OBs", [128, 8, NS], BF16)
    kn_s = f.sbuf("kn_s", [64, 4, NS], F32)
    kn_sb = f.sbuf("kn_sb", [64, 4, NS], BF16)
    VBs_f = f.sbuf("VBs_f", [NS, 256], F32)
    VBs_b = f.sbuf("VBs_b", [NS, 256], BF16)
    sqx = f.sbuf("sqxB", [128, 8, 512], BF16)
    rs0 = f.sbuf("rs0B", [128, 512], F32)
    xkv = f.sbuf("xkv", [128, 8, 512], BF16)
    sq_r = Ring([f.sbuf(f"sqC{i}", [64, 512], BF16) for i in range(4)])
    rs_r = Ring([f.sbuf(f"rsC{i}", [64, 512], F32) for i in range(4)])
    kn_r = Ring([f.sbuf(f"knC{i}", [64, 512], F32) for i in range(4)])
    t1_r = Ring([f.sbuf(f"t1C{i}", [16, 512], F32) for i in range(2)])
    t2_r = Ring([f.sbuf(f"t2C{i}", [16, 512], F32) for i in range(2)])
    pT_r = Ring([f.sbuf(f"pTC{i}", [128, 512], BF16) for i in range(4)])
    lt_r = Ring([f.sbuf(f"ltC{i}", [128, 512], F32) for i in range(3)])
    tabc = Ring([f.sbuf(f"tabcC{i}", [16, 2, 512], F32) for i in range(2)])
    kvst = f.sbuf("kvst", [128, 2, 256], F32)
    ringP = Ring(PS[6:8])
    ringS = Ring(PS[0:3])
    ringO = Ring(PS[3:6])
    B64 = cb.t[0:64, CB_B96:CB_B96 + 64]
    P16 = cf.t[0:16, CF_P16:CF_P16 + 16]
    maskD4 = cb.t[:, CB_MD:CB_MD + 512]
    maskP4 = cb.t[:, CB_MP:CB_MP + 512]

    def proj_batch(specs, N, tb):
        n = len(specs)
        for i, sp in enumerate(specs):
            for j in range(8):
                MM(PS[i], PS[i].t[0:64, 0:N], sp["lhsT"](j), sp["rhs"](j), j == 0, j == 7, sp["rd"])
        sqs, rss, kns = [], [], []
        for i in range(n):
            sq = sq_r.next(); sqs.append(sq)
            ACT(sq.t[0:64, 0:N], PS[i].t[0:64, 0:N], AF.Square, [PS[i]], [sq])
        for i in range(n):
            MM(PS[4 + i], PS[4 + i].t[0:64, 0:N], B64, sqs[i].t[0:64, 0:N], True, True, [cb, sqs[i]])
        for i in range(n):
            rs = rs_r.next(); rss.append(rs)
            rstd_chain(PS[4 + i], 64, N, rs, 1.0)
        for i, sp in enumerate(specs):
            kn = kn_r.next(); kns.append(kn)
            if sp.get("dst") is not None:
                dap, dbuf = sp["dst"]
                vw = lambda ap: ap.rearrange("p (t q) -> p t q", t=4)
                STT("dve", dap(0, 64), vw(PS[i].t[0:64, 0:N]), vcol(sp["gcol"], 0, 64), vw(rss[i].t[0:64, 0:N]), ALU.mult, ALU.mult, [PS[i], vecs, rss[i]], [dbuf])
                STT("dve", kn.t[0:16, 0:N], PS[i].t[0:16, 0:N], vcol(sp["gcol"], 0, 16), rss[i].t[0:16, 0:N], ALU.mult, ALU.mult, [PS[i], vecs, rss[i]], [kn])
            else:
                STT("dve", kn.t[0:64, 0:N], PS[i].t[0:64, 0:N], vcol(sp["gcol"], 0, 64), rss[i].t[0:64, 0:N], ALU.mult, ALU.mult, [PS[i], vecs, rss[i]], [kn])
        for i in range(n):
            MM(PS[4 + i], PS[4 + i].t[0:16, 0:N], P16, kns[i].t[0:16, 0:N], True, True, [cf, kns[i]])
        for i, sp in enumerate(specs):
            kn = kns[i]
            t1 = t1_r.next(); t2 = t2_r.next()
            TT("pool", t1.t[0:16, 0:N], kn.t[0:16, 0:N], tb.t[0:16, 0, 0:N], ALU.mult, [kn, tb], [t1])
            TT("dve", t2.t[0:16, 0:N], PS[4 + i].t[0:16, 0:N], tb.t[0:16, 1, 0:N], ALU.mult, [PS[4 + i], tb], [t2])
            if sp.get("dst") is not None:
                dap, dbuf = sp["dst"]
                vw = lambda ap: ap.rearrange("p (t q) -> p t q", t=4)
                TT("dve", dap(0, 16), vw(t1.t[0:16, 0:N]), vw(t2.t[0:16, 0:N]), ALU.add, [t1, t2], [dbuf])
            else:
                TT("dve", kn.t[0:16, 0:N], t1.t[0:16, 0:N], t2.t[0:16, 0:N], ALU.add, [t1, t2], [kn])
                sp["consume"](kn)

    for ci, (c0, N) in enumerate(CHUNKS):
        samp = c0 >= SEQ
        tb = tabc.next()
        LD("sp", tb.t[0:16, 0, 0:N], tabB_d[0, :, c0:c0 + N], [tb])
        LD("sp", tb.t[0:16, 1, 0:N], tabB_d[1, :, c0:c0 + N], [tb])
        ACT(sqx.t[:, :, 0:N], hT.t[:, :, c0:c0 + N], AF.Square, [hT], [sqx])
        pss = ringP.next()
        for j in range(8):
            MM(pss, pss.t[:, 0:N], ones_b, sqx.t[:, j, 0:N], j == 0, j == 7, [cb, sqx])
        rstd_chain(pss, 128, N, rs0, 1.0 / D)
        xnB = sqx
        for j in range(8):
            STT("dve", xkv.t[:, j, 0:N], hT.t[:, j, c0:c0 + N], vcol(V_GKV + j), rs0.t[:, 0:N], ALU.mult, ALU.mult, [hT, vecs, rs0], [xkv])
            STT("dve", xnB.t[:, j, 0:N], hT.t[:, j, c0:c0 + N], vcol(V_GB + j), rs0.t[:, 0:N], ALU.mult, ALU.mult, [hT, vecs, rs0], [xnB])
        kn_keep = {}

        def k_consume(kvh):
            def fn(kn):
                if not samp:
                    CP("pool", KB_c.t[0:64, kvh, 128:128 + N], kn.t[0:64, 0:N], [kn], [KB_c])
                    kn_keep[kvh] = kn
                else:
                    CP("pool", kn_s.t[0:64, kvh, :], kn.t[0:64, 0:NS], [kn], [kn_s])
                    CP("pool", kn_sb.t[0:64, kvh, :], kn.t[0:64, 0:NS], [kn], [kn_sb])
            return fn
        proj_batch([dict(lhsT=(lambda j, kvh=kvh: wkv.t[:, j, kvh * 64:(kvh + 1) * 64]), rhs=(lambda j: xkv.t[:, j, 0:N]),
                         rd=[wkv, xkv], gcol=V_GKB, consume=k_consume(kvh)) for kvh in range(N_KV)], N, tb)
        if ci == 3:
            for kvh in range(N_KV):
                pbT = ringP.next()
                TR(pbT, pbT.t[:, 0:64], kn_keep[kvh].t[0:64, 384:512], cf.t[0:64, CF_ID:CF_ID + 64], [kn_keep[kvh], cf])
                CP("act", kvst.t[:, 0, kvh * 64:(kvh + 1) * 64], pbT.t[:, 0:64], [pbT], [kvst])
            ST("sp", wk_p[:, :], kvst.t[:, 0, :], [kvst])
        if not samp:
            for ti in range(4):
                bv = ringP.next()
                for j in range(8):
                    MM(bv, bv.t[:, 0:256], xkv.t[:, j, ti * 128:(ti + 1) * 128], wkv.t[:, j, 256:512], j == 0, j == 7, [xkv, wkv])
                CP("act", VB_c.t[:, 1 + ti, :, 64:128], bv.t[:, 0:256].rearrange("p (k d) -> p k d", k=4), [bv], [VB_c])
                if ci == 3 and ti == 3:
                    CP("dve", kvst.t[:, 1, :], bv.t[:, 0:256], [bv], [kvst])
                    ST("sp", wv_p[:, :], kvst.t[:, 1, :], [kvst])
        else:
            bv = ringP.next()
            for j in range(8):
                MM(bv, bv.t[0:NS, 0:256], xkv.t[:, j, 0:NS], wkv.t[:, j, 256:512], j == 0, j == 7, [xkv, wkv])
            CP("act", VBs_f.t[:, :], bv.t[0:NS, 0:256], [bv], [VBs_f])
            CP("dve", VBs_b.t[:, :], bv.t[0:NS, 0:256], [bv], [VBs_b])
            pbT = ringP.next()
            for kvh in range(N_KV):
                TR(pbT, pbT.t[0:NS, kvh * 64:(kvh + 1) * 64], kn_s.t[0:64, kvh, :], cf.t[0:64, CF_ID:CF_ID + 64], [kn_s, cf])
            ktok = f.sbuf("ktok", [NS, 256], F32)
            CP("act", ktok.t[:, :], pbT.t[0:NS, 0:256], [pbT], [ktok])
            kw_all = f.sbuf("kw_all", [128, 4, 256], F32)
            vw_all = f.sbuf("vw_all", [128, 4, 256], F32)
            vwb_all = f.sbuf("vwb_all", [128, 4, 256], BF16)
            for b in range(4):
                LD("sp", kw_all.t[:, b, :], swk[b, :, :], [kw_all])
                LD("sp", vw_all.t[:, b, :], swv[b, :, :], [vw_all])
            CP("pool", vwb_all.t[:, :, :], vw_all.t[:, :, :], [vw_all], [vwb_all])
            for b in range(4):
                ST("sp", wk_s[b, 0:124, :], kw_all.t[4:128, b, :], [kw_all])
                ST("sp", wv_s[b, 0:124, :], vw_all.t[4:128, b, :], [vw_all])
                ST("sp", wk_s[b, 124:128, :], ktok.t[b * 4:(b + 1) * 4, :], [ktok])
                ST("sp", wv_s[b, 124:128, :], VBs_f.t[b * 4:(b + 1) * 4, :], [VBs_f])
            Kcat = f.sbuf("Kcat", [64, 144], BF16)
            scb = f.sbuf("scb", [16, 144], F32)
            pb_ = f.sbuf("pbB", [16, 144], BF16)
            pTw = f.sbuf("pTw", [128, 32], BF16)
            st2 = f.sbuf("st2", [16, 8], F32)
            onb = f.sbuf("onb", [16, 128], BF16)

            def s2(i):
                return st2.t[0:16, i:i + 1]
        for half in range(2):
            wqh = wq_r.next(); wboh = wbo_r.next()
            load_w(wqh, w_q_b, 8, half * 512, 512)
            for pr in range(4):
                LD("pool", wboh.t[:, pr, :], w_b_out[(half * 4 + pr) * 128:(half * 4 + pr + 1) * 128, :], [wboh])
            def q_consume(h8):
                def fn(kn):
                    if not samp:
                        CP("pool", QB_c.t[0:64, :, h8, :], kn.t[0:64, 0:512].rearrange("p (t q) -> p t q", t=4), [kn], [QB_c])
                    else:
                        CP("pool", QBs.t[0:64, :, h8, :], kn.t[0:64, 0:NS].rearrange("p (b t) -> p b t", b=4), [kn], [QBs])
                return fn
            def q_dst(h8):
                if samp:
                    return None
                slot8 = (h8 // 4) * 4 + [0, 2, 1, 3][h8 % 4]
                return ((lambda p0, p1, slot8=slot8: QB_c.t[p0:p1, :, slot8, :]), QB_c)
            for qb in range(2):
                proj_batch([dict(lhsT=(lambda j, h8=h8: wqh.t[:, j, h8 * 64:(h8 + 1) * 64]), rhs=(lambda j: xnB.t[:, j, 0:N]),
                                 rd=[wqh, xnB], gcol=V_GQB, consume=q_consume(h8), dst=q_dst(h8)) for h8 in range(qb * 4, qb * 4 + 4)], N, tb)
            if not samp:
                iters = []
                for kk in range(2):
                    for ti in range(4):
                        gi = ci * 4 + ti
                        kts = [kt for kt in (gi - 1, gi) if kt >= 0]
                        for n_, kt in enumerate(kts):
                            iters.append(dict(kk=kk, ti=ti, gi=gi, kt=kt, first=(n_ == 0), last=(n_ == len(kts) - 1), n=len(iters)))

                def emitS(it):
                    kk, ti, kt, gi = it["kk"], it["ti"], it["kt"], it["gi"]
                    kvh = half * 2 + kk
                    slot = kt - (ci * 4 - 1)
                    bs_ = ringS.next()
                    MM(bs_, bs_.t[:, 0:512], KB_c.t[0:64, kvh, slot * 128:(slot + 1) * 128],
                       QB_c.t[0:64, ti, kk * 4:(kk + 1) * 4, :].rearrange("p h q -> p (h q)"), True, True, [KB_c, QB_c])
                    pT = pT_r.next()
                    ACT(pT.t[:, :], bs_.t[:, :], AF.Exp, [bs_], [pT], scale=SCALE_B)
                    TT("pool" if it["n"] % 4 == 3 else "dve", pT.t[:, :], pT.t[:, :], maskD4 if kt == gi else maskP4, ALU.mult, [pT, cb], [pT])
                    it["pT"] = pT; it["slot"] = slot; it["kvh"] = kvh
                curB = {}

                def emitPV(it):
                    kk, ti, kvh = it["kk"], it["ti"], it["kvh"]
                    if it["first"]:
                        curB["bo"] = ringO.next()
                    bo = curB["bo"]
                    MM(bo, bo.t[:, 0:256], VB_c.t[:, it["slot"], kvh, 64:192], it["pT"].t[:, 0:256], it["first"], it["last"], [VB_c, it["pT"]], sgc=True)
                    MM(bo, bo.t[:, 256:512], VB_c.t[:, it["slot"], kvh, 0:128], it["pT"].t[:, 256:512], False, it["last"], [VB_c, it["pT"]], sgc=True)
                    if it["last"]:
                        pend_norm.append((it["idx_emit"], bo, kvh, kk, ti))

                def emitNorm(bo, kvh, kk, ti):
                    lt = lt_r.next()
                    v3 = lambda ap: ap.rearrange("p (h q) -> p h q", h=2)
                    eskv = esk.t[:, kvh * 4:(kvh + 1) * 4].rearrange("p (h2 two) -> p h2 two", two=2)
                    TT("dve", v3(lt.t[64:128, 0:256]), v3(bo.t[64:128, 0:256]), eskv[64:128, :, 0].unsqueeze(2).broadcast_to([64, 2, 128]), ALU.add, [bo, esk], [lt])
                    TT("dve", v3(lt.t[0:64, 256:512]), v3(bo.t[0:64, 256:512]), eskv[0:64, :, 1].unsqueeze(2).broadcast_to([64, 2, 128]), ALU.add, [bo, esk], [lt])
                    ACT(lt.t[64:128, 0:256], lt.t[64:128, 0:256], AF.Ln, [lt], [lt])
                    ACT(lt.t[0:64, 256:512], lt.t[0:64, 256:512], AF.Ln, [lt], [lt])
                    ACT(lt.t[64:128, 0:256], lt.t[64:128, 0:256], AF.Exp, [lt], [lt], scale=-1.0)
                    ACT(lt.t[0:64, 256:512], lt.t[0:64, 256:512], AF.Exp, [lt], [lt], scale=-1.0)
                    TT("dve", OB_c.t[0:64, kk * 2:(kk + 1) * 2, ti * 128:(ti + 1) * 128], v3(bo.t[0:64, 0:256]), v3(lt.t[64:128, 0:256]), ALU.mult, [bo, lt], [OB_c])
                    TT("dve", OB_c.t[64:128, kk * 2:(kk + 1) * 2, ti * 128:(ti + 1) * 128], v3(bo.t[64:128, 256:512]), v3(lt.t[0:64, 256:512]), ALU.mult, [bo, lt], [OB_c])
                DEPTH = 2
                NDELAY = 0
                pend_norm = []
                for idx in range(len(iters) + DEPTH + NDELAY):
                    if idx < len(iters):
                        emitS(iters[idx])
                    if 0 <= idx - DEPTH < len(iters):
                        iters[idx - DEPTH]["idx_emit"] = idx
                        emitPV(iters[idx - DEPTH])
                    while pend_norm and pend_norm[0][0] + NDELAY <= idx:
                        _, bo_, kvh_, kk_, ti_ = pend_norm.pop(0)
                        emitNorm(bo_, kvh_, kk_, ti_)
                assert not pend_norm
                for j in range(8):
                    pb = ringP.next()
                    for pr in range(4):
                        MM(pb, pb.t[:, 0:512], wboh.t[:, pr, j * 128:(j + 1) * 128], OB_c.t[:, pr, :], pr == 0, pr == 3, [wboh, OB_c])
                    TT("dve", hT.t[:, j, c0:c0 + 512], pb.t[:, 0:512], hT.t[:, j, c0:c0 + 512], ALU.add, [pb, hT], [hT])
            else:
                for b in range(4):
                    for kk in range(2):
                        kvh = half * 2 + kk
                        pbK = ringP.next()
                        TR(pbK, pbK.t[0:64, 0:128], kw_all.t[:, b, kvh * 64:(kvh + 1) * 64], identf, [kw_all, cf])
                        CP("act", Kcat.t[0:64, 0:128], pbK.t[0:64, 0:128], [pbK], [Kcat])
                        CP("pool", Kcat.t[0:64, 128:144], kn_sb.t[0:64, kvh, :], [kn_sb], [Kcat])
                        bs_ = ringS.next()
                        MM(bs_, bs_.t[0:16, 0:144], QBs.t[0:64, b, kk * 4:(kk + 1) * 4, :].rearrange("p h t -> p (h t)"), Kcat.t[0:64, 0:144], True, True, [QBs, Kcat])
                        TT("dve", scb.t[:, :], bs_.t[0:16, 0:144], maskB.t[:, b, :], ALU.add, [bs_, maskB], [scb])
                        f.op("dve", lambda e: e.tensor_reduce(out=s2(0), in_=scb.t[:, :], axis=AX.X, op=ALU.max), reads=[scb], writes=[st2])
                        TS("dve", s2(1), s2(0), -SCALE_B, ALU.mult, [st2], [st2])
                        MSET("pool", s2(2), 0.0, [st2])
                        ACT(pb_.t[:, :], scb.t[:, :], AF.Exp, [scb, st2], [pb_, st2], scale=SCALE_B, bias=s2(1), accum_out=s2(2))
                        ACT(s2(3), vecs.t[0:16, V_SINKC + kvh:V_SINKC + kvh + 1], AF.Exp, [vecs, st2], [st2], scale=1.0, bias=s2(1))
                        TT("dve", s2(4), s2(2), s2(3), ALU.add, [st2], [st2])
                        f.op("dve", lambda e: e.reciprocal(out=s2(5), in_=s2(4)), reads=[st2], writes=[st2])
                        pbP = ringP.next()
                        TR(pbP, bfv(pbP)[:, 0:16], pb_.t[0:16, 0:128], idb.t[0:16, 0:16], [pb_, idb])
                        TR(pbP, bfv(pbP)[0:16, 16:32], pb_.t[0:16, 128:144], idb.t[0:16, 0:16], [pb_, idb])
                        CP("act", pTw.t[:, 0:32], bfv(pbP)[:, 0:32], [pbP], [pTw])
                        bo = ringO.next()
                        MM(bo, bo.t[0:16, 0:64], pTw.t[:, 0:16], vwb_all.t[:, b, kvh * 64:(kvh + 1) * 64], True, False, [pTw, vwb_all])
                        MM(bo, bo.t[0:16, 0:64], pTw.t[0:16, 16:32], VBs_b.t[0:NS, kvh * 64:(kvh + 1) * 64], False, True, [pTw, VBs_b])
                        TS("dve", onb.t[:, 0:64], bo.t[0:16, 0:64], s2(5), ALU.mult, [bo, st2], [onb])
                        TS("dve", onb.t[:, 64:128], bo.t[0:16, 0:64], s2(5), ALU.mult, [bo, st2], [onb])
                        pbO = ringP.next()
                        TR(pbO, bfv(pbO)[0:128, 0:16], onb.t[0:16, 0:128], idb.t[0:16, 0:16], [onb, idb])
                        for par in range(2):
                            CP("act", OBs.t[par * 64:(par + 1) * 64, kk * 4:(kk + 1) * 4, b * 4:(b + 1) * 4].rearrange("p (h2 two) t -> p h2 two t", two=2)[:, :, par, :],
                               bfv(pbO)[par * 64:(par + 1) * 64, 0:16].rearrange("p (h2 two t) -> p h2 two t", two=2, t=4)[:, :, par, :], [pbO], [OBs])
                for j in range(8):
                    for par in range(2):
                        pb = ringP.next()
                        for pr in range(4):
                            h8 = pr * 2 + par
                            MM(pb, pb.t[:, 0:NS], wboh.t[par * 64:(par + 1) * 64, pr, j * 128:(j + 1) * 128], OBs.t[par * 64:(par + 1) * 64, h8, :], pr == 0, pr == 3, [wboh, OBs])
                        TT("dve", hT.t[:, j, SEQ:NT], pb.t[:, 0:NS], hT.t[:, j, SEQ:NT], ALU.add, [pb, hT], [hT])
        if ci < 3:
            CP("pool", KB_c.t[0:64, :, 0:128], KB_c.t[0:64, :, 512:640], [KB_c], [KB_c])
            CP("pool", VB_c.t[:, 0, :, 64:128], VB_c.t[:, 4, :, 64:128], [VB_c], [VB_c])
    f.release(mB)
    if stop_after == "B":
        emit_output()
        f.finish()
        return nc

    ffn(1)

    emit_output()
    f.finish()
    return nc


def _rope_tab(n_rot, pos):
    inv = np.power(np.float32(THETA), (-np.arange(0, n_rot, 2, dtype=np.float32) / np.float32(n_rot)).astype(np.float32)).astype(np.float32)
    ang = (pos.astype(np.float32)[:, None] * inv[None, :]).astype(np.float32)
    return np.cos(ang.astype(np.float64)).astype(np.float32), np.sin(ang.astype(np.float64)).astype(np.float32)


def _constants():
    pos = np.concatenate([np.arange(SEQ), np.tile(PAST + np.arange(4), 4)]).astype(np.int64)
    cA, sA = _rope_tab(D_ROPE, pos)
    tabA = np.zeros((2, 96, NT), np.float32)
    tabA[0, :64] = 1.0
    for d in range(32):
        tabA[0, 64 + d] = cA[:, d % 16]
        tabA[1, 64 + d] = sA[:, d % 16]
    cB, sB = _rope_tab(16, pos)
    tabB = np.zeros((2, 16, NT), np.float32)
    for d in range(16):
        tabB[0, d] = cB[:, d % 8]
        tabB[1, d] = sB[:, d % 8]
    cf = np.zeros((128, CF_W), np.float32)
    cf[:, CF_ID:CF_ID + 128] = np.eye(128, dtype=np.float32)
    for d in range(16):
        cf[64 + d + 16, CF_P96 + 64 + d] = -1.0
        cf[64 + d, CF_P96 + 64 + d + 16] = 1.0
    for d in range(8):
        cf[d + 8, CF_P16 + d] = -1.0
        cf[d, CF_P16 + d + 8] = 1.0
    cb = np.zeros((128, CB_W), np.float32)
    cb[:, CB_ONES:CB_ONES + 128] = 1.0
    cb[0:64, CB_B96:CB_B96 + 64] = 1.0 / 64
    cb[64:96, CB_B96 + 64:CB_B96 + 96] = 1.0 / 32
    for jc in range(8):
        for p in range(128):
            hh = 2 * jc + p // 64
            cb[p, CB_BS + jc * 64 + hh * 4:CB_BS + jc * 64 + hh * 4 + 4] = 1.0 / 64
    pp = np.arange(128)[:, None]; cc = np.arange(128)[None, :]
    mD = (cc >= pp).astype(np.float32); mP = (cc < pp).astype(np.float32)
    cb[:, CB_MD:CB_MD + 512] = np.tile(mD, (1, 4))
    cb[:, CB_MP:CB_MP + 512] = np.tile(mP, (1, 4))
    maskS = np.full((64, 4, 16), NEG, np.float32)
    for hh in range(H):
        for t in range(4):
            for b in range(4):
                for t2 in range(t + 1):
                    maskS[hh * 4 + t, b, b * 4 + t2] = 0.0
    maskB = np.full((16, 4, 144), NEG, np.float32)
    for hq in range(4):
        for t in range(4):
            for b in range(4):
                maskB[hq * 4 + t, b, t + 1:128] = 0.0
                for t2 in range(t + 1):
                    maskB[hq * 4 + t, b, 128 + b * 4 + t2] = 0.0
    return dict(tabA=tabA, tabB=tabB, cf32=cf, cb32=cb, maskS=maskS.reshape(64, 64), maskB=maskB.reshape(16, 576))


def _vecs(inp):
    v = np.zeros((128, NV), np.float32)

    def colmaj(g, c0):
        n = g.shape[0] // 128
        v[:, c0:c0 + n] = g.reshape(n, 128).T
    colmaj(inp["norm_attn"][0], V_GA); colmaj(inp["norm_ffn"][0], V_GFA); colmaj(inp["g_kv_shared"], V_GKV)
    colmaj(inp["norm_attn"][1], V_GB); colmaj(inp["norm_ffn"][1], V_GFB)
    colmaj(inp["g_qc"][0], V_GQC); colmaj(inp["g_ckv"][0], V_GCKV)
    v[0:64, V_GQ96] = inp["g_qn_a"][0]; v[64:96, V_GQ96] = inp["g_qr_a"][0]
    v[0:64, V_GK96] = inp["g_kn_a"][0]; v[64:96, V_GK96] = inp["g_kr_a"][0]
    v[0:64, V_GKB] = inp["g_k_b"]; v[0:64, V_GQB] = inp["g_q_b"][0]
    sk = inp["sinks"][0]
    for kvh in range(4):
        for hq in range(4):
            v[hq * 4:hq * 4 + 4, V_SINKC + kvh] = sk[kvh * 4 + hq]
    v[:, V_SINKR:V_SINKR + H] = sk[None, :]
    return v


_PROG = {}


def kernel(_ncores=8, _stop_after=None, **inp):
    inp = {k: np.asarray(v) for k, v in inp.items()}
    key = _stop_after
    if key not in _PROG:
        _PROG[key] = build_program(_stop_after)
    nc = _PROG[key]
    consts = _constants()
    vecs = _vecs(inp)
    cache2d = np.ascontiguousarray(inp["cache_mla"][0].reshape(NPOOL * 128, D_CKV))
    shared = dict(
        cache=cache2d,
        w_a_in=np.ascontiguousarray(inp["w_a_in"][0]), w_uq=np.ascontiguousarray(inp["w_uq"][0]),
        w_uk=np.ascontiguousarray(inp["w_uk"][0]), w_uv=np.ascontiguousarray(inp["w_uv"][0]),
        w_a_out=np.ascontiguousarray(inp["w_a_out"][0]), w_kv=np.ascontiguousarray(inp["w_kv_shared"]),
        w_q_b=np.ascontiguousarray(inp["w_q_b"][0]), w_b_out=np.ascontiguousarray(inp["w_b_out"][0]),
        w_ffn_in=np.ascontiguousarray(inp["w_ffn_in"]), w_ffn_out=np.ascontiguousarray(inp["w_ffn_out"]),
        vecs=vecs, **consts)
    in_maps = []
    for c in range(_ncores):
        m = dict(shared)
        m["x_p"] = np.ascontiguousarray(inp["x_prompt"][c])
        m["x_s"] = np.ascontiguousarray(inp["x_sample"][4 * c:4 * c + 4].reshape(NS, D))
        m["ptab"] = np.ascontiguousarray(inp["page_table"][4 * c:4 * c + 4].reshape(1, 512).astype(np.int32))
        m["swk"] = np.ascontiguousarray(inp["state_win_k"][4 * c:4 * c + 4].reshape(4, 128, 256))
        m["swv"] = np.ascontiguousarray(inp["state_win_v"][4 * c:4 * c + 4].reshape(4, 128, 256))
        in_maps.append(m)
    res = run_bass_kernel_spmd(nc, in_maps, core_ids=list(range(_ncores)))
    R = res.results
    n = _ncores
    y_prompt = np.stack([R[c]["y_p"] for c in range(n)])
    y_sample = np.concatenate([R[c]["y_s"].reshape(4, 4, D) for c in range(n)])
    rows_pr = np.stack([R[c]["rows_p"] for c in range(n)])[None]
    rows_sa = np.concatenate([R[c]["rows_s"].reshape(4, 4, D_CKV) for c in range(n)])[None]
    wkp = np.stack([R[c]["wk_p"].reshape(128, 4, 64) for c in range(n)])
    wvp = np.stack([R[c]["wv_p"].reshape(128, 4, 64) for c in range(n)])
    wks = np.concatenate([R[c]["wk_s"].reshape(4, 128, 4, 64) for c in range(n)])
    wvs = np.concatenate([R[c]["wv_s"].reshape(4, 128, 4, 64) for c in range(n)])
    f32 = np.float32
    return (y_prompt.astype(f32), y_sample.astype(f32), rows_pr.astype(f32), rows_sa.astype(f32),
            wkp.astype(f32), wvp.astype(f32), wks.astype(f32), wvs.astype(f32))
```

```python
import numpy as np
import concourse.bass as bass
import concourse.mybir as mybir
from concourse.bass_utils import run_bass_kernel_spmd

F32 = mybir.dt.float32
BF16 = mybir.dt.bfloat16
I32 = mybir.dt.int32
AF = mybir.ActivationFunctionType
ALU = mybir.AluOpType
AX = mybir.AxisListType

ENGS = ["pe", "act", "dve", "pool", "sp"]

D = 1024
SEQ = 2048
NS = 16
NT = SEQ + NS
H = 16
D_NOPE, D_ROPE, D_V = 64, 32, 64
D_QC, D_C = 384, 256
D_CKV = D_C + D_ROPE
SCALE_A = float((D_NOPE + D_ROPE) ** -0.5)
N_KV, HD = 4, 64
SCALE_B = float(HD ** -0.5)
D_FF = 2816
EPS = 1e-6
THETA = 500000.0
PAST = 16384
NPOOL = 5120
NEG = -1e30
FILLER = 0
CHUNKS = [(0, 512), (512, 512), (1024, 512), (1536, 512), (2048, 16)]
NV = 69
V_GA, V_GFA, V_GKV, V_GB, V_GFB, V_GQC, V_GCKV, V_GQ96, V_GK96, V_GKB, V_GQB, V_SINKC, V_SINKR = 0, 8, 16, 24, 32, 40, 43, 45, 46, 47, 48, 49, 53
CB_ONES, CB_B96, CB_BS, CB_MD, CB_MP, CB_W = 0, 128, 224, 736, 1248, 1760
CF_ID, CF_P96, CF_P16, CF_W = 0, 128, 224, 240


class Buf:
    __slots__ = ("name", "t", "last_w", "readers", "dsem", "dcnt", "excl")

    def __init__(self, name, t, excl=False, init=()):
        self.name = name
        self.t = t
        self.last_w = []
        self.readers = list(init)
        self.dsem = None
        self.dcnt = 0
        self.excl = excl

    def __getitem__(self, idx):
        return self.t[idx]


def _compact(evs):
    d = {}
    for k, v in evs:
        if d.get(k, 0) < v:
            d[k] = v
    return list(d.items())


class FW:
    def __init__(self, nc):
        self.nc = nc
        self.streams = {e: [] for e in ENGS}
        self.esem = {}
        self.ecnt = {e: 0 for e in ENGS}
        self.seen = {e: {} for e in ENGS}
        self.sems = {}
        self.dma_bufs = []
        self._ctx = []
        self.free_events = []
        self.free_sems = []
        for e in ["pe", "act", "dve", "pool"]:
            self.esem[e] = self._newsem("e_" + e)

    def _newsem(self, name):
        cm = self.nc.semaphore(name)
        h = cm.__enter__()
        self._ctx.append((cm, None))
        self.sems[name] = h
        return name

    def mark(self):
        return len(self._ctx)

    def release(self, mark):
        ev = list(self.free_events)
        while len(self._ctx) > mark:
            cm, b = self._ctx.pop()
            if b is not None:
                ev.extend(b.last_w)
                ev.extend(b.readers)
                if b.dsem is not None:
                    ev.append((b.dsem, b.dcnt))
                cm.__exit__(None, None, None)
            else:
                self._keep.append((cm, b))
        self.free_events = _compact(ev)

    _keep = []

    def sbuf(self, name, shape, dtype):
        self._uid = getattr(self, "_uid", 0) + 1
        name = "s%d_%s" % (self._uid, name)
        cm = self.nc.sbuf_tensor(name, list(shape), dtype)
        t = cm.__enter__()
        b = Buf(name, t, init=self.free_events)
        self._ctx.append((cm, b))
        return b

    def psum(self, name, shape, dtype=F32):
        cm = self.nc.psum_tensor(name, list(shape), dtype)
        t = cm.__enter__()
        b = Buf(name, t, excl=True)
        self._ctx.append((cm, b))
        return b

    def _deps(self, reads, writes):
        ev = []
        for b in reads:
            ev.extend(b.last_w)
            if b.excl:
                ev.extend(b.readers)
        for b in writes:
            ev.extend(b.last_w)
            ev.extend(b.readers)
        return ev

    def _waits(self, eng, ev):
        need = {}
        for (k, v) in ev:
            if need.get(k, 0) < v:
                need[k] = v
        seen = self.seen[eng]
        out = []
        for k, v in need.items():
            if seen.get(k, 0) >= v:
                continue
            seen[k] = v
            out.append((k, v))
        return out

    def _record(self, reads, writes, event, nowaw=False):
        for b in reads:
            if b.excl:
                b.last_w = [event]
                b.readers = []
            else:
                b.readers.append(event)
                if len(b.readers) > 48:
                    b.readers = _compact(b.readers)
        for b in writes:
            if nowaw:
                b.last_w.append(event)
                b.last_w = _compact(b.last_w)
            else:
                b.last_w = [event]
            b.readers = []

    def op(self, eng, fn, reads=(), writes=()):
        ev = self._deps(reads, writes)
        waits = self._waits(eng, ev)
        self.ecnt[eng] += 1
        val = self.ecnt[eng]
        semname = self.esem[eng]
        if eng == "pe":
            self.seen[eng][semname] = val
        sems = self.sems

        def thunk(e, fn=fn, waits=waits, semname=semname):
            for (k, v) in waits:
                e.wait_ge(sems[k], v)
            fn(e).then_inc(sems[semname], 1)
        self.streams[eng].append(thunk)
        self._record(reads, writes, (semname, val))

    def dma(self, q, fn, reads=(), writes=(), own=None):
        if own is None:
            own = writes[0] if writes else reads[0]
        if own.dsem is None:
            cm = self.nc.semaphore("d_" + own.name)
            h = cm.__enter__()
            self._keep.append((cm, None))
            self.sems["d_" + own.name] = h
            own.dsem = "d_" + own.name
            self.dma_bufs.append(own)
        ev = []
        for b in reads:
            ev.extend(b.last_w)
        for b in writes:
            ev.extend([e for e in b.last_w if e[0] != own.dsem])
            ev.extend(b.readers)
        waits = self._waits(q, ev)
        own.dcnt += 16
        val = own.dcnt
        semname = own.dsem
        sems = self.sems

        def thunk(e, fn=fn, waits=waits, semname=semname):
            for (k, v) in waits:
                e.wait_ge(sems[k], v)
            fn(e).then_inc(sems[semname], 16)
        self.streams[q].append(thunk)
        self._record(reads, writes, (semname, val), nowaw=True)

    def finish(self):
        finals = [(b.dsem, b.dcnt) for b in self.dma_bufs]
        sems = self.sems
        nc = self.nc
        streams = self.streams
        with nc.Block() as block:
            @block.sync
            def _(e):
                for th in streams["sp"]:
                    th(e)
                for (k, v) in finals:
                    e.wait_ge(sems[k], v)

            @block.tensor
            def _(e):
                for th in streams["pe"]:
                    th(e)

            @block.scalar
            def _(e):
                for th in streams["act"]:
                    th(e)

            @block.vector
            def _(e):
                for th in streams["dve"]:
                    th(e)

            @block.gpsimd
            def _(e):
                for th in streams["pool"]:
                    th(e)
        while self._ctx:
            cm, b = self._ctx.pop()
            cm.__exit__(None, None, None)
        for cm, b in reversed(self._keep):
            cm.__exit__(None, None, None)
        FW._keep = []


class Ring:
    def __init__(self, items):
        self.items = items
        self.i = 0

    def next(self):
        b = self.items[self.i % len(self.items)]
        self.i += 1
        return b


def build_program(stop_after=None, npool=NPOOL):
    nc = bass.Bass("TRN2", target_bir_lowering=False)
    FW._keep = []
    f = FW(nc)

    def din(name, shape, dt=F32):
        return nc.dram_tensor(name, list(shape), dt, kind="ExternalInput").ap()

    def dout(name, shape, dt=F32):
        return nc.dram_tensor(name, list(shape), dt, kind="ExternalOutput").ap()

    x_p = din("x_p", [SEQ, D]); x_s = din("x_s", [NS, D])
    cache = din("cache", [npool * 128, D_CKV])
    ptab = din("ptab", [1, 512], I32)
    swk = din("swk", [4, 128, 256]); swv = din("swv", [4, 128, 256])
    w_a_in = din("w_a_in", [D, 672]); w_uq = din("w_uq", [D_QC, 1536])
    w_uk = din("w_uk", [D_C, 1024]); w_uv = din("w_uv", [D_C, 1024])
    w_a_out = din("w_a_out", [1024, D]); w_kv = din("w_kv", [D, 512])
    w_q_b = din("w_q_b", [D, 1024]); w_b_out = din("w_b_out", [1024, D])
    w_ffn_in = din("w_ffn_in", [2, D, 2 * D_FF]); w_ffn_out = din("w_ffn_out", [2, D_FF, D])
    vecs_d = din("vecs", [128, NV]); cf_d = din("cf32", [128, CF_W]); cb_d = din("cb32", [128, CB_W])
    tabA_d = din("tabA", [2, 96, NT]); tabB_d = din("tabB", [2, 16, NT])
    maskS_d = din("maskS", [64, 4 * 16]); maskB_d = din("maskB", [16, 4 * 144])

    y_p = dout("y_p", [SEQ, D]); y_s = dout("y_s", [NS, D])
    rows_p = dout("rows_p", [SEQ, D_CKV]); rows_s = dout("rows_s", [NS, D_CKV])
    wk_p = dout("wk_p", [128, 256]); wv_p = dout("wv_p", [128, 256])
    wk_s = dout("wk_s", [4, 128, 256]); wv_s = dout("wv_s", [4, 128, 256])

    def MM(pb, out, lhsT, rhs, start, stop, rd, sgc=False):
        if sgc:
            f.op("pe", lambda e: e.matmul(out, lhsT=lhsT, rhs=rhs, start=start, stop=stop, skip_group_check=True), reads=rd, writes=[pb])
        else:
            f.op("pe", lambda e: e.matmul(out, lhsT=lhsT, rhs=rhs, start=start, stop=stop), reads=rd, writes=[pb])

    def TR(pb, out, in_, ident, rd):
        f.op("pe", lambda e: e.transpose(out=out, in_=in_, identity=ident), reads=rd, writes=[pb])

    def ACT(out, in_, func, rd, wr, **kw):
        f.op("act", lambda e: e.activation(out=out, in_=in_, func=func, **kw), reads=rd, writes=wr)

    def CP(eng, out, in_, rd, wr):
        if eng == "act":
            f.op("act", lambda e: e.copy(out=out, in_=in_), reads=rd, writes=wr)
        else:
            f.op(eng, lambda e: e.tensor_copy(out=out, in_=in_), reads=rd, writes=wr)

    def TT(eng, out, in0, in1, op, rd, wr):
        f.op(eng, lambda e: e.tensor_tensor(out=out, in0=in0, in1=in1, op=op), reads=rd, writes=wr)

    def STT(eng, out, in0, scalar, in1, op0, op1, rd, wr):
        f.op(eng, lambda e: e.scalar_tensor_tensor(out=out, in0=in0, scalar=scalar, in1=in1, op0=op0, op1=op1), reads=rd, writes=wr)

    def TS(eng, out, in0, s1, op0, rd, wr):
        f.op(eng, lambda e: e.tensor_scalar(out=out, in0=in0, scalar1=s1, scalar2=None, op0=op0), reads=rd, writes=wr)

    def MSET(eng, ap, val, wr):
        f.op(eng, lambda e: e.memset(ap, val), writes=wr)

    def LD(q, out, in_, wr, rd=()):
        f.dma(q, lambda e: e.dma_start(out=out, in_=in_), reads=list(rd), writes=list(wr))

    def ST(q, out, in_, rd):
        f.dma(q, lambda e: e.dma_start(out=out, in_=in_), reads=list(rd), writes=[])

    PS = [f.psum(f"ps{i}", [128, 512], F32) for i in range(8)]

    def bfv(pb):
        return pb.t[:, :].bitcast(BF16)

    hT = f.sbuf("hT", [128, 8, NT], F32)
    vecs = f.sbuf("vecs", [128, NV], F32)
    cf = f.sbuf("cf", [128, CF_W], F32)
    cb = f.sbuf("cb", [128, CB_W], BF16)
    idb = f.sbuf("idb", [128, 128], BF16)
    LD("sp", vecs[:, :], vecs_d[:, :], [vecs])
    LD("sp", cf[:, :], cf_d[:, :], [cf])
    LD("pool", cb[:, :], cb_d[:, :], [cb])
    CP("pool", idb[:, :], cf[:, CF_ID:CF_ID + 128], [cf], [idb])
    identf = cf.t[:, CF_ID:CF_ID + 128]
    ones_b = cb.t[:, CB_ONES:CB_ONES + 128]

    def vcol(c, p0=0, p1=128):
        return vecs.t[p0:p1, c:c + 1]

    def rstd_chain(pb, M, N, rs, scale):
        ACT(rs.t[0:M, 0:N], pb.t[0:M, 0:N], AF.Ln, [pb], [rs], scale=scale, bias=EPS)
        ACT(rs.t[0:M, 0:N], rs.t[0:M, 0:N], AF.Exp, [rs], [rs], scale=-0.5)

    def load_w(dst, src2d, kchunks, c0, ncols, prows=128):
        for j in range(kchunks):
            LD("pool", dst.t[0:prows, j, 0:ncols], src2d[j * prows:(j + 1) * prows, c0:c0 + ncols], [dst])

    def emit_output():
        ys_r = Ring([f.sbuf(f"ys{i}", [128, D], F32) for i in range(2)])
        for i in range(17):
            ys = ys_r.next()
            rows = 128 if i < 16 else NS
            for half in range(2):
                pb = PS[(2 * i + half) % 8]
                for q in range(4):
                    j = half * 4 + q
                    TR(pb, pb.t[0:rows, q * 128:(q + 1) * 128], hT.t[:, j, i * 128:i * 128 + rows], identf, [hT, cf])
                CP("act" if half == 0 else "dve", ys.t[0:rows, half * 512:(half + 1) * 512], pb.t[0:rows, 0:512], [pb], [ys])
            if i < 16:
                ST("sp", y_p[i * 128:(i + 1) * 128, :], ys.t[0:rows, :], [ys])
            else:
                ST("sp", y_s[:, :], ys.t[0:rows, :], [ys])


    m0 = f.mark()
    xs_ring = Ring([f.sbuf(f"xs{i}", [128, D], F32) for i in range(2)])
    for i in range(17):
        xs = xs_ring.next()
        rows = 128 if i < 16 else NS
        src = x_p[i * 128:(i + 1) * 128, :] if i < 16 else x_s[:, :]
        LD("sp", xs.t[0:rows, :], src, [xs])
        for half in range(2):
            pb = PS[(2 * i + half) % 8]
            for q in range(4):
                j = half * 4 + q
                TR(pb, pb.t[:, q * 128:q * 128 + rows], xs.t[0:rows, j * 128:(j + 1) * 128], identf[0:rows, 0:rows], [xs, cf])
            srcv = pb.t[:, :].rearrange("p (q c) -> p q c", q=4)[:, :, 0:rows]
            CP("act" if half == 0 else "dve", hT.t[:, half * 4:half * 4 + 4, i * 128:i * 128 + rows], srcv, [pb], [hT])
    f.release(m0)

    mA = f.mark()
    cqT = f.sbuf("cqT", [128, 3, NT], BF16)
    ckvT = f.sbuf("ckvT", [128, 2, NT], BF16)
    kpe_b = f.sbuf("kpe_b", [96, NT], BF16)
    rown_b = f.sbuf("rown_b", [NS, 256], BF16)
    qs_all = f.sbuf("qs_all", [96, H, NS], BF16)

    mA1 = f.mark()
    wain = f.sbuf("wain", [128, 8, 672], BF16)
    load_w(wain, w_a_in, 8, 0, 672)
    sq8 = f.sbuf("sq8", [128, 8, 512], BF16)
    xn = f.sbuf("xn", [128, 8, 512], BF16)
    rs_r = Ring([f.sbuf(f"rsA{i}", [128, 512], F32) for i in range(2)])
    ckv_f = f.sbuf("ckv_f", [128, 2, 512], F32)
    kpn = f.sbuf("kpn", [96, 512], F32)
    kpe_f = f.sbuf("kpe_f", [96, 512], F32)
    t1 = f.sbuf("t1A", [96, 512], F32)
    tabc = Ring([f.sbuf(f"tabcA{i}", [96, 2, 512], F32) for i in range(2)])
    rstage = Ring([f.sbuf(f"rstage{i}", [128, D_CKV], F32) for i in range(2)])
    for (c0, N) in CHUNKS:
        tb = tabc.next()
        LD("sp", tb.t[64:96, 0, 0:N], tabA_d[0, 64:96, c0:c0 + N], [tb])
        LD("sp", tb.t[64:96, 1, 0:N], tabA_d[1, 64:96, c0:c0 + N], [tb])
        ACT(sq8.t[:, :, 0:N], hT.t[:, :, c0:c0 + N], AF.Square, [hT], [sq8])
        pss = PS[6]
        for j in range(8):
            MM(pss, pss.t[:, 0:N], ones_b, sq8.t[:, j, 0:N], j == 0, j == 7, [cb, sq8])
        rs = rs_r.next()
        rstd_chain(pss, 128, N, rs, 1.0 / D)
        for j in range(8):
            STT("dve", xn.t[:, j, 0:N], hT.t[:, j, c0:c0 + N], vcol(V_GA + j), rs.t[:, 0:N], ALU.mult, ALU.mult, [hT, vecs, rs], [xn])
        mts = [(0, 128), (128, 128), (256, 128), (384, 128), (512, 128), (576, 96)]
        for mi, (mc, M) in enumerate(mts):
            pb = PS[mi]
            for j in range(8):
                MM(pb, pb.t[0:M, 0:N], wain.t[:, j, mc:mc + M], xn.t[:, j, 0:N], j == 0, j == 7, [wain, xn])
        for m in range(3):
            ACT(sq8.t[:, m, 0:N], PS[m].t[:, 0:N], AF.Square, [PS[m]], [sq8])
        for m in range(3):
            MM(pss, pss.t[:, 0:N], ones_b, sq8.t[:, m, 0:N], m == 0, m == 2, [cb, sq8])
        rs = rs_r.next()
        rstd_chain(pss, 128, N, rs, 1.0 / D_QC)
        for m in range(3):
            STT("dve", cqT.t[:, m, c0:c0 + N], PS[m].t[:, 0:N], vcol(V_GQC + m), rs.t[:, 0:N], ALU.mult, ALU.mult, [PS[m], vecs, rs], [cqT])
        for m in range(2):
            ACT(sq8.t[:, 3 + m, 0:N], PS[3 + m].t[:, 0:N], AF.Square, [PS[3 + m]], [sq8])
        for m in range(2):
            MM(pss, pss.t[:, 0:N], ones_b, sq8.t[:, 3 + m, 0:N], m == 0, m == 1, [cb, sq8])
        rs = rs_r.next()
        rstd_chain(pss, 128, N, rs, 1.0 / D_C)
        for m in range(2):
            STT("dve", ckv_f.t[:, m, 0:N], PS[3 + m].t[:, 0:N], vcol(V_GCKV + m), rs.t[:, 0:N], ALU.mult, ALU.mult, [PS[3 + m], vecs, rs], [ckv_f])
        CP("pool", ckvT.t[:, :, c0:c0 + N], ckv_f.t[:, :, 0:N], [ckv_f], [ckvT])
        ACT(sq8.t[0:96, 5, 0:N], PS[5].t[0:96, 0:N], AF.Square, [PS[5]], [sq8])
        MM(pss, pss.t[0:96, 0:N], cb.t[0:96, CB_B96:CB_B96 + 96], sq8.t[0:96, 5, 0:N], True, True, [cb, sq8])
        rs = rs_r.next()
        rstd_chain(pss, 96, N, rs, 1.0)
        STT("dve", kpn.t[0:96, 0:N], PS[5].t[0:96, 0:N], vcol(V_GK96, 0, 96), rs.t[0:96, 0:N], ALU.mult, ALU.mult, [PS[5], vecs, rs], [kpn])
        pr = PS[7]
        MM(pr, pr.t[0:96, 0:N], cf.t[0:96, CF_P96:CF_P96 + 96], kpn.t[0:96, 0:N], True, True, [cf, kpn])
        TT("pool", t1.t[64:96, 0:N], kpn.t[64:96, 0:N], tb.t[64:96, 0, 0:N], ALU.mult, [kpn, tb], [t1])
        TT("dve", kpe_f.t[64:96, 0:N], pr.t[64:96, 0:N], tb.t[64:96, 1, 0:N], ALU.mult, [pr, tb], [kpe_f])
        TT("dve", kpe_f.t[64:96, 0:N], kpe_f.t[64:96, 0:N], t1.t[64:96, 0:N], ALU.add, [kpe_f, t1], [kpe_f])
        CP("pool", kpe_b.t[64:96, c0:c0 + N], kpe_f.t[64:96, 0:N], [kpe_f], [kpe_b])
        ntile = (N + 127) // 128
        for ti in range(ntile):
            r = min(128, N - ti * 128)
            pbT = PS[7]
            for m in range(2):
                TR(pbT, pbT.t[0:r, m * 128:(m + 1) * 128], ckv_f.t[:, m, ti * 128:ti * 128 + r], identf, [ckv_f, cf])
            TR(pbT, pbT.t[0:r, 256:288], kpe_f.t[64:96, ti * 128:ti * 128 + r], cf.t[64:96, CF_ID + 64:CF_ID + 96], [kpe_f, cf])
            stg = rstage.next()
            CP("act", stg.t[0:r, :], pbT.t[0:r, 0:D_CKV], [pbT], [stg])
            if c0 < SEQ:
                ST("sp", rows_p[c0 + ti * 128:c0 + ti * 128 + r, :], stg.t[0:r, :], [stg])
            else:
                ST("sp", rows_s[:, :], stg.t[0:r, :], [stg])
                CP("pool", rown_b.t[0:NS, :], stg.t[0:NS, 0:256], [stg], [rown_b])
    f.release(mA1)
    if stop_after == "A1":
        f.release(mA)
        f.finish()
        return nc

    mA2 = f.mark()
    G = 2
    NG = H // G
    wq_r = Ring([f.sbuf(f"wq_g{i}", [128, 3, G * 96], BF16) for i in range(2)])
    wk_r = Ring([f.sbuf(f"wk_g{i}", [128, 2, G * 64], BF16) for i in range(2)])
    wv_r = Ring([f.sbuf(f"wv_g{i}", [128, 2, G * 64], BF16) for i in range(2)])
    wo_r = Ring([f.sbuf(f"wo_g{i}", [128, D], BF16) for i in range(2)])
    qT_g = f.sbuf("qT_g", [128, G, NT], BF16)
    KT_g = f.sbuf("KT_g", [128, G, NT], BF16)
    V_g = f.sbuf("V_g", [128, 16, G, 128], BF16)
    OT_r = Ring([f.sbuf(f"OT_g{i}", [128, SEQ], BF16) for i in range(2)])
    sq_r = Ring([f.sbuf(f"sqB{i}", [128, 512], BF16) for i in range(4)])
    rs_r = Ring([f.sbuf(f"rsB{i}", [96, 512], F32) for i in range(4)])
    qn_r = Ring([f.sbuf(f"qnB{i}", [96, 512], F32) for i in range(2)])
    t1_r = Ring([f.sbuf(f"t1B{i}", [96, 512], F32) for i in range(2)])
    t2_r = Ring([f.sbuf(f"t2B{i}", [96, 512], F32) for i in range(2)])
    pT_r = Ring([f.sbuf(f"pT{i}", [128, 512], BF16) for i in range(4)])
    rl_r = Ring([f.sbuf(f"rl{i}", [128, 512], F32) for i in range(2)])
    tabc = Ring([f.sbuf(f"tabcB{i}", [96, 2, 512], F32) for i in range(2)])
    MSET("pool", qT_g.t[:, :, :], 0.0, [qT_g])
    MSET("pool", KT_g.t[:, :, :], 0.0, [KT_g])
    MSET("pool", V_g.t[:, :, 0, 64:128], 1.0, [V_g])
    MSET("pool", V_g.t[:, :, 1, 0:64], 1.0, [V_g])
    ringP = Ring(PS[5:8])
    ringS = Ring(PS[0:3])
    ringO = Ring(PS[3:5])
    ringA = Ring(PS[0:8])
    maskD = cb.t[:, CB_MD:CB_MD + 128]
    B96 = cb.t[0:96, CB_B96:CB_B96 + 96]
    B64 = cb.t[0:64, CB_B96:CB_B96 + 64]
    P96 = cf.t[0:96, CF_P96:CF_P96 + 96]
    for g in range(NG):
        wq = wq_r.next(); wk = wk_r.next(); wv = wv_r.next(); wo = wo_r.next()
        OT_g = OT_r.next()
        if g % 2 == 0:
            pend = []
        pend.append((wo, OT_g))
        load_w(wq, w_uq, 3, g * G * 96, G * 96)
        load_w(wk, w_uk, 2, g * G * 64, G * 64)
        load_w(wv, w_uv, 2, g * G * 64, G * 64)
        LD("pool", wo.t[:, :], w_a_out[g * 128:(g + 1) * 128, :], [wo])
        def A_S1(ci):
            c0, N = CHUNKS[ci]
            A = PS[0:4] if ci % 2 == 0 else PS[4:8]
            tb = tabc.next()
            LD("sp", tb.t[64:96, 0, 0:N], tabA_d[0, 64:96, c0:c0 + N], [tb])
            LD("sp", tb.t[64:96, 1, 0:N], tabA_d[1, 64:96, c0:c0 + N], [tb])
            chs = []
            for hl in range(G):
                bq = A[hl]
                for m in range(3):
                    MM(bq, bq.t[0:96, 0:N], wq.t[:, m, hl * 96:(hl + 1) * 96], cqT.t[:, m, c0:c0 + N], m == 0, m == 2, [wq, cqT])
                chs.append(dict(q=True, hl=hl, b=bq, M=96, tb=tb))
            for hl in range(G):
                bk = A[2 + hl]
                for m in range(2):
                    MM(bk, bk.t[0:64, 0:N], wk.t[:, m, hl * 64:(hl + 1) * 64], ckvT.t[:, m, c0:c0 + N], m == 0, m == 1, [wk, ckvT])
                chs.append(dict(q=False, hl=hl, b=bk, M=64, tb=tb))
            return chs

        def A_mid(ci, chs):
            c0, N = CHUNKS[ci]
            Bset = PS[4:8] if ci % 2 == 0 else PS[0:4]
            for c in chs:
                M = c["M"]
                c["sq"] = sq_r.next()
                ACT(c["sq"].t[0:M, 0:N], c["b"].t[0:M, 0:N], AF.Square, [c["b"]], [c["sq"]])
            for ic, c in enumerate(chs):
                M = c["M"]
                c["bs"] = Bset[ic]
                MM(c["bs"], c["bs"].t[0:M, 0:N], cb.t[0:M, CB_B96:CB_B96 + M], c["sq"].t[0:M, 0:N], True, True, [cb, c["sq"]])
            for c in chs:
                c["rs"] = rs_r.next()
                rstd_chain(c["bs"], c["M"], N, c["rs"], 1.0)
            for c in chs:
                hl = c["hl"]
                if c["q"]:
                    c["qn"] = qn_r.next()
                    STT("dve", c["qn"].t[0:96, 0:N], c["b"].t[0:96, 0:N], vcol(V_GQ96, 0, 96), c["rs"].t[0:96, 0:N], ALU.mult, ALU.mult, [c["b"], vecs, c["rs"]], [c["qn"]])
                    STT("dve", qT_g.t[0:64, hl, c0:c0 + N], c["b"].t[0:64, 0:N], vcol(V_GQ96, 0, 64), c["rs"].t[0:64, 0:N], ALU.mult, ALU.mult, [c["b"], vecs, c["rs"]], [qT_g])
                else:
                    STT("dve", KT_g.t[0:64, hl, c0:c0 + N], c["b"].t[0:64, 0:N], vcol(V_GK96, 0, 64), c["rs"].t[0:64, 0:N], ALU.mult, ALU.mult, [c["b"], vecs, c["rs"]], [KT_g])
                    CP("pool", KT_g.t[64:96, hl, c0:c0 + N], kpe_b.t[64:96, c0:c0 + N], [kpe_b], [KT_g])

        def A_tail(ci, chs):
            c0, N = CHUNKS[ci]
            A = PS[0:4] if ci % 2 == 0 else PS[4:8]
            for c in chs:
                if c["q"]:
                    c["br"] = A[c["hl"]]
                    MM(c["br"], c["br"].t[0:96, 0:N], P96, c["qn"].t[0:96, 0:N], True, True, [cf, c["qn"]])
            if c0 < SEQ:
                for ti in range(ci * 4, ci * 4 + 4):
                    bv = A[2 + ti % 2]
                    for m in range(2):
                        MM(bv, bv.t[:, 0:G * 64], ckvT.t[:, m, ti * 128:(ti + 1) * 128], wv.t[:, m, 0:G * 64], m == 0, m == 1, [ckvT, wv])
                    CP("dve" if ti % 2 else "act", V_g.t[:, ti, 0, 0:64], bv.t[:, 0:64], [bv], [V_g])
                    CP("act" if ti % 2 else "dve", V_g.t[:, ti, 1, 64:128], bv.t[:, 64:128], [bv], [V_g])
            for c in chs:
                if c["q"]:
                    hl = c["hl"]; qn = c["qn"]; br = c["br"]; tb = c["tb"]
                    t1 = t1_r.next(); t2 = t2_r.next()
                    TT("pool", t1.t[64:96, 0:N], qn.t[64:96, 0:N], tb.t[64:96, 0, 0:N], ALU.mult, [qn, tb], [t1])
                    TT("dve", t2.t[64:96, 0:N], br.t[64:96, 0:N], tb.t[64:96, 1, 0:N], ALU.mult, [br, tb], [t2])
                    TT("dve", qT_g.t[64:96, hl, c0:c0 + N], t1.t[64:96, 0:N], t2.t[64:96, 0:N], ALU.add, [t1, t2], [qT_g])
            if c0 >= SEQ:
                for hl in range(G):
                    CP("pool", qs_all.t[0:96, g * G + hl, :], qT_g.t[0:96, hl, SEQ:NT], [qT_g], [qs_all])

        chs_next = A_S1(0)
        for ci in range(len(CHUNKS)):
            chs_cur = chs_next
            A_mid(ci, chs_cur)
            if ci + 1 < len(CHUNKS):
                chs_next = A_S1(ci + 1)
            A_tail(ci, chs_cur)
        iters = []
        for hl in range(G):
            for qc in range(4):
                nkt = 4 * qc + 4
                for kt in range(nkt):
                    iters.append(dict(hl=hl, qc=qc, kt=kt, nkt=nkt))

        def emitS(it):
            hl, qc, kt = it["hl"], it["qc"], it["kt"]
            n0 = max(qc * 512, kt * 128)
            W = qc * 512 + 512 - n0
            bs_ = ringS.next()
            MM(bs_, bs_.t[:, 0:W], KT_g.t[:, hl, kt * 128:(kt + 1) * 128], qT_g.t[:, hl, n0:n0 + W], True, True, [KT_g, qT_g])
            pT = pT_r.next()
            ACT(pT.t[:, 0:W], bs_.t[:, 0:W], AF.Exp, [bs_], [pT], scale=SCALE_A)
            if kt * 128 >= qc * 512:
                TT("pool", pT.t[:, 0:128], pT.t[:, 0:128], maskD, ALU.mult, [pT, cb], [pT])
            it["pT"] = pT; it["W"] = W; it["o0"] = n0 - qc * 512

        cur = {}

        def emitPV(it):
            hl, qc, kt, nkt = it["hl"], it["qc"], it["kt"], it["nkt"]
            if kt == 0:
                cur["bo"] = ringO.next()
            bo = cur["bo"]
            MM(bo, bo.t[:, it["o0"]:512], V_g.t[:, kt, hl, :], it["pT"].t[:, 0:it["W"]], kt == 0, kt == nkt - 1, [V_g, it["pT"]])
            if kt == nkt - 1:
                rl = rl_r.next()
                oL, oH = (0, 64) if hl == 0 else (64, 128)
                lL, lH = (64, 128) if hl == 0 else (0, 64)
                f.op("dve", lambda e, rl=rl, bo=bo, lL=lL, lH=lH: e.reciprocal(out=rl.t[lL:lH, :], in_=bo.t[lL:lH, :]), reads=[bo], writes=[rl])
                TT("dve", OT_g.t[oL:oH, qc * 512:(qc + 1) * 512], bo.t[oL:oH, :], rl.t[lL:lH, :], ALU.mult, [bo, rl], [OT_g])
        DEPTH = 2
        for idx in range(len(iters) + DEPTH):
            if idx < len(iters):
                emitS(iters[idx])
            if FILLER:
                MM(PS[7], PS[7].t[:, 0:FILLER], KT_g.t[0:96, 0, 0:128], qT_g.t[0:96, 0, 0:FILLER], True, True, [KT_g, qT_g])
            if idx - DEPTH >= 0:
                emitPV(iters[idx - DEPTH])
        if g % 2 == 1:
            for qc in range(4):
                c0 = qc * 512
                for j in range(8):
                    pb = ringP.next()
                    for ip, (wo_, OT_) in enumerate(pend):
                        MM(pb, pb.t[:, 0:512], wo_.t[:, j * 128:(j + 1) * 128], OT_.t[:, c0:c0 + 512], ip == 0, ip == len(pend) - 1, [wo_, OT_])
                    TT("dve", hT.t[:, j, c0:c0 + 512], pb.t[:, 0:512], hT.t[:, j, c0:c0 + 512], ALU.add, [pb, hT], [hT])
    f.release(mA2)
    if stop_after == "A2":
        f.release(mA)
        emit_output()
        f.finish()
        return nc

    mA3 = f.mark()
    wuk = f.sbuf("wuk", [128, 2, 1024], BF16)
    wuv = f.sbuf("wuv", [128, 2, 1024], BF16)
    load_w(wuk, w_uk, 2, 0, 1024)
    load_w(wuv, w_uv, 2, 0, 1024)
    OTs = f.sbuf("OTs", [64, H, NS], BF16)
    wukT = f.sbuf("wukT", [64, H, 256], BF16)
    qabsT = f.sbuf("qabsT", [128, 2, 4, 64], BF16)
    qpeT = f.sbuf("qpeT", [96, 4, 64], BF16)
    maskS = f.sbuf("maskS", [64, 4, 16], F32)
    LD("sp", maskS.t[:, :, :], maskS_d.rearrange("p (b k) -> p b k", b=4), [maskS])
    pti = f.sbuf("pti", [128, 512], I32)
    ptf = f.sbuf("ptf", [128, 512], F32)
    iot = f.sbuf("iot", [128, 1], I32)
    iof = f.sbuf("iof", [128, 1], F32)
    ridx = f.sbuf("ridx", [128, 512], I32)
    LD("sp", pti.t[:, :], ptab[0:1, :].partition_broadcast(128), [pti])
    f.op("pool", lambda e: e.iota(iot.t[:, :], pattern=[[0, 1]], base=0, channel_multiplier=1), writes=[iot])
    CP("dve", iof.t[:, :], iot.t[:, :], [iot], [iof])
    CP("dve", ptf.t[:, :], pti.t[:, :], [pti], [ptf])
    f.op("dve", lambda e: e.tensor_scalar(out=ptf.t[:, :], in0=ptf.t[:, :], scalar1=128.0, scalar2=iof.t[:, 0:1], op0=ALU.mult, op1=ALU.add), reads=[ptf, iof], writes=[ptf])
    CP("dve", ridx.t[:, :], ptf.t[:, :], [ptf], [ridx])
    for hb in range(4):
        pb = PS[hb]
        for hq in range(4):
            for m in range(2):
                slot = hq * 2 + m
                TR(pb, bfv(pb)[0:64, slot * 128:(slot + 1) * 128], wuk.t[:, m, (hb * 4 + hq) * 64:(hb * 4 + hq + 1) * 64], idb.t[:, :], [wuk, idb])
        CP("dve" if hb % 2 else "act", wukT.t[0:64, hb * 4:(hb + 1) * 4, :], bfv(pb)[0:64, 0:1024].rearrange("p (h l) -> p h l", h=4), [pb], [wukT])
    qsg = f.sbuf("qsg", [64, H, NS], BF16)
    TS("dve", qsg.t[0:64, :, :], qs_all.t[0:64, :, :], vcol(V_GK96, 0, 64), ALU.mult, [qs_all, vecs], [qsg])
    pb = PS[4]
    for m in range(2):
        for hh in range(H):
            col = (m * H + hh) * NS
            MM(pb, pb.t[:, col:col + NS], wukT.t[0:64, hh, m * 128:(m + 1) * 128], qsg.t[0:64, hh, :], True, True, [wukT, qsg])
    for m in range(2):
        CP("dve", qabsT.t[:, m, :, :].rearrange("p b (h t) -> p b h t", h=H),
           pb.t[:, m * 256:(m + 1) * 256].rearrange("p (h b t) -> p b h t", h=H, b=4), [pb], [qabsT])
    CP("pool", qpeT.t[64:96, :, :].rearrange("p b (h t) -> p b h t", h=H),
       qs_all.t[64:96, :, :].rearrange("p h (b t) -> p b h t", b=4), [qs_all], [qpeT])
    BS = cb.t[:, CB_BS:CB_BS + 512].rearrange("p (j c) -> p j c", j=8)
    mA3b = f.mark()
    rowsb_r = Ring([f.sbuf(f"rowsb{i}", [128, 4, D_CKV], BF16) for i in range(8)])
    cT_r = Ring([f.sbuf(f"cT{i}", [128, 2, 512], BF16) for i in range(3)])
    kpT_r = Ring([f.sbuf(f"kpT{i}", [96, 512], BF16) for i in range(3)])
    sqk_r = Ring([f.sbuf(f"sqk{i}", [128, 512], BF16) for i in range(3)])
    rs_r3 = Ring([f.sbuf(f"rsS{i}", [64, 512], F32) for i in range(2)])
    sc_r3 = Ring([f.sbuf(f"scS{i}", [64, 512], F32) for i in range(2)])
    pS_r3 = Ring([f.sbuf(f"pS{i}", [64, 512], BF16) for i in range(2)])
    pTs_r3 = Ring([f.sbuf(f"pTs{i}", [128, 4, 64], BF16) for i in range(2)])
    tmp_r3 = Ring([f.sbuf(f"tmpS{i}", [64, 8], F32) for i in range(2)])
    m_run = f.sbuf("m_run", [64, 1], F32)
    l_run = f.sbuf("l_run", [64, 2], F32)
    acc = f.sbuf("accS", [64, 256], F32)
    accn = f.sbuf("accn", [64, 256], BF16)
    olT = f.sbuf("olT", [128, 2, 64], BF16)
    bX, bY, bK0, bK1, bSS, bN, bP, bT2 = PS
    ringK = Ring([bK0, bK1])

    units = []
    for b in range(4):
        for gi in range(32):
            units.append(dict(b=b, gi=gi, N=512, first=(gi == 0), last=False))
        units.append(dict(b=b, gi=None, N=NS, first=False, last=True))

    def stageG(u):
        if u["gi"] is None:
            return
        rb = rowsb_r.next()
        u["rb"] = rb
        for pg in range(4):
            col = u["b"] * 128 + u["gi"] * 4 + pg
            f.dma("pool", lambda e, rb=rb, pg=pg, col=col: e.indirect_dma_start(
                out=rb.t[:, pg, :], out_offset=None, in_=cache[:, :],
                in_offset=bass.IndirectOffsetOnAxis(ap=ridx.t[:, col:col + 1], axis=0)), reads=[ridx], writes=[rb])

    def stageA_tr(u, pg):
        if u["gi"] is None:
            return
        rb = u["rb"]
        for m in range(2):
            slot = m * 4 + pg
            TR(bX, bfv(bX)[:, slot * 128:(slot + 1) * 128], rb.t[:, pg, m * 128:(m + 1) * 128], idb.t[:, :], [rb, idb])
        TR(bY, bfv(bY)[0:96, pg * 128:(pg + 1) * 128], rb.t[:, pg, 192:288], idb.t[:, :], [rb, idb])

    def stageA_cp(u):
        if u["gi"] is None:
            u["cT"] = lambda m: ckvT.t[:, m, SEQ:NT]
            u["kpT"] = kpe_b.t[64:96, SEQ:NT]
            u["nat"] = [(rown_b.t[0:NS, :], NS, 0)]
            u["cbufs"] = [ckvT, kpe_b, rown_b]
            u["mask"] = maskS.t[:, u["b"], :]
            return
        rb = u["rb"]
        cT = cT_r.next(); kpT = kpT_r.next()
        CP("dve", cT.t[:, :, :].rearrange("p m n -> p (m n)"), bfv(bX)[:, 0:1024], [bX], [cT])
        CP("dve", kpT.t[64:96, :], bfv(bY)[64:96, 0:512], [bY], [kpT])
        u["cT"] = lambda m, cT=cT: cT.t[:, m, :]
        u["kpT"] = kpT.t[64:96, :]
        u["nat"] = [(rb.t[:, pg, 0:256], 128, pg * 128) for pg in range(4)]
        u["cbufs"] = [cT, kpT, rb]
        u["mask"] = None

    def stageA(u):
        for pg in range(4):
            stageA_tr(u, pg)
        stageA_cp(u)

    def stageB_step(u, k):
        N = u["N"]; cT_ap = u["cT"]; cbufs = u["cbufs"]; b = u["b"]

        def kr(jc):
            bk = ringK.next()
            for m in range(2):
                MM(bk, bk.t[:, 0:N], wuk.t[:, m, jc * 128:(jc + 1) * 128], cT_ap(m), m == 0, m == 1, [wuk] + cbufs)
            sqk = sqk_r.next()
            ACT(sqk.t[:, 0:N], bk.t[:, 0:N], AF.Square, [bk], [sqk])
            u.setdefault("sq", {})[jc] = sqk

        def ss(jc):
            sqk = u["sq"][jc]
            MM(bSS, bSS.t[0:64, 0:N], BS[:, jc, :], sqk.t[:, 0:N], jc == 0, jc == 7, [cb, sqk])
        if k == 0:
            kr(0); kr(1)
        elif k < 7:
            ss(k - 1); kr(k + 1)
            if k == 1:
                for m in range(2):
                    MM(bN, bN.t[0:64, 0:N], qabsT.t[:, m, b, :], cT_ap(m), m == 0, m == 1, [qabsT] + cbufs)
                MM(bP, bP.t[0:64, 0:N], qpeT.t[64:96, b, :], u["kpT"], True, True, [qpeT] + cbufs)
        else:
            ss(6); ss(7)

    def stageC1(u):
        N = u["N"]
        rs = rs_r3.next(); sc = sc_r3.next()
        rstd_chain(bSS, 64, N, rs, 1.0)
        TT("dve", sc.t[0:64, 0:N], bN.t[0:64, 0:N], rs.t[0:64, 0:N], ALU.mult, [bN, rs], [sc])
        TT("dve", sc.t[0:64, 0:N], bP.t[0:64, 0:N], sc.t[0:64, 0:N], ALU.add, [bP, sc], [sc])
        if u["mask"] is not None:
            TT("dve", sc.t[0:64, 0:N], sc.t[0:64, 0:N], u["mask"], ALU.add, [sc, maskS], [sc])
        u["sc"] = sc

    def stageC2a(u):
        N = u["N"]; sc = u["sc"]; nat = u["nat"]; cbufs = u["cbufs"]; b = u["b"]
        if u["first"]:
            MSET("pool", m_run.t[:, :], NEG, [m_run])
            MSET("pool", l_run.t[:, 0:1], 0.0, [l_run])
            MSET("pool", acc.t[:, :], 0.0, [acc])
        tmp = tmp_r3.next(); pS = pS_r3.next(); pTs = pTs_r3.next()

        def tc(i):
            return tmp.t[0:64, i:i + 1]
        MSET("dve", tc(4), 0.0, [tmp])
        f.op("dve", lambda e: e.tensor_reduce(out=tc(0), in_=sc.t[0:64, 0:N], axis=AX.X, op=ALU.max), reads=[sc], writes=[tmp])
        TT("dve", tc(1), m_run.t[:, 0:1], tc(0), ALU.max, [m_run, tmp], [tmp])
        TS("dve", tc(2), tc(1), -SCALE_A, ALU.mult, [tmp], [tmp])
        u["tmp"] = tmp; u["pS"] = pS; u["pTs"] = pTs

    def stageC2a2(u):
        N = u["N"]; sc = u["sc"]
        tmp = u["tmp"]; pS = u["pS"]

        def tc(i):
            return tmp.t[0:64, i:i + 1]
        ACT(tc(3), m_run.t[:, 0:1], AF.Exp, [m_run, tmp], [tmp], scale=SCALE_A, bias=tc(2))
        ACT(pS.t[0:64, 0:N], sc.t[0:64, 0:N], AF.Exp, [sc, tmp], [pS, tmp], scale=SCALE_A, bias=tc(2), accum_out=tc(4))
        STT("dve", l_run.t[:, 0:1], l_run.t[:, 0:1], tc(3), tc(4), ALU.mult, ALU.add, [l_run, tmp], [l_run])
        CP("dve", m_run.t[:, 0:1], tc(1), [tmp], [m_run])

    def stageC2b(u):
        N = u["N"]; nat = u["nat"]; cbufs = u["cbufs"]; b = u["b"]
        tmp = u["tmp"]; pS = u["pS"]; pTs = u["pTs"]

        def tc(i):
            return tmp.t[0:64, i:i + 1]
        npg = len(nat)
        for pg, (rows_ap, K, col0) in enumerate(nat):
            TR(bT2, bfv(bT2)[0:K, pg * 64:(pg + 1) * 64], pS.t[0:64, col0:col0 + K], idb.t[0:64, 0:64], [pS, idb])
        Kmax = max(K for (_, K, _) in nat)
        CP("dve", pTs.t[0:Kmax, 0:npg, :], bfv(bT2)[0:Kmax, 0:npg * 64].rearrange("p (g c) -> p g c", g=npg), [bT2], [pTs])

    def stageC2c(u):
        N = u["N"]; nat = u["nat"]; cbufs = u["cbufs"]; b = u["b"]
        tmp = u["tmp"]; pS = u["pS"]; pTs = u["pTs"]

        def tc(i):
            return tmp.t[0:64, i:i + 1]
        npg = len(nat)
        for pg, (rows_ap, K, col0) in enumerate(nat):
            MM(bT2, bT2.t[0:64, 256:512], pTs.t[0:K, pg, :], rows_ap, pg == 0, pg == npg - 1, [pTs] + cbufs)
        STT("dve", acc.t[:, :], acc.t[:, :], tc(3), bT2.t[0:64, 256:512], ALU.mult, ALU.add, [acc, tmp, bT2], [acc])
        if u["last"]:
            f.op("dve", lambda e: e.reciprocal(out=l_run.t[:, 1:2], in_=l_run.t[:, 0:1]), reads=[l_run], writes=[l_run])
            TS("dve", accn.t[:, :], acc.t[:, :], l_run.t[:, 1:2], ALU.mult, [acc, l_run], [accn])
            for m in range(2):
                TR(bT2, bfv(bT2)[:, m * 64:(m + 1) * 64], accn.t[0:64, m * 128:(m + 1) * 128], idb.t[0:64, 0:64], [accn, idb])
            CP("act", olT.t[:, :, :], bfv(bT2)[:, 0:128].rearrange("p (m c) -> p m c", m=2), [bT2], [olT])
            for hh in range(H):
                for m in range(2):
                    MM(bT2, bT2.t[0:64, 256 + hh * 4:256 + (hh + 1) * 4], wuv.t[:, m, hh * 64:(hh + 1) * 64], olT.t[:, m, hh * 4:(hh + 1) * 4], m == 0, m == 1, [wuv, olT])
            CP("dve", OTs.t[0:64, :, b * 4:(b + 1) * 4], bT2.t[0:64, 256:320].rearrange("p (h t) -> p h t", h=H), [bT2], [OTs])

    nu = len(units)
    for i in range(min(4, nu)):
        stageG(units[i])
    stageA(units[0]); stageA(units[1])
    for k in range(8):
        stageB_step(units[0], k)
    stageC1(units[0])
    for i in range(nu):
        u0 = units[i]
        u1 = units[i + 1] if i + 1 < nu else None
        u2 = units[i + 2] if i + 2 < nu else None
        if i + 4 < nu:
            stageG(units[i + 4])
        for k in range(8):
            if u1 is not None:
                stageB_step(u1, k)
            if k < 4 and u2 is not None:
                stageA_tr(u2, k)
            if k == 0:
                stageC2a(u0)
            if k == 3 and u2 is not None:
                stageA_cp(u2)
            if k == 4:
                stageC2a2(u0)
            if k == 6:
                stageC2b(u0)
            if k == 7:
                stageC2c(u0)
        if u1 is not None:
            stageC1(u1)
    f.release(mA3b)
    wao = f.sbuf("wao", [64, H, D], BF16)
    for hh in range(H):
        LD("pool", wao.t[0:64, hh, :], w_a_out[hh * 64:(hh + 1) * 64, :], [wao])
    for j in range(8):
        pb = ringK.next()
        for hh in range(H):
            MM(pb, pb.t[:, 0:NS], wao.t[0:64, hh, j * 128:(j + 1) * 128], OTs.t[0:64, hh, :], hh == 0, hh == H - 1, [wao, OTs])
        TT("dve", hT.t[:, j, SEQ:NT], pb.t[:, 0:NS], hT.t[:, j, SEQ:NT], ALU.add, [pb, hT], [hT])
    f.release(mA3)
    f.release(mA)
    if stop_after == "A3":
        emit_output()
        f.finish()
        return nc

    def ffn(l):
        mF = f.mark()
        xnT = f.sbuf("xnT", [128, 8, NT], BF16)
        sq8 = f.sbuf("sq8F", [128, 8, 512], BF16)
        rs = f.sbuf("rsF", [128, 512], F32)
        gcol = V_GFA if l == 0 else V_GFB
        for (c0, N) in CHUNKS:
            ACT(sq8.t[:, :, 0:N], hT.t[:, :, c0:c0 + N], AF.Square, [hT], [sq8])
            pss = PS[7]
            for j in range(8):
                MM(pss, pss.t[:, 0:N], ones_b, sq8.t[:, j, 0:N], j == 0, j == 7, [cb, sq8])
            rstd_chain(pss, 128, N, rs, 1.0 / D)
            for j in range(8):
                STT("dve", xnT.t[:, j, c0:c0 + N], hT.t[:, j, c0:c0 + N], vcol(gcol + j), rs.t[:, 0:N], ALU.mult, ALU.mult, [hT, vecs, rs], [xnT])
        wg_r = Ring([f.sbuf(f"wg{i}", [128, 8, 512], BF16) for i in range(2)])
        wu_r = Ring([f.sbuf(f"wu{i}", [128, 8, 512], BF16) for i in range(2)])
        wo_r2 = Ring([f.sbuf(f"wo{i}", [128, 4, D], BF16) for i in range(2)])
        uT_r = Ring([f.sbuf(f"uT{i}", [128, 4, 512], BF16) for i in range(2)])
        sg_r = Ring([f.sbuf(f"sg{i}", [128, 512], F32) for i in range(2)])
        ring = Ring(PS[0:8])
        nblk = (D_FF + 511) // 512
        for hb in range(nblk):
            h0 = hb * 512
            HW = min(512, D_FF - h0)
            nhc = HW // 128
            wg = wg_r.next(); wu = wu_r.next(); wo = wo_r2.next()
            load_w(wg, w_ffn_in[l], 8, h0, HW)
            load_w(wu, w_ffn_in[l], 8, D_FF + h0, HW)
            for hc in range(nhc):
                LD("pool", wo.t[:, hc, :], w_ffn_out[l, h0 + hc * 128:h0 + (hc + 1) * 128, :], [wo])
            for (c0, N) in CHUNKS:
                uT = uT_r.next()
                for hc in range(nhc):
                    pg_ = ring.next()
                    for j in range(8):
                        MM(pg_, pg_.t[:, 0:N], wg.t[:, j, hc * 128:(hc + 1) * 128], xnT.t[:, j, c0:c0 + N], j == 0, j == 7, [wg, xnT])
                    pu_ = ring.next()
                    for j in range(8):
                        MM(pu_, pu_.t[:, 0:N], wu.t[:, j, hc * 128:(hc + 1) * 128], xnT.t[:, j, c0:c0 + N], j == 0, j == 7, [wu, xnT])
                    sg = sg_r.next()
                    ACT(sg.t[:, 0:N], pg_.t[:, 0:N], AF.Silu, [pg_], [sg])
                    TT("dve", uT.t[:, hc, 0:N], pu_.t[:, 0:N], sg.t[:, 0:N], ALU.mult, [pu_, sg], [uT])
                for j in range(8):
                    po = ring.next()
                    for hc in range(nhc):
                        MM(po, po.t[:, 0:N], wo.t[:, hc, j * 128:(j + 1) * 128], uT.t[:, hc, 0:N], hc == 0, hc == nhc - 1, [wo, uT])
                    TT("dve", hT.t[:, j, c0:c0 + N], po.t[:, 0:N], hT.t[:, j, c0:c0 + N], ALU.add, [po, hT], [hT])
        f.release(mF)

    ffn(0)
    if stop_after == "FA":
        emit_output()
        f.finish()
        return nc

    mB = f.mark()
    wkv = f.sbuf("wkv", [128, 8, 512], BF16)
    load_w(wkv, w_kv, 8, 0, 512)
    wq_r = Ring([f.sbuf("wqh0", [128, 8, 512], BF16)])
    wbo_r = Ring([f.sbuf("wboh0", [128, 4, D], BF16)])
    esk = f.sbuf("esk", [128, H], F32)
    ACT(esk.t[:, :], vecs.t[:, V_SINKR:V_SINKR + H], AF.Exp, [vecs], [esk])
    maskB = f.sbuf("maskB", [16, 4, 144], F32)
    LD("sp", maskB.t[:, :, :], maskB_d.rearrange("p (b k) -> p b k", b=4), [maskB])
    KB_c = f.sbuf("KB_c", [64, 4, 640], BF16)
    VB_c = f.sbuf("VB_c", [128, 5, 4, 192], BF16)
    MSET("pool", VB_c.t[:, :, :, 0:64], 1.0, [VB_c])
    MSET("pool", VB_c.t[:, :, :, 128:192], 1.0, [VB_c])
    QB_c = f.sbuf("QB_c", [64, 4, 8, 128], BF16)
    OB_c = f.sbuf("OB_c", [128, 4, 512], BF16)
    QBs = f.sbuf("QBs", [64, 4, 8, 4], BF16)
    OBs = f.sbuf("OBs", [128, 8, NS], BF16)
    kn_s = f.sbuf("kn_s", [64, 4, NS], F32)
    kn_sb = f.sbuf("kn_sb", [64, 4, NS], BF16)
    VBs_f = f.sbuf("VBs_f", [NS, 256], F32)
    VBs_b = f.sbuf("VBs_b", [NS, 256], BF16)
    sqx = f.sbuf("sqxB", [128, 8, 512], BF16)
    rs0 = f.sbuf("rs0B", [128, 512], F32)
    xkv = f.sbuf("xkv", [128, 8, 512], BF16)
    sq_r = Ring([f.sbuf(f"sqC{i}", [64, 512], BF16) for i in range(4)])
    rs_r = Ring([f.sbuf(f"rsC{i}", [64, 512], F32) for i in range(4)])
    kn_r = Ring([f.sbuf(f"knC{i}", [64, 512], F32) for i in range(4)])
    t1_r = Ring([f.sbuf(f"t1C{i}", [16, 512], F32) for i in range(2)])
    t2_r = Ring([f.sbuf(f"t2C{i}", [16, 512], F32) for i in range(2)])
    pT_r = Ring([f.sbuf(f"pTC{i}", [128, 512], BF16) for i in range(4)])
    lt_r = Ring([f.sbuf(f"ltC{i}", [128, 512], F32) for i in range(3)])
    tabc = Ring([f.sbuf(f"tabcC{i}", [16, 2, 512], F32) for i in range(2)])
    kvst = f.sbuf("kvst", [128, 2, 256], F32)
    ringP = Ring(PS[6:8])
    ringS = Ring(PS[0:3])
    ringO = Ring(PS[3:6])
    B64 = cb.t[0:64, CB_B96:CB_B96 + 64]
    P16 = cf.t[0:16, CF_P16:CF_P16 + 16]
    maskD4 = cb.t[:, CB_MD:CB_MD + 512]
    maskP4 = cb.t[:, CB_MP:CB_MP + 512]

    def pj_S1(specs, N, A):
        for i, sp in enumerate(specs):
            for j in range(8):
                MM(A[i], A[i].t[0:64, 0:N], sp["lhsT"](j), sp["rhs"](j), j == 0, j == 7, sp["rd"])

    def pj_mid(specs, N, A, Bs):
        n = len(specs)
        sqs, rss = [], []
        for i in range(n):
            sq = sq_r.next(); sqs.append(sq)
            ACT(sq.t[0:64, 0:N], A[i].t[0:64, 0:N], AF.Square, [A[i]], [sq])
        for i in range(n):
            MM(Bs[i], Bs[i].t[0:64, 0:N], B64, sqs[i].t[0:64, 0:N], True, True, [cb, sqs[i]])
        for i in range(n):
            rs = rs_r.next(); rss.append(rs)
            rstd_chain(Bs[i], 64, N, rs, 1.0)
        for i, sp in enumerate(specs):
            kn = kn_r.next(); sp["kn"] = kn
            if sp.get("dst") is not None:
                dap, dbuf = sp["dst"]
                vw = lambda ap: ap.rearrange("p (t q) -> p t q", t=4)
                STT("dve", dap(0, 64), vw(A[i].t[0:64, 0:N]), vcol(sp["gcol"], 0, 64), vw(rss[i].t[0:64, 0:N]), ALU.mult, ALU.mult, [A[i], vecs, rss[i]], [dbuf])
                STT("dve", kn.t[0:16, 0:N], A[i].t[0:16, 0:N], vcol(sp["gcol"], 0, 16), rss[i].t[0:16, 0:N], ALU.mult, ALU.mult, [A[i], vecs, rss[i]], [kn])
            else:
                STT("dve", kn.t[0:64, 0:N], A[i].t[0:64, 0:N], vcol(sp["gcol"], 0, 64), rss[i].t[0:64, 0:N], ALU.mult, ALU.mult, [A[i], vecs, rss[i]], [kn])

    def pj_tail(specs, N, tb, A):
        for i, sp in enumerate(specs):
            MM(A[i], A[i].t[0:16, 0:N], P16, sp["kn"].t[0:16, 0:N], True, True, [cf, sp["kn"]])
        for i, sp in enumerate(specs):
            kn = sp["kn"]
            t1 = t1_r.next(); t2 = t2_r.next()
            TT("pool", t1.t[0:16, 0:N], kn.t[0:16, 0:N], tb.t[0:16, 0, 0:N], ALU.mult, [kn, tb], [t1])
            TT("dve", t2.t[0:16, 0:N], A[i].t[0:16, 0:N], tb.t[0:16, 1, 0:N], ALU.mult, [A[i], tb], [t2])
            if sp.get("dst") is not None:
                dap, dbuf = sp["dst"]
                vw = lambda ap: ap.rearrange("p (t q) -> p t q", t=4)
                TT("dve", dap(0, 16), vw(t1.t[0:16, 0:N]), vw(t2.t[0:16, 0:N]), ALU.add, [t1, t2], [dbuf])
            else:
                TT("dve", kn.t[0:16, 0:N], t1.t[0:16, 0:N], t2.t[0:16, 0:N], ALU.add, [t1, t2], [kn])
                sp["consume"](kn)

    def proj_pipeline(batches, N, tb):
        sets = [PS[0:4], PS[4:8]]
        pj_S1(batches[0]["specs"], N, sets[0])
        for k, bt in enumerate(batches):
            A = sets[k % 2]; Bs = sets[(k + 1) % 2]
            pj_mid(bt["specs"], N, A, Bs)
            if k + 1 < len(batches):
                pj_S1(batches[k + 1]["specs"], N, Bs)
            pj_tail(bt["specs"], N, tb, A)
            if bt.get("after") is not None:
                bt["after"](A)

    for ci, (c0, N) in enumerate(CHUNKS):
        samp = c0 >= SEQ
        tb = tabc.next()
        LD("sp", tb.t[0:16, 0, 0:N], tabB_d[0, :, c0:c0 + N], [tb])
        LD("sp", tb.t[0:16, 1, 0:N], tabB_d[1, :, c0:c0 + N], [tb])
        ACT(sqx.t[:, :, 0:N], hT.t[:, :, c0:c0 + N], AF.Square, [hT], [sqx])
        pss = ringP.next()
        for j in range(8):
            MM(pss, pss.t[:, 0:N], ones_b, sqx.t[:, j, 0:N], j == 0, j == 7, [cb, sqx])
        rstd_chain(pss, 128, N, rs0, 1.0 / D)
        xnB = sqx
        for j in range(8):
            STT("dve", xkv.t[:, j, 0:N], hT.t[:, j, c0:c0 + N], vcol(V_GKV + j), rs0.t[:, 0:N], ALU.mult, ALU.mult, [hT, vecs, rs0], [xkv])
            STT("dve", xnB.t[:, j, 0:N], hT.t[:, j, c0:c0 + N], vcol(V_GB + j), rs0.t[:, 0:N], ALU.mult, ALU.mult, [hT, vecs, rs0], [xnB])
        kn_keep = {}

        def k_consume(kvh):
            def fn(kn):
                if not samp:
                    CP("pool", KB_c.t[0:64, kvh, 128:128 + N], kn.t[0:64, 0:N], [kn], [KB_c])
                    kn_keep[kvh] = kn
                else:
                    CP("pool", kn_s.t[0:64, kvh, :], kn.t[0:64, 0:NS], [kn], [kn_s])
                    CP("pool", kn_sb.t[0:64, kvh, :], kn.t[0:64, 0:NS], [kn], [kn_sb])
            return fn
        k_batch = dict(specs=[dict(lhsT=(lambda j, kvh=kvh: wkv.t[:, j, kvh * 64:(kvh + 1) * 64]), rhs=(lambda j: xkv.t[:, j, 0:N]),
                                    rd=[wkv, xkv], gcol=V_GKB, consume=k_consume(kvh)) for kvh in range(N_KV)], after=None)
        if ci == 3:
            def k_after(A):
                for kvh in range(N_KV):
                    pbT = A[3]
                    TR(pbT, pbT.t[:, 0:64], kn_keep[kvh].t[0:64, 384:512], cf.t[0:64, CF_ID:CF_ID + 64], [kn_keep[kvh], cf])
                    CP("act", kvst.t[:, 0, kvh * 64:(kvh + 1) * 64], pbT.t[:, 0:64], [pbT], [kvst])
                ST("sp", wk_p[:, :], kvst.t[:, 0, :], [kvst])
            k_batch["after"] = k_after
        if samp:
            proj_pipeline([k_batch], N, tb)
        if not samp:
            for ti in range(4):
                bv = ringP.next()
                for j in range(8):
                    MM(bv, bv.t[:, 0:256], xkv.t[:, j, ti * 128:(ti + 1) * 128], wkv.t[:, j, 256:512], j == 0, j == 7, [xkv, wkv])
                CP("act", VB_c.t[:, 1 + ti, :, 64:128], bv.t[:, 0:256].rearrange("p (k d) -> p k d", k=4), [bv], [VB_c])
                if ci == 3 and ti == 3:
                    CP("dve", kvst.t[:, 1, :], bv.t[:, 0:256], [bv], [kvst])
                    ST("sp", wv_p[:, :], kvst.t[:, 1, :], [kvst])
        else:
            bv = ringP.next()
            for j in range(8):
                MM(bv, bv.t[0:NS, 0:256], xkv.t[:, j, 0:NS], wkv.t[:, j, 256:512], j == 0, j == 7, [xkv, wkv])
            CP("act", VBs_f.t[:, :], bv.t[0:NS, 0:256], [bv], [VBs_f])
            CP("dve", VBs_b.t[:, :], bv.t[0:NS, 0:256], [bv], [VBs_b])
            pbT = ringP.next()
            for kvh in range(N_KV):
                TR(pbT, pbT.t[0:NS, kvh * 64:(kvh + 1) * 64], kn_s.t[0:64, kvh, :], cf.t[0:64, CF_ID:CF_ID + 64], [kn_s, cf])
            ktok = f.sbuf("ktok", [NS, 256], F32)
            CP("act", ktok.t[:, :], pbT.t[0:NS, 0:256], [pbT], [ktok])
            kw_all = f.sbuf("kw_all", [128, 4, 256], F32)
            vw_all = f.sbuf("vw_all", [128, 4, 256], F32)
            vwb_all = f.sbuf("vwb_all", [128, 4, 256], BF16)
            for b in range(4):
                LD("sp", kw_all.t[:, b, :], swk[b, :, :], [kw_all])
                LD("sp", vw_all.t[:, b, :], swv[b, :, :], [vw_all])
            CP("pool", vwb_all.t[:, :, :], vw_all.t[:, :, :], [vw_all], [vwb_all])
            for b in range(4):
                ST("sp", wk_s[b, 0:124, :], kw_all.t[4:128, b, :], [kw_all])
                ST("sp", wv_s[b, 0:124, :], vw_all.t[4:128, b, :], [vw_all])
                ST("sp", wk_s[b, 124:128, :], ktok.t[b * 4:(b + 1) * 4, :], [ktok])
                ST("sp", wv_s[b, 124:128, :], VBs_f.t[b * 4:(b + 1) * 4, :], [VBs_f])
            Kcat = f.sbuf("Kcat", [64, 144], BF16)
            scb = f.sbuf("scb", [16, 144], F32)
            pb_ = f.sbuf("pbB", [16, 144], BF16)
            pTw = f.sbuf("pTw", [128, 32], BF16)
            st2 = f.sbuf("st2", [16, 8], F32)
            onb = f.sbuf("onb", [16, 128], BF16)

            def s2(i):
                return st2.t[0:16, i:i + 1]
        for half in range(2):
            wqh = wq_r.next(); wboh = wbo_r.next()
            load_w(wqh, w_q_b, 8, half * 512, 512)
            for pr in range(4):
                LD("pool", wboh.t[:, pr, :], w_b_out[(half * 4 + pr) * 128:(half * 4 + pr + 1) * 128, :], [wboh])
            def q_consume(h8):
                def fn(kn):
                    if not samp:
                        CP("pool", QB_c.t[0:64, :, h8, :], kn.t[0:64, 0:512].rearrange("p (t q) -> p t q", t=4), [kn], [QB_c])
                    else:
                        CP("pool", QBs.t[0:64, :, h8, :], kn.t[0:64, 0:NS].rearrange("p (b t) -> p b t", b=4), [kn], [QBs])
                return fn
            def q_dst(h8):
                if samp:
                    return None
                slot8 = (h8 // 4) * 4 + [0, 2, 1, 3][h8 % 4]
                return ((lambda p0, p1, slot8=slot8: QB_c.t[p0:p1, :, slot8, :]), QB_c)
            batches = [dict(specs=[dict(lhsT=(lambda j, h8=h8: wqh.t[:, j, h8 * 64:(h8 + 1) * 64]), rhs=(lambda j: xnB.t[:, j, 0:N]),
                                        rd=[wqh, xnB], gcol=V_GQB, consume=q_consume(h8), dst=q_dst(h8)) for h8 in range(qb * 4, qb * 4 + 4)], after=None)
                       for qb in range(2)]
            if half == 0 and not samp:
                batches = [k_batch] + batches
            proj_pipeline(batches, N, tb)
            if not samp:
                iters = []
                for kk in range(2):
                    for ti in range(4):
                        gi = ci * 4 + ti
                        kts = [kt for kt in (gi - 1, gi) if kt >= 0]
                        for n_, kt in enumerate(kts):
                            iters.append(dict(kk=kk, ti=ti, gi=gi, kt=kt, first=(n_ == 0), last=(n_ == len(kts) - 1), n=len(iters)))

                def emitS(it):
                    kk, ti, kt, gi = it["kk"], it["ti"], it["kt"], it["gi"]
                    kvh = half * 2 + kk
                    slot = kt - (ci * 4 - 1)
                    bs_ = ringS.next()
                    MM(bs_, bs_.t[:, 0:512], KB_c.t[0:64, kvh, slot * 128:(slot + 1) * 128],
                       QB_c.t[0:64, ti, kk * 4:(kk + 1) * 4, :].rearrange("p h q -> p (h q)"), True, True, [KB_c, QB_c])
                    pT = pT_r.next()
                    ACT(pT.t[:, :], bs_.t[:, :], AF.Exp, [bs_], [pT], scale=SCALE_B)
                    TT("pool" if it["n"] % 4 == 3 else "dve", pT.t[:, :], pT.t[:, :], maskD4 if kt == gi else maskP4, ALU.mult, [pT, cb], [pT])
                    it["pT"] = pT; it["slot"] = slot; it["kvh"] = kvh
                curB = {}

                def emitPV(it):
                    kk, ti, kvh = it["kk"], it["ti"], it["kvh"]
                    if it["first"]:
                        curB["bo"] = ringO.next()
                    bo = curB["bo"]
                    MM(bo, bo.t[:, 0:256], VB_c.t[:, it["slot"], kvh, 64:192], it["pT"].t[:, 0:256], it["first"], it["last"], [VB_c, it["pT"]], sgc=True)
                    MM(bo, bo.t[:, 256:512], VB_c.t[:, it["slot"], kvh, 0:128], it["pT"].t[:, 256:512], False, it["last"], [VB_c, it["pT"]], sgc=True)
                    if it["last"]:
                        pend_norm.append((it["idx_emit"], bo, kvh, kk, ti))

                def emitNorm(bo, kvh, kk, ti):
                    lt = lt_r.next()
                    v3 = lambda ap: ap.rearrange("p (h q) -> p h q", h=2)
                    eskv = esk.t[:, kvh * 4:(kvh + 1) * 4].rearrange("p (h2 two) -> p h2 two", two=2)
                    TT("dve", v3(lt.t[64:128, 0:256]), v3(bo.t[64:128, 0:256]), eskv[64:128, :, 0].unsqueeze(2).broadcast_to([64, 2, 128]), ALU.add, [bo, esk], [lt])
                    TT("dve", v3(lt.t[0:64, 256:512]), v3(bo.t[0:64, 256:512]), eskv[0:64, :, 1].unsqueeze(2).broadcast_to([64, 2, 128]), ALU.add, [bo, esk], [lt])
                    ACT(lt.t[64:128, 0:256], lt.t[64:128, 0:256], AF.Ln, [lt], [lt])
                    ACT(lt.t[0:64, 256:512], lt.t[0:64, 256:512], AF.Ln, [lt], [lt])
                    ACT(lt.t[64:128, 0:256], lt.t[64:128, 0:256], AF.Exp, [lt], [lt], scale=-1.0)
                    ACT(lt.t[0:64, 256:512], lt.t[0:64, 256:512], AF.Exp, [lt], [lt], scale=-1.0)
                    TT("dve", OB_c.t[0:64, kk * 2:(kk + 1) * 2, ti * 128:(ti + 1) * 128], v3(bo.t[0:64, 0:256]), v3(lt.t[64:128, 0:256]), ALU.mult, [bo, lt], [OB_c])
                    TT("dve", OB_c.t[64:128, kk * 2:(kk + 1) * 2, ti * 128:(ti + 1) * 128], v3(bo.t[64:128, 256:512]), v3(lt.t[0:64, 256:512]), ALU.mult, [bo, lt], [OB_c])
                DEPTH = 2
                NDELAY = 0
                pend_norm = []
                for idx in range(len(iters) + DEPTH + NDELAY):
                    if idx < len(iters):
                        emitS(iters[idx])
                    if 0 <= idx - DEPTH < len(iters):
                        iters[idx - DEPTH]["idx_emit"] = idx
                        emitPV(iters[idx - DEPTH])
                    while pend_norm and pend_norm[0][0] + NDELAY <= idx:
                        _, bo_, kvh_, kk_, ti_ = pend_norm.pop(0)
                        emitNorm(bo_, kvh_, kk_, ti_)
                assert not pend_norm
                for j in range(8):
                    pb = ringP.next()
                    for pr in range(4):
                        MM(pb, pb.t[:, 0:512], wboh.t[:, pr, j * 128:(j + 1) * 128], OB_c.t[:, pr, :], pr == 0, pr == 3, [wboh, OB_c])
                    TT("dve", hT.t[:, j, c0:c0 + 512], pb.t[:, 0:512], hT.t[:, j, c0:c0 + 512], ALU.add, [pb, hT], [hT])
            else:
                for b in range(4):
                    for kk in range(2):
                        kvh = half * 2 + kk
                        pbK = ringP.next()
                        TR(pbK, pbK.t[0:64, 0:128], kw_all.t[:, b, kvh * 64:(kvh + 1) * 64], identf, [kw_all, cf])
                        CP("act", Kcat.t[0:64, 0:128], pbK.t[0:64, 0:128], [pbK], [Kcat])
                        CP("pool", Kcat.t[0:64, 128:144], kn_sb.t[0:64, kvh, :], [kn_sb], [Kcat])
                        bs_ = ringS.next()
                        MM(bs_, bs_.t[0:16, 0:144], QBs.t[0:64, b, kk * 4:(kk + 1) * 4, :].rearrange("p h t -> p (h t)"), Kcat.t[0:64, 0:144], True, True, [QBs, Kcat])
                        TT("dve", scb.t[:, :], bs_.t[0:16, 0:144], maskB.t[:, b, :], ALU.add, [bs_, maskB], [scb])
                        f.op("dve", lambda e: e.tensor_reduce(out=s2(0), in_=scb.t[:, :], axis=AX.X, op=ALU.max), reads=[scb], writes=[st2])
                        TS("dve", s2(1), s2(0), -SCALE_B, ALU.mult, [st2], [st2])
                        MSET("pool", s2(2), 0.0, [st2])
                        ACT(pb_.t[:, :], scb.t[:, :], AF.Exp, [scb, st2], [pb_, st2], scale=SCALE_B, bias=s2(1), accum_out=s2(2))
                        ACT(s2(3), vecs.t[0:16, V_SINKC + kvh:V_SINKC + kvh + 1], AF.Exp, [vecs, st2], [st2], scale=1.0, bias=s2(1))
                        TT("dve", s2(4), s2(2), s2(3), ALU.add, [st2], [st2])
                        f.op("dve", lambda e: e.reciprocal(out=s2(5), in_=s2(4)), reads=[st2], writes=[st2])
                        pbP = ringP.next()
                        TR(pbP, bfv(pbP)[:, 0:16], pb_.t[0:16, 0:128], idb.t[0:16, 0:16], [pb_, idb])
                        TR(pbP, bfv(pbP)[0:16, 16:32], pb_.t[0:16, 128:144], idb.t[0:16, 0:16], [pb_, idb])
                        CP("act", pTw.t[:, 0:32], bfv(pbP)[:, 0:32], [pbP], [pTw])
                        bo = ringO.next()
                        MM(bo, bo.t[0:16, 0:64], pTw.t[:, 0:16], vwb_all.t[:, b, kvh * 64:(kvh + 1) * 64], True, False, [pTw, vwb_all])
                        MM(bo, bo.t[0:16, 0:64], pTw.t[0:16, 16:32], VBs_b.t[0:NS, kvh * 64:(kvh + 1) * 64], False, True, [pTw, VBs_b])
                        TS("dve", onb.t[:, 0:64], bo.t[0:16, 0:64], s2(5), ALU.mult, [bo, st2], [onb])
                        TS("dve", onb.t[:, 64:128], bo.t[0:16, 0:64], s2(5), ALU.mult, [bo, st2], [onb])
                        pbO = ringP.next()
                        TR(pbO, bfv(pbO)[0:128, 0:16], onb.t[0:16, 0:128], idb.t[0:16, 0:16], [onb, idb])
                        for par in range(2):
                            CP("act", OBs.t[par * 64:(par + 1) * 64, kk * 4:(kk + 1) * 4, b * 4:(b + 1) * 4].rearrange("p (h2 two) t -> p h2 two t", two=2)[:, :, par, :],
                               bfv(pbO)[par * 64:(par + 1) * 64, 0:16].rearrange("p (h2 two t) -> p h2 two t", two=2, t=4)[:, :, par, :], [pbO], [OBs])
                for j in range(8):
                    for par in range(2):
                        pb = ringP.next()
                        for pr in range(4):
                            h8 = pr * 2 + par
                            MM(pb, pb.t[:, 0:NS], wboh.t[par * 64:(par + 1) * 64, pr, j * 128:(j + 1) * 128], OBs.t[par * 64:(par + 1) * 64, h8, :], pr == 0, pr == 3, [wboh, OBs])
                        TT("dve", hT.t[:, j, SEQ:NT], pb.t[:, 0:NS], hT.t[:, j, SEQ:NT], ALU.add, [pb, hT], [hT])
        if ci < 3:
            CP("pool", KB_c.t[0:64, :, 0:128], KB_c.t[0:64, :, 512:640], [KB_c], [KB_c])
            CP("pool", VB_c.t[:, 0, :, 64:128], VB_c.t[:, 4, :, 64:128], [VB_c], [VB_c])
    f.release(mB)
    if stop_after == "B":
        emit_output()
        f.finish()
        return nc

    ffn(1)

    emit_output()
    f.finish()
    return nc


def _rope_tab(n_rot, pos):
    inv = np.power(np.float32(THETA), (-np.arange(0, n_rot, 2, dtype=np.float32) / np.float32(n_rot)).astype(np.float32)).astype(np.float32)
    ang = (pos.astype(np.float32)[:, None] * inv[None, :]).astype(np.float32)
    return np.cos(ang.astype(np.float64)).astype(np.float32), np.sin(ang.astype(np.float64)).astype(np.float32)


def _constants():
    pos = np.concatenate([np.arange(SEQ), np.tile(PAST + np.arange(4), 4)]).astype(np.int64)
    cA, sA = _rope_tab(D_ROPE, pos)
    tabA = np.zeros((2, 96, NT), np.float32)
    tabA[0, :64] = 1.0
    for d in range(32):
        tabA[0, 64 + d] = cA[:, d % 16]
        tabA[1, 64 + d] = sA[:, d % 16]
    cB, sB = _rope_tab(16, pos)
    tabB = np.zeros((2, 16, NT), np.float32)
    for d in range(16):
        tabB[0, d] = cB[:, d % 8]
        tabB[1, d] = sB[:, d % 8]
    cf = np.zeros((128, CF_W), np.float32)
    cf[:, CF_ID:CF_ID + 128] = np.eye(128, dtype=np.float32)
    for d in range(16):
        cf[64 + d + 16, CF_P96 + 64 + d] = -1.0
        cf[64 + d, CF_P96 + 64 + d + 16] = 1.0
    for d in range(8):
        cf[d + 8, CF_P16 + d] = -1.0
        cf[d, CF_P16 + d + 8] = 1.0
    cb = np.zeros((128, CB_W), np.float32)
    cb[:, CB_ONES:CB_ONES + 128] = 1.0
    cb[0:64, CB_B96:CB_B96 + 64] = 1.0 / 64
    cb[64:96, CB_B96 + 64:CB_B96 + 96] = 1.0 / 32
    for jc in range(8):
        for p in range(128):
            hh = 2 * jc + p // 64
            cb[p, CB_BS + jc * 64 + hh * 4:CB_BS + jc * 64 + hh * 4 + 4] = 1.0 / 64
    pp = np.arange(128)[:, None]; cc = np.arange(128)[None, :]
    mD = (cc >= pp).astype(np.float32); mP = (cc < pp).astype(np.float32)
    cb[:, CB_MD:CB_MD + 512] = np.tile(mD, (1, 4))
    cb[:, CB_MP:CB_MP + 512] = np.tile(mP, (1, 4))
    maskS = np.full((64, 4, 16), NEG, np.float32)
    for hh in range(H):
        for t in range(4):
            for b in range(4):
                for t2 in range(t + 1):
                    maskS[hh * 4 + t, b, b * 4 + t2] = 0.0
    maskB = np.full((16, 4, 144), NEG, np.float32)
    for hq in range(4):
        for t in range(4):
            for b in range(4):
                maskB[hq * 4 + t, b, t + 1:128] = 0.0
                for t2 in range(t + 1):
                    maskB[hq * 4 + t, b, 128 + b * 4 + t2] = 0.0
    return dict(tabA=tabA, tabB=tabB, cf32=cf, cb32=cb, maskS=maskS.reshape(64, 64), maskB=maskB.reshape(16, 576))


def _vecs(inp):
    v = np.zeros((128, NV), np.float32)

    def colmaj(g, c0):
        n = g.shape[0] // 128
        v[:, c0:c0 + n] = g.reshape(n, 128).T
    colmaj(inp["norm_attn"][0], V_GA); colmaj(inp["norm_ffn"][0], V_GFA); colmaj(inp["g_kv_shared"], V_GKV)
    colmaj(inp["norm_attn"][1], V_GB); colmaj(inp["norm_ffn"][1], V_GFB)
    colmaj(inp["g_qc"][0], V_GQC); colmaj(inp["g_ckv"][0], V_GCKV)
    v[0:64, V_GQ96] = inp["g_qn_a"][0]; v[64:96, V_GQ96] = inp["g_qr_a"][0]
    v[0:64, V_GK96] = inp["g_kn_a"][0]; v[64:96, V_GK96] = inp["g_kr_a"][0]
    v[0:64, V_GKB] = inp["g_k_b"]; v[0:64, V_GQB] = inp["g_q_b"][0]
    sk = inp["sinks"][0]
    for kvh in range(4):
        for hq in range(4):
            v[hq * 4:hq * 4 + 4, V_SINKC + kvh] = sk[kvh * 4 + hq]
    v[:, V_SINKR:V_SINKR + H] = sk[None, :]
    return v


_PROG = {}


def kernel(_ncores=8, _stop_after=None, **inp):
    inp = {k: np.asarray(v) for k, v in inp.items()}
    key = _stop_after
    if key not in _PROG:
        _PROG[key] = build_program(_stop_after)
    nc = _PROG[key]
    consts = _constants()
    vecs = _vecs(inp)
    cache2d = np.ascontiguousarray(inp["cache_mla"][0].reshape(NPOOL * 128, D_CKV))
    shared = dict(
        cache=cache2d,
        w_a_in=np.ascontiguousarray(inp["w_a_in"][0]), w_uq=np.ascontiguousarray(inp["w_uq"][0]),
        w_uk=np.ascontiguousarray(inp["w_uk"][0]), w_uv=np.ascontiguousarray(inp["w_uv"][0]),
        w_a_out=np.ascontiguousarray(inp["w_a_out"][0]), w_kv=np.ascontiguousarray(inp["w_kv_shared"]),
        w_q_b=np.ascontiguousarray(inp["w_q_b"][0]), w_b_out=np.ascontiguousarray(inp["w_b_out"][0]),
        w_ffn_in=np.ascontiguousarray(inp["w_ffn_in"]), w_ffn_out=np.ascontiguousarray(inp["w_ffn_out"]),
        vecs=vecs, **consts)
    in_maps = []
    for c in range(_ncores):
        m = dict(shared)
        m["x_p"] = np.ascontiguousarray(inp["x_prompt"][c])
        m["x_s"] = np.ascontiguousarray(inp["x_sample"][4 * c:4 * c + 4].reshape(NS, D))
        m["ptab"] = np.ascontiguousarray(inp["page_table"][4 * c:4 * c + 4].reshape(1, 512).astype(np.int32))
        m["swk"] = np.ascontiguousarray(inp["state_win_k"][4 * c:4 * c + 4].reshape(4, 128, 256))
        m["swv"] = np.ascontiguousarray(inp["state_win_v"][4 * c:4 * c + 4].reshape(4, 128, 256))
        in_maps.append(m)
    res = run_bass_kernel_spmd(nc, in_maps, core_ids=list(range(_ncores)))
    R = res.results
    n = _ncores
    y_prompt = np.stack([R[c]["y_p"] for c in range(n)])
    y_sample = np.concatenate([R[c]["y_s"].reshape(4, 4, D) for c in range(n)])
    rows_pr = np.stack([R[c]["rows_p"] for c in range(n)])[None]
    rows_sa = np.concatenate([R[c]["rows_s"].reshape(4, 4, D_CKV) for c in range(n)])[None]
    wkp = np.stack([R[c]["wk_p"].reshape(128, 4, 64) for c in range(n)])
    wvp = np.stack([R[c]["wv_p"].reshape(128, 4, 64) for c in range(n)])
    wks = np.concatenate([R[c]["wk_s"].reshape(4, 128, 4, 64) for c in range(n)])
    wvs = np.concatenate([R[c]["wv_s"].reshape(4, 128, 4, 64) for c in range(n)])
    f32 = np.float32
    return (y_prompt.astype(f32), y_sample.astype(f32), rows_pr.astype(f32), rows_sa.astype(f32),
            wkp.astype(f32), wvp.astype(f32), wks.astype(f32), wvs.astype(f32))
```

```python
import numpy as np
import concourse.bass as bass
import concourse.mybir as mybir
from concourse.bass_utils import run_bass_kernel_spmd

F32 = mybir.dt.float32
BF16 = mybir.dt.bfloat16
I32 = mybir.dt.int32
AF = mybir.ActivationFunctionType
ALU = mybir.AluOpType
AX = mybir.AxisListType

ENGS = ["pe", "act", "dve", "pool", "sp"]

D = 1024
SEQ = 2048
NS = 16
NT = SEQ + NS
H = 16
D_NOPE, D_ROPE, D_V = 64, 32, 64
D_QC, D_C = 384, 256
D_CKV = D_C + D_ROPE
SCALE_A = float((D_NOPE + D_ROPE) ** -0.5)
N_KV, HD = 4, 64
SCALE_B = float(HD ** -0.5)
D_FF = 2816
EPS = 1e-6
THETA = 500000.0
PAST = 16384
NPOOL = 5120
NEG = -1e30
FILLER = 0
CHUNKS = [(0, 512), (512, 512), (1024, 512), (1536, 512), (2048, 16)]
NV = 69
V_GA, V_GFA, V_GKV, V_GB, V_GFB, V_GQC, V_GCKV, V_GQ96, V_GK96, V_GKB, V_GQB, V_SINKC, V_SINKR = 0, 8, 16, 24, 32, 40, 43, 45, 46, 47, 48, 49, 53
CB_ONES, CB_B96, CB_BS, CB_MD, CB_MP, CB_W = 0, 128, 224, 1248, 1760, 2272
CF_ID, CF_P96, CF_P16, CF_W = 0, 128, 224, 240


class Buf:
    __slots__ = ("name", "t", "last_w", "readers", "dsem", "dcnt", "excl")

    def __init__(self, name, t, excl=False, init=()):
        self.name = name
        self.t = t
        self.last_w = []
        self.readers = list(init)
        self.dsem = None
        self.dcnt = 0
        self.excl = excl

    def __getitem__(self, idx):
        return self.t[idx]


def _compact(evs):
    d = {}
    for k, v in evs:
        if d.get(k, 0) < v:
            d[k] = v
    return list(d.items())


class FW:
    def __init__(self, nc):
        self.nc = nc
        self.streams = {e: [] for e in ENGS}
        self.esem = {}
        self.ecnt = {e: 0 for e in ENGS}
        self.seen = {e: {} for e in ENGS}
        self.sems = {}
        self.dma_bufs = []
        self._ctx = []
        self.free_events = []
        self.free_sems = []
        for e in ["pe", "act", "dve", "pool"]:
            self.esem[e] = self._newsem("e_" + e)

    def _newsem(self, name):
        cm = self.nc.semaphore(name)
        h = cm.__enter__()
        self._ctx.append((cm, None))
        self.sems[name] = h
        return name

    def mark(self):
        return len(self._ctx)

    def release(self, mark):
        ev = list(self.free_events)
        while len(self._ctx) > mark:
            cm, b = self._ctx.pop()
            if b is not None:
                ev.extend(b.last_w)
                ev.extend(b.readers)
                if b.dsem is not None:
                    ev.append((b.dsem, b.dcnt))
                cm.__exit__(None, None, None)
            else:
                self._keep.append((cm, b))
        self.free_events = _compact(ev)

    _keep = []

    def sbuf(self, name, shape, dtype):
        self._uid = getattr(self, "_uid", 0) + 1
        name = "s%d_%s" % (self._uid, name)
        cm = self.nc.sbuf_tensor(name, list(shape), dtype)
        t = cm.__enter__()
        b = Buf(name, t, init=self.free_events)
        self._ctx.append((cm, b))
        return b

    def psum(self, name, shape, dtype=F32):
        cm = self.nc.psum_tensor(name, list(shape), dtype)
        t = cm.__enter__()
        b = Buf(name, t, excl=True)
        self._ctx.append((cm, b))
        return b

    def _deps(self, reads, writes):
        ev = []
        for b in reads:
            ev.extend(b.last_w)
            if b.excl:
                ev.extend(b.readers)
        for b in writes:
            ev.extend(b.last_w)
            ev.extend(b.readers)
        return ev

    def _waits(self, eng, ev):
        need = {}
        for (k, v) in ev:
            if need.get(k, 0) < v:
                need[k] = v
        seen = self.seen[eng]
        out = []
        for k, v in need.items():
            if seen.get(k, 0) >= v:
                continue
            seen[k] = v
            out.append((k, v))
        return out

    def _record(self, reads, writes, event, nowaw=False):
        for b in reads:
            if b.excl:
                b.last_w = [event]
                b.readers = []
            else:
                b.readers.append(event)
                if len(b.readers) > 48:
                    b.readers = _compact(b.readers)
        for b in writes:
            if nowaw:
                b.last_w.append(event)
                b.last_w = _compact(b.last_w)
            else:
                b.last_w = [event]
            b.readers = []

    def op(self, eng, fn, reads=(), writes=()):
        ev = self._deps(reads, writes)
        waits = self._waits(eng, ev)
        self.ecnt[eng] += 1
        val = self.ecnt[eng]
        semname = self.esem[eng]
        if eng == "pe":
            self.seen[eng][semname] = val
        sems = self.sems

        def thunk(e, fn=fn, waits=waits, semname=semname):
            for (k, v) in waits:
                e.wait_ge(sems[k], v)
            fn(e).then_inc(sems[semname], 1)
        self.streams[eng].append(thunk)
        self._record(reads, writes, (semname, val))

    def dma(self, q, fn, reads=(), writes=(), own=None):
        if own is None:
            own = writes[0] if writes else reads[0]
        if own.dsem is None:
            cm = self.nc.semaphore("d_" + own.name)
            h = cm.__enter__()
            self._keep.append((cm, None))
            self.sems["d_" + own.name] = h
            own.dsem = "d_" + own.name
            self.dma_bufs.append(own)
        ev = []
        for b in reads:
            ev.extend(b.last_w)
        for b in writes:
            ev.extend([e for e in b.last_w if e[0] != own.dsem])
            ev.extend(b.readers)
        waits = self._waits(q, ev)
        own.dcnt += 16
        val = own.dcnt
        semname = own.dsem
        sems = self.sems

        def thunk(e, fn=fn, waits=waits, semname=semname):
            for (k, v) in waits:
                e.wait_ge(sems[k], v)
            fn(e).then_inc(sems[semname], 16)
        self.streams[q].append(thunk)
        self._record(reads, writes, (semname, val), nowaw=True)

    def finish(self):
        finals = [(b.dsem, b.dcnt) for b in self.dma_bufs]
        sems = self.sems
        nc = self.nc
        streams = self.streams
        with nc.Block() as block:
            @block.sync
            def _(e):
                for th in streams["sp"]:
                    th(e)
                for (k, v) in finals:
                    e.wait_ge(sems[k], v)

            @block.tensor
            def _(e):
                for th in streams["pe"]:
                    th(e)

            @block.scalar
            def _(e):
                for th in streams["act"]:
                    th(e)

            @block.vector
            def _(e):
                for th in streams["dve"]:
                    th(e)

            @block.gpsimd
            def _(e):
                for th in streams["pool"]:
                    th(e)
        while self._ctx:
            cm, b = self._ctx.pop()
            cm.__exit__(None, None, None)
        for cm, b in reversed(self._keep):
            cm.__exit__(None, None, None)
        FW._keep = []


class Ring:
    def __init__(self, items):
        self.items = items
        self.i = 0

    def next(self):
        b = self.items[self.i % len(self.items)]
        self.i += 1
        return b


def build_program(stop_after=None, npool=NPOOL):
    nc = bass.Bass("TRN2", target_bir_lowering=False)
    FW._keep = []
    f = FW(nc)

    def din(name, shape, dt=F32):
        return nc.dram_tensor(name, list(shape), dt, kind="ExternalInput").ap()

    def dout(name, shape, dt=F32):
        return nc.dram_tensor(name, list(shape), dt, kind="ExternalOutput").ap()

    x_p = din("x_p", [SEQ, D]); x_s = din("x_s", [NS, D])
    cache = din("cache", [npool * 128, D_CKV])
    ptab = din("ptab", [1, 512], I32)
    swk = din("swk", [4, 128, 256]); swv = din("swv", [4, 128, 256])
    w_a_in = din("w_a_in", [D, 672]); w_uq = din("w_uq", [D_QC, 1536])
    w_uk = din("w_uk", [D_C, 1024]); w_uv = din("w_uv", [D_C, 1024])
    w_a_out = din("w_a_out", [1024, D]); w_kv = din("w_kv", [D, 512])
    w_q_b = din("w_q_b", [D, 1024]); w_b_out = din("w_b_out", [1024, D])
    w_ffn_in = din("w_ffn_in", [2, D, 2 * D_FF]); w_ffn_out = din("w_ffn_out", [2, D_FF, D])
    vecs_d = din("vecs", [128, NV]); cf_d = din("cf32", [128, CF_W]); cb_d = din("cb32", [128, CB_W])
    tabA_d = din("tabA", [2, 96, NT]); tabB_d = din("tabB", [2, 16, NT])
    maskS_d = din("maskS", [64, 4 * 16]); maskB_d = din("maskB", [16, 4 * 144])

    y_p = dout("y_p", [SEQ, D]); y_s = dout("y_s", [NS, D])
    rows_p = dout("rows_p", [SEQ, D_CKV]); rows_s = dout("rows_s", [NS, D_CKV])
    wk_p = dout("wk_p", [128, 256]); wv_p = dout("wv_p", [128, 256])
    wk_s = dout("wk_s", [4, 128, 256]); wv_s = dout("wv_s", [4, 128, 256])

    def MM(pb, out, lhsT, rhs, start, stop, rd, sgc=False):
        if sgc:
            f.op("pe", lambda e: e.matmul(out, lhsT=lhsT, rhs=rhs, start=start, stop=stop, skip_group_check=True), reads=rd, writes=[pb])
        else:
            f.op("pe", lambda e: e.matmul(out, lhsT=lhsT, rhs=rhs, start=start, stop=stop), reads=rd, writes=[pb])

    def TR(pb, out, in_, ident, rd):
        f.op("pe", lambda e: e.transpose(out=out, in_=in_, identity=ident), reads=rd, writes=[pb])

    def ACT(out, in_, func, rd, wr, **kw):
        f.op("act", lambda e: e.activation(out=out, in_=in_, func=func, **kw), reads=rd, writes=wr)

    def CP(eng, out, in_, rd, wr):
        if eng == "act":
            f.op("act", lambda e: e.copy(out=out, in_=in_), reads=rd, writes=wr)
        else:
            f.op(eng, lambda e: e.tensor_copy(out=out, in_=in_), reads=rd, writes=wr)

    def TT(eng, out, in0, in1, op, rd, wr):
        f.op(eng, lambda e: e.tensor_tensor(out=out, in0=in0, in1=in1, op=op), reads=rd, writes=wr)

    def STT(eng, out, in0, scalar, in1, op0, op1, rd, wr):
        f.op(eng, lambda e: e.scalar_tensor_tensor(out=out, in0=in0, scalar=scalar, in1=in1, op0=op0, op1=op1), reads=rd, writes=wr)

    def TS(eng, out, in0, s1, op0, rd, wr):
        f.op(eng, lambda e: e.tensor_scalar(out=out, in0=in0, scalar1=s1, scalar2=None, op0=op0), reads=rd, writes=wr)

    def MSET(eng, ap, val, wr):
        f.op(eng, lambda e: e.memset(ap, val), writes=wr)

    def LD(q, out, in_, wr, rd=()):
        f.dma(q, lambda e: e.dma_start(out=out, in_=in_), reads=list(rd), writes=list(wr))

    def ST(q, out, in_, rd):
        f.dma(q, lambda e: e.dma_start(out=out, in_=in_), reads=list(rd), writes=[])

    PS = [f.psum(f"ps{i}", [128, 512], F32) for i in range(8)]

    def bfv(pb):
        return pb.t[:, :].bitcast(BF16)

    hT = f.sbuf("hT", [128, 8, NT], F32)
    vecs = f.sbuf("vecs", [128, NV], F32)
    cf = f.sbuf("cf", [128, CF_W], F32)
    cb = f.sbuf("cb", [128, CB_W], BF16)
    idb = f.sbuf("idb", [128, 128], BF16)
    LD("sp", vecs[:, :], vecs_d[:, :], [vecs])
    LD("sp", cf[:, :], cf_d[:, :], [cf])
    LD("pool", cb[:, :], cb_d[:, :], [cb])
    CP("pool", idb[:, :], cf[:, CF_ID:CF_ID + 128], [cf], [idb])
    identf = cf.t[:, CF_ID:CF_ID + 128]
    ones_b = cb.t[:, CB_ONES:CB_ONES + 128]

    def vcol(c, p0=0, p1=128):
        return vecs.t[p0:p1, c:c + 1]

    def rstd_chain(pb, M, N, rs, scale):
        ACT(rs.t[0:M, 0:N], pb.t[0:M, 0:N], AF.Ln, [pb], [rs], scale=scale, bias=EPS)
        ACT(rs.t[0:M, 0:N], rs.t[0:M, 0:N], AF.Exp, [rs], [rs], scale=-0.5)

    def load_w(dst, src2d, kchunks, c0, ncols, prows=128):
        for j in range(kchunks):
            LD("pool", dst.t[0:prows, j, 0:ncols], src2d[j * prows:(j + 1) * prows, c0:c0 + ncols], [dst])

    def emit_output():
        ys_r = Ring([f.sbuf(f"ys{i}", [128, D], F32) for i in range(2)])
        for i in range(17):
            ys = ys_r.next()
            rows = 128 if i < 16 else NS
            for half in range(2):
                pb = PS[(2 * i + half) % 8]
                for q in range(4):
                    j = half * 4 + q
                    TR(pb, pb.t[0:rows, q * 128:(q + 1) * 128], hT.t[:, j, i * 128:i * 128 + rows], identf, [hT, cf])
                CP("act" if half == 0 else "dve", ys.t[0:rows, half * 512:(half + 1) * 512], pb.t[0:rows, 0:512], [pb], [ys])
            if i < 16:
                ST("sp", y_p[i * 128:(i + 1) * 128, :], ys.t[0:rows, :], [ys])
            else:
                ST("sp", y_s[:, :], ys.t[0:rows, :], [ys])


    m0 = f.mark()
    xs_ring = Ring([f.sbuf(f"xs{i}", [128, D], F32) for i in range(2)])
    for i in range(17):
        xs = xs_ring.next()
        rows = 128 if i < 16 else NS
        src = x_p[i * 128:(i + 1) * 128, :] if i < 16 else x_s[:, :]
        LD("sp", xs.t[0:rows, :], src, [xs])
        for half in range(2):
            pb = PS[(2 * i + half) % 8]
            for q in range(4):
                j = half * 4 + q
                TR(pb, pb.t[:, q * 128:q * 128 + rows], xs.t[0:rows, j * 128:(j + 1) * 128], identf[0:rows, 0:rows], [xs, cf])
            srcv = pb.t[:, :].rearrange("p (q c) -> p q c", q=4)[:, :, 0:rows]
            CP("act" if half == 0 else "dve", hT.t[:, half * 4:half * 4 + 4, i * 128:i * 128 + rows], srcv, [pb], [hT])
    f.release(m0)

    mA = f.mark()
    cqT = f.sbuf("cqT", [128, 3, NT], BF16)
    ckvT = f.sbuf("ckvT", [128, 2, NT], BF16)
    kpe_b = f.sbuf("kpe_b", [96, NT], BF16)
    rown_b = f.sbuf("rown_b", [NS, 256], BF16)
    qs_all = f.sbuf("qs_all", [96, H, NS], BF16)

    mA1 = f.mark()
    wain = f.sbuf("wain", [128, 8, 672], BF16)
    load_w(wain, w_a_in, 8, 0, 672)
    sq8 = f.sbuf("sq8", [128, 8, 512], BF16)
    xn = f.sbuf("xn", [128, 8, 512], BF16)
    rs_r = Ring([f.sbuf(f"rsA{i}", [128, 512], F32) for i in range(2)])
    ckv_f = f.sbuf("ckv_f", [128, 2, 512], F32)
    kpn = f.sbuf("kpn", [96, 512], F32)
    kpe_f = f.sbuf("kpe_f", [96, 512], F32)
    t1 = f.sbuf("t1A", [96, 512], F32)
    tabc = Ring([f.sbuf(f"tabcA{i}", [96, 2, 512], F32) for i in range(2)])
    rstage = Ring([f.sbuf(f"rstage{i}", [128, D_CKV], F32) for i in range(2)])
    for (c0, N) in CHUNKS:
        tb = tabc.next()
        LD("sp", tb.t[64:96, 0, 0:N], tabA_d[0, 64:96, c0:c0 + N], [tb])
        LD("sp", tb.t[64:96, 1, 0:N], tabA_d[1, 64:96, c0:c0 + N], [tb])
        ACT(sq8.t[:, :, 0:N], hT.t[:, :, c0:c0 + N], AF.Square, [hT], [sq8])
        pss = PS[6]
        for j in range(8):
            MM(pss, pss.t[:, 0:N], ones_b, sq8.t[:, j, 0:N], j == 0, j == 7, [cb, sq8])
        rs = rs_r.next()
        rstd_chain(pss, 128, N, rs, 1.0 / D)
        for j in range(8):
            STT("dve", xn.t[:, j, 0:N], hT.t[:, j, c0:c0 + N], vcol(V_GA + j), rs.t[:, 0:N], ALU.mult, ALU.mult, [hT, vecs, rs], [xn])
        mts = [(0, 128), (128, 128), (256, 128), (384, 128), (512, 128), (576, 96)]
        for mi, (mc, M) in enumerate(mts):
            pb = PS[mi]
            for j in range(8):
                MM(pb, pb.t[0:M, 0:N], wain.t[:, j, mc:mc + M], xn.t[:, j, 0:N], j == 0, j == 7, [wain, xn])
        for m in range(3):
            ACT(sq8.t[:, m, 0:N], PS[m].t[:, 0:N], AF.Square, [PS[m]], [sq8])
        for m in range(3):
            MM(pss, pss.t[:, 0:N], ones_b, sq8.t[:, m, 0:N], m == 0, m == 2, [cb, sq8])
        rs = rs_r.next()
        rstd_chain(pss, 128, N, rs, 1.0 / D_QC)
        for m in range(3):
            STT("dve", cqT.t[:, m, c0:c0 + N], PS[m].t[:, 0:N], vcol(V_GQC + m), rs.t[:, 0:N], ALU.mult, ALU.mult, [PS[m], vecs, rs], [cqT])
        for m in range(2):
            ACT(sq8.t[:, 3 + m, 0:N], PS[3 + m].t[:, 0:N], AF.Square, [PS[3 + m]], [sq8])
        for m in range(2):
            MM(pss, pss.t[:, 0:N], ones_b, sq8.t[:, 3 + m, 0:N], m == 0, m == 1, [cb, sq8])
        rs = rs_r.next()
        rstd_chain(pss, 128, N, rs, 1.0 / D_C)
        for m in range(2):
            STT("dve", ckv_f.t[:, m, 0:N], PS[3 + m].t[:, 0:N], vcol(V_GCKV + m), rs.t[:, 0:N], ALU.mult, ALU.mult, [PS[3 + m], vecs, rs], [ckv_f])
        CP("pool", ckvT.t[:, :, c0:c0 + N], ckv_f.t[:, :, 0:N], [ckv_f], [ckvT])
        ACT(sq8.t[0:96, 5, 0:N], PS[5].t[0:96, 0:N], AF.Square, [PS[5]], [sq8])
        MM(pss, pss.t[0:96, 0:N], cb.t[0:96, CB_B96:CB_B96 + 96], sq8.t[0:96, 5, 0:N], True, True, [cb, sq8])
        rs = rs_r.next()
        rstd_chain(pss, 96, N, rs, 1.0)
        STT("dve", kpn.t[0:96, 0:N], PS[5].t[0:96, 0:N], vcol(V_GK96, 0, 96), rs.t[0:96, 0:N], ALU.mult, ALU.mult, [PS[5], vecs, rs], [kpn])
        pr = PS[7]
        MM(pr, pr.t[0:96, 0:N], cf.t[0:96, CF_P96:CF_P96 + 96], kpn.t[0:96, 0:N], True, True, [cf, kpn])
        TT("pool", t1.t[64:96, 0:N], kpn.t[64:96, 0:N], tb.t[64:96, 0, 0:N], ALU.mult, [kpn, tb], [t1])
        TT("dve", kpe_f.t[64:96, 0:N], pr.t[64:96, 0:N], tb.t[64:96, 1, 0:N], ALU.mult, [pr, tb], [kpe_f])
        TT("dve", kpe_f.t[64:96, 0:N], kpe_f.t[64:96, 0:N], t1.t[64:96, 0:N], ALU.add, [kpe_f, t1], [kpe_f])
        CP("pool", kpe_b.t[64:96, c0:c0 + N], kpe_f.t[64:96, 0:N], [kpe_f], [kpe_b])
        ntile = (N + 127) // 128
        for ti in range(ntile):
            r = min(128, N - ti * 128)
            pbT = PS[7]
            for m in range(2):
                TR(pbT, pbT.t[0:r, m * 128:(m + 1) * 128], ckv_f.t[:, m, ti * 128:ti * 128 + r], identf, [ckv_f, cf])
            TR(pbT, pbT.t[0:r, 256:288], kpe_f.t[64:96, ti * 128:ti * 128 + r], cf.t[64:96, CF_ID + 64:CF_ID + 96], [kpe_f, cf])
            stg = rstage.next()
            CP("act", stg.t[0:r, :], pbT.t[0:r, 0:D_CKV], [pbT], [stg])
            if c0 < SEQ:
                ST("sp", rows_p[c0 + ti * 128:c0 + ti * 128 + r, :], stg.t[0:r, :], [stg])
            else:
                ST("sp", rows_s[:, :], stg.t[0:r, :], [stg])
                CP("pool", rown_b.t[0:NS, :], stg.t[0:NS, 0:256], [stg], [rown_b])
    f.release(mA1)
    if stop_after == "A1":
        f.release(mA)
        f.finish()
        return nc

    mA2 = f.mark()
    G = 2
    NG = H // G
    wq_r = Ring([f.sbuf(f"wq_g{i}", [128, 3, G * 96], BF16) for i in range(2)])
    wk_r = Ring([f.sbuf(f"wk_g{i}", [128, 2, G * 64], BF16) for i in range(2)])
    wv_r = Ring([f.sbuf(f"wv_g{i}", [128, 2, G * 64], BF16) for i in range(2)])
    wo_r = Ring([f.sbuf(f"wo_g{i}", [128, D], BF16) for i in range(2)])
    qT_g = f.sbuf("qT_g", [128, G, NT], BF16)
    KT_g = f.sbuf("KT_g", [128, G, NT], BF16)
    V_g = f.sbuf("V_g", [128, 16, G, 128], BF16)
    OT_r = Ring([f.sbuf(f"OT_g{i}", [128, SEQ], BF16) for i in range(2)])
    sq_r = Ring([f.sbuf(f"sqB{i}", [128, 512], BF16) for i in range(4)])
    rs_r = Ring([f.sbuf(f"rsB{i}", [96, 512], F32) for i in range(4)])
    qn_r = Ring([f.sbuf(f"qnB{i}", [96, 512], F32) for i in range(2)])
    t1_r = Ring([f.sbuf(f"t1B{i}", [96, 512], F32) for i in range(2)])
    t2_r = Ring([f.sbuf(f"t2B{i}", [96, 512], F32) for i in range(2)])
    pT_r = Ring([f.sbuf(f"pT{i}", [128, 512], BF16) for i in range(4)])
    rl_r = Ring([f.sbuf(f"rl{i}", [128, 512], F32) for i in range(2)])
    tabc = Ring([f.sbuf(f"tabcB{i}", [96, 2, 512], F32) for i in range(2)])
    MSET("pool", qT_g.t[:, :, :], 0.0, [qT_g])
    MSET("pool", KT_g.t[:, :, :], 0.0, [KT_g])
    MSET("pool", V_g.t[:, :, 0, 64:128], 1.0, [V_g])
    MSET("pool", V_g.t[:, :, 1, 0:64], 1.0, [V_g])
    ringP = Ring(PS[5:8])
    ringS = Ring(PS[0:3])
    ringO = Ring(PS[3:5])
    ringA = Ring(PS[0:8])
    maskD = cb.t[:, CB_MD:CB_MD + 128]
    B96 = cb.t[0:96, CB_B96:CB_B96 + 96]
    B64 = cb.t[0:64, CB_B96:CB_B96 + 64]
    P96 = cf.t[0:96, CF_P96:CF_P96 + 96]
    for g in range(NG):
        wq = wq_r.next(); wk = wk_r.next(); wv = wv_r.next(); wo = wo_r.next()
        OT_g = OT_r.next()
        if g % 2 == 0:
            pend = []
        pend.append((wo, OT_g))
        load_w(wq, w_uq, 3, g * G * 96, G * 96)
        load_w(wk, w_uk, 2, g * G * 64, G * 64)
        load_w(wv, w_uv, 2, g * G * 64, G * 64)
        LD("pool", wo.t[:, :], w_a_out[g * 128:(g + 1) * 128, :], [wo])
        def A_S1(ci):
            c0, N = CHUNKS[ci]
            A = PS[0:4] if ci % 2 == 0 else PS[4:8]
            tb = tabc.next()
            LD("sp", tb.t[64:96, 0, 0:N], tabA_d[0, 64:96, c0:c0 + N], [tb])
            LD("sp", tb.t[64:96, 1, 0:N], tabA_d[1, 64:96, c0:c0 + N], [tb])
            chs = []
            for hl in range(G):
                bq = A[hl]
                for m in range(3):
                    MM(bq, bq.t[0:96, 0:N], wq.t[:, m, hl * 96:(hl + 1) * 96], cqT.t[:, m, c0:c0 + N], m == 0, m == 2, [wq, cqT])
                chs.append(dict(q=True, hl=hl, b=bq, M=96, tb=tb))
            for hl in range(G):
                bk = A[2 + hl]
                for m in range(2):
                    MM(bk, bk.t[0:64, 0:N], wk.t[:, m, hl * 64:(hl + 1) * 64], ckvT.t[:, m, c0:c0 + N], m == 0, m == 1, [wk, ckvT])
                chs.append(dict(q=False, hl=hl, b=bk, M=64, tb=tb))
            return chs

        def A_mid(ci, chs):
            c0, N = CHUNKS[ci]
            Bset = PS[4:8] if ci % 2 == 0 else PS[0:4]
            for c in chs:
                M = c["M"]
                c["sq"] = sq_r.next()
                ACT(c["sq"].t[0:M, 0:N], c["b"].t[0:M, 0:N], AF.Square, [c["b"]], [c["sq"]])
            for ic, c in enumerate(chs):
                M = c["M"]
                c["bs"] = Bset[ic]
                MM(c["bs"], c["bs"].t[0:M, 0:N], cb.t[0:M, CB_B96:CB_B96 + M], c["sq"].t[0:M, 0:N], True, True, [cb, c["sq"]])
            for c in chs:
                c["rs"] = rs_r.next()
                rstd_chain(c["bs"], c["M"], N, c["rs"], 1.0)
            for c in chs:
                hl = c["hl"]
                if c["q"]:
                    c["qn"] = qn_r.next()
                    STT("dve", c["qn"].t[0:96, 0:N], c["b"].t[0:96, 0:N], vcol(V_GQ96, 0, 96), c["rs"].t[0:96, 0:N], ALU.mult, ALU.mult, [c["b"], vecs, c["rs"]], [c["qn"]])
                    STT("dve", qT_g.t[0:64, hl, c0:c0 + N], c["b"].t[0:64, 0:N], vcol(V_GQ96, 0, 64), c["rs"].t[0:64, 0:N], ALU.mult, ALU.mult, [c["b"], vecs, c["rs"]], [qT_g])
                else:
                    STT("dve", KT_g.t[0:64, hl, c0:c0 + N], c["b"].t[0:64, 0:N], vcol(V_GK96, 0, 64), c["rs"].t[0:64, 0:N], ALU.mult, ALU.mult, [c["b"], vecs, c["rs"]], [KT_g])
                    CP("pool", KT_g.t[64:96, hl, c0:c0 + N], kpe_b.t[64:96, c0:c0 + N], [kpe_b], [KT_g])

        def A_tail(ci, chs):
            c0, N = CHUNKS[ci]
            A = PS[0:4] if ci % 2 == 0 else PS[4:8]
            for c in chs:
                if c["q"]:
                    c["br"] = A[c["hl"]]
                    MM(c["br"], c["br"].t[0:96, 0:N], P96, c["qn"].t[0:96, 0:N], True, True, [cf, c["qn"]])
            if c0 < SEQ:
                for ti in range(ci * 4, ci * 4 + 4):
                    bv = A[2 + ti % 2]
                    for m in range(2):
                        MM(bv, bv.t[:, 0:G * 64], ckvT.t[:, m, ti * 128:(ti + 1) * 128], wv.t[:, m, 0:G * 64], m == 0, m == 1, [ckvT, wv])
                    CP("dve" if ti % 2 else "act", V_g.t[:, ti, 0, 0:64], bv.t[:, 0:64], [bv], [V_g])
                    CP("act" if ti % 2 else "dve", V_g.t[:, ti, 1, 64:128], bv.t[:, 64:128], [bv], [V_g])
            for c in chs:
                if c["q"]:
                    hl = c["hl"]; qn = c["qn"]; br = c["br"]; tb = c["tb"]
                    t1 = t1_r.next(); t2 = t2_r.next()
                    TT("pool", t1.t[64:96, 0:N], qn.t[64:96, 0:N], tb.t[64:96, 0, 0:N], ALU.mult, [qn, tb], [t1])
                    TT("dve", t2.t[64:96, 0:N], br.t[64:96, 0:N], tb.t[64:96, 1, 0:N], ALU.mult, [br, tb], [t2])
                    TT("dve", qT_g.t[64:96, hl, c0:c0 + N], t1.t[64:96, 0:N], t2.t[64:96, 0:N], ALU.add, [t1, t2], [qT_g])
            if c0 >= SEQ:
                for hl in range(G):
                    CP("pool", qs_all.t[0:96, g * G + hl, :], qT_g.t[0:96, hl, SEQ:NT], [qT_g], [qs_all])

        chs_next = A_S1(0)
        for ci in range(len(CHUNKS)):
            chs_cur = chs_next
            A_mid(ci, chs_cur)
            if ci + 1 < len(CHUNKS):
                chs_next = A_S1(ci + 1)
            A_tail(ci, chs_cur)
        iters = []
        for hl in range(G):
            for qc in range(4):
                nkt = 4 * qc + 4
                for kt in range(nkt):
                    iters.append(dict(hl=hl, qc=qc, kt=kt, nkt=nkt))

        def emitS(it):
            hl, qc, kt = it["hl"], it["qc"], it["kt"]
            n0 = max(qc * 512, kt * 128)
            W = qc * 512 + 512 - n0
            bs_ = ringS.next()
            MM(bs_, bs_.t[:, 0:W], KT_g.t[:, hl, kt * 128:(kt + 1) * 128], qT_g.t[:, hl, n0:n0 + W], True, True, [KT_g, qT_g])
            pT = pT_r.next()
            ACT(pT.t[:, 0:W], bs_.t[:, 0:W], AF.Exp, [bs_], [pT], scale=SCALE_A)
            if kt * 128 >= qc * 512:
                TT("pool", pT.t[:, 0:128], pT.t[:, 0:128], maskD, ALU.mult, [pT, cb], [pT])
            it["pT"] = pT; it["W"] = W; it["o0"] = n0 - qc * 512

        cur = {}

        def emitPV(it):
            hl, qc, kt, nkt = it["hl"], it["qc"], it["kt"], it["nkt"]
            if kt == 0:
                cur["bo"] = ringO.next()
            bo = cur["bo"]
            MM(bo, bo.t[:, it["o0"]:512], V_g.t[:, kt, hl, :], it["pT"].t[:, 0:it["W"]], kt == 0, kt == nkt - 1, [V_g, it["pT"]])
            if kt == nkt - 1:
                rl = rl_r.next()
                oL, oH = (0, 64) if hl == 0 else (64, 128)
                lL, lH = (64, 128) if hl == 0 else (0, 64)
                f.op("dve", lambda e, rl=rl, bo=bo, lL=lL, lH=lH: e.reciprocal(out=rl.t[lL:lH, :], in_=bo.t[lL:lH, :]), reads=[bo], writes=[rl])
                TT("dve", OT_g.t[oL:oH, qc * 512:(qc + 1) * 512], bo.t[oL:oH, :], rl.t[lL:lH, :], ALU.mult, [bo, rl], [OT_g])
        DEPTH = 2
        for idx in range(len(iters) + DEPTH):
            if idx < len(iters):
                emitS(iters[idx])
            if FILLER:
                MM(PS[7], PS[7].t[:, 0:FILLER], KT_g.t[0:96, 0, 0:128], qT_g.t[0:96, 0, 0:FILLER], True, True, [KT_g, qT_g])
            if idx - DEPTH >= 0:
                emitPV(iters[idx - DEPTH])
        if g % 2 == 1:
            for qc in range(4):
                c0 = qc * 512
                for j in range(8):
                    pb = ringP.next()
                    for ip, (wo_, OT_) in enumerate(pend):
                        MM(pb, pb.t[:, 0:512], wo_.t[:, j * 128:(j + 1) * 128], OT_.t[:, c0:c0 + 512], ip == 0, ip == len(pend) - 1, [wo_, OT_])
                    TT("dve", hT.t[:, j, c0:c0 + 512], pb.t[:, 0:512], hT.t[:, j, c0:c0 + 512], ALU.add, [pb, hT], [hT])
    f.release(mA2)
    if stop_after == "A2":
        f.release(mA)
        emit_output()
        f.finish()
        return nc

    mA3 = f.mark()
    wuk = f.sbuf("wuk", [128, 2, 1024], BF16)
    wuv = f.sbuf("wuv", [128, 2, 1024], BF16)
    load_w(wuk, w_uk, 2, 0, 1024)
    load_w(wuv, w_uv, 2, 0, 1024)
    OTs = f.sbuf("OTs", [64, H, NS], BF16)
    wukT = f.sbuf("wukT", [64, H, 256], BF16)
    qabsT = f.sbuf("qabsT", [128, 2, 4, 128], BF16)
    qpeT = f.sbuf("qpeT", [96, 4, 128], BF16)
    MSET("pool", qabsT.t[:, :, :, :], 0.0, [qabsT])
    MSET("pool", qpeT.t[:, :, :], 0.0, [qpeT])
    maskS = f.sbuf("maskS", [64, 4, 16], F32)
    LD("sp", maskS.t[:, :, :], maskS_d.rearrange("p (b k) -> p b k", b=4), [maskS])
    pti = f.sbuf("pti", [128, 512], I32)
    ptf = f.sbuf("ptf", [128, 512], F32)
    iot = f.sbuf("iot", [128, 1], I32)
    iof = f.sbuf("iof", [128, 1], F32)
    ridx = f.sbuf("ridx", [128, 512], I32)
    LD("sp", pti.t[:, :], ptab[0:1, :].partition_broadcast(128), [pti])
    f.op("pool", lambda e: e.iota(iot.t[:, :], pattern=[[0, 1]], base=0, channel_multiplier=1), writes=[iot])
    CP("dve", iof.t[:, :], iot.t[:, :], [iot], [iof])
    CP("dve", ptf.t[:, :], pti.t[:, :], [pti], [ptf])
    f.op("dve", lambda e: e.tensor_scalar(out=ptf.t[:, :], in0=ptf.t[:, :], scalar1=128.0, scalar2=iof.t[:, 0:1], op0=ALU.mult, op1=ALU.add), reads=[ptf, iof], writes=[ptf])
    CP("dve", ridx.t[:, :], ptf.t[:, :], [ptf], [ridx])
    for hb in range(4):
        pb = PS[hb]
        for hq in range(4):
            for m in range(2):
                slot = hq * 2 + m
                TR(pb, bfv(pb)[0:64, slot * 128:(slot + 1) * 128], wuk.t[:, m, (hb * 4 + hq) * 64:(hb * 4 + hq + 1) * 64], idb.t[:, :], [wuk, idb])
        CP("dve" if hb % 2 else "act", wukT.t[0:64, hb * 4:(hb + 1) * 4, :], bfv(pb)[0:64, 0:1024].rearrange("p (h l) -> p h l", h=4), [pb], [wukT])
    qsg = f.sbuf("qsg", [64, H, NS], BF16)
    TS("dve", qsg.t[0:64, :, :], qs_all.t[0:64, :, :], vcol(V_GK96, 0, 64), ALU.mult, [qs_all, vecs], [qsg])
    pb = PS[4]
    for m in range(2):
        for hh in range(H):
            col = (m * H + hh) * NS
            MM(pb, pb.t[:, col:col + NS], wukT.t[0:64, hh, m * 128:(m + 1) * 128], qsg.t[0:64, hh, :], True, True, [wukT, qsg])
    for m in range(2):
        CP("dve", qabsT.t[:, m, :, 0:64].rearrange("p b (h t) -> p b h t", h=H),
           pb.t[:, m * 256:(m + 1) * 256].rearrange("p (h b t) -> p b h t", h=H, b=4), [pb], [qabsT])
    CP("pool", qpeT.t[64:96, :, 0:64].rearrange("p b (h t) -> p b h t", h=H),
       qs_all.t[64:96, :, :].rearrange("p h (b t) -> p b h t", b=4), [qs_all], [qpeT])
    BS = cb.t[:, CB_BS:CB_BS + 1024].rearrange("p (j c) -> p j c", j=8)
    mA3b = f.mark()
    rowsb_r = Ring([f.sbuf(f"rowsb{i}", [128, 4, D_CKV], BF16) for i in range(8)])
    cT_r = Ring([f.sbuf(f"cT{i}", [128, 2, 512], BF16) for i in range(3)])
    kpT_r = Ring([f.sbuf(f"kpT{i}", [96, 512], BF16) for i in range(3)])
    sqk_r = Ring([f.sbuf(f"sqk{i}", [128, 512], BF16) for i in range(3)])
    rs_r3 = Ring([f.sbuf(f"rsS{i}", [64, 512], F32) for i in range(2)])
    sc_r3 = Ring([f.sbuf(f"scS{i}", [64, 512], F32) for i in range(2)])
    pS_r3 = Ring([f.sbuf(f"pS{i}", [64, 512], BF16) for i in range(2)])
    pTs_r3 = Ring([f.sbuf(f"pTs{i}", [128, 4, 64], BF16) for i in range(2)])
    tmp_r3 = Ring([f.sbuf(f"tmpS{i}", [64, 8], F32) for i in range(2)])
    m_run = f.sbuf("m_run", [64, 1], F32)
    l_run = f.sbuf("l_run", [64, 2], F32)
    acc = f.sbuf("accS", [64, 256], F32)
    accn = f.sbuf("accn", [64, 256], BF16)
    olT = f.sbuf("olT", [128, 2, 64], BF16)
    bX, bY, bK0, bK1, bSS, bN, bP, bT2 = PS
    ringK = Ring([bK0, bK1])

    units = []
    for b in range(4):
        for gi in range(32):
            units.append(dict(b=b, gi=gi, N=512, first=(gi == 0), last=False))
        units.append(dict(b=b, gi=None, N=NS, first=False, last=True))

    def stageG(u):
        if u["gi"] is None:
            return
        rb = rowsb_r.next()
        u["rb"] = rb
        for pg in range(4):
            col = u["b"] * 128 + u["gi"] * 4 + pg
            f.dma("pool", lambda e, rb=rb, pg=pg, col=col: e.indirect_dma_start(
                out=rb.t[:, pg, :], out_offset=None, in_=cache[:, :],
                in_offset=bass.IndirectOffsetOnAxis(ap=ridx.t[:, col:col + 1], axis=0)), reads=[ridx], writes=[rb])

    def stageA_tr(u, pg):
        if u["gi"] is None:
            return
        rb = u["rb"]
        for m in range(2):
            slot = m * 4 + pg
            TR(bX, bfv(bX)[:, slot * 128:(slot + 1) * 128], rb.t[:, pg, m * 128:(m + 1) * 128], idb.t[:, :], [rb, idb])
        TR(bY, bfv(bY)[0:96, pg * 128:(pg + 1) * 128], rb.t[:, pg, 192:288], idb.t[:, :], [rb, idb])

    def stageA_cp(u):
        if u["gi"] is None:
            u["cT"] = lambda m: ckvT.t[:, m, SEQ:NT]
            u["kpT"] = kpe_b.t[64:96, SEQ:NT]
            u["nat"] = [(rown_b.t[0:NS, :], NS, 0)]
            u["cbufs"] = [ckvT, kpe_b, rown_b]
            u["mask"] = maskS.t[:, u["b"], :]
            return
        rb = u["rb"]
        cT = cT_r.next(); kpT = kpT_r.next()
        CP("dve", cT.t[:, :, :].rearrange("p m n -> p (m n)"), bfv(bX)[:, 0:1024], [bX], [cT])
        CP("dve", kpT.t[64:96, :], bfv(bY)[64:96, 0:512], [bY], [kpT])
        u["cT"] = lambda m, cT=cT: cT.t[:, m, :]
        u["kpT"] = kpT.t[64:96, :]
        u["nat"] = [(rb.t[:, pg, 0:256], 128, pg * 128) for pg in range(4)]
        u["cbufs"] = [cT, kpT, rb]
        u["mask"] = None

    def stageA(u):
        for pg in range(4):
            stageA_tr(u, pg)
        stageA_cp(u)

    def stageB_step(u, k):
        N = u["N"]; cT_ap = u["cT"]; cbufs = u["cbufs"]; b = u["b"]

        def kr(jc):
            bk = ringK.next()
            for m in range(2):
                MM(bk, bk.t[:, 0:N], wuk.t[:, m, jc * 128:(jc + 1) * 128], cT_ap(m), m == 0, m == 1, [wuk] + cbufs)
            sqk = sqk_r.next()
            ACT(sqk.t[:, 0:N], bk.t[:, 0:N], AF.Square, [bk], [sqk])
            u.setdefault("sq", {})[jc] = sqk

        def ss(jc):
            sqk = u["sq"][jc]
            MM(bSS, bSS.t[0:128, 0:N], BS[:, jc, :], sqk.t[:, 0:N], jc == 0, jc == 7, [cb, sqk])
        if k == 0:
            kr(0); kr(1)
        elif k < 7:
            ss(k - 1); kr(k + 1)
            if k == 1:
                for m in range(2):
                    MM(bN, bN.t[0:128, 0:N], qabsT.t[:, m, b, :], cT_ap(m), m == 0, m == 1, [qabsT] + cbufs)
                MM(bP, bP.t[0:128, 0:N], qpeT.t[64:96, b, :], u["kpT"], True, True, [qpeT] + cbufs)
        else:
            ss(6); ss(7)

    def stageC1(u):
        N = u["N"]
        rs = rs_r3.next(); sc = sc_r3.next()
        rstd_chain(bSS, 64, N, rs, 1.0)
        TT("dve", sc.t[0:64, 0:N], bN.t[0:64, 0:N], rs.t[0:64, 0:N], ALU.mult, [bN, rs], [sc])
        TT("dve", sc.t[0:64, 0:N], bP.t[0:64, 0:N], sc.t[0:64, 0:N], ALU.add, [bP, sc], [sc])
        if u["mask"] is not None:
            TT("dve", sc.t[0:64, 0:N], sc.t[0:64, 0:N], u["mask"], ALU.add, [sc, maskS], [sc])
        u["sc"] = sc

    def stageC2a(u):
        N = u["N"]; sc = u["sc"]; nat = u["nat"]; cbufs = u["cbufs"]; b = u["b"]
        if u["first"]:
            MSET("pool", m_run.t[:, :], NEG, [m_run])
            MSET("pool", l_run.t[:, 0:1], 0.0, [l_run])
            MSET("pool", acc.t[:, :], 0.0, [acc])
        tmp = tmp_r3.next(); pS = pS_r3.next(); pTs = pTs_r3.next()

        def tc(i):
            return tmp.t[0:64, i:i + 1]
        MSET("dve", tc(4), 0.0, [tmp])
        f.op("dve", lambda e: e.tensor_reduce(out=tc(0), in_=sc.t[0:64, 0:N], axis=AX.X, op=ALU.max), reads=[sc], writes=[tmp])
        TT("dve", tc(1), m_run.t[:, 0:1], tc(0), ALU.max, [m_run, tmp], [tmp])
        TS("dve", tc(2), tc(1), -SCALE_A, ALU.mult, [tmp], [tmp])
        u["tmp"] = tmp; u["pS"] = pS; u["pTs"] = pTs

    def stageC2a2(u):
        N = u["N"]; sc = u["sc"]
        tmp = u["tmp"]; pS = u["pS"]

        def tc(i):
            return tmp.t[0:64, i:i + 1]
        ACT(tc(3), m_run.t[:, 0:1], AF.Exp, [m_run, tmp], [tmp], scale=SCALE_A, bias=tc(2))
        ACT(pS.t[0:64, 0:N], sc.t[0:64, 0:N], AF.Exp, [sc, tmp], [pS, tmp], scale=SCALE_A, bias=tc(2), accum_out=tc(4))
        STT("dve", l_run.t[:, 0:1], l_run.t[:, 0:1], tc(3), tc(4), ALU.mult, ALU.add, [l_run, tmp], [l_run])
        CP("dve", m_run.t[:, 0:1], tc(1), [tmp], [m_run])

    def stageC2b(u):
        N = u["N"]; nat = u["nat"]; cbufs = u["cbufs"]; b = u["b"]
        tmp = u["tmp"]; pS = u["pS"]; pTs = u["pTs"]

        def tc(i):
            return tmp.t[0:64, i:i + 1]
        npg = len(nat)
        for pg, (rows_ap, K, col0) in enumerate(nat):
            TR(bT2, bfv(bT2)[0:K, pg * 64:(pg + 1) * 64], pS.t[0:64, col0:col0 + K], idb.t[0:64, 0:64], [pS, idb])
        Kmax = max(K for (_, K, _) in nat)
        CP("dve", pTs.t[0:Kmax, 0:npg, :], bfv(bT2)[0:Kmax, 0:npg * 64].rearrange("p (g c) -> p g c", g=npg), [bT2], [pTs])

    def stageC2c(u):
        N = u["N"]; nat = u["nat"]; cbufs = u["cbufs"]; b = u["b"]
        tmp = u["tmp"]; pS = u["pS"]; pTs = u["pTs"]

        def tc(i):
            return tmp.t[0:64, i:i + 1]
        npg = len(nat)
        for pg, (rows_ap, K, col0) in enumerate(nat):
            MM(bT2, bT2.t[0:64, 256:512], pTs.t[0:K, pg, :], rows_ap, pg == 0, pg == npg - 1, [pTs] + cbufs)
        STT("dve", acc.t[:, :], acc.t[:, :], tc(3), bT2.t[0:64, 256:512], ALU.mult, ALU.add, [acc, tmp, bT2], [acc])
        if u["last"]:
            f.op("dve", lambda e: e.reciprocal(out=l_run.t[:, 1:2], in_=l_run.t[:, 0:1]), reads=[l_run], writes=[l_run])
            TS("dve", accn.t[:, :], acc.t[:, :], l_run.t[:, 1:2], ALU.mult, [acc, l_run], [accn])
            for m in range(2):
                TR(bT2, bfv(bT2)[:, m * 64:(m + 1) * 64], accn.t[0:64, m * 128:(m + 1) * 128], idb.t[0:64, 0:64], [accn, idb])
            CP("act", olT.t[:, :, :], bfv(bT2)[:, 0:128].rearrange("p (m c) -> p m c", m=2), [bT2], [olT])
            for hh in range(H):
                for m in range(2):
                    MM(bT2, bT2.t[0:64, 256 + hh * 4:256 + (hh + 1) * 4], wuv.t[:, m, hh * 64:(hh + 1) * 64], olT.t[:, m, hh * 4:(hh + 1) * 4], m == 0, m == 1, [wuv, olT])
            CP("dve", OTs.t[0:64, :, b * 4:(b + 1) * 4], bT2.t[0:64, 256:320].rearrange("p (h t) -> p h t", h=H), [bT2], [OTs])

    nu = len(units)
    for i in range(min(4, nu)):
        stageG(units[i])
    stageA(units[0]); stageA(units[1])
    for k in range(8):
        stageB_step(units[0], k)
    stageC1(units[0])
    for i in range(nu):
        u0 = units[i]
        u1 = units[i + 1] if i + 1 < nu else None
        u2 = units[i + 2] if i + 2 < nu else None
        if i + 4 < nu:
            stageG(units[i + 4])
        for k in range(8):
            if u1 is not None:
                stageB_step(u1, k)
            if k < 4 and u2 is not None:
                stageA_tr(u2, k)
            if k == 0:
                stageC2a(u0)
            if k == 3 and u2 is not None:
                stageA_cp(u2)
            if k == 4:
                stageC2a2(u0)
            if k == 6:
                stageC2b(u0)
            if k == 7:
                stageC2c(u0)
        if u1 is not None:
            stageC1(u1)
    f.release(mA3b)
    wao = f.sbuf("wao", [64, H, D], BF16)
    for hh in range(H):
        LD("pool", wao.t[0:64, hh, :], w_a_out[hh * 64:(hh + 1) * 64, :], [wao])
    for j in range(8):
        pb = ringK.next()
        for hh in range(H):
            MM(pb, pb.t[:, 0:NS], wao.t[0:64, hh, j * 128:(j + 1) * 128], OTs.t[0:64, hh, :], hh == 0, hh == H - 1, [wao, OTs])
        TT("dve", hT.t[:, j, SEQ:NT], pb.t[:, 0:NS], hT.t[:, j, SEQ:NT], ALU.add, [pb, hT], [hT])
    f.release(mA3)
    f.release(mA)
    if stop_after == "A3":
        emit_output()
        f.finish()
        return nc

    def ffn(l):
        mF = f.mark()
        xnT = f.sbuf("xnT", [128, 8, NT], BF16)
        sq8 = f.sbuf("sq8F", [128, 8, 512], BF16)
        rs = f.sbuf("rsF", [128, 512], F32)
        gcol = V_GFA if l == 0 else V_GFB
        for (c0, N) in CHUNKS:
            ACT(sq8.t[:, :, 0:N], hT.t[:, :, c0:c0 + N], AF.Square, [hT], [sq8])
            pss = PS[7]
            for j in range(8):
                MM(pss, pss.t[:, 0:N], ones_b, sq8.t[:, j, 0:N], j == 0, j == 7, [cb, sq8])
            rstd_chain(pss, 128, N, rs, 1.0 / D)
            for j in range(8):
                STT("dve", xnT.t[:, j, c0:c0 + N], hT.t[:, j, c0:c0 + N], vcol(gcol + j), rs.t[:, 0:N], ALU.mult, ALU.mult, [hT, vecs, rs], [xnT])
        wg_r = Ring([f.sbuf(f"wg{i}", [128, 8, 512], BF16) for i in range(2)])
        wu_r = Ring([f.sbuf(f"wu{i}", [128, 8, 512], BF16) for i in range(2)])
        wo_r2 = Ring([f.sbuf(f"wo{i}", [128, 4, D], BF16) for i in range(2)])
        uT_r = Ring([f.sbuf(f"uT{i}", [128, 4, 512], BF16) for i in range(2)])
        sg_r = Ring([f.sbuf(f"sg{i}", [128, 512], F32) for i in range(2)])
        ring = Ring(PS[0:8])
        nblk = (D_FF + 511) // 512
        for hb in range(nblk):
            h0 = hb * 512
            HW = min(512, D_FF - h0)
            nhc = HW // 128
            wg = wg_r.next(); wu = wu_r.next(); wo = wo_r2.next()
            load_w(wg, w_ffn_in[l], 8, h0, HW)
            load_w(wu, w_ffn_in[l], 8, D_FF + h0, HW)
            for hc in range(nhc):
                LD("pool", wo.t[:, hc, :], w_ffn_out[l, h0 + hc * 128:h0 + (hc + 1) * 128, :], [wo])
            for (c0, N) in CHUNKS:
                uT = uT_r.next()
                for hc in range(nhc):
                    pg_ = ring.next()
                    for j in range(8):
                        MM(pg_, pg_.t[:, 0:N], wg.t[:, j, hc * 128:(hc + 1) * 128], xnT.t[:, j, c0:c0 + N], j == 0, j == 7, [wg, xnT])
                    pu_ = ring.next()
                    for j in range(8):
                        MM(pu_, pu_.t[:, 0:N], wu.t[:, j, hc * 128:(hc + 1) * 128], xnT.t[:, j, c0:c0 + N], j == 0, j == 7, [wu, xnT])
                    sg = sg_r.next()
                    ACT(sg.t[:, 0:N], pg_.t[:, 0:N], AF.Silu, [pg_], [sg])
                    TT("dve", uT.t[:, hc, 0:N], pu_.t[:, 0:N], sg.t[:, 0:N], ALU.mult, [pu_, sg], [uT])
                for j in range(8):
                    po = ring.next()
                    for hc in range(nhc):
                        MM(po, po.t[:, 0:N], wo.t[:, hc, j * 128:(j + 1) * 128], uT.t[:, hc, 0:N], hc == 0, hc == nhc - 1, [wo, uT])
                    TT("dve", hT.t[:, j, c0:c0 + N], po.t[:, 0:N], hT.t[:, j, c0:c0 + N], ALU.add, [po, hT], [hT])
        f.release(mF)

    ffn(0)
    if stop_after == "FA":
        emit_output()
        f.finish()
        return nc

    mB = f.mark()
    wkv = f.sbuf("wkv", [128, 8, 512], BF16)
    load_w(wkv, w_kv, 8, 0, 512)
    wq_r = Ring([f.sbuf("wqh0", [128, 8, 512], BF16)])
    wbo_r = Ring([f.sbuf("wboh0", [128, 4, D], BF16)])
    esk = f.sbuf("esk", [128, H], F32)
    ACT(esk.t[:, :], vecs.t[:, V_SINKR:V_SINKR + H], AF.Exp, [vecs], [esk])
    maskB = f.sbuf("maskB", [16, 4, 144], F32)
    LD("sp", maskB.t[:, :, :], maskB_d.rearrange("p (b k) -> p b k", b=4), [maskB])
    KB_c = f.sbuf("KB_c", [64, 4, 640], BF16)
    VB_c = f.sbuf("VB_c", [128, 5, 4, 192], BF16)
    MSET("pool", VB_c.t[:, :, :, 0:64], 1.0, [VB_c])
    MSET("pool", VB_c.t[:, :, :, 128:192], 1.0, [VB_c])
    QB_c = f.sbuf("QB_c", [64, 4, 8, 128], BF16)
    OB_c = f.sbuf("OB_c", [128, 4, 512], BF16)
    QBs = f.sbuf("QBs", [64, 4, 8, 4], BF16)
    OBs = f.sbuf("OBs", [128, 8, NS], BF16)
    kn_s = f.sbuf("kn_s", [64, 4, NS], F32)
    kn_sb = f.sbuf("kn_sb", [64, 4, NS], BF16)
    VBs_f = f.sbuf("VBs_f", [NS, 256], F32)
    VBs_b = f.sbuf("VBs_b", [NS, 256], BF16)
    sqx = f.sbuf("sqxB", [128, 8, 512], BF16)
    rs0 = f.sbuf("rs0B", [128, 512], F32)
    xkv = f.sbuf("xkv", [128, 8, 512], BF16)
    sq_r = Ring([f.sbuf(f"sqC{i}", [64, 512], BF16) for i in range(4)])
    rs_r = Ring([f.sbuf(f"rsC{i}", [64, 512], F32) for i in range(4)])
    kn_r = Ring([f.sbuf(f"knC{i}", [64, 512], F32) for i in range(4)])
    t1_r = Ring([f.sbuf(f"t1C{i}", [16, 512], F32) for i in range(2)])
    t2_r = Ring([f.sbuf(f"t2C{i}", [16, 512], F32) for i in range(2)])
    pT_r = Ring([f.sbuf(f"pTC{i}", [128, 512], BF16) for i in range(4)])
    lt_r = Ring([f.sbuf(f"ltC{i}", [128, 512], F32) for i in range(3)])
    tabc = Ring([f.sbuf(f"tabcC{i}", [16, 2, 512], F32) for i in range(2)])
    kvst = f.sbuf("kvst", [128, 2, 256], F32)
    ringP = Ring(PS[6:8])
    ringS = Ring(PS[0:3])
    ringO = Ring(PS[3:6])
    B64 = cb.t[0:64, CB_B96:CB_B96 + 64]
    P16 = cf.t[0:16, CF_P16:CF_P16 + 16]
    maskD4 = cb.t[:, CB_MD:CB_MD + 512]
    maskP4 = cb.t[:, CB_MP:CB_MP + 512]

    def pj_S1(specs, N, A):
        for i, sp in enumerate(specs):
            for j in range(8):
                MM(A[i], A[i].t[0:64, 0:N], sp["lhsT"](j), sp["rhs"](j), j == 0, j == 7, sp["rd"])

    def pj_mid(specs, N, A, Bs):
        n = len(specs)
        sqs, rss = [], []
        for i in range(n):
            sq = sq_r.next(); sqs.append(sq)
            ACT(sq.t[0:64, 0:N], A[i].t[0:64, 0:N], AF.Square, [A[i]], [sq])
        for i in range(n):
            MM(Bs[i], Bs[i].t[0:64, 0:N], B64, sqs[i].t[0:64, 0:N], True, True, [cb, sqs[i]])
        for i in range(n):
            rs = rs_r.next(); rss.append(rs)
            rstd_chain(Bs[i], 64, N, rs, 1.0)
        for i, sp in enumerate(specs):
            kn = kn_r.next(); sp["kn"] = kn
            if sp.get("dst") is not None:
                dap, dbuf = sp["dst"]
                vw = lambda ap: ap.rearrange("p (t q) -> p t q", t=4)
                STT("dve", dap(0, 64), vw(A[i].t[0:64, 0:N]), vcol(sp["gcol"], 0, 64), vw(rss[i].t[0:64, 0:N]), ALU.mult, ALU.mult, [A[i], vecs, rss[i]], [dbuf])
                STT("dve", kn.t[0:16, 0:N], A[i].t[0:16, 0:N], vcol(sp["gcol"], 0, 16), rss[i].t[0:16, 0:N], ALU.mult, ALU.mult, [A[i], vecs, rss[i]], [kn])
            else:
                STT("dve", kn.t[0:64, 0:N], A[i].t[0:64, 0:N], vcol(sp["gcol"], 0, 64), rss[i].t[0:64, 0:N], ALU.mult, ALU.mult, [A[i], vecs, rss[i]], [kn])

    def pj_tail(specs, N, tb, A):
        for i, sp in enumerate(specs):
            MM(A[i], A[i].t[0:16, 0:N], P16, sp["kn"].t[0:16, 0:N], True, True, [cf, sp["kn"]])
        for i, sp in enumerate(specs):
            kn = sp["kn"]
            t1 = t1_r.next(); t2 = t2_r.next()
            TT("pool", t1.t[0:16, 0:N], kn.t[0:16, 0:N], tb.t[0:16, 0, 0:N], ALU.mult, [kn, tb], [t1])
            TT("dve", t2.t[0:16, 0:N], A[i].t[0:16, 0:N], tb.t[0:16, 1, 0:N], ALU.mult, [A[i], tb], [t2])
            if sp.get("dst") is not None:
                dap, dbuf = sp["dst"]
                vw = lambda ap: ap.rearrange("p (t q) -> p t q", t=4)
                TT("dve", dap(0, 16), vw(t1.t[0:16, 0:N]), vw(t2.t[0:16, 0:N]), ALU.add, [t1, t2], [dbuf])
            else:
                TT("dve", kn.t[0:16, 0:N], t1.t[0:16, 0:N], t2.t[0:16, 0:N], ALU.add, [t1, t2], [kn])
                sp["consume"](kn)

    def proj_pipeline(batches, N, tb):
        sets = [PS[0:4], PS[4:8]]
        pj_S1(batches[0]["specs"], N, sets[0])
        for k, bt in enumerate(batches):
            A = sets[k % 2]; Bs = sets[(k + 1) % 2]
            pj_mid(bt["specs"], N, A, Bs)
            if k + 1 < len(batches):
                pj_S1(batches[k + 1]["specs"], N, Bs)
            pj_tail(bt["specs"], N, tb, A)
            if bt.get("after") is not None:
                bt["after"](A)

    for ci, (c0, N) in enumerate(CHUNKS):
        samp = c0 >= SEQ
        tb = tabc.next()
        LD("sp", tb.t[0:16, 0, 0:N], tabB_d[0, :, c0:c0 + N], [tb])
        LD("sp", tb.t[0:16, 1, 0:N], tabB_d[1, :, c0:c0 + N], [tb])
        ACT(sqx.t[:, :, 0:N], hT.t[:, :, c0:c0 + N], AF.Square, [hT], [sqx])
        pss = ringP.next()
        for j in range(8):
            MM(pss, pss.t[:, 0:N], ones_b, sqx.t[:, j, 0:N], j == 0, j == 7, [cb, sqx])
        rstd_chain(pss, 128, N, rs0, 1.0 / D)
        xnB = sqx
        for j in range(8):
            STT("dve", xkv.t[:, j, 0:N], hT.t[:, j, c0:c0 + N], vcol(V_GKV + j), rs0.t[:, 0:N], ALU.mult, ALU.mult, [hT, vecs, rs0], [xkv])
            STT("dve", xnB.t[:, j, 0:N], hT.t[:, j, c0:c0 + N], vcol(V_GB + j), rs0.t[:, 0:N], ALU.mult, ALU.mult, [hT, vecs, rs0], [xnB])
        kn_keep = {}

        def k_consume(kvh):
            def fn(kn):
                if not samp:
                    CP("pool", KB_c.t[0:64, kvh, 128:128 + N], kn.t[0:64, 0:N], [kn], [KB_c])
                    kn_keep[kvh] = kn
                else:
                    CP("pool", kn_s.t[0:64, kvh, :], kn.t[0:64, 0:NS], [kn], [kn_s])
                    CP("pool", kn_sb.t[0:64, kvh, :], kn.t[0:64, 0:NS], [kn], [kn_sb])
            return fn
        k_batch = dict(specs=[dict(lhsT=(lambda j, kvh=kvh: wkv.t[:, j, kvh * 64:(kvh + 1) * 64]), rhs=(lambda j: xkv.t[:, j, 0:N]),
                                    rd=[wkv, xkv], gcol=V_GKB, consume=k_consume(kvh)) for kvh in range(N_KV)], after=None)
        if ci == 3:
            def k_after(A):
                for kvh in range(N_KV):
                    pbT = A[3]
                    TR(pbT, pbT.t[:, 0:64], kn_keep[kvh].t[0:64, 384:512], cf.t[0:64, CF_ID:CF_ID + 64], [kn_keep[kvh], cf])
                    CP("act", kvst.t[:, 0, kvh * 64:(kvh + 1) * 64], pbT.t[:, 0:64], [pbT], [kvst])
                ST("sp", wk_p[:, :], kvst.t[:, 0, :], [kvst])
            k_batch["after"] = k_after
        if samp:
            proj_pipeline([k_batch], N, tb)
        if not samp:
            for ti in range(4):
                bv = ringP.next()
                for j in range(8):
                    MM(bv, bv.t[:, 0:256], xkv.t[:, j, ti * 128:(ti + 1) * 128], wkv.t[:, j, 256:512], j == 0, j == 7, [xkv, wkv])
                CP("act", VB_c.t[:, 1 + ti, :, 64:128], bv.t[:, 0:256].rearrange("p (k d) -> p k d", k=4), [bv], [VB_c])
                if ci == 3 and ti == 3:
                    CP("dve", kvst.t[:, 1, :], bv.t[:, 0:256], [bv], [kvst])
                    ST("sp", wv_p[:, :], kvst.t[:, 1, :], [kvst])
        else:
            bv = ringP.next()
            for j in range(8):
                MM(bv, bv.t[0:NS, 0:256], xkv.t[:, j, 0:NS], wkv.t[:, j, 256:512], j == 0, j == 7, [xkv, wkv])
            CP("act", VBs_f.t[:, :], bv.t[0:NS, 0:256], [bv], [VBs_f])
            CP("dve", VBs_b.t[:, :], bv.t[0:NS, 0:256], [bv], [VBs_b])
            pbT = ringP.next()
            for kvh in range(N_KV):
                TR(pbT, pbT.t[0:NS, kvh * 64:(kvh + 1) * 64], kn_s.t[0:64, kvh, :], cf.t[0:64, CF_ID:CF_ID + 64], [kn_s, cf])
            ktok = f.sbuf("ktok", [NS, 256], F32)
            CP("act", ktok.t[:, :], pbT.t[0:NS, 0:256], [pbT], [ktok])
            kw_all = f.sbuf("kw_all", [128, 4, 256], F32)
            vw_all = f.sbuf("vw_all", [128, 4, 256], F32)
            vwb_all = f.sbuf("vwb_all", [128, 4, 256], BF16)
            for b in range(4):
                LD("sp", kw_all.t[:, b, :], swk[b, :, :], [kw_all])
                LD("sp", vw_all.t[:, b, :], swv[b, :, :], [vw_all])
            CP("pool", vwb_all.t[:, :, :], vw_all.t[:, :, :], [vw_all], [vwb_all])
            for b in range(4):
                ST("sp", wk_s[b, 0:124, :], kw_all.t[4:128, b, :], [kw_all])
                ST("sp", wv_s[b, 0:124, :], vw_all.t[4:128, b, :], [vw_all])
                ST("sp", wk_s[b, 124:128, :], ktok.t[b * 4:(b + 1) * 4, :], [ktok])
                ST("sp", wv_s[b, 124:128, :], VBs_f.t[b * 4:(b + 1) * 4, :], [VBs_f])
            Kcat = f.sbuf("Kcat", [64, 144], BF16)
            scb = f.sbuf("scb", [16, 144], F32)
            pb_ = f.sbuf("pbB", [16, 144], BF16)
            pTw = f.sbuf("pTw", [128, 32], BF16)
            st2 = f.sbuf("st2", [16, 8], F32)
            onb = f.sbuf("onb", [16, 128], BF16)

            def s2(i):
                return st2.t[0:16, i:i + 1]
        for half in range(2):
            wqh = wq_r.next(); wboh = wbo_r.next()
            load_w(wqh, w_q_b, 8, half * 512, 512)
            for pr in range(4):
                LD("pool", wboh.t[:, pr, :], w_b_out[(half * 4 + pr) * 128:(half * 4 + pr + 1) * 128, :], [wboh])
            def q_consume(h8):
                def fn(kn):
                    if not samp:
                        CP("pool", QB_c.t[0:64, :, h8, :], kn.t[0:64, 0:512].rearrange("p (t q) -> p t q", t=4), [kn], [QB_c])
                    else:
                        CP("pool", QBs.t[0:64, :, h8, :], kn.t[0:64, 0:NS].rearrange("p (b t) -> p b t", b=4), [kn], [QBs])
                return fn
            def q_dst(h8):
                if samp:
                    return None
                slot8 = (h8 // 4) * 4 + [0, 2, 1, 3][h8 % 4]
                return ((lambda p0, p1, slot8=slot8: QB_c.t[p0:p1, :, slot8, :]), QB_c)
            batches = [dict(specs=[dict(lhsT=(lambda j, h8=h8: wqh.t[:, j, h8 * 64:(h8 + 1) * 64]), rhs=(lambda j: xnB.t[:, j, 0:N]),
                                        rd=[wqh, xnB], gcol=V_GQB, consume=q_consume(h8), dst=q_dst(h8)) for h8 in range(qb * 4, qb * 4 + 4)], after=None)
                       for qb in range(2)]
            if half == 0 and not samp:
                batches = [k_batch] + batches
            proj_pipeline(batches, N, tb)
            if not samp:
                iters = []
                for kk in range(2):
                    for ti in range(4):
                        gi = ci * 4 + ti
                        kts = [kt for kt in (gi - 1, gi) if kt >= 0]
                        for n_, kt in enumerate(kts):
                            iters.append(dict(kk=kk, ti=ti, gi=gi, kt=kt, first=(n_ == 0), last=(n_ == len(kts) - 1), n=len(iters)))

                def emitS(it):
                    kk, ti, kt, gi = it["kk"], it["ti"], it["kt"], it["gi"]
                    kvh = half * 2 + kk
                    slot = kt - (ci * 4 - 1)
                    bs_ = ringS.next()
                    MM(bs_, bs_.t[:, 0:512], KB_c.t[0:64, kvh, slot * 128:(slot + 1) * 128],
                       QB_c.t[0:64, ti, kk * 4:(kk + 1) * 4, :].rearrange("p h q -> p (h q)"), True, True, [KB_c, QB_c])
                    pT = pT_r.next()
                    ACT(pT.t[:, :], bs_.t[:, :], AF.Exp, [bs_], [pT], scale=SCALE_B)
                    TT("pool" if it["n"] % 4 == 3 else "dve", pT.t[:, :], pT.t[:, :], maskD4 if kt == gi else maskP4, ALU.mult, [pT, cb], [pT])
                    it["pT"] = pT; it["slot"] = slot; it["kvh"] = kvh
                curB = {}

                def emitPV(it):
                    kk, ti, kvh = it["kk"], it["ti"], it["kvh"]
                    if it["first"]:
                        curB["bo"] = ringO.next()
                    bo = curB["bo"]
                    MM(bo, bo.t[:, 0:256], VB_c.t[:, it["slot"], kvh, 64:192], it["pT"].t[:, 0:256], it["first"], it["last"], [VB_c, it["pT"]], sgc=True)
                    MM(bo, bo.t[:, 256:512], VB_c.t[:, it["slot"], kvh, 0:128], it["pT"].t[:, 256:512], False, it["last"], [VB_c, it["pT"]], sgc=True)
                    if it["last"]:
                        pend_norm.append((it["idx_emit"], bo, kvh, kk, ti))

                def emitNorm(bo, kvh, kk, ti):
                    lt = lt_r.next()
                    v3 = lambda ap: ap.rearrange("p (h q) -> p h q", h=2)
                    eskv = esk.t[:, kvh * 4:(kvh + 1) * 4].rearrange("p (h2 two) -> p h2 two", two=2)
                    TT("dve", v3(lt.t[64:128, 0:256]), v3(bo.t[64:128, 0:256]), eskv[64:128, :, 0].unsqueeze(2).broadcast_to([64, 2, 128]), ALU.add, [bo, esk], [lt])
                    TT("dve", v3(lt.t[0:64, 256:512]), v3(bo.t[0:64, 256:512]), eskv[0:64, :, 1].unsqueeze(2).broadcast_to([64, 2, 128]), ALU.add, [bo, esk], [lt])
                    ACT(lt.t[64:128, 0:256], lt.t[64:128, 0:256], AF.Ln, [lt], [lt])
                    ACT(lt.t[0:64, 256:512], lt.t[0:64, 256:512], AF.Ln, [lt], [lt])
                    ACT(lt.t[64:128, 0:256], lt.t[64:128, 0:256], AF.Exp, [lt], [lt], scale=-1.0)
                    ACT(lt.t[0:64, 256:512], lt.t[0:64, 256:512], AF.Exp, [lt], [lt], scale=-1.0)
                    TT("dve", OB_c.t[0:64, kk * 2:(kk + 1) * 2, ti * 128:(ti + 1) * 128], v3(bo.t[0:64, 0:256]), v3(lt.t[64:128, 0:256]), ALU.mult, [bo, lt], [OB_c])
                    TT("dve", OB_c.t[64:128, kk * 2:(kk + 1) * 2, ti * 128:(ti + 1) * 128], v3(bo.t[64:128, 256:512]), v3(lt.t[0:64, 256:512]), ALU.mult, [bo, lt], [OB_c])
                DEPTH = 2
                NDELAY = 0
                pend_norm = []
                for idx in range(len(iters) + DEPTH + NDELAY):
                    if idx < len(iters):
                        emitS(iters[idx])
                    if 0 <= idx - DEPTH < len(iters):
                        iters[idx - DEPTH]["idx_emit"] = idx
                        emitPV(iters[idx - DEPTH])
                    while pend_norm and pend_norm[0][0] + NDELAY <= idx:
                        _, bo_, kvh_, kk_, ti_ = pend_norm.pop(0)
                        emitNorm(bo_, kvh_, kk_, ti_)
                assert not pend_norm
                for j in range(8):
                    pb = ringP.next()
                    for pr in range(4):
                        MM(pb, pb.t[:, 0:512], wboh.t[:, pr, j * 128:(j + 1) * 128], OB_c.t[:, pr, :], pr == 0, pr == 3, [wboh, OB_c])
                    TT("dve", hT.t[:, j, c0:c0 + 512], pb.t[:, 0:512], hT.t[:, j, c0:c0 + 512], ALU.add, [pb, hT], [hT])
            else:
                for b in range(4):
                    for kk in range(2):
                        kvh = half * 2 + kk
                        pbK = ringP.next()
                        TR(pbK, pbK.t[0:64, 0:128], kw_all.t[:, b, kvh * 64:(kvh + 1) * 64], identf, [kw_all, cf])
                        CP("act", Kcat.t[0:64, 0:128], pbK.t[0:64, 0:128], [pbK], [Kcat])
                        CP("pool", Kcat.t[0:64, 128:144], kn_sb.t[0:64, kvh, :], [kn_sb], [Kcat])
                        bs_ = ringS.next()
                        MM(bs_, bs_.t[0:16, 0:144], QBs.t[0:64, b, kk * 4:(kk + 1) * 4, :].rearrange("p h t -> p (h t)"), Kcat.t[0:64, 0:144], True, True, [QBs, Kcat])
                        TT("dve", scb.t[:, :], bs_.t[0:16, 0:144], maskB.t[:, b, :], ALU.add, [bs_, maskB], [scb])
                        f.op("dve", lambda e: e.tensor_reduce(out=s2(0), in_=scb.t[:, :], axis=AX.X, op=ALU.max), reads=[scb], writes=[st2])
                        TS("dve", s2(1), s2(0), -SCALE_B, ALU.mult, [st2], [st2])
                        MSET("pool", s2(2), 0.0, [st2])
                        ACT(pb_.t[:, :], scb.t[:, :], AF.Exp, [scb, st2], [pb_, st2], scale=SCALE_B, bias=s2(1), accum_out=s2(2))
                        ACT(s2(3), vecs.t[0:16, V_SINKC + kvh:V_SINKC + kvh + 1], AF.Exp, [vecs, st2], [st2], scale=1.0, bias=s2(1))
                        TT("dve", s2(4), s2(2), s2(3), ALU.add, [st2], [st2])
                        f.op("dve", lambda e: e.reciprocal(out=s2(5), in_=s2(4)), reads=[st2], writes=[st2])
                        pbP = ringP.next()
                        TR(pbP, bfv(pbP)[:, 0:16], pb_.t[0:16, 0:128], idb.t[0:16, 0:16], [pb_, idb])
                        TR(pbP, bfv(pbP)[0:16, 16:32], pb_.t[0:16, 128:144], idb.t[0:16, 0:16], [pb_, idb])
                        CP("act", pTw.t[:, 0:32], bfv(pbP)[:, 0:32], [pbP], [pTw])
                        bo = ringO.next()
                        MM(bo, bo.t[0:16, 0:64], pTw.t[:, 0:16], vwb_all.t[:, b, kvh * 64:(kvh + 1) * 64], True, False, [pTw, vwb_all])
                        MM(bo, bo.t[0:16, 0:64], pTw.t[0:16, 16:32], VBs_b.t[0:NS, kvh * 64:(kvh + 1) * 64], False, True, [pTw, VBs_b])
                        TS("dve", onb.t[:, 0:64], bo.t[0:16, 0:64], s2(5), ALU.mult, [bo, st2], [onb])
                        TS("dve", onb.t[:, 64:128], bo.t[0:16, 0:64], s2(5), ALU.mult, [bo, st2], [onb])
                        pbO = ringP.next()
                        TR(pbO, bfv(pbO)[0:128, 0:16], onb.t[0:16, 0:128], idb.t[0:16, 0:16], [onb, idb])
                        for par in range(2):
                            CP("act", OBs.t[par * 64:(par + 1) * 64, kk * 4:(kk + 1) * 4, b * 4:(b + 1) * 4].rearrange("p (h2 two) t -> p h2 two t", two=2)[:, :, par, :],
                               bfv(pbO)[par * 64:(par + 1) * 64, 0:16].rearrange("p (h2 two t) -> p h2 two t", two=2, t=4)[:, :, par, :], [pbO], [OBs])
                for j in range(8):
                    for par in range(2):
                        pb = ringP.next()
                        for pr in range(4):
                            h8 = pr * 2 + par
                            MM(pb, pb.t[:, 0:NS], wboh.t[par * 64:(par + 1) * 64, pr, j * 128:(j + 1) * 128], OBs.t[par * 64:(par + 1) * 64, h8, :], pr == 0, pr == 3, [wboh, OBs])
                        TT("dve", hT.t[:, j, SEQ:NT], pb.t[:, 0:NS], hT.t[:, j, SEQ:NT], ALU.add, [pb, hT], [hT])
        if ci < 3:
            CP("pool", KB_c.t[0:64, :, 0:128], KB_c.t[0:64, :, 512:640], [KB_c], [KB_c])
            CP("pool", VB_c.t[:, 0, :, 64:128], VB_c.t[:, 4, :, 64:128], [VB_c], [VB_c])
    f.release(mB)
    if stop_after == "B":
        emit_output()
        f.finish()
        return nc

    ffn(1)

    emit_output()
    f.finish()
    return nc


def _rope_tab(n_rot, pos):
    inv = np.power(np.float32(THETA), (-np.arange(0, n_rot, 2, dtype=np.float32) / np.float32(n_rot)).astype(np.float32)).astype(np.float32)
    ang = (pos.astype(np.float32)[:, None] * inv[None, :]).astype(np.float32)
    return np.cos(ang.astype(np.float64)).astype(np.float32), np.sin(ang.astype(np.float64)).astype(np.float32)


def _constants():
    pos = np.concatenate([np.arange(SEQ), np.tile(PAST + np.arange(4), 4)]).astype(np.int64)
    cA, sA = _rope_tab(D_ROPE, pos)
    tabA = np.zeros((2, 96, NT), np.float32)
    tabA[0, :64] = 1.0
    for d in range(32):
        tabA[0, 64 + d] = cA[:, d % 16]
        tabA[1, 64 + d] = sA[:, d % 16]
    cB, sB = _rope_tab(16, pos)
    tabB = np.zeros((2, 16, NT), np.float32)
    for d in range(16):
        tabB[0, d] = cB[:, d % 8]
        tabB[1, d] = sB[:, d % 8]
    cf = np.zeros((128, CF_W), np.float32)
    cf[:, CF_ID:CF_ID + 128] = np.eye(128, dtype=np.float32)
    for d in range(16):
        cf[64 + d + 16, CF_P96 + 64 + d] = -1.0
        cf[64 + d, CF_P96 + 64 + d + 16] = 1.0
    for d in range(8):
        cf[d + 8, CF_P16 + d] = -1.0
        cf[d, CF_P16 + d + 8] = 1.0
    cb = np.zeros((128, CB_W), np.float32)
    cb[:, CB_ONES:CB_ONES + 128] = 1.0
    cb[0:64, CB_B96:CB_B96 + 64] = 1.0 / 64
    cb[64:96, CB_B96 + 64:CB_B96 + 96] = 1.0 / 32
    for jc in range(8):
        for p in range(128):
            hh = 2 * jc + p // 64
            cb[p, CB_BS + jc * 128 + hh * 4:CB_BS + jc * 128 + hh * 4 + 4] = 1.0 / 64
    pp = np.arange(128)[:, None]; cc = np.arange(128)[None, :]
    mD = (cc >= pp).astype(np.float32); mP = (cc < pp).astype(np.float32)
    cb[:, CB_MD:CB_MD + 512] = np.tile(mD, (1, 4))
    cb[:, CB_MP:CB_MP + 512] = np.tile(mP, (1, 4))
    maskS = np.full((64, 4, 16), NEG, np.float32)
    for hh in range(H):
        for t in range(4):
            for b in range(4):
                for t2 in range(t + 1):
                    maskS[hh * 4 + t, b, b * 4 + t2] = 0.0
    maskB = np.full((16, 4, 144), NEG, np.float32)
    for hq in range(4):
        for t in range(4):
            for b in range(4):
                maskB[hq * 4 + t, b, t + 1:128] = 0.0
                for t2 in range(t + 1):
                    maskB[hq * 4 + t, b, 128 + b * 4 + t2] = 0.0
    return dict(tabA=tabA, tabB=tabB, cf32=cf, cb32=cb, maskS=maskS.reshape(64, 64), maskB=maskB.reshape(16, 576))


def _vecs(inp):
    v = np.zeros((128, NV), np.float32)

    def colmaj(g, c0):
        n = g.shape[0] // 128
        v[:, c0:c0 + n] = g.reshape(n, 128).T
    colmaj(inp["norm_attn"][0], V_GA); colmaj(inp["norm_ffn"][0], V_GFA); colmaj(inp["g_kv_shared"], V_GKV)
    colmaj(inp["norm_attn"][1], V_GB); colmaj(inp["norm_ffn"][1], V_GFB)
    colmaj(inp["g_qc"][0], V_GQC); colmaj(inp["g_ckv"][0], V_GCKV)
    v[0:64, V_GQ96] = inp["g_qn_a"][0]; v[64:96, V_GQ96] = inp["g_qr_a"][0]
    v[0:64, V_GK96] = inp["g_kn_a"][0]; v[64:96, V_GK96] = inp["g_kr_a"][0]
    v[0:64, V_GKB] = inp["g_k_b"]; v[0:64, V_GQB] = inp["g_q_b"][0]
    sk = inp["sinks"][0]
    for kvh in range(4):
        for hq in range(4):
            v[hq * 4:hq * 4 + 4, V_SINKC + kvh] = sk[kvh * 4 + hq]
    v[:, V_SINKR:V_SINKR + H] = sk[None, :]
    return v


_PROG = {}


def kernel(_ncores=8, _stop_after=None, **inp):
    inp = {k: np.asarray(v) for k, v in inp.items()}
    key = _stop_after
    if key not in _PROG:
        _PROG[key] = build_program(_stop_after)
    nc = _PROG[key]
    consts = _constants()
    vecs = _vecs(inp)
    cache2d = np.ascontiguousarray(inp["cache_mla"][0].reshape(NPOOL * 128, D_CKV))
    shared = dict(
        cache=cache2d,
        w_a_in=np.ascontiguousarray(inp["w_a_in"][0]), w_uq=np.ascontiguousarray(inp["w_uq"][0]),
        w_uk=np.ascontiguousarray(inp["w_uk"][0]), w_uv=np.ascontiguousarray(inp["w_uv"][0]),
        w_a_out=np.ascontiguousarray(inp["w_a_out"][0]), w_kv=np.ascontiguousarray(inp["w_kv_shared"]),
        w_q_b=np.ascontiguousarray(inp["w_q_b"][0]), w_b_out=np.ascontiguousarray(inp["w_b_out"][0]),
        w_ffn_in=np.ascontiguousarray(inp["w_ffn_in"]), w_ffn_out=np.ascontiguousarray(inp["w_ffn_out"]),
        vecs=vecs, **consts)
    in_maps = []
    for c in range(_ncores):
        m = dict(shared)
        m["x_p"] = np.ascontiguousarray(inp["x_prompt"][c])
        m["x_s"] = np.ascontiguousarray(inp["x_sample"][4 * c:4 * c + 4].reshape(NS, D))
        m["ptab"] = np.ascontiguousarray(inp["page_table"][4 * c:4 * c + 4].reshape(1, 512).astype(np.int32))
        m["swk"] = np.ascontiguousarray(inp["state_win_k"][4 * c:4 * c + 4].reshape(4, 128, 256))
        m["swv"] = np.ascontiguousarray(inp["state_win_v"][4 * c:4 * c + 4].reshape(4, 128, 256))
        in_maps.append(m)
    res = run_bass_kernel_spmd(nc, in_maps, core_ids=list(range(_ncores)))
    R = res.results
    n = _ncores
    y_prompt = np.stack([R[c]["y_p"] for c in range(n)])
    y_sample = np.concatenate([R[c]["y_s"].reshape(4, 4, D) for c in range(n)])
    rows_pr = np.stack([R[c]["rows_p"] for c in range(n)])[None]
    rows_sa = np.concatenate([R[c]["rows_s"].reshape(4, 4, D_CKV) for c in range(n)])[None]
    wkp = np.stack([R[c]["wk_p"].reshape(128, 4, 64) for c in range(n)])
    wvp = np.stack([R[c]["wv_p"].reshape(128, 4, 64) for c in range(n)])
    wks = np.concatenate([R[c]["wk_s"].reshape(4, 128, 4, 64) for c in range(n)])
    wvs = np.concatenate([R[c]["wv_s"].reshape(4, 128, 4, 64) for c in range(n)])
    f32 = np.float32
    return (y_prompt.astype(f32), y_sample.astype(f32), rows_pr.astype(f32), rows_sa.astype(f32),
            wkp.astype(f32), wvp.astype(f32), wks.astype(f32), wvs.astype(f32))
```

```python
import numpy as np
import concourse.bass as bass
import concourse.mybir as mybir
from concourse.bass_utils import run_bass_kernel_spmd

F32 = mybir.dt.float32
BF16 = mybir.dt.bfloat16
I32 = mybir.dt.int32
AF = mybir.ActivationFunctionType
ALU = mybir.AluOpType
AX = mybir.AxisListType

ENGS = ["pe", "act", "dve", "pool", "sp"]

D = 1024
SEQ = 2048
NS = 16
NT = SEQ + NS
H = 16
D_NOPE, D_ROPE, D_V = 64, 32, 64
D_QC, D_C = 384, 256
D_CKV = D_C + D_ROPE
SCALE_A = float((D_NOPE + D_ROPE) ** -0.5)
N_KV, HD = 4, 64
SCALE_B = float(HD ** -0.5)
D_FF = 2816
EPS = 1e-6
THETA = 500000.0
PAST = 16384
NPOOL = 5120
NEG = -1e30
FILLER = 0
CHUNKS = [(0, 512), (512, 512), (1024, 512), (1536, 512), (2048, 16)]
NV = 69
V_GA, V_GFA, V_GKV, V_GB, V_GFB, V_GQC, V_GCKV, V_GQ96, V_GK96, V_GKB, V_GQB, V_SINKC, V_SINKR = 0, 8, 16, 24, 32, 40, 43, 45, 46, 47, 48, 49, 53
CB_ONES, CB_B96, CB_BS, CB_MD, CB_MP, CB_W = 0, 128, 224, 1248, 1760, 2272
CF_ID, CF_P96, CF_P16, CF_W = 0, 128, 224, 240


class Buf:
    __slots__ = ("name", "t", "last_w", "readers", "dsem", "dcnt", "excl")

    def __init__(self, name, t, excl=False, init=()):
        self.name = name
        self.t = t
        self.last_w = []
        self.readers = list(init)
        self.dsem = None
        self.dcnt = 0
        self.excl = excl

    def __getitem__(self, idx):
        return self.t[idx]


def _compact(evs):
    d = {}
    for k, v in evs:
        if d.get(k, 0) < v:
            d[k] = v
    return list(d.items())


class FW:
    def __init__(self, nc):
        self.nc = nc
        self.streams = {e: [] for e in ENGS}
        self.esem = {}
        self.ecnt = {e: 0 for e in ENGS}
        self.seen = {e: {} for e in ENGS}
        self.sems = {}
        self.dma_bufs = []
        self._ctx = []
        self.free_events = []
        self.free_sems = []
        for e in ["pe", "act", "dve", "pool"]:
            self.esem[e] = self._newsem("e_" + e)

    def _newsem(self, name):
        cm = self.nc.semaphore(name)
        h = cm.__enter__()
        self._ctx.append((cm, None))
        self.sems[name] = h
        return name

    def mark(self):
        return len(self._ctx)

    def release(self, mark):
        ev = list(self.free_events)
        while len(self._ctx) > mark:
            cm, b = self._ctx.pop()
            if b is not None:
                ev.extend(b.last_w)
                ev.extend(b.readers)
                if b.dsem is not None:
                    ev.append((b.dsem, b.dcnt))
                cm.__exit__(None, None, None)
            else:
                self._keep.append((cm, b))
        self.free_events = _compact(ev)

    _keep = []

    def sbuf(self, name, shape, dtype):
        self._uid = getattr(self, "_uid", 0) + 1
        name = "s%d_%s" % (self._uid, name)
        cm = self.nc.sbuf_tensor(name, list(shape), dtype)
        t = cm.__enter__()
        b = Buf(name, t, init=self.free_events)
        self._ctx.append((cm, b))
        return b

    def psum(self, name, shape, dtype=F32):
        cm = self.nc.psum_tensor(name, list(shape), dtype)
        t = cm.__enter__()
        b = Buf(name, t, excl=True)
        self._ctx.append((cm, b))
        return b

    def _deps(self, reads, writes):
        ev = []
        for b in reads:
            ev.extend(b.last_w)
            if b.excl:
                ev.extend(b.readers)
        for b in writes:
            ev.extend(b.last_w)
            ev.extend(b.readers)
        return ev

    def _waits(self, eng, ev):
        need = {}
        for (k, v) in ev:
            if need.get(k, 0) < v:
                need[k] = v
        seen = self.seen[eng]
        out = []
        for k, v in need.items():
            if seen.get(k, 0) >= v:
                continue
            seen[k] = v
            out.append((k, v))
        return out

    def _record(self, reads, writes, event, nowaw=False):
        for b in reads:
            if b.excl:
                b.last_w = [event]
                b.readers = []
            else:
                b.readers.append(event)
                if len(b.readers) > 48:
                    b.readers = _compact(b.readers)
        for b in writes:
            if nowaw:
                b.last_w.append(event)
                b.last_w = _compact(b.last_w)
            else:
                b.last_w = [event]
            b.readers = []

    def op(self, eng, fn, reads=(), writes=()):
        ev = self._deps(reads, writes)
        waits = self._waits(eng, ev)
        self.ecnt[eng] += 1
        val = self.ecnt[eng]
        semname = self.esem[eng]
        if eng == "pe":
            self.seen[eng][semname] = val
        sems = self.sems

        def thunk(e, fn=fn, waits=waits, semname=semname):
            for (k, v) in waits:
                e.wait_ge(sems[k], v)
            fn(e).then_inc(sems[semname], 1)
        self.streams[eng].append(thunk)
        self._record(reads, writes, (semname, val))

    def dma(self, q, fn, reads=(), writes=(), own=None):
        if own is None:
            own = writes[0] if writes else reads[0]
        if own.dsem is None:
            cm = self.nc.semaphore("d_" + own.name)
            h = cm.__enter__()
            self._keep.append((cm, None))
            self.sems["d_" + own.name] = h
            own.dsem = "d_" + own.name
            self.dma_bufs.append(own)
        ev = []
        for b in reads:
            ev.extend(b.last_w)
        for b in writes:
            ev.extend([e for e in b.last_w if e[0] != own.dsem])
            ev.extend(b.readers)
        waits = self._waits(q, ev)
        own.dcnt += 16
        val = own.dcnt
        semname = own.dsem
        sems = self.sems

        def thunk(e, fn=fn, waits=waits, semname=semname):
            for (k, v) in waits:
                e.wait_ge(sems[k], v)
            fn(e).then_inc(sems[semname], 16)
        self.streams[q].append(thunk)
        self._record(reads, writes, (semname, val), nowaw=True)

    def finish(self):
        finals = [(b.dsem, b.dcnt) for b in self.dma_bufs]
        sems = self.sems
        nc = self.nc
        streams = self.streams
        with nc.Block() as block:
            @block.sync
            def _(e):
                for th in streams["sp"]:
                    th(e)
                for (k, v) in finals:
                    e.wait_ge(sems[k], v)

            @block.tensor
            def _(e):
                for th in streams["pe"]:
                    th(e)

            @block.scalar
            def _(e):
                for th in streams["act"]:
                    th(e)

            @block.vector
            def _(e):
                for th in streams["dve"]:
                    th(e)

            @block.gpsimd
            def _(e):
                for th in streams["pool"]:
                    th(e)
        while self._ctx:
            cm, b = self._ctx.pop()
            cm.__exit__(None, None, None)
        for cm, b in reversed(self._keep):
            cm.__exit__(None, None, None)
        FW._keep = []


class Ring:
    def __init__(self, items):
        self.items = items
        self.i = 0

    def next(self):
        b = self.items[self.i % len(self.items)]
        self.i += 1
        return b


def build_program(stop_after=None, npool=NPOOL):
    nc = bass.Bass("TRN2", target_bir_lowering=False)
    FW._keep = []
    f = FW(nc)

    def din(name, shape, dt=F32):
        return nc.dram_tensor(name, list(shape), dt, kind="ExternalInput").ap()

    def dout(name, shape, dt=F32):
        return nc.dram_tensor(name, list(shape), dt, kind="ExternalOutput").ap()

    x_p = din("x_p", [SEQ, D]); x_s = din("x_s", [NS, D])
    cache = din("cache", [npool * 128, D_CKV])
    ptab = din("ptab", [1, 512], I32)
    swk = din("swk", [4, 128, 256]); swv = din("swv", [4, 128, 256])
    w_a_in = din("w_a_in", [D, 672]); w_uq = din("w_uq", [D_QC, 1536])
    w_uk = din("w_uk", [D_C, 1024]); w_uv = din("w_uv", [D_C, 1024])
    w_a_out = din("w_a_out", [1024, D]); w_kv = din("w_kv", [D, 512])
    w_q_b = din("w_q_b", [D, 1024]); w_b_out = din("w_b_out", [1024, D])
    w_ffn_in = din("w_ffn_in", [2, D, 2 * D_FF]); w_ffn_out = din("w_ffn_out", [2, D_FF, D])
    vecs_d = din("vecs", [128, NV]); cf_d = din("cf32", [128, CF_W]); cb_d = din("cb32", [128, CB_W])
    tabA_d = din("tabA", [2, 96, NT]); tabB_d = din("tabB", [2, 16, NT])
    maskS_d = din("maskS", [64, 4 * 16]); maskB_d = din("maskB", [16, 4 * 144])

    y_p = dout("y_p", [SEQ, D]); y_s = dout("y_s", [NS, D])
    rows_p = dout("rows_p", [SEQ, D_CKV]); rows_s = dout("rows_s", [NS, D_CKV])
    wk_p = dout("wk_p", [128, 256]); wv_p = dout("wv_p", [128, 256])
    wk_s = dout("wk_s", [4, 128, 256]); wv_s = dout("wv_s", [4, 128, 256])

    def MM(pb, out, lhsT, rhs, start, stop, rd, sgc=False):
        if sgc:
            f.op("pe", lambda e: e.matmul(out, lhsT=lhsT, rhs=rhs, start=start, stop=stop, skip_group_check=True), reads=rd, writes=[pb])
        else:
            f.op("pe", lambda e: e.matmul(out, lhsT=lhsT, rhs=rhs, start=start, stop=stop), reads=rd, writes=[pb])

    def TR(pb, out, in_, ident, rd):
        f.op("pe", lambda e: e.transpose(out=out, in_=in_, identity=ident), reads=rd, writes=[pb])

    def ACT(out, in_, func, rd, wr, **kw):
        f.op("act", lambda e: e.activation(out=out, in_=in_, func=func, **kw), reads=rd, writes=wr)

    def CP(eng, out, in_, rd, wr):
        if eng == "act":
            f.op("act", lambda e: e.copy(out=out, in_=in_), reads=rd, writes=wr)
        else:
            f.op(eng, lambda e: e.tensor_copy(out=out, in_=in_), reads=rd, writes=wr)

    def TT(eng, out, in0, in1, op, rd, wr):
        f.op(eng, lambda e: e.tensor_tensor(out=out, in0=in0, in1=in1, op=op), reads=rd, writes=wr)

    def STT(eng, out, in0, scalar, in1, op0, op1, rd, wr):
        f.op(eng, lambda e: e.scalar_tensor_tensor(out=out, in0=in0, scalar=scalar, in1=in1, op0=op0, op1=op1), reads=rd, writes=wr)

    def TS(eng, out, in0, s1, op0, rd, wr):
        f.op(eng, lambda e: e.tensor_scalar(out=out, in0=in0, scalar1=s1, scalar2=None, op0=op0), reads=rd, writes=wr)

    def MSET(eng, ap, val, wr):
        f.op(eng, lambda e: e.memset(ap, val), writes=wr)

    def LD(q, out, in_, wr, rd=()):
        f.dma(q, lambda e: e.dma_start(out=out, in_=in_), reads=list(rd), writes=list(wr))

    def ST(q, out, in_, rd):
        f.dma(q, lambda e: e.dma_start(out=out, in_=in_), reads=list(rd), writes=[])

    PS = [f.psum(f"ps{i}", [128, 512], F32) for i in range(8)]

    def bfv(pb):
        return pb.t[:, :].bitcast(BF16)

    hT = f.sbuf("hT", [128, 8, NT], F32)
    vecs = f.sbuf("vecs", [128, NV], F32)
    cf = f.sbuf("cf", [128, CF_W], F32)
    cb = f.sbuf("cb", [128, CB_W], BF16)
    idb = f.sbuf("idb", [128, 128], BF16)
    LD("sp", vecs[:, :], vecs_d[:, :], [vecs])
    LD("sp", cf[:, :], cf_d[:, :], [cf])
    LD("pool", cb[:, :], cb_d[:, :], [cb])
    CP("pool", idb[:, :], cf[:, CF_ID:CF_ID + 128], [cf], [idb])
    identf = cf.t[:, CF_ID:CF_ID + 128]
    ones_b = cb.t[:, CB_ONES:CB_ONES + 128]

    def vcol(c, p0=0, p1=128):
        return vecs.t[p0:p1, c:c + 1]

    def rstd_chain(pb, M, N, rs, scale):
        ACT(rs.t[0:M, 0:N], pb.t[0:M, 0:N], AF.Ln, [pb], [rs], scale=scale, bias=EPS)
        ACT(rs.t[0:M, 0:N], rs.t[0:M, 0:N], AF.Exp, [rs], [rs], scale=-0.5)

    def load_w(dst, src2d, kchunks, c0, ncols, prows=128):
        for j in range(kchunks):
            LD("pool", dst.t[0:prows, j, 0:ncols], src2d[j * prows:(j + 1) * prows, c0:c0 + ncols], [dst])

    def emit_output():
        ys_r = Ring([f.sbuf(f"ys{i}", [128, D], F32) for i in range(2)])
        for i in range(17):
            ys = ys_r.next()
            rows = 128 if i < 16 else NS
            for half in range(2):
                pb = PS[(2 * i + half) % 8]
                for q in range(4):
                    j = half * 4 + q
                    TR(pb, pb.t[0:rows, q * 128:(q + 1) * 128], hT.t[:, j, i * 128:i * 128 + rows], identf, [hT, cf])
                CP("act" if half == 0 else "dve", ys.t[0:rows, half * 512:(half + 1) * 512], pb.t[0:rows, 0:512], [pb], [ys])
            if i < 16:
                ST("sp", y_p[i * 128:(i + 1) * 128, :], ys.t[0:rows, :], [ys])
            else:
                ST("sp", y_s[:, :], ys.t[0:rows, :], [ys])


    m0 = f.mark()
    xs_ring = Ring([f.sbuf(f"xs{i}", [128, D], F32) for i in range(2)])
    for i in range(17):
        xs = xs_ring.next()
        rows = 128 if i < 16 else NS
        src = x_p[i * 128:(i + 1) * 128, :] if i < 16 else x_s[:, :]
        LD("sp", xs.t[0:rows, :], src, [xs])
        for half in range(2):
            pb = PS[(2 * i + half) % 8]
            for q in range(4):
                j = half * 4 + q
                TR(pb, pb.t[:, q * 128:q * 128 + rows], xs.t[0:rows, j * 128:(j + 1) * 128], identf[0:rows, 0:rows], [xs, cf])
            srcv = pb.t[:, :].rearrange("p (q c) -> p q c", q=4)[:, :, 0:rows]
            CP("act" if half == 0 else "dve", hT.t[:, half * 4:half * 4 + 4, i * 128:i * 128 + rows], srcv, [pb], [hT])
    f.release(m0)

    mA = f.mark()
    cqT = f.sbuf("cqT", [128, 3, NT], BF16)
    ckvT = f.sbuf("ckvT", [128, 2, NT], BF16)
    kpe_b = f.sbuf("kpe_b", [96, NT], BF16)
    rown_b = f.sbuf("rown_b", [NS, 256], BF16)
    qs_all = f.sbuf("qs_all", [96, H, NS], BF16)

    mA1 = f.mark()
    wain = f.sbuf("wain", [128, 8, 672], BF16)
    load_w(wain, w_a_in, 8, 0, 672)
    sq8 = f.sbuf("sq8", [128, 8, 512], BF16)
    xn = f.sbuf("xn", [128, 8, 512], BF16)
    rs_r = Ring([f.sbuf(f"rsA{i}", [128, 512], F32) for i in range(2)])
    ckv_f = f.sbuf("ckv_f", [128, 2, 512], F32)
    kpn = f.sbuf("kpn", [96, 512], F32)
    kpe_f = f.sbuf("kpe_f", [96, 512], F32)
    t1 = f.sbuf("t1A", [96, 512], F32)
    tabc = Ring([f.sbuf(f"tabcA{i}", [96, 2, 512], F32) for i in range(2)])
    rstage = Ring([f.sbuf(f"rstage{i}", [128, D_CKV], F32) for i in range(2)])
    for (c0, N) in CHUNKS:
        tb = tabc.next()
        LD("sp", tb.t[64:96, 0, 0:N], tabA_d[0, 64:96, c0:c0 + N], [tb])
        LD("sp", tb.t[64:96, 1, 0:N], tabA_d[1, 64:96, c0:c0 + N], [tb])
        ACT(sq8.t[:, :, 0:N], hT.t[:, :, c0:c0 + N], AF.Square, [hT], [sq8])
        pss = PS[6]
        for j in range(8):
            MM(pss, pss.t[:, 0:N], ones_b, sq8.t[:, j, 0:N], j == 0, j == 7, [cb, sq8])
        rs = rs_r.next()
        rstd_chain(pss, 128, N, rs, 1.0 / D)
        for j in range(8):
            STT("dve", xn.t[:, j, 0:N], hT.t[:, j, c0:c0 + N], vcol(V_GA + j), rs.t[:, 0:N], ALU.mult, ALU.mult, [hT, vecs, rs], [xn])
        mts = [(0, 128), (128, 128), (256, 128), (384, 128), (512, 128), (576, 96)]
        for mi, (mc, M) in enumerate(mts):
            pb = PS[mi]
            for j in range(8):
                MM(pb, pb.t[0:M, 0:N], wain.t[:, j, mc:mc + M], xn.t[:, j, 0:N], j == 0, j == 7, [wain, xn])
        for m in range(3):
            ACT(sq8.t[:, m, 0:N], PS[m].t[:, 0:N], AF.Square, [PS[m]], [sq8])
        for m in range(3):
            MM(pss, pss.t[:, 0:N], ones_b, sq8.t[:, m, 0:N], m == 0, m == 2, [cb, sq8])
        rs = rs_r.next()
        rstd_chain(pss, 128, N, rs, 1.0 / D_QC)
        for m in range(3):
            STT("dve", cqT.t[:, m, c0:c0 + N], PS[m].t[:, 0:N], vcol(V_GQC + m), rs.t[:, 0:N], ALU.mult, ALU.mult, [PS[m], vecs, rs], [cqT])
        for m in range(2):
            ACT(sq8.t[:, 3 + m, 0:N], PS[3 + m].t[:, 0:N], AF.Square, [PS[3 + m]], [sq8])
        for m in range(2):
            MM(pss, pss.t[:, 0:N], ones_b, sq8.t[:, 3 + m, 0:N], m == 0, m == 1, [cb, sq8])
        rs = rs_r.next()
        rstd_chain(pss, 128, N, rs, 1.0 / D_C)
        for m in range(2):
            STT("dve", ckv_f.t[:, m, 0:N], PS[3 + m].t[:, 0:N], vcol(V_GCKV + m), rs.t[:, 0:N], ALU.mult, ALU.mult, [PS[3 + m], vecs, rs], [ckv_f])
        CP("pool", ckvT.t[:, :, c0:c0 + N], ckv_f.t[:, :, 0:N], [ckv_f], [ckvT])
        ACT(sq8.t[0:96, 5, 0:N], PS[5].t[0:96, 0:N], AF.Square, [PS[5]], [sq8])
        MM(pss, pss.t[0:96, 0:N], cb.t[0:96, CB_B96:CB_B96 + 96], sq8.t[0:96, 5, 0:N], True, True, [cb, sq8])
        rs = rs_r.next()
        rstd_chain(pss, 96, N, rs, 1.0)
        STT("dve", kpn.t[0:96, 0:N], PS[5].t[0:96, 0:N], vcol(V_GK96, 0, 96), rs.t[0:96, 0:N], ALU.mult, ALU.mult, [PS[5], vecs, rs], [kpn])
        pr = PS[7]
        MM(pr, pr.t[0:96, 0:N], cf.t[0:96, CF_P96:CF_P96 + 96], kpn.t[0:96, 0:N], True, True, [cf, kpn])
        TT("pool", t1.t[64:96, 0:N], kpn.t[64:96, 0:N], tb.t[64:96, 0, 0:N], ALU.mult, [kpn, tb], [t1])
        TT("dve", kpe_f.t[64:96, 0:N], pr.t[64:96, 0:N], tb.t[64:96, 1, 0:N], ALU.mult, [pr, tb], [kpe_f])
        TT("dve", kpe_f.t[64:96, 0:N], kpe_f.t[64:96, 0:N], t1.t[64:96, 0:N], ALU.add, [kpe_f, t1], [kpe_f])
        CP("pool", kpe_b.t[64:96, c0:c0 + N], kpe_f.t[64:96, 0:N], [kpe_f], [kpe_b])
        ntile = (N + 127) // 128
        for ti in range(ntile):
            r = min(128, N - ti * 128)
            pbT = PS[7]
            for m in range(2):
                TR(pbT, pbT.t[0:r, m * 128:(m + 1) * 128], ckv_f.t[:, m, ti * 128:ti * 128 + r], identf, [ckv_f, cf])
            TR(pbT, pbT.t[0:r, 256:288], kpe_f.t[64:96, ti * 128:ti * 128 + r], cf.t[64:96, CF_ID + 64:CF_ID + 96], [kpe_f, cf])
            stg = rstage.next()
            CP("act", stg.t[0:r, :], pbT.t[0:r, 0:D_CKV], [pbT], [stg])
            if c0 < SEQ:
                ST("sp", rows_p[c0 + ti * 128:c0 + ti * 128 + r, :], stg.t[0:r, :], [stg])
            else:
                ST("sp", rows_s[:, :], stg.t[0:r, :], [stg])
                CP("pool", rown_b.t[0:NS, :], stg.t[0:NS, 0:256], [stg], [rown_b])
    f.release(mA1)
    if stop_after == "A1":
        f.release(mA)
        f.finish()
        return nc

    mA2 = f.mark()
    G = 2
    NG = H // G
    wq_r = Ring([f.sbuf(f"wq_g{i}", [128, 3, G * 96], BF16) for i in range(2)])
    wk_r = Ring([f.sbuf(f"wk_g{i}", [128, 2, G * 64], BF16) for i in range(2)])
    wv_r = Ring([f.sbuf(f"wv_g{i}", [128, 2, G * 64], BF16) for i in range(2)])
    wo_r = Ring([f.sbuf(f"wo_g{i}", [128, D], BF16) for i in range(2)])
    qT_g = f.sbuf("qT_g", [128, G, NT], BF16)
    KT_g = f.sbuf("KT_g", [128, G, NT], BF16)
    V_g = f.sbuf("V_g", [128, 16, G, 128], BF16)
    OT_r = Ring([f.sbuf(f"OT_g{i}", [128, SEQ], BF16) for i in range(2)])
    sq_r = Ring([f.sbuf(f"sqB{i}", [128, 512], BF16) for i in range(4)])
    rs_r = Ring([f.sbuf(f"rsB{i}", [96, 512], F32) for i in range(4)])
    qn_r = Ring([f.sbuf(f"qnB{i}", [96, 512], F32) for i in range(2)])
    t1_r = Ring([f.sbuf(f"t1B{i}", [96, 512], F32) for i in range(2)])
    t2_r = Ring([f.sbuf(f"t2B{i}", [96, 512], F32) for i in range(2)])
    pT_r = Ring([f.sbuf(f"pT{i}", [128, 512], BF16) for i in range(4)])
    rl_r = Ring([f.sbuf(f"rl{i}", [128, 512], F32) for i in range(2)])
    tabc = Ring([f.sbuf(f"tabcB{i}", [96, 2, 512], F32) for i in range(2)])
    MSET("pool", qT_g.t[:, :, :], 0.0, [qT_g])
    MSET("pool", KT_g.t[:, :, :], 0.0, [KT_g])
    MSET("pool", V_g.t[:, :, 0, 64:128], 1.0, [V_g])
    MSET("pool", V_g.t[:, :, 1, 0:64], 1.0, [V_g])
    ringP = Ring(PS[5:8])
    ringS = Ring(PS[0:3])
    ringO = Ring(PS[3:5])
    ringA = Ring(PS[0:8])
    maskD = cb.t[:, CB_MD:CB_MD + 128]
    B96 = cb.t[0:96, CB_B96:CB_B96 + 96]
    B64 = cb.t[0:64, CB_B96:CB_B96 + 64]
    P96 = cf.t[0:96, CF_P96:CF_P96 + 96]
    for g in range(NG):
        wq = wq_r.next(); wk = wk_r.next(); wv = wv_r.next(); wo = wo_r.next()
        OT_g = OT_r.next()
        if g % 2 == 0:
            pend = []
        pend.append((wo, OT_g))
        load_w(wq, w_uq, 3, g * G * 96, G * 96)
        load_w(wk, w_uk, 2, g * G * 64, G * 64)
        load_w(wv, w_uv, 2, g * G * 64, G * 64)
        LD("pool", wo.t[:, :], w_a_out[g * 128:(g + 1) * 128, :], [wo])
        def A_S1(ci):
            c0, N = CHUNKS[ci]
            A = PS[0:4] if ci % 2 == 0 else PS[4:8]
            tb = tabc.next()
            LD("sp", tb.t[64:96, 0, 0:N], tabA_d[0, 64:96, c0:c0 + N], [tb])
            LD("sp", tb.t[64:96, 1, 0:N], tabA_d[1, 64:96, c0:c0 + N], [tb])
            chs = []
            for hl in range(G):
                bq = A[hl]
                for m in range(3):
                    MM(bq, bq.t[0:96, 0:N], wq.t[:, m, hl * 96:(hl + 1) * 96], cqT.t[:, m, c0:c0 + N], m == 0, m == 2, [wq, cqT])
                chs.append(dict(q=True, hl=hl, b=bq, M=96, tb=tb))
            for hl in range(G):
                bk = A[2 + hl]
                for m in range(2):
                    MM(bk, bk.t[0:64, 0:N], wk.t[:, m, hl * 64:(hl + 1) * 64], ckvT.t[:, m, c0:c0 + N], m == 0, m == 1, [wk, ckvT])
                chs.append(dict(q=False, hl=hl, b=bk, M=64, tb=tb))
            return chs

        def A_mid(ci, chs):
            c0, N = CHUNKS[ci]
            Bset = PS[4:8] if ci % 2 == 0 else PS[0:4]
            for c in chs:
                M = c["M"]
                c["sq"] = sq_r.next()
                ACT(c["sq"].t[0:M, 0:N], c["b"].t[0:M, 0:N], AF.Square, [c["b"]], [c["sq"]])
            for ic, c in enumerate(chs):
                M = c["M"]
                c["bs"] = Bset[ic]
                MM(c["bs"], c["bs"].t[0:M, 0:N], cb.t[0:M, CB_B96:CB_B96 + M], c["sq"].t[0:M, 0:N], True, True, [cb, c["sq"]])
            for c in chs:
                c["rs"] = rs_r.next()
                rstd_chain(c["bs"], c["M"], N, c["rs"], 1.0)
            for c in chs:
                hl = c["hl"]
                if c["q"]:
                    c["qn"] = qn_r.next()
                    STT("dve", c["qn"].t[0:96, 0:N], c["b"].t[0:96, 0:N], vcol(V_GQ96, 0, 96), c["rs"].t[0:96, 0:N], ALU.mult, ALU.mult, [c["b"], vecs, c["rs"]], [c["qn"]])
                    STT("dve", qT_g.t[0:64, hl, c0:c0 + N], c["b"].t[0:64, 0:N], vcol(V_GQ96, 0, 64), c["rs"].t[0:64, 0:N], ALU.mult, ALU.mult, [c["b"], vecs, c["rs"]], [qT_g])
                else:
                    STT("dve", KT_g.t[0:64, hl, c0:c0 + N], c["b"].t[0:64, 0:N], vcol(V_GK96, 0, 64), c["rs"].t[0:64, 0:N], ALU.mult, ALU.mult, [c["b"], vecs, c["rs"]], [KT_g])
                    CP("pool", KT_g.t[64:96, hl, c0:c0 + N], kpe_b.t[64:96, c0:c0 + N], [kpe_b], [KT_g])

        def A_tail(ci, chs):
            c0, N = CHUNKS[ci]
            A = PS[0:4] if ci % 2 == 0 else PS[4:8]
            for c in chs:
                if c["q"]:
                    c["br"] = A[c["hl"]]
                    MM(c["br"], c["br"].t[0:96, 0:N], P96, c["qn"].t[0:96, 0:N], True, True, [cf, c["qn"]])
            if c0 < SEQ:
                for ti in range(ci * 4, ci * 4 + 4):
                    bv = A[2 + ti % 2]
                    for m in range(2):
                        MM(bv, bv.t[:, 0:G * 64], ckvT.t[:, m, ti * 128:(ti + 1) * 128], wv.t[:, m, 0:G * 64], m == 0, m == 1, [ckvT, wv])
                    CP("dve" if ti % 2 else "act", V_g.t[:, ti, 0, 0:64], bv.t[:, 0:64], [bv], [V_g])
                    CP("act" if ti % 2 else "dve", V_g.t[:, ti, 1, 64:128], bv.t[:, 64:128], [bv], [V_g])
            for c in chs:
                if c["q"]:
                    hl = c["hl"]; qn = c["qn"]; br = c["br"]; tb = c["tb"]
                    t1 = t1_r.next(); t2 = t2_r.next()
                    TT("pool", t1.t[64:96, 0:N], qn.t[64:96, 0:N], tb.t[64:96, 0, 0:N], ALU.mult, [qn, tb], [t1])
                    TT("dve", t2.t[64:96, 0:N], br.t[64:96, 0:N], tb.t[64:96, 1, 0:N], ALU.mult, [br, tb], [t2])
                    TT("dve", qT_g.t[64:96, hl, c0:c0 + N], t1.t[64:96, 0:N], t2.t[64:96, 0:N], ALU.add, [t1, t2], [qT_g])
            if c0 >= SEQ:
                for hl in range(G):
                    CP("pool", qs_all.t[0:96, g * G + hl, :], qT_g.t[0:96, hl, SEQ:NT], [qT_g], [qs_all])

        chs_next = A_S1(0)
        for ci in range(len(CHUNKS)):
            chs_cur = chs_next
            A_mid(ci, chs_cur)
            if ci + 1 < len(CHUNKS):
                chs_next = A_S1(ci + 1)
            A_tail(ci, chs_cur)
        iters = []
        for hl in range(G):
            for qc in range(4):
                nkt = 4 * qc + 4
                for kt in range(nkt):
                    iters.append(dict(hl=hl, qc=qc, kt=kt, nkt=nkt))

        def emitS(it):
            hl, qc, kt = it["hl"], it["qc"], it["kt"]
            n0 = max(qc * 512, kt * 128)
            W = qc * 512 + 512 - n0
            bs_ = ringS.next()
            MM(bs_, bs_.t[:, 0:W], KT_g.t[:, hl, kt * 128:(kt + 1) * 128], qT_g.t[:, hl, n0:n0 + W], True, True, [KT_g, qT_g])
            pT = pT_r.next()
            ACT(pT.t[:, 0:W], bs_.t[:, 0:W], AF.Exp, [bs_], [pT], scale=SCALE_A)
            if kt * 128 >= qc * 512:
                TT("pool", pT.t[:, 0:128], pT.t[:, 0:128], maskD, ALU.mult, [pT, cb], [pT])
            it["pT"] = pT; it["W"] = W; it["o0"] = n0 - qc * 512

        cur = {}

        def emitPV(it):
            hl, qc, kt, nkt = it["hl"], it["qc"], it["kt"], it["nkt"]
            if kt == 0:
                cur["bo"] = ringO.next()
            bo = cur["bo"]
            MM(bo, bo.t[:, it["o0"]:512], V_g.t[:, kt, hl, :], it["pT"].t[:, 0:it["W"]], kt == 0, kt == nkt - 1, [V_g, it["pT"]])
            if kt == nkt - 1:
                rl = rl_r.next()
                oL, oH = (0, 64) if hl == 0 else (64, 128)
                lL, lH = (64, 128) if hl == 0 else (0, 64)
                f.op("dve", lambda e, rl=rl, bo=bo, lL=lL, lH=lH: e.reciprocal(out=rl.t[lL:lH, :], in_=bo.t[lL:lH, :]), reads=[bo], writes=[rl])
                TT("dve", OT_g.t[oL:oH, qc * 512:(qc + 1) * 512], bo.t[oL:oH, :], rl.t[lL:lH, :], ALU.mult, [bo, rl], [OT_g])
        DEPTH = 2
        for idx in range(len(iters) + DEPTH):
            if idx < len(iters):
                emitS(iters[idx])
            if FILLER:
                MM(PS[7], PS[7].t[:, 0:FILLER], KT_g.t[0:96, 0, 0:128], qT_g.t[0:96, 0, 0:FILLER], True, True, [KT_g, qT_g])
            if idx - DEPTH >= 0:
                emitPV(iters[idx - DEPTH])
        if g % 2 == 1:
            for qc in range(4):
                c0 = qc * 512
                for j in range(8):
                    pb = ringP.next()
                    for ip, (wo_, OT_) in enumerate(pend):
                        MM(pb, pb.t[:, 0:512], wo_.t[:, j * 128:(j + 1) * 128], OT_.t[:, c0:c0 + 512], ip == 0, ip == len(pend) - 1, [wo_, OT_])
                    TT("dve", hT.t[:, j, c0:c0 + 512], pb.t[:, 0:512], hT.t[:, j, c0:c0 + 512], ALU.add, [pb, hT], [hT])
    f.release(mA2)
    if stop_after == "A2":
        f.release(mA)
        emit_output()
        f.finish()
        return nc

    mA3 = f.mark()
    wuk = f.sbuf("wuk", [128, 2, 1024], BF16)
    wuv = f.sbuf("wuv", [128, 2, 1024], BF16)
    load_w(wuk, w_uk, 2, 0, 1024)
    load_w(wuv, w_uv, 2, 0, 1024)
    OTs = f.sbuf("OTs", [64, H, NS], BF16)
    wukT = f.sbuf("wukT", [64, H, 256], BF16)
    qabsT = f.sbuf("qabsT", [128, 2, 4, 128], BF16)
    qpeT = f.sbuf("qpeT", [96, 4, 128], BF16)
    MSET("pool", qabsT.t[:, :, :, :], 0.0, [qabsT])
    MSET("pool", qpeT.t[:, :, :], 0.0, [qpeT])
    maskS = f.sbuf("maskS", [64, 4, 16], F32)
    LD("sp", maskS.t[:, :, :], maskS_d.rearrange("p (b k) -> p b k", b=4), [maskS])
    pti = f.sbuf("pti", [128, 512], I32)
    ptf = f.sbuf("ptf", [128, 512], F32)
    iot = f.sbuf("iot", [128, 1], I32)
    iof = f.sbuf("iof", [128, 1], F32)
    ridx = f.sbuf("ridx", [128, 512], I32)
    LD("sp", pti.t[:, :], ptab[0:1, :].partition_broadcast(128), [pti])
    f.op("pool", lambda e: e.iota(iot.t[:, :], pattern=[[0, 1]], base=0, channel_multiplier=1), writes=[iot])
    CP("dve", iof.t[:, :], iot.t[:, :], [iot], [iof])
    CP("dve", ptf.t[:, :], pti.t[:, :], [pti], [ptf])
    f.op("dve", lambda e: e.tensor_scalar(out=ptf.t[:, :], in0=ptf.t[:, :], scalar1=128.0, scalar2=iof.t[:, 0:1], op0=ALU.mult, op1=ALU.add), reads=[ptf, iof], writes=[ptf])
    CP("dve", ridx.t[:, :], ptf.t[:, :], [ptf], [ridx])
    for hb in range(4):
        pb = PS[hb]
        for hq in range(4):
            for m in range(2):
                slot = hq * 2 + m
                TR(pb, bfv(pb)[0:64, slot * 128:(slot + 1) * 128], wuk.t[:, m, (hb * 4 + hq) * 64:(hb * 4 + hq + 1) * 64], idb.t[:, :], [wuk, idb])
        CP("dve" if hb % 2 else "act", wukT.t[0:64, hb * 4:(hb + 1) * 4, :], bfv(pb)[0:64, 0:1024].rearrange("p (h l) -> p h l", h=4), [pb], [wukT])
    qsg = f.sbuf("qsg", [64, H, NS], BF16)
    TS("dve", qsg.t[0:64, :, :], qs_all.t[0:64, :, :], vcol(V_GK96, 0, 64), ALU.mult, [qs_all, vecs], [qsg])
    pb = PS[4]
    for m in range(2):
        for hh in range(H):
            col = (m * H + hh) * NS
            MM(pb, pb.t[:, col:col + NS], wukT.t[0:64, hh, m * 128:(m + 1) * 128], qsg.t[0:64, hh, :], True, True, [wukT, qsg])
    for m in range(2):
        CP("dve", qabsT.t[:, m, :, 0:64].rearrange("p b (h t) -> p b h t", h=H),
           pb.t[:, m * 256:(m + 1) * 256].rearrange("p (h b t) -> p b h t", h=H, b=4), [pb], [qabsT])
    CP("pool", qpeT.t[64:96, :, 0:64].rearrange("p b (h t) -> p b h t", h=H),
       qs_all.t[64:96, :, :].rearrange("p h (b t) -> p b h t", b=4), [qs_all], [qpeT])
    BS = cb.t[:, CB_BS:CB_BS + 1024].rearrange("p (j c) -> p j c", j=8)
    mA3b = f.mark()
    rowsb_r = Ring([f.sbuf(f"rowsb{i}", [128, 4, D_CKV], BF16) for i in range(8)])
    cT_r = Ring([f.sbuf(f"cT{i}", [128, 2, 512], BF16) for i in range(3)])
    kpT_r = Ring([f.sbuf(f"kpT{i}", [96, 512], BF16) for i in range(3)])
    sqk_r = Ring([f.sbuf(f"sqk{i}", [128, 512], BF16) for i in range(3)])
    rs_r3 = Ring([f.sbuf(f"rsS{i}", [64, 512], F32) for i in range(2)])
    sc_r3 = Ring([f.sbuf(f"scS{i}", [64, 512], F32) for i in range(2)])
    pS_r3 = Ring([f.sbuf(f"pS{i}", [64, 512], BF16) for i in range(2)])
    pTs_r3 = Ring([f.sbuf(f"pTs{i}", [128, 4, 64], BF16) for i in range(2)])
    tmp_r3 = Ring([f.sbuf(f"tmpS{i}", [64, 8], F32) for i in range(2)])
    m_run = f.sbuf("m_run", [64, 1], F32)
    l_run = f.sbuf("l_run", [64, 2], F32)
    acc = f.sbuf("accS", [64, 256], F32)
    accn = f.sbuf("accn", [64, 256], BF16)
    olT = f.sbuf("olT", [128, 2, 64], BF16)
    bX, bY, bK0, bK1, bSS, bN, bP, bT2 = PS
    ringK = Ring([bK0, bK1])

    units = []
    for b in range(4):
        for gi in range(32):
            units.append(dict(b=b, gi=gi, N=512, first=(gi == 0), last=False))
        units.append(dict(b=b, gi=None, N=NS, first=False, last=True))

    def stageG(u):
        if u["gi"] is None:
            return
        rb = rowsb_r.next()
        u["rb"] = rb
        for pg in range(4):
            col = u["b"] * 128 + u["gi"] * 4 + pg
            f.dma("pool", lambda e, rb=rb, pg=pg, col=col: e.indirect_dma_start(
                out=rb.t[:, pg, :], out_offset=None, in_=cache[:, :],
                in_offset=bass.IndirectOffsetOnAxis(ap=ridx.t[:, col:col + 1], axis=0)), reads=[ridx], writes=[rb])

    def stageA_tr(u, pg):
        if u["gi"] is None:
            return
        rb = u["rb"]
        for m in range(2):
            slot = m * 4 + pg
            TR(bX, bfv(bX)[:, slot * 128:(slot + 1) * 128], rb.t[:, pg, m * 128:(m + 1) * 128], idb.t[:, :], [rb, idb])
        TR(bY, bfv(bY)[0:96, pg * 128:(pg + 1) * 128], rb.t[:, pg, 192:288], idb.t[:, :], [rb, idb])

    def stageA_cp(u):
        if u["gi"] is None:
            u["cT"] = lambda m: ckvT.t[:, m, SEQ:NT]
            u["kpT"] = kpe_b.t[64:96, SEQ:NT]
            u["nat"] = [(rown_b.t[0:NS, :], NS, 0)]
            u["cbufs"] = [ckvT, kpe_b, rown_b]
            u["mask"] = maskS.t[:, u["b"], :]
            return
        rb = u["rb"]
        cT = cT_r.next(); kpT = kpT_r.next()
        CP("dve", cT.t[:, :, :].rearrange("p m n -> p (m n)"), bfv(bX)[:, 0:1024], [bX], [cT])
        CP("dve", kpT.t[64:96, :], bfv(bY)[64:96, 0:512], [bY], [kpT])
        u["cT"] = lambda m, cT=cT: cT.t[:, m, :]
        u["kpT"] = kpT.t[64:96, :]
        u["nat"] = [(rb.t[:, pg, 0:256], 128, pg * 128) for pg in range(4)]
        u["cbufs"] = [cT, kpT, rb]
        u["mask"] = None

    def stageA(u):
        for pg in range(4):
            stageA_tr(u, pg)
        stageA_cp(u)

    def stageB_step(u, k):
        N = u["N"]; cT_ap = u["cT"]; cbufs = u["cbufs"]; b = u["b"]

        def kr(jc):
            bk = ringK.next()
            for m in range(2):
                MM(bk, bk.t[:, 0:N], wuk.t[:, m, jc * 128:(jc + 1) * 128], cT_ap(m), m == 0, m == 1, [wuk] + cbufs)
            sqk = sqk_r.next()
            ACT(sqk.t[:, 0:N], bk.t[:, 0:N], AF.Square, [bk], [sqk])
            u.setdefault("sq", {})[jc] = sqk

        def ss(jc):
            sqk = u["sq"][jc]
            MM(bSS, bSS.t[0:128, 0:N], BS[:, jc, :], sqk.t[:, 0:N], jc == 0, jc == 7, [cb, sqk])
        if k == 0:
            kr(0); kr(1)
        elif k < 7:
            ss(k - 1); kr(k + 1)
            if k == 1:
                for m in range(2):
                    MM(bN, bN.t[0:128, 0:N], qabsT.t[:, m, b, :], cT_ap(m), m == 0, m == 1, [qabsT] + cbufs)
                MM(bP, bP.t[0:128, 0:N], qpeT.t[64:96, b, :], u["kpT"], True, True, [qpeT] + cbufs)
        else:
            ss(6); ss(7)

    def stageC1(u):
        N = u["N"]
        rs = rs_r3.next(); sc = sc_r3.next()
        rstd_chain(bSS, 64, N, rs, 1.0)
        TT("dve", sc.t[0:64, 0:N], bN.t[0:64, 0:N], rs.t[0:64, 0:N], ALU.mult, [bN, rs], [sc])
        TT("dve", sc.t[0:64, 0:N], bP.t[0:64, 0:N], sc.t[0:64, 0:N], ALU.add, [bP, sc], [sc])
        if u["mask"] is not None:
            TT("dve", sc.t[0:64, 0:N], sc.t[0:64, 0:N], u["mask"], ALU.add, [sc, maskS], [sc])
        u["sc"] = sc

    def stageC2a(u):
        N = u["N"]; sc = u["sc"]; nat = u["nat"]; cbufs = u["cbufs"]; b = u["b"]
        if u["first"]:
            MSET("pool", m_run.t[:, :], NEG, [m_run])
            MSET("pool", l_run.t[:, 0:1], 0.0, [l_run])
            MSET("pool", acc.t[:, :], 0.0, [acc])
        tmp = tmp_r3.next(); pS = pS_r3.next(); pTs = pTs_r3.next()

        def tc(i):
            return tmp.t[0:64, i:i + 1]
        MSET("dve", tc(4), 0.0, [tmp])
        f.op("dve", lambda e: e.tensor_reduce(out=tc(0), in_=sc.t[0:64, 0:N], axis=AX.X, op=ALU.max), reads=[sc], writes=[tmp])
        TT("dve", tc(1), m_run.t[:, 0:1], tc(0), ALU.max, [m_run, tmp], [tmp])
        TS("dve", tc(2), tc(1), -SCALE_A, ALU.mult, [tmp], [tmp])
        u["tmp"] = tmp; u["pS"] = pS; u["pTs"] = pTs

    def stageC2a2(u):
        N = u["N"]; sc = u["sc"]
        tmp = u["tmp"]; pS = u["pS"]

        def tc(i):
            return tmp.t[0:64, i:i + 1]
        ACT(tc(3), m_run.t[:, 0:1], AF.Exp, [m_run, tmp], [tmp], scale=SCALE_A, bias=tc(2))
        ACT(pS.t[0:64, 0:N], sc.t[0:64, 0:N], AF.Exp, [sc, tmp], [pS, tmp], scale=SCALE_A, bias=tc(2), accum_out=tc(4))
        STT("dve", l_run.t[:, 0:1], l_run.t[:, 0:1], tc(3), tc(4), ALU.mult, ALU.add, [l_run, tmp], [l_run])
        CP("dve", m_run.t[:, 0:1], tc(1), [tmp], [m_run])

    def stageC2b(u):
        N = u["N"]; nat = u["nat"]; cbufs = u["cbufs"]; b = u["b"]
        tmp = u["tmp"]; pS = u["pS"]; pTs = u["pTs"]

        def tc(i):
            return tmp.t[0:64, i:i + 1]
        npg = len(nat)
        for pg, (rows_ap, K, col0) in enumerate(nat):
            TR(bT2, bfv(bT2)[0:K, pg * 64:(pg + 1) * 64], pS.t[0:64, col0:col0 + K], idb.t[0:64, 0:64], [pS, idb])
        Kmax = max(K for (_, K, _) in nat)
        CP("dve", pTs.t[0:Kmax, 0:npg, :], bfv(bT2)[0:Kmax, 0:npg * 64].rearrange("p (g c) -> p g c", g=npg), [bT2], [pTs])

    def stageC2c(u):
        N = u["N"]; nat = u["nat"]; cbufs = u["cbufs"]; b = u["b"]
        tmp = u["tmp"]; pS = u["pS"]; pTs = u["pTs"]

        def tc(i):
            return tmp.t[0:64, i:i + 1]
        npg = len(nat)
        for pg, (rows_ap, K, col0) in enumerate(nat):
            MM(bT2, bT2.t[0:64, 256:512], pTs.t[0:K, pg, :], rows_ap, pg == 0, pg == npg - 1, [pTs] + cbufs)
        STT("dve", acc.t[:, :], acc.t[:, :], tc(3), bT2.t[0:64, 256:512], ALU.mult, ALU.add, [acc, tmp, bT2], [acc])
        if u["last"]:
            f.op("dve", lambda e: e.reciprocal(out=l_run.t[:, 1:2], in_=l_run.t[:, 0:1]), reads=[l_run], writes=[l_run])
            TS("dve", accn.t[:, :], acc.t[:, :], l_run.t[:, 1:2], ALU.mult, [acc, l_run], [accn])
            for m in range(2):
                TR(bT2, bfv(bT2)[:, m * 64:(m + 1) * 64], accn.t[0:64, m * 128:(m + 1) * 128], idb.t[0:64, 0:64], [accn, idb])
            CP("act", olT.t[:, :, :], bfv(bT2)[:, 0:128].rearrange("p (m c) -> p m c", m=2), [bT2], [olT])
            for hh in range(H):
                for m in range(2):
                    MM(bT2, bT2.t[0:64, 256 + hh * 4:256 + (hh + 1) * 4], wuv.t[:, m, hh * 64:(hh + 1) * 64], olT.t[:, m, hh * 4:(hh + 1) * 4], m == 0, m == 1, [wuv, olT])
            CP("dve", OTs.t[0:64, :, b * 4:(b + 1) * 4], bT2.t[0:64, 256:320].rearrange("p (h t) -> p h t", h=H), [bT2], [OTs])

    nu = len(units)
    for i in range(min(4, nu)):
        stageG(units[i])
    stageA(units[0]); stageA(units[1])
    for k in range(8):
        stageB_step(units[0], k)
    stageC1(units[0])
    for i in range(nu):
        u0 = units[i]
        u1 = units[i + 1] if i + 1 < nu else None
        u2 = units[i + 2] if i + 2 < nu else None
        if i + 4 < nu:
            stageG(units[i + 4])
        for k in range(8):
            if u1 is not None:
                stageB_step(u1, k)
            if k < 4 and u2 is not None:
                stageA_tr(u2, k)
            if k == 0:
                stageC2a(u0)
            if k == 3 and u2 is not None:
                stageA_cp(u2)
            if k == 4:
                stageC2a2(u0)
            if k == 6:
                stageC2b(u0)
            if k == 7:
                stageC2c(u0)
        if u1 is not None:
            stageC1(u1)
    f.release(mA3b)
    wao = f.sbuf("wao", [64, H, D], BF16)
    for hh in range(H):
        LD("pool", wao.t[0:64, hh, :], w_a_out[hh * 64:(hh + 1) * 64, :], [wao])
    for j in range(8):
        pb = ringK.next()
        for hh in range(H):
            MM(pb, pb.t[:, 0:NS], wao.t[0:64, hh, j * 128:(j + 1) * 128], OTs.t[0:64, hh, :], hh == 0, hh == H - 1, [wao, OTs])
        TT("dve", hT.t[:, j, SEQ:NT], pb.t[:, 0:NS], hT.t[:, j, SEQ:NT], ALU.add, [pb, hT], [hT])
    f.release(mA3)
    f.release(mA)
    if stop_after == "A3":
        emit_output()
        f.finish()
        return nc

    def ffn(l):
        mF = f.mark()
        xnT = f.sbuf("xnT", [128, 8, NT], BF16)
        sq8 = f.sbuf("sq8F", [128, 8, 512], BF16)
        rs = f.sbuf("rsF", [128, 512], F32)
        gcol = V_GFA if l == 0 else V_GFB
        for (c0, N) in CHUNKS:
            ACT(sq8.t[:, :, 0:N], hT.t[:, :, c0:c0 + N], AF.Square, [hT], [sq8])
            pss = PS[7]
            for j in range(8):
                MM(pss, pss.t[:, 0:N], ones_b, sq8.t[:, j, 0:N], j == 0, j == 7, [cb, sq8])
            rstd_chain(pss, 128, N, rs, 1.0 / D)
            for j in range(8):
                STT("dve", xnT.t[:, j, c0:c0 + N], hT.t[:, j, c0:c0 + N], vcol(gcol + j), rs.t[:, 0:N], ALU.mult, ALU.mult, [hT, vecs, rs], [xnT])
        wg_r = Ring([f.sbuf(f"wg{i}", [128, 8, 512], BF16) for i in range(2)])
        wu_r = Ring([f.sbuf(f"wu{i}", [128, 8, 512], BF16) for i in range(2)])
        wo_r2 = Ring([f.sbuf(f"wo{i}", [128, 4, D], BF16) for i in range(2)])
        uT_r = Ring([f.sbuf(f"uT{i}", [128, 4, 512], BF16) for i in range(2)])
        sg_r = Ring([f.sbuf(f"sg{i}", [128, 512], F32) for i in range(2)])
        ring = Ring(PS[0:8])
        nblk = (D_FF + 511) // 512
        for hb in range(nblk):
            h0 = hb * 512
            HW = min(512, D_FF - h0)
            nhc = HW // 128
            wg = wg_r.next(); wu = wu_r.next(); wo = wo_r2.next()
            load_w(wg, w_ffn_in[l], 8, h0, HW)
            load_w(wu, w_ffn_in[l], 8, D_FF + h0, HW)
            for hc in range(nhc):
                LD("pool", wo.t[:, hc, :], w_ffn_out[l, h0 + hc * 128:h0 + (hc + 1) * 128, :], [wo])
            for (c0, N) in CHUNKS:
                uT = uT_r.next()
                for hc in range(nhc):
                    pg_ = ring.next()
                    for j in range(8):
                        MM(pg_, pg_.t[:, 0:N], wg.t[:, j, hc * 128:(hc + 1) * 128], xnT.t[:, j, c0:c0 + N], j == 0, j == 7, [wg, xnT])
                    pu_ = ring.next()
                    for j in range(8):
                        MM(pu_, pu_.t[:, 0:N], wu.t[:, j, hc * 128:(hc + 1) * 128], xnT.t[:, j, c0:c0 + N], j == 0, j == 7, [wu, xnT])
                    sg = sg_r.next()
                    ACT(sg.t[:, 0:N], pg_.t[:, 0:N], AF.Silu, [pg_], [sg])
                    TT("dve", uT.t[:, hc, 0:N], pu_.t[:, 0:N], sg.t[:, 0:N], ALU.mult, [pu_, sg], [uT])
                for j in range(8):
                    po = ring.next()
                    for hc in range(nhc):
                        MM(po, po.t[:, 0:N], wo.t[:, hc, j * 128:(j + 1) * 128], uT.t[:, hc, 0:N], hc == 0, hc == nhc - 1, [wo, uT])
                    TT("dve", hT.t[:, j, c0:c0 + N], po.t[:, 0:N], hT.t[:, j, c0:c0 + N], ALU.add, [po, hT], [hT])
        f.release(mF)

    ffn(0)
    if stop_after == "FA":
        emit_output()
        f.finish()
        return nc

    mB = f.mark()
    wkv = f.sbuf("wkv", [128, 8, 512], BF16)
    load_w(wkv, w_kv, 8, 0, 512)
    wq_r = Ring([f.sbuf("wqh0", [128, 8, 512], BF16)])
    wbo_r = Ring([f.sbuf("wboh0", [128, 4, D], BF16)])
    esk = f.sbuf("esk", [128, H], F32)
    ACT(esk.t[:, :], vecs.t[:, V_SINKR:V_SINKR + H], AF.Exp, [vecs], [esk])
    maskB = f.sbuf("maskB", [16, 4, 144], F32)
    LD("sp", maskB.t[:, :, :], maskB_d.rearrange("p (b k) -> p b k", b=4), [maskB])
    KB_c = f.sbuf("KB_c", [64, 4, 640], BF16)
    VB_c = f.sbuf("VB_c", [128, 5, 4, 192], BF16)
    MSET("pool", VB_c.t[:, :, :, 0:64], 1.0, [VB_c])
    MSET("pool", VB_c.t[:, :, :, 128:192], 1.0, [VB_c])
    QB_c = f.sbuf("QB_c", [64, 4, 8, 128], BF16)
    OB_c = f.sbuf("OB_c", [128, 4, 512], BF16)
    QBs = f.sbuf("QBs", [64, 4, 8, 4], BF16)
    OBs = f.sbuf("OBs", [128, 8, NS], BF16)
    kn_s = f.sbuf("kn_s", [64, 4, NS], F32)
    kn_sb = f.sbuf("kn_sb", [64, 4, NS], BF16)
    VBs_f = f.sbuf("VBs_f", [NS, 256], F32)
    VBs_b = f.sbuf("VBs_b", [NS, 256], BF16)
    sqx = f.sbuf("sqxB", [128, 8, 512], BF16)
    rs0 = f.sbuf("rs0B", [128, 512], F32)
    xkv = f.sbuf("xkv", [128, 8, 512], BF16)
    sq_r = Ring([f.sbuf(f"sqC{i}", [64, 512], BF16) for i in range(4)])
    rs_r = Ring([f.sbuf(f"rsC{i}", [64, 512], F32) for i in range(4)])
    kn_r = Ring([f.sbuf(f"knC{i}", [64, 512], F32) for i in range(4)])
    t1_r = Ring([f.sbuf(f"t1C{i}", [16, 512], F32) for i in range(2)])
    t2_r = Ring([f.sbuf(f"t2C{i}", [16, 512], F32) for i in range(2)])
    pT_r = Ring([f.sbuf(f"pTC{i}", [128, 512], BF16) for i in range(6)])
    lt_r = Ring([f.sbuf(f"ltC{i}", [128, 512], F32) for i in range(3)])
    tabc = Ring([f.sbuf(f"tabcC{i}", [16, 2, 512], F32) for i in range(2)])
    kvst = f.sbuf("kvst", [128, 2, 256], F32)
    ringP = Ring(PS[6:8])
    ringS = Ring(PS[0:3])
    ringO = Ring(PS[3:6])
    B64 = cb.t[0:64, CB_B96:CB_B96 + 64]
    P16 = cf.t[0:16, CF_P16:CF_P16 + 16]
    maskD4 = cb.t[:, CB_MD:CB_MD + 512]
    maskP4 = cb.t[:, CB_MP:CB_MP + 512]

    def pj_S1(specs, N, A):
        for i, sp in enumerate(specs):
            for j in range(8):
                MM(A[i], A[i].t[0:64, 0:N], sp["lhsT"](j), sp["rhs"](j), j == 0, j == 7, sp["rd"])

    def pj_mid(specs, N, A, Bs):
        n = len(specs)
        sqs, rss = [], []
        for i in range(n):
            sq = sq_r.next(); sqs.append(sq)
            ACT(sq.t[0:64, 0:N], A[i].t[0:64, 0:N], AF.Square, [A[i]], [sq])
        for i in range(n):
            MM(Bs[i], Bs[i].t[0:64, 0:N], B64, sqs[i].t[0:64, 0:N], True, True, [cb, sqs[i]])
        for i in range(n):
            rs = rs_r.next(); rss.append(rs)
            rstd_chain(Bs[i], 64, N, rs, 1.0)
        for i, sp in enumerate(specs):
            kn = kn_r.next(); sp["kn"] = kn
            if sp.get("dst") is not None:
                dap, dbuf = sp["dst"]
                vw = lambda ap: ap.rearrange("p (t q) -> p t q", t=4)
                STT("dve", dap(0, 64), vw(A[i].t[0:64, 0:N]), vcol(sp["gcol"], 0, 64), vw(rss[i].t[0:64, 0:N]), ALU.mult, ALU.mult, [A[i], vecs, rss[i]], [dbuf])
                STT("dve", kn.t[0:16, 0:N], A[i].t[0:16, 0:N], vcol(sp["gcol"], 0, 16), rss[i].t[0:16, 0:N], ALU.mult, ALU.mult, [A[i], vecs, rss[i]], [kn])
            else:
                STT("dve", kn.t[0:64, 0:N], A[i].t[0:64, 0:N], vcol(sp["gcol"], 0, 64), rss[i].t[0:64, 0:N], ALU.mult, ALU.mult, [A[i], vecs, rss[i]], [kn])

    def pj_tail(specs, N, tb, A):
        for i, sp in enumerate(specs):
            MM(A[i], A[i].t[0:16, 0:N], P16, sp["kn"].t[0:16, 0:N], True, True, [cf, sp["kn"]])
        for i, sp in enumerate(specs):
            kn = sp["kn"]
            t1 = t1_r.next(); t2 = t2_r.next()
            TT("pool", t1.t[0:16, 0:N], kn.t[0:16, 0:N], tb.t[0:16, 0, 0:N], ALU.mult, [kn, tb], [t1])
            TT("dve", t2.t[0:16, 0:N], A[i].t[0:16, 0:N], tb.t[0:16, 1, 0:N], ALU.mult, [A[i], tb], [t2])
            if sp.get("dst") is not None:
                dap, dbuf = sp["dst"]
                vw = lambda ap: ap.rearrange("p (t q) -> p t q", t=4)
                TT("dve", dap(0, 16), vw(t1.t[0:16, 0:N]), vw(t2.t[0:16, 0:N]), ALU.add, [t1, t2], [dbuf])
            else:
                TT("dve", kn.t[0:16, 0:N], t1.t[0:16, 0:N], t2.t[0:16, 0:N], ALU.add, [t1, t2], [kn])
                sp["consume"](kn)

    def proj_pipeline(batches, N, tb):
        sets = [PS[0:4], PS[4:8]]
        pj_S1(batches[0]["specs"], N, sets[0])
        for k, bt in enumerate(batches):
            A = sets[k % 2]; Bs = sets[(k + 1) % 2]
            pj_mid(bt["specs"], N, A, Bs)
            if k + 1 < len(batches):
                pj_S1(batches[k + 1]["specs"], N, Bs)
            pj_tail(bt["specs"], N, tb, A)
            if bt.get("after") is not None:
                bt["after"](A)

    for ci, (c0, N) in enumerate(CHUNKS):
        samp = c0 >= SEQ
        tb = tabc.next()
        LD("sp", tb.t[0:16, 0, 0:N], tabB_d[0, :, c0:c0 + N], [tb])
        LD("sp", tb.t[0:16, 1, 0:N], tabB_d[1, :, c0:c0 + N], [tb])
        ACT(sqx.t[:, :, 0:N], hT.t[:, :, c0:c0 + N], AF.Square, [hT], [sqx])
        pss = ringP.next()
        for j in range(8):
            MM(pss, pss.t[:, 0:N], ones_b, sqx.t[:, j, 0:N], j == 0, j == 7, [cb, sqx])
        rstd_chain(pss, 128, N, rs0, 1.0 / D)
        xnB = sqx
        for j in range(8):
            STT("dve", xkv.t[:, j, 0:N], hT.t[:, j, c0:c0 + N], vcol(V_GKV + j), rs0.t[:, 0:N], ALU.mult, ALU.mult, [hT, vecs, rs0], [xkv])
            STT("dve", xnB.t[:, j, 0:N], hT.t[:, j, c0:c0 + N], vcol(V_GB + j), rs0.t[:, 0:N], ALU.mult, ALU.mult, [hT, vecs, rs0], [xnB])
        kn_keep = {}

        def k_consume(kvh):
            def fn(kn):
                if not samp:
                    CP("pool", KB_c.t[0:64, kvh, 128:128 + N], kn.t[0:64, 0:N], [kn], [KB_c])
                    kn_keep[kvh] = kn
                else:
                    CP("pool", kn_s.t[0:64, kvh, :], kn.t[0:64, 0:NS], [kn], [kn_s])
                    CP("pool", kn_sb.t[0:64, kvh, :], kn.t[0:64, 0:NS], [kn], [kn_sb])
            return fn
        k_batch = dict(specs=[dict(lhsT=(lambda j, kvh=kvh: wkv.t[:, j, kvh * 64:(kvh + 1) * 64]), rhs=(lambda j: xkv.t[:, j, 0:N]),
                                    rd=[wkv, xkv], gcol=V_GKB, consume=k_consume(kvh)) for kvh in range(N_KV)], after=None)
        if ci == 3:
            def k_after(A):
                for kvh in range(N_KV):
                    pbT = A[3]
                    TR(pbT, pbT.t[:, 0:64], kn_keep[kvh].t[0:64, 384:512], cf.t[0:64, CF_ID:CF_ID + 64], [kn_keep[kvh], cf])
                    CP("act", kvst.t[:, 0, kvh * 64:(kvh + 1) * 64], pbT.t[:, 0:64], [pbT], [kvst])
                ST("sp", wk_p[:, :], kvst.t[:, 0, :], [kvst])
            k_batch["after"] = k_after
        if samp:
            proj_pipeline([k_batch], N, tb)
        if not samp:
            for ti in range(4):
                bv = ringP.next()
                for j in range(8):
                    MM(bv, bv.t[:, 0:256], xkv.t[:, j, ti * 128:(ti + 1) * 128], wkv.t[:, j, 256:512], j == 0, j == 7, [xkv, wkv])
                CP("act", VB_c.t[:, 1 + ti, :, 64:128], bv.t[:, 0:256].rearrange("p (k d) -> p k d", k=4), [bv], [VB_c])
                if ci == 3 and ti == 3:
                    CP("dve", kvst.t[:, 1, :], bv.t[:, 0:256], [bv], [kvst])
                    ST("sp", wv_p[:, :], kvst.t[:, 1, :], [kvst])
        else:
            bv = ringP.next()
            for j in range(8):
                MM(bv, bv.t[0:NS, 0:256], xkv.t[:, j, 0:NS], wkv.t[:, j, 256:512], j == 0, j == 7, [xkv, wkv])
            CP("act", VBs_f.t[:, :], bv.t[0:NS, 0:256], [bv], [VBs_f])
            CP("dve", VBs_b.t[:, :], bv.t[0:NS, 0:256], [bv], [VBs_b])
            pbT = ringP.next()
            for kvh in range(N_KV):
                TR(pbT, pbT.t[0:NS, kvh * 64:(kvh + 1) * 64], kn_s.t[0:64, kvh, :], cf.t[0:64, CF_ID:CF_ID + 64], [kn_s, cf])
            ktok = f.sbuf("ktok", [NS, 256], F32)
            CP("act", ktok.t[:, :], pbT.t[0:NS, 0:256], [pbT], [ktok])
            kw_all = f.sbuf("kw_all", [128, 4, 256], F32)
            vw_all = f.sbuf("vw_all", [128, 4, 256], F32)
            vwb_all = f.sbuf("vwb_all", [128, 4, 256], BF16)
            for b in range(4):
                LD("sp", kw_all.t[:, b, :], swk[b, :, :], [kw_all])
                LD("sp", vw_all.t[:, b, :], swv[b, :, :], [vw_all])
            CP("pool", vwb_all.t[:, :, :], vw_all.t[:, :, :], [vw_all], [vwb_all])
            for b in range(4):
                ST("sp", wk_s[b, 0:124, :], kw_all.t[4:128, b, :], [kw_all])
                ST("sp", wv_s[b, 0:124, :], vw_all.t[4:128, b, :], [vw_all])
                ST("sp", wk_s[b, 124:128, :], ktok.t[b * 4:(b + 1) * 4, :], [ktok])
                ST("sp", wv_s[b, 124:128, :], VBs_f.t[b * 4:(b + 1) * 4, :], [VBs_f])
            Kcat = f.sbuf("Kcat", [64, 144], BF16)
            scb = f.sbuf("scb", [16, 144], F32)
            pb_ = f.sbuf("pbB", [16, 144], BF16)
            pTw = f.sbuf("pTw", [128, 32], BF16)
            st2 = f.sbuf("st2", [16, 8], F32)
            onb = f.sbuf("onb", [16, 128], BF16)

            def s2(i):
                return st2.t[0:16, i:i + 1]
        for half in range(2):
            wqh = wq_r.next(); wboh = wbo_r.next()
            load_w(wqh, w_q_b, 8, half * 512, 512)
            for pr in range(4):
                LD("pool", wboh.t[:, pr, :], w_b_out[(half * 4 + pr) * 128:(half * 4 + pr + 1) * 128, :], [wboh])
            def q_consume(h8):
                def fn(kn):
                    if not samp:
                        CP("pool", QB_c.t[0:64, :, h8, :], kn.t[0:64, 0:512].rearrange("p (t q) -> p t q", t=4), [kn], [QB_c])
                    else:
                        CP("pool", QBs.t[0:64, :, h8, :], kn.t[0:64, 0:NS].rearrange("p (b t) -> p b t", b=4), [kn], [QBs])
                return fn
            def q_dst(h8):
                if samp:
                    return None
                slot8 = (h8 // 4) * 4 + [0, 2, 1, 3][h8 % 4]
                return ((lambda p0, p1, slot8=slot8: QB_c.t[p0:p1, :, slot8, :]), QB_c)
            batches = [dict(specs=[dict(lhsT=(lambda j, h8=h8: wqh.t[:, j, h8 * 64:(h8 + 1) * 64]), rhs=(lambda j: xnB.t[:, j, 0:N]),
                                        rd=[wqh, xnB], gcol=V_GQB, consume=q_consume(h8), dst=q_dst(h8)) for h8 in range(qb * 4, qb * 4 + 4)], after=None)
                       for qb in range(2)]
            if half == 0 and not samp:
                batches = [k_batch] + batches
            proj_pipeline(batches, N, tb)
            if not samp:
                iters = []
                for kk in range(2):
                    for ti in range(4):
                        gi = ci * 4 + ti
                        kts = [kt for kt in (gi - 1, gi) if kt >= 0]
                        for n_, kt in enumerate(kts):
                            iters.append(dict(kk=kk, ti=ti, gi=gi, kt=kt, first=(n_ == 0), last=(n_ == len(kts) - 1), n=len(iters)))

                def emitS(it):
                    kk, ti, kt, gi = it["kk"], it["ti"], it["kt"], it["gi"]
                    kvh = half * 2 + kk
                    slot = kt - (ci * 4 - 1)
                    bs_ = ringS.next()
                    MM(bs_, bs_.t[:, 0:512], KB_c.t[0:64, kvh, slot * 128:(slot + 1) * 128],
                       QB_c.t[0:64, ti, kk * 4:(kk + 1) * 4, :].rearrange("p h q -> p (h q)"), True, True, [KB_c, QB_c])
                    pT = pT_r.next()
                    ACT(pT.t[:, :], bs_.t[:, :], AF.Exp, [bs_], [pT], scale=SCALE_B)
                    TT("pool" if it["n"] % 4 == 3 else "dve", pT.t[:, :], pT.t[:, :], maskD4 if kt == gi else maskP4, ALU.mult, [pT, cb], [pT])
                    it["pT"] = pT; it["slot"] = slot; it["kvh"] = kvh
                curB = {}

                def emitPV(it):
                    kk, ti, kvh = it["kk"], it["ti"], it["kvh"]
                    if it["first"]:
                        curB["bo"] = ringO.next()
                    bo = curB["bo"]
                    MM(bo, bo.t[:, 0:256], VB_c.t[:, it["slot"], kvh, 64:192], it["pT"].t[:, 0:256], it["first"], it["last"], [VB_c, it["pT"]], sgc=True)
                    MM(bo, bo.t[:, 256:512], VB_c.t[:, it["slot"], kvh, 0:128], it["pT"].t[:, 256:512], False, it["last"], [VB_c, it["pT"]], sgc=True)
                    if it["last"]:
                        pend_norm.append((it["idx_emit"], bo, kvh, kk, ti))

                def emitNormA(bo, kvh, kk, ti):
                    lt = lt_r.next()
                    v3 = lambda ap: ap.rearrange("p (h q) -> p h q", h=2)
                    eskv = esk.t[:, kvh * 4:(kvh + 1) * 4].rearrange("p (h2 two) -> p h2 two", two=2)
                    TT("dve", v3(lt.t[64:128, 0:256]), v3(bo.t[64:128, 0:256]), eskv[64:128, :, 0].unsqueeze(2).broadcast_to([64, 2, 128]), ALU.add, [bo, esk], [lt])
                    TT("dve", v3(lt.t[0:64, 256:512]), v3(bo.t[0:64, 256:512]), eskv[0:64, :, 1].unsqueeze(2).broadcast_to([64, 2, 128]), ALU.add, [bo, esk], [lt])
                    ACT(lt.t[64:128, 0:256], lt.t[64:128, 0:256], AF.Ln, [lt], [lt])
                    ACT(lt.t[0:64, 256:512], lt.t[0:64, 256:512], AF.Ln, [lt], [lt])
                    ACT(lt.t[64:128, 0:256], lt.t[64:128, 0:256], AF.Exp, [lt], [lt], scale=-1.0)
                    ACT(lt.t[0:64, 256:512], lt.t[0:64, 256:512], AF.Exp, [lt], [lt], scale=-1.0)
                    return lt

                def emitNormB(bo, lt, kk, ti):
                    v3 = lambda ap: ap.rearrange("p (h q) -> p h q", h=2)
                    TT("dve", OB_c.t[0:64, kk * 2:(kk + 1) * 2, ti * 128:(ti + 1) * 128], v3(bo.t[0:64, 0:256]), v3(lt.t[64:128, 0:256]), ALU.mult, [bo, lt], [OB_c])
                    TT("dve", OB_c.t[64:128, kk * 2:(kk + 1) * 2, ti * 128:(ti + 1) * 128], v3(bo.t[64:128, 256:512]), v3(lt.t[0:64, 256:512]), ALU.mult, [bo, lt], [OB_c])
                DEPTH = 3
                NDA = 1
                NDB = 2
                pend_norm = []
                pend_b = []
                for idx in range(len(iters) + DEPTH + NDA + NDB):
                    if idx < len(iters):
                        emitS(iters[idx])
                    if 0 <= idx - DEPTH < len(iters):
                        iters[idx - DEPTH]["idx_emit"] = idx
                        emitPV(iters[idx - DEPTH])
                    while pend_norm and pend_norm[0][0] + NDA <= idx:
                        _, bo_, kvh_, kk_, ti_ = pend_norm.pop(0)
                        lt_ = emitNormA(bo_, kvh_, kk_, ti_)
                        pend_b.append((idx, bo_, lt_, kk_, ti_))
                    while pend_b and pend_b[0][0] + NDB <= idx:
                        _, bo_, lt_, kk_, ti_ = pend_b.pop(0)
                        emitNormB(bo_, lt_, kk_, ti_)
                assert not pend_norm and not pend_b
                for j in range(8):
                    pb = ringP.next()
                    for pr in range(4):
                        MM(pb, pb.t[:, 0:512], wboh.t[:, pr, j * 128:(j + 1) * 128], OB_c.t[:, pr, :], pr == 0, pr == 3, [wboh, OB_c])
                    TT("dve", hT.t[:, j, c0:c0 + 512], pb.t[:, 0:512], hT.t[:, j, c0:c0 + 512], ALU.add, [pb, hT], [hT])
            else:
                for b in range(4):
                    for kk in range(2):
                        kvh = half * 2 + kk
                        pbK = ringP.next()
                        TR(pbK, pbK.t[0:64, 0:128], kw_all.t[:, b, kvh * 64:(kvh + 1) * 64], identf, [kw_all, cf])
                        CP("act", Kcat.t[0:64, 0:128], pbK.t[0:64, 0:128], [pbK], [Kcat])
                        CP("pool", Kcat.t[0:64, 128:144], kn_sb.t[0:64, kvh, :], [kn_sb], [Kcat])
                        bs_ = ringS.next()
                        MM(bs_, bs_.t[0:16, 0:144], QBs.t[0:64, b, kk * 4:(kk + 1) * 4, :].rearrange("p h t -> p (h t)"), Kcat.t[0:64, 0:144], True, True, [QBs, Kcat])
                        TT("dve", scb.t[:, :], bs_.t[0:16, 0:144], maskB.t[:, b, :], ALU.add, [bs_, maskB], [scb])
                        f.op("dve", lambda e: e.tensor_reduce(out=s2(0), in_=scb.t[:, :], axis=AX.X, op=ALU.max), reads=[scb], writes=[st2])
                        TS("dve", s2(1), s2(0), -SCALE_B, ALU.mult, [st2], [st2])
                        MSET("pool", s2(2), 0.0, [st2])
                        ACT(pb_.t[:, :], scb.t[:, :], AF.Exp, [scb, st2], [pb_, st2], scale=SCALE_B, bias=s2(1), accum_out=s2(2))
                        ACT(s2(3), vecs.t[0:16, V_SINKC + kvh:V_SINKC + kvh + 1], AF.Exp, [vecs, st2], [st2], scale=1.0, bias=s2(1))
                        TT("dve", s2(4), s2(2), s2(3), ALU.add, [st2], [st2])
                        f.op("dve", lambda e: e.reciprocal(out=s2(5), in_=s2(4)), reads=[st2], writes=[st2])
                        pbP = ringP.next()
                        TR(pbP, bfv(pbP)[:, 0:16], pb_.t[0:16, 0:128], idb.t[0:16, 0:16], [pb_, idb])
                        TR(pbP, bfv(pbP)[0:16, 16:32], pb_.t[0:16, 128:144], idb.t[0:16, 0:16], [pb_, idb])
                        CP("act", pTw.t[:, 0:32], bfv(pbP)[:, 0:32], [pbP], [pTw])
                        bo = ringO.next()
                        MM(bo, bo.t[0:16, 0:64], pTw.t[:, 0:16], vwb_all.t[:, b, kvh * 64:(kvh + 1) * 64], True, False, [pTw, vwb_all])
                        MM(bo, bo.t[0:16, 0:64], pTw.t[0:16, 16:32], VBs_b.t[0:NS, kvh * 64:(kvh + 1) * 64], False, True, [pTw, VBs_b])
                        TS("dve", onb.t[:, 0:64], bo.t[0:16, 0:64], s2(5), ALU.mult, [bo, st2], [onb])
                        TS("dve", onb.t[:, 64:128], bo.t[0:16, 0:64], s2(5), ALU.mult, [bo, st2], [onb])
                        pbO = ringP.next()
                        TR(pbO, bfv(pbO)[0:128, 0:16], onb.t[0:16, 0:128], idb.t[0:16, 0:16], [onb, idb])
                        for par in range(2):
                            CP("act", OBs.t[par * 64:(par + 1) * 64, kk * 4:(kk + 1) * 4, b * 4:(b + 1) * 4].rearrange("p (h2 two) t -> p h2 two t", two=2)[:, :, par, :],
                               bfv(pbO)[par * 64:(par + 1) * 64, 0:16].rearrange("p (h2 two t) -> p h2 two t", two=2, t=4)[:, :, par, :], [pbO], [OBs])
                for j in range(8):
                    for par in range(2):
                        pb = ringP.next()
                        for pr in range(4):
                            h8 = pr * 2 + par
                            MM(pb, pb.t[:, 0:NS], wboh.t[par * 64:(par + 1) * 64, pr, j * 128:(j + 1) * 128], OBs.t[par * 64:(par + 1) * 64, h8, :], pr == 0, pr == 3, [wboh, OBs])
                        TT("dve", hT.t[:, j, SEQ:NT], pb.t[:, 0:NS], hT.t[:, j, SEQ:NT], ALU.add, [pb, hT], [hT])
        if ci < 3:
            CP("pool", KB_c.t[0:64, :, 0:128], KB_c.t[0:64, :, 512:640], [KB_c], [KB_c])
            CP("pool", VB_c.t[:, 0, :, 64:128], VB_c.t[:, 4, :, 64:128], [VB_c], [VB_c])
    f.release(mB)
    if stop_after == "B":
        emit_output()
        f.finish()
        return nc

    ffn(1)

    emit_output()
    f.finish()
    return nc


def _rope_tab(n_rot, pos):
    inv = np.power(np.float32(THETA), (-np.arange(0, n_rot, 2, dtype=np.float32) / np.float32(n_rot)).astype(np.float32)).astype(np.float32)
    ang = (pos.astype(np.float32)[:, None] * inv[None, :]).astype(np.float32)
    return np.cos(ang.astype(np.float64)).astype(np.float32), np.sin(ang.astype(np.float64)).astype(np.float32)


def _constants():
    pos = np.concatenate([np.arange(SEQ), np.tile(PAST + np.arange(4), 4)]).astype(np.int64)
    cA, sA = _rope_tab(D_ROPE, pos)
    tabA = np.zeros((2, 96, NT), np.float32)
    tabA[0, :64] = 1.0
    for d in range(32):
        tabA[0, 64 + d] = cA[:, d % 16]
        tabA[1, 64 + d] = sA[:, d % 16]
    cB, sB = _rope_tab(16, pos)
    tabB = np.zeros((2, 16, NT), np.float32)
    for d in range(16):
        tabB[0, d] = cB[:, d % 8]
        tabB[1, d] = sB[:, d % 8]
    cf = np.zeros((128, CF_W), np.float32)
    cf[:, CF_ID:CF_ID + 128] = np.eye(128, dtype=np.float32)
    for d in range(16):
        cf[64 + d + 16, CF_P96 + 64 + d] = -1.0
        cf[64 + d, CF_P96 + 64 + d + 16] = 1.0
    for d in range(8):
        cf[d + 8, CF_P16 + d] = -1.0
        cf[d, CF_P16 + d + 8] = 1.0
    cb = np.zeros((128, CB_W), np.float32)
    cb[:, CB_ONES:CB_ONES + 128] = 1.0
    cb[0:64, CB_B96:CB_B96 + 64] = 1.0 / 64
    cb[64:96, CB_B96 + 64:CB_B96 + 96] = 1.0 / 32
    for jc in range(8):
        for p in range(128):
            hh = 2 * jc + p // 64
            cb[p, CB_BS + jc * 128 + hh * 4:CB_BS + jc * 128 + hh * 4 + 4] = 1.0 / 64
    pp = np.arange(128)[:, None]; cc = np.arange(128)[None, :]
    mD = (cc >= pp).astype(np.float32); mP = (cc < pp).astype(np.float32)
    cb[:, CB_MD:CB_MD + 512] = np.tile(mD, (1, 4))
    cb[:, CB_MP:CB_MP + 512] = np.tile(mP, (1, 4))
    maskS = np.full((64, 4, 16), NEG, np.float32)
    for hh in range(H):
        for t in range(4):
            for b in range(4):
                for t2 in range(t + 1):
                    maskS[hh * 4 + t, b, b * 4 + t2] = 0.0
    maskB = np.full((16, 4, 144), NEG, np.float32)
    for hq in range(4):
        for t in range(4):
            for b in range(4):
                maskB[hq * 4 + t, b, t + 1:128] = 0.0
                for t2 in range(t + 1):
                    maskB[hq * 4 + t, b, 128 + b * 4 + t2] = 0.0
    return dict(tabA=tabA, tabB=tabB, cf32=cf, cb32=cb, maskS=maskS.reshape(64, 64), maskB=maskB.reshape(16, 576))


def _vecs(inp):
    v = np.zeros((128, NV), np.float32)

    def colmaj(g, c0):
        n = g.shape[0] // 128
        v[:, c0:c0 + n] = g.reshape(n, 128).T
    colmaj(inp["norm_attn"][0], V_GA); colmaj(inp["norm_ffn"][0], V_GFA); colmaj(inp["g_kv_shared"], V_GKV)
    colmaj(inp["norm_attn"][1], V_GB); colmaj(inp["norm_ffn"][1], V_GFB)
    colmaj(inp["g_qc"][0], V_GQC); colmaj(inp["g_ckv"][0], V_GCKV)
    v[0:64, V_GQ96] = inp["g_qn_a"][0]; v[64:96, V_GQ96] = inp["g_qr_a"][0]
    v[0:64, V_GK96] = inp["g_kn_a"][0]; v[64:96, V_GK96] = inp["g_kr_a"][0]
    v[0:64, V_GKB] = inp["g_k_b"]; v[0:64, V_GQB] = inp["g_q_b"][0]
    sk = inp["sinks"][0]
    for kvh in range(4):
        for hq in range(4):
            v[hq * 4:hq * 4 + 4, V_SINKC + kvh] = sk[kvh * 4 + hq]
    v[:, V_SINKR:V_SINKR + H] = sk[None, :]
    return v


_PROG = {}


def kernel(_ncores=8, _stop_after=None, **inp):
    inp = {k: np.asarray(v) for k, v in inp.items()}
    key = _stop_after
    if key not in _PROG:
        _PROG[key] = build_program(_stop_after)
    nc = _PROG[key]
    consts = _constants()
    vecs = _vecs(inp)
    cache2d = np.ascontiguousarray(inp["cache_mla"][0].reshape(NPOOL * 128, D_CKV))
    shared = dict(
        cache=cache2d,
        w_a_in=np.ascontiguousarray(inp["w_a_in"][0]), w_uq=np.ascontiguousarray(inp["w_uq"][0]),
        w_uk=np.ascontiguousarray(inp["w_uk"][0]), w_uv=np.ascontiguousarray(inp["w_uv"][0]),
        w_a_out=np.ascontiguousarray(inp["w_a_out"][0]), w_kv=np.ascontiguousarray(inp["w_kv_shared"]),
        w_q_b=np.ascontiguousarray(inp["w_q_b"][0]), w_b_out=np.ascontiguousarray(inp["w_b_out"][0]),
        w_ffn_in=np.ascontiguousarray(inp["w_ffn_in"]), w_ffn_out=np.ascontiguousarray(inp["w_ffn_out"]),
        vecs=vecs, **consts)
    in_maps = []
    for c in range(_ncores):
        m = dict(shared)
        m["x_p"] = np.ascontiguousarray(inp["x_prompt"][c])
        m["x_s"] = np.ascontiguousarray(inp["x_sample"][4 * c:4 * c + 4].reshape(NS, D))
        m["ptab"] = np.ascontiguousarray(inp["page_table"][4 * c:4 * c + 4].reshape(1, 512).astype(np.int32))
        m["swk"] = np.ascontiguousarray(inp["state_win_k"][4 * c:4 * c + 4].reshape(4, 128, 256))
        m["swv"] = np.ascontiguousarray(inp["state_win_v"][4 * c:4 * c + 4].reshape(4, 128, 256))
        in_maps.append(m)
    res = run_bass_kernel_spmd(nc, in_maps, core_ids=list(range(_ncores)))
    R = res.results
    n = _ncores
    y_prompt = np.stack([R[c]["y_p"] for c in range(n)])
    y_sample = np.concatenate([R[c]["y_s"].reshape(4, 4, D) for c in range(n)])
    rows_pr = np.stack([R[c]["rows_p"] for c in range(n)])[None]
    rows_sa = np.concatenate([R[c]["rows_s"].reshape(4, 4, D_CKV) for c in range(n)])[None]
    wkp = np.stack([R[c]["wk_p"].reshape(128, 4, 64) for c in range(n)])
    wvp = np.stack([R[c]["wv_p"].reshape(128, 4, 64) for c in range(n)])
    wks = np.concatenate([R[c]["wk_s"].reshape(4, 128, 4, 64) for c in range(n)])
    wvs = np.concatenate([R[c]["wv_s"].reshape(4, 128, 4, 64) for c in range(n)])
    f32 = np.float32
    return (y_prompt.astype(f32), y_sample.astype(f32), rows_pr.astype(f32), rows_sa.astype(f32),
            wkp.astype(f32), wvp.astype(f32), wks.astype(f32), wvs.astype(f32))
```

```python
import numpy as np
import concourse.bass as bass
import concourse.mybir as mybir
from concourse.bass_utils import run_bass_kernel_spmd

F32 = mybir.dt.float32
BF16 = mybir.dt.bfloat16
I32 = mybir.dt.int32
AF = mybir.ActivationFunctionType
ALU = mybir.AluOpType
AX = mybir.AxisListType

ENGS = ["pe", "act", "dve", "pool", "sp"]

D = 1024
SEQ = 2048
NS = 16
NT = SEQ + NS
H = 16
D_NOPE, D_ROPE, D_V = 64, 32, 64
D_QC, D_C = 384, 256
D_CKV = D_C + D_ROPE
SCALE_A = float((D_NOPE + D_ROPE) ** -0.5)
N_KV, HD = 4, 64
SCALE_B = float(HD ** -0.5)
D_FF = 2816
EPS = 1e-6
THETA = 500000.0
PAST = 16384
NPOOL = 5120
NEG = -1e30
FILLER = 0
CHUNKS = [(0, 512), (512, 512), (1024, 512), (1536, 512), (2048, 16)]
NV = 69
V_GA, V_GFA, V_GKV, V_GB, V_GFB, V_GQC, V_GCKV, V_GQ96, V_GK96, V_GKB, V_GQB, V_SINKC, V_SINKR = 0, 8, 16, 24, 32, 40, 43, 45, 46, 47, 48, 49, 53
CB_ONES, CB_B96, CB_BS, CB_MD, CB_MP, CB_W = 0, 128, 224, 1248, 1760, 2272
CF_ID, CF_P96, CF_P16, CF_W = 0, 128, 224, 240


class Buf:
    __slots__ = ("name", "t", "last_w", "readers", "dsem", "dcnt", "excl")

    def __init__(self, name, t, excl=False, init=()):
        self.name = name
        self.t = t
        self.last_w = []
        self.readers = list(init)
        self.dsem = None
        self.dcnt = 0
        self.excl = excl

    def __getitem__(self, idx):
        return self.t[idx]


def _compact(evs):
    d = {}
    for k, v in evs:
        if d.get(k, 0) < v:
            d[k] = v
    return list(d.items())


class FW:
    def __init__(self, nc):
        self.nc = nc
        self.streams = {e: [] for e in ENGS}
        self.esem = {}
        self.ecnt = {e: 0 for e in ENGS}
        self.seen = {e: {} for e in ENGS}
        self.sems = {}
        self.dma_bufs = []
        self._ctx = []
        self.free_events = []
        self.free_sems = []
        for e in ["pe", "act", "dve", "pool"]:
            self.esem[e] = self._newsem("e_" + e)

    def _newsem(self, name):
        cm = self.nc.semaphore(name)
        h = cm.__enter__()
        self._ctx.append((cm, None))
        self.sems[name] = h
        return name

    def mark(self):
        return len(self._ctx)

    def release(self, mark):
        ev = list(self.free_events)
        while len(self._ctx) > mark:
            cm, b = self._ctx.pop()
            if b is not None:
                ev.extend(b.last_w)
                ev.extend(b.readers)
                if b.dsem is not None:
                    ev.append((b.dsem, b.dcnt))
                cm.__exit__(None, None, None)
            else:
                self._keep.append((cm, b))
        self.free_events = _compact(ev)

    _keep = []

    def sbuf(self, name, shape, dtype):
        self._uid = getattr(self, "_uid", 0) + 1
        name = "s%d_%s" % (self._uid, name)
        cm = self.nc.sbuf_tensor(name, list(shape), dtype)
        t = cm.__enter__()
        b = Buf(name, t, init=self.free_events)
        self._ctx.append((cm, b))
        return b

    def psum(self, name, shape, dtype=F32):
        cm = self.nc.psum_tensor(name, list(shape), dtype)
        t = cm.__enter__()
        b = Buf(name, t, excl=True)
        self._ctx.append((cm, b))
        return b

    def _deps(self, reads, writes):
        ev = []
        for b in reads:
            ev.extend(b.last_w)
            if b.excl:
                ev.extend(b.readers)
        for b in writes:
            ev.extend(b.last_w)
            ev.extend(b.readers)
        return ev

    def _waits(self, eng, ev):
        need = {}
        for (k, v) in ev:
            if need.get(k, 0) < v:
                need[k] = v
        seen = self.seen[eng]
        out = []
        for k, v in need.items():
            if seen.get(k, 0) >= v:
                continue
            seen[k] = v
            out.append((k, v))
        return out

    def _record(self, reads, writes, event, nowaw=False):
        for b in reads:
            if b.excl:
                b.last_w = [event]
                b.readers = []
            else:
                b.readers.append(event)
                if len(b.readers) > 48:
                    b.readers = _compact(b.readers)
        for b in writes:
            if nowaw:
                b.last_w.append(event)
                b.last_w = _compact(b.last_w)
            else:
                b.last_w = [event]
            b.readers = []

    def op(self, eng, fn, reads=(), writes=()):
        ev = self._deps(reads, writes)
        waits = self._waits(eng, ev)
        self.ecnt[eng] += 1
        val = self.ecnt[eng]
        semname = self.esem[eng]
        if eng == "pe":
            self.seen[eng][semname] = val
        sems = self.sems

        def thunk(e, fn=fn, waits=waits, semname=semname):
            for (k, v) in waits:
                e.wait_ge(sems[k], v)
            fn(e).then_inc(sems[semname], 1)
        self.streams[eng].append(thunk)
        self._record(reads, writes, (semname, val))

    def dma(self, q, fn, reads=(), writes=(), own=None):
        if own is None:
            own = writes[0] if writes else reads[0]
        if own.dsem is None:
            cm = self.nc.semaphore("d_" + own.name)
            h = cm.__enter__()
            self._keep.append((cm, None))
            self.sems["d_" + own.name] = h
            own.dsem = "d_" + own.name
            self.dma_bufs.append(own)
        ev = []
        for b in reads:
            ev.extend(b.last_w)
        for b in writes:
            ev.extend([e for e in b.last_w if e[0] != own.dsem])
            ev.extend(b.readers)
        waits = self._waits(q, ev)
        own.dcnt += 16
        val = own.dcnt
        semname = own.dsem
        sems = self.sems

        def thunk(e, fn=fn, waits=waits, semname=semname):
            for (k, v) in waits:
                e.wait_ge(sems[k], v)
            fn(e).then_inc(sems[semname], 16)
        self.streams[q].append(thunk)
        self._record(reads, writes, (semname, val), nowaw=True)

    def finish(self):
        finals = [(b.dsem, b.dcnt) for b in self.dma_bufs]
        sems = self.sems
        nc = self.nc
        streams = self.streams
        with nc.Block() as block:
            @block.sync
            def _(e):
                for th in streams["sp"]:
                    th(e)
                for (k, v) in finals:
                    e.wait_ge(sems[k], v)

            @block.tensor
            def _(e):
                for th in streams["pe"]:
                    th(e)

            @block.scalar
            def _(e):
                for th in streams["act"]:
                    th(e)

            @block.vector
            def _(e):
                for th in streams["dve"]:
                    th(e)

            @block.gpsimd
            def _(e):
                for th in streams["pool"]:
                    th(e)
        while self._ctx:
            cm, b = self._ctx.pop()
            cm.__exit__(None, None, None)
        for cm, b in reversed(self._keep):
            cm.__exit__(None, None, None)
        FW._keep = []


class Ring:
    def __init__(self, items):
        self.items = items
        self.i = 0

    def next(self):
        b = self.items[self.i % len(self.items)]
        self.i += 1
        return b


def build_program(stop_after=None, npool=NPOOL):
    nc = bass.Bass("TRN2", target_bir_lowering=False)
    FW._keep = []
    f = FW(nc)

    def din(name, shape, dt=F32):
        return nc.dram_tensor(name, list(shape), dt, kind="ExternalInput").ap()

    def dout(name, shape, dt=F32):
        return nc.dram_tensor(name, list(shape), dt, kind="ExternalOutput").ap()

    x_p = din("x_p", [SEQ, D]); x_s = din("x_s", [NS, D])
    cache = din("cache", [npool * 128, D_CKV])
    ptab = din("ptab", [1, 512], I32)
    swk = din("swk", [4, 128, 256]); swv = din("swv", [4, 128, 256])
    w_a_in = din("w_a_in", [D, 672]); w_uq = din("w_uq", [D_QC, 1536])
    w_uk = din("w_uk", [D_C, 1024]); w_uv = din("w_uv", [D_C, 1024])
    w_a_out = din("w_a_out", [1024, D]); w_kv = din("w_kv", [D, 512])
    w_q_b = din("w_q_b", [D, 1024]); w_b_out = din("w_b_out", [1024, D])
    w_ffn_in = din("w_ffn_in", [2, D, 2 * D_FF]); w_ffn_out = din("w_ffn_out", [2, D_FF, D])
    vecs_d = din("vecs", [128, NV]); cf_d = din("cf32", [128, CF_W]); cb_d = din("cb32", [128, CB_W])
    tabA_d = din("tabA", [2, 96, NT]); tabB_d = din("tabB", [2, 16, NT])
    maskS_d = din("maskS", [64, 4 * 16]); maskB_d = din("maskB", [16, 4 * 144])

    y_p = dout("y_p", [SEQ, D]); y_s = dout("y_s", [NS, D])
    rows_p = dout("rows_p", [SEQ, D_CKV]); rows_s = dout("rows_s", [NS, D_CKV])
    wk_p = dout("wk_p", [128, 256]); wv_p = dout("wv_p", [128, 256])
    wk_s = dout("wk_s", [4, 128, 256]); wv_s = dout("wv_s", [4, 128, 256])

    def MM(pb, out, lhsT, rhs, start, stop, rd, sgc=False):
        if sgc:
            f.op("pe", lambda e: e.matmul(out, lhsT=lhsT, rhs=rhs, start=start, stop=stop, skip_group_check=True), reads=rd, writes=[pb])
        else:
            f.op("pe", lambda e: e.matmul(out, lhsT=lhsT, rhs=rhs, start=start, stop=stop), reads=rd, writes=[pb])

    def TR(pb, out, in_, ident, rd):
        f.op("pe", lambda e: e.transpose(out=out, in_=in_, identity=ident), reads=rd, writes=[pb])

    def ACT(out, in_, func, rd, wr, **kw):
        f.op("act", lambda e: e.activation(out=out, in_=in_, func=func, **kw), reads=rd, writes=wr)

    def CP(eng, out, in_, rd, wr):
        if eng == "act":
            f.op("act", lambda e: e.copy(out=out, in_=in_), reads=rd, writes=wr)
        else:
            f.op(eng, lambda e: e.tensor_copy(out=out, in_=in_), reads=rd, writes=wr)

    def TT(eng, out, in0, in1, op, rd, wr):
        f.op(eng, lambda e: e.tensor_tensor(out=out, in0=in0, in1=in1, op=op), reads=rd, writes=wr)

    def STT(eng, out, in0, scalar, in1, op0, op1, rd, wr):
        f.op(eng, lambda e: e.scalar_tensor_tensor(out=out, in0=in0, scalar=scalar, in1=in1, op0=op0, op1=op1), reads=rd, writes=wr)

    def TS(eng, out, in0, s1, op0, rd, wr):
        f.op(eng, lambda e: e.tensor_scalar(out=out, in0=in0, scalar1=s1, scalar2=None, op0=op0), reads=rd, writes=wr)

    def MSET(eng, ap, val, wr):
        f.op(eng, lambda e: e.memset(ap, val), writes=wr)

    def LD(q, out, in_, wr, rd=()):
        f.dma(q, lambda e: e.dma_start(out=out, in_=in_), reads=list(rd), writes=list(wr))

    def ST(q, out, in_, rd):
        f.dma(q, lambda e: e.dma_start(out=out, in_=in_), reads=list(rd), writes=[])

    PS = [f.psum(f"ps{i}", [128, 512], F32) for i in range(8)]

    def bfv(pb):
        return pb.t[:, :].bitcast(BF16)

    hT = f.sbuf("hT", [128, 8, NT], F32)
    vecs = f.sbuf("vecs", [128, NV], F32)
    cf = f.sbuf("cf", [128, CF_W], F32)
    cb = f.sbuf("cb", [128, CB_W], BF16)
    idb = f.sbuf("idb", [128, 128], BF16)
    LD("sp", vecs[:, :], vecs_d[:, :], [vecs])
    LD("sp", cf[:, :], cf_d[:, :], [cf])
    LD("pool", cb[:, :], cb_d[:, :], [cb])
    CP("pool", idb[:, :], cf[:, CF_ID:CF_ID + 128], [cf], [idb])
    identf = cf.t[:, CF_ID:CF_ID + 128]
    ones_b = cb.t[:, CB_ONES:CB_ONES + 128]

    def vcol(c, p0=0, p1=128):
        return vecs.t[p0:p1, c:c + 1]

    def rstd_chain(pb, M, N, rs, scale):
        ACT(rs.t[0:M, 0:N], pb.t[0:M, 0:N], AF.Ln, [pb], [rs], scale=scale, bias=EPS)
        ACT(rs.t[0:M, 0:N], rs.t[0:M, 0:N], AF.Exp, [rs], [rs], scale=-0.5)

    def load_w(dst, src2d, kchunks, c0, ncols, prows=128):
        for j in range(kchunks):
            LD("pool", dst.t[0:prows, j, 0:ncols], src2d[j * prows:(j + 1) * prows, c0:c0 + ncols], [dst])

    def emit_output():
        ys_r = Ring([f.sbuf(f"ys{i}", [128, D], F32) for i in range(2)])
        for i in range(17):
            ys = ys_r.next()
            rows = 128 if i < 16 else NS
            for half in range(2):
                pb = PS[(2 * i + half) % 8]
                for q in range(4):
                    j = half * 4 + q
                    TR(pb, pb.t[0:rows, q * 128:(q + 1) * 128], hT.t[:, j, i * 128:i * 128 + rows], identf, [hT, cf])
                CP("act" if half == 0 else "dve", ys.t[0:rows, half * 512:(half + 1) * 512], pb.t[0:rows, 0:512], [pb], [ys])
            if i < 16:
                ST("sp", y_p[i * 128:(i + 1) * 128, :], ys.t[0:rows, :], [ys])
            else:
                ST("sp", y_s[:, :], ys.t[0:rows, :], [ys])


    m0 = f.mark()
    xs_ring = Ring([f.sbuf(f"xs{i}", [128, D], F32) for i in range(2)])
    for i in range(17):
        xs = xs_ring.next()
        rows = 128 if i < 16 else NS
        src = x_p[i * 128:(i + 1) * 128, :] if i < 16 else x_s[:, :]
        LD("sp", xs.t[0:rows, :], src, [xs])
        for half in range(2):
            pb = PS[(2 * i + half) % 8]
            for q in range(4):
                j = half * 4 + q
                TR(pb, pb.t[:, q * 128:q * 128 + rows], xs.t[0:rows, j * 128:(j + 1) * 128], identf[0:rows, 0:rows], [xs, cf])
            srcv = pb.t[:, :].rearrange("p (q c) -> p q c", q=4)[:, :, 0:rows]
            CP("act" if half == 0 else "dve", hT.t[:, half * 4:half * 4 + 4, i * 128:i * 128 + rows], srcv, [pb], [hT])
    f.release(m0)

    mA = f.mark()
    cqT = f.sbuf("cqT", [128, 3, NT], BF16)
    ckvT = f.sbuf("ckvT", [128, 2, NT], BF16)
    kpe_b = f.sbuf("kpe_b", [96, NT], BF16)
    rown_b = f.sbuf("rown_b", [NS, 256], BF16)
    qs_all = f.sbuf("qs_all", [96, H, NS], BF16)

    mA1 = f.mark()
    wain = f.sbuf("wain", [128, 8, 672], BF16)
    load_w(wain, w_a_in, 8, 0, 672)
    sq8 = f.sbuf("sq8", [128, 8, 512], BF16)
    xn = f.sbuf("xn", [128, 8, 512], BF16)
    rs_r = Ring([f.sbuf(f"rsA{i}", [128, 512], F32) for i in range(2)])
    ckv_f = f.sbuf("ckv_f", [128, 2, 512], F32)
    kpn = f.sbuf("kpn", [96, 512], F32)
    kpe_f = f.sbuf("kpe_f", [96, 512], F32)
    t1 = f.sbuf("t1A", [96, 512], F32)
    tabc = Ring([f.sbuf(f"tabcA{i}", [96, 2, 512], F32) for i in range(2)])
    rstage = Ring([f.sbuf(f"rstage{i}", [128, D_CKV], F32) for i in range(2)])
    for (c0, N) in CHUNKS:
        tb = tabc.next()
        LD("sp", tb.t[64:96, 0, 0:N], tabA_d[0, 64:96, c0:c0 + N], [tb])
        LD("sp", tb.t[64:96, 1, 0:N], tabA_d[1, 64:96, c0:c0 + N], [tb])
        ACT(sq8.t[:, :, 0:N], hT.t[:, :, c0:c0 + N], AF.Square, [hT], [sq8])
        pss = PS[6]
        for j in range(8):
            MM(pss, pss.t[:, 0:N], ones_b, sq8.t[:, j, 0:N], j == 0, j == 7, [cb, sq8])
        rs = rs_r.next()
        rstd_chain(pss, 128, N, rs, 1.0 / D)
        for j in range(8):
            STT("dve", xn.t[:, j, 0:N], hT.t[:, j, c0:c0 + N], vcol(V_GA + j), rs.t[:, 0:N], ALU.mult, ALU.mult, [hT, vecs, rs], [xn])
        mts = [(0, 128), (128, 128), (256, 128), (384, 128), (512, 128), (576, 96)]
        for mi, (mc, M) in enumerate(mts):
            pb = PS[mi]
            for j in range(8):
                MM(pb, pb.t[0:M, 0:N], wain.t[:, j, mc:mc + M], xn.t[:, j, 0:N], j == 0, j == 7, [wain, xn])
        for m in range(3):
            ACT(sq8.t[:, m, 0:N], PS[m].t[:, 0:N], AF.Square, [PS[m]], [sq8])
        for m in range(3):
            MM(pss, pss.t[:, 0:N], ones_b, sq8.t[:, m, 0:N], m == 0, m == 2, [cb, sq8])
        rs = rs_r.next()
        rstd_chain(pss, 128, N, rs, 1.0 / D_QC)
        for m in range(3):
            STT("dve", cqT.t[:, m, c0:c0 + N], PS[m].t[:, 0:N], vcol(V_GQC + m), rs.t[:, 0:N], ALU.mult, ALU.mult, [PS[m], vecs, rs], [cqT])
        for m in range(2):
            ACT(sq8.t[:, 3 + m, 0:N], PS[3 + m].t[:, 0:N], AF.Square, [PS[3 + m]], [sq8])
        for m in range(2):
            MM(pss, pss.t[:, 0:N], ones_b, sq8.t[:, 3 + m, 0:N], m == 0, m == 1, [cb, sq8])
        rs = rs_r.next()
        rstd_chain(pss, 128, N, rs, 1.0 / D_C)
        for m in range(2):
            STT("dve", ckv_f.t[:, m, 0:N], PS[3 + m].t[:, 0:N], vcol(V_GCKV + m), rs.t[:, 0:N], ALU.mult, ALU.mult, [PS[3 + m], vecs, rs], [ckv_f])
        CP("pool", ckvT.t[:, :, c0:c0 + N], ckv_f.t[:, :, 0:N], [ckv_f], [ckvT])
        ACT(sq8.t[0:96, 5, 0:N], PS[5].t[0:96, 0:N], AF.Square, [PS[5]], [sq8])
        MM(pss, pss.t[0:96, 0:N], cb.t[0:96, CB_B96:CB_B96 + 96], sq8.t[0:96, 5, 0:N], True, True, [cb, sq8])
        rs = rs_r.next()
        rstd_chain(pss, 96, N, rs, 1.0)
        STT("dve", kpn.t[0:96, 0:N], PS[5].t[0:96, 0:N], vcol(V_GK96, 0, 96), rs.t[0:96, 0:N], ALU.mult, ALU.mult, [PS[5], vecs, rs], [kpn])
        pr = PS[7]
        MM(pr, pr.t[0:96, 0:N], cf.t[0:96, CF_P96:CF_P96 + 96], kpn.t[0:96, 0:N], True, True, [cf, kpn])
        TT("pool", t1.t[64:96, 0:N], kpn.t[64:96, 0:N], tb.t[64:96, 0, 0:N], ALU.mult, [kpn, tb], [t1])
        TT("dve", kpe_f.t[64:96, 0:N], pr.t[64:96, 0:N], tb.t[64:96, 1, 0:N], ALU.mult, [pr, tb], [kpe_f])
        TT("dve", kpe_f.t[64:96, 0:N], kpe_f.t[64:96, 0:N], t1.t[64:96, 0:N], ALU.add, [kpe_f, t1], [kpe_f])
        CP("pool", kpe_b.t[64:96, c0:c0 + N], kpe_f.t[64:96, 0:N], [kpe_f], [kpe_b])
        ntile = (N + 127) // 128
        for ti in range(ntile):
            r = min(128, N - ti * 128)
            pbT = PS[7]
            for m in range(2):
                TR(pbT, pbT.t[0:r, m * 128:(m + 1) * 128], ckv_f.t[:, m, ti * 128:ti * 128 + r], identf, [ckv_f, cf])
            TR(pbT, pbT.t[0:r, 256:288], kpe_f.t[64:96, ti * 128:ti * 128 + r], cf.t[64:96, CF_ID + 64:CF_ID + 96], [kpe_f, cf])
            stg = rstage.next()
            CP("act", stg.t[0:r, :], pbT.t[0:r, 0:D_CKV], [pbT], [stg])
            if c0 < SEQ:
                ST("sp", rows_p[c0 + ti * 128:c0 + ti * 128 + r, :], stg.t[0:r, :], [stg])
            else:
                ST("sp", rows_s[:, :], stg.t[0:r, :], [stg])
                CP("pool", rown_b.t[0:NS, :], stg.t[0:NS, 0:256], [stg], [rown_b])
    f.release(mA1)
    if stop_after == "A1":
        f.release(mA)
        f.finish()
        return nc

    mA2 = f.mark()
    G = 2
    NG = H // G
    wq_r = Ring([f.sbuf(f"wq_g{i}", [128, 3, G * 96], BF16) for i in range(2)])
    wk_r = Ring([f.sbuf(f"wk_g{i}", [128, 2, G * 64], BF16) for i in range(2)])
    wv_r = Ring([f.sbuf(f"wv_g{i}", [128, 2, G * 64], BF16) for i in range(2)])
    wo_r = Ring([f.sbuf(f"wo_g{i}", [128, D], BF16) for i in range(2)])
    qT_g = f.sbuf("qT_g", [128, G, NT], BF16)
    KT_g = f.sbuf("KT_g", [128, G, NT], BF16)
    V_g = f.sbuf("V_g", [128, 16, G, 128], BF16)
    OT_r = Ring([f.sbuf(f"OT_g{i}", [128, SEQ], BF16) for i in range(2)])
    sq_r = Ring([f.sbuf(f"sqB{i}", [128, 512], BF16) for i in range(4)])
    rs_r = Ring([f.sbuf(f"rsB{i}", [96, 512], F32) for i in range(4)])
    qn_r = Ring([f.sbuf(f"qnB{i}", [96, 512], F32) for i in range(2)])
    t1_r = Ring([f.sbuf(f"t1B{i}", [96, 512], F32) for i in range(2)])
    t2_r = Ring([f.sbuf(f"t2B{i}", [96, 512], F32) for i in range(2)])
    pT_r = Ring([f.sbuf(f"pT{i}", [128, 512], BF16) for i in range(7)])
    rl_r = Ring([f.sbuf(f"rl{i}", [128, 512], F32) for i in range(2)])
    tabc = Ring([f.sbuf(f"tabcB{i}", [96, 2, 512], F32) for i in range(2)])
    MSET("pool", qT_g.t[:, :, :], 0.0, [qT_g])
    MSET("pool", KT_g.t[:, :, :], 0.0, [KT_g])
    MSET("pool", V_g.t[:, :, 0, 64:128], 1.0, [V_g])
    MSET("pool", V_g.t[:, :, 1, 0:64], 1.0, [V_g])
    ringP = Ring(PS[6:8])
    ringS = Ring(PS[0:3])
    ringO = Ring(PS[3:6])
    ringA = Ring(PS[0:8])
    maskD = cb.t[:, CB_MD:CB_MD + 128]
    B96 = cb.t[0:96, CB_B96:CB_B96 + 96]
    B64 = cb.t[0:64, CB_B96:CB_B96 + 64]
    P96 = cf.t[0:96, CF_P96:CF_P96 + 96]
    for g in range(NG):
        wq = wq_r.next(); wk = wk_r.next(); wv = wv_r.next(); wo = wo_r.next()
        OT_g = OT_r.next()
        if g % 2 == 0:
            pend = []
        pend.append((wo, OT_g))
        load_w(wq, w_uq, 3, g * G * 96, G * 96)
        load_w(wk, w_uk, 2, g * G * 64, G * 64)
        load_w(wv, w_uv, 2, g * G * 64, G * 64)
        LD("pool", wo.t[:, :], w_a_out[g * 128:(g + 1) * 128, :], [wo])
        def A_S1(ci):
            c0, N = CHUNKS[ci]
            A = PS[0:4] if ci % 2 == 0 else PS[4:8]
            tb = tabc.next()
            LD("sp", tb.t[64:96, 0, 0:N], tabA_d[0, 64:96, c0:c0 + N], [tb])
            LD("sp", tb.t[64:96, 1, 0:N], tabA_d[1, 64:96, c0:c0 + N], [tb])
            chs = []
            for hl in range(G):
                bq = A[hl]
                for m in range(3):
                    MM(bq, bq.t[0:96, 0:N], wq.t[:, m, hl * 96:(hl + 1) * 96], cqT.t[:, m, c0:c0 + N], m == 0, m == 2, [wq, cqT])
                chs.append(dict(q=True, hl=hl, b=bq, M=96, tb=tb))
            for hl in range(G):
                bk = A[2 + hl]
                for m in range(2):
                    MM(bk, bk.t[0:64, 0:N], wk.t[:, m, hl * 64:(hl + 1) * 64], ckvT.t[:, m, c0:c0 + N], m == 0, m == 1, [wk, ckvT])
                chs.append(dict(q=False, hl=hl, b=bk, M=64, tb=tb))
            return chs

        def A_mid(ci, chs):
            c0, N = CHUNKS[ci]
            Bset = PS[4:8] if ci % 2 == 0 else PS[0:4]
            for c in chs:
                M = c["M"]
                c["sq"] = sq_r.next()
                ACT(c["sq"].t[0:M, 0:N], c["b"].t[0:M, 0:N], AF.Square, [c["b"]], [c["sq"]])
            for ic, c in enumerate(chs):
                M = c["M"]
                c["bs"] = Bset[ic]
                MM(c["bs"], c["bs"].t[0:M, 0:N], cb.t[0:M, CB_B96:CB_B96 + M], c["sq"].t[0:M, 0:N], True, True, [cb, c["sq"]])
            for c in chs:
                c["rs"] = rs_r.next()
                rstd_chain(c["bs"], c["M"], N, c["rs"], 1.0)
            for c in chs:
                hl = c["hl"]
                if c["q"]:
                    c["qn"] = qn_r.next()
                    STT("dve", c["qn"].t[0:96, 0:N], c["b"].t[0:96, 0:N], vcol(V_GQ96, 0, 96), c["rs"].t[0:96, 0:N], ALU.mult, ALU.mult, [c["b"], vecs, c["rs"]], [c["qn"]])
                    STT("dve", qT_g.t[0:64, hl, c0:c0 + N], c["b"].t[0:64, 0:N], vcol(V_GQ96, 0, 64), c["rs"].t[0:64, 0:N], ALU.mult, ALU.mult, [c["b"], vecs, c["rs"]], [qT_g])
                else:
                    STT("dve", KT_g.t[0:64, hl, c0:c0 + N], c["b"].t[0:64, 0:N], vcol(V_GK96, 0, 64), c["rs"].t[0:64, 0:N], ALU.mult, ALU.mult, [c["b"], vecs, c["rs"]], [KT_g])
                    CP("pool", KT_g.t[64:96, hl, c0:c0 + N], kpe_b.t[64:96, c0:c0 + N], [kpe_b], [KT_g])

        def A_tail(ci, chs):
            c0, N = CHUNKS[ci]
            A = PS[0:4] if ci % 2 == 0 else PS[4:8]
            for c in chs:
                if c["q"]:
                    c["br"] = A[c["hl"]]
                    MM(c["br"], c["br"].t[0:96, 0:N], P96, c["qn"].t[0:96, 0:N], True, True, [cf, c["qn"]])
            if c0 < SEQ:
                for ti in range(ci * 4, ci * 4 + 4):
                    bv = A[2 + ti % 2]
                    for m in range(2):
                        MM(bv, bv.t[:, 0:G * 64], ckvT.t[:, m, ti * 128:(ti + 1) * 128], wv.t[:, m, 0:G * 64], m == 0, m == 1, [ckvT, wv])
                    CP("dve" if ti % 2 else "act", V_g.t[:, ti, 0, 0:64], bv.t[:, 0:64], [bv], [V_g])
                    CP("act" if ti % 2 else "dve", V_g.t[:, ti, 1, 64:128], bv.t[:, 64:128], [bv], [V_g])
            for c in chs:
                if c["q"]:
                    hl = c["hl"]; qn = c["qn"]; br = c["br"]; tb = c["tb"]
                    t1 = t1_r.next(); t2 = t2_r.next()
                    TT("pool", t1.t[64:96, 0:N], qn.t[64:96, 0:N], tb.t[64:96, 0, 0:N], ALU.mult, [qn, tb], [t1])
                    TT("dve", t2.t[64:96, 0:N], br.t[64:96, 0:N], tb.t[64:96, 1, 0:N], ALU.mult, [br, tb], [t2])
                    TT("dve", qT_g.t[64:96, hl, c0:c0 + N], t1.t[64:96, 0:N], t2.t[64:96, 0:N], ALU.add, [t1, t2], [qT_g])
            if c0 >= SEQ:
                for hl in range(G):
                    CP("pool", qs_all.t[0:96, g * G + hl, :], qT_g.t[0:96, hl, SEQ:NT], [qT_g], [qs_all])

        chs_next = A_S1(0)
        for ci in range(len(CHUNKS)):
            chs_cur = chs_next
            A_mid(ci, chs_cur)
            if ci + 1 < len(CHUNKS):
                chs_next = A_S1(ci + 1)
            A_tail(ci, chs_cur)
        iters = []
        for hl in range(G):
            for qc in range(4):
                nkt = 4 * qc + 4
                for kt in range(nkt):
                    iters.append(dict(hl=hl, qc=qc, kt=kt, nkt=nkt))

        def emitS(it):
            hl, qc, kt = it["hl"], it["qc"], it["kt"]
            n0 = max(qc * 512, kt * 128)
            W = qc * 512 + 512 - n0
            bs_ = ringS.next()
            MM(bs_, bs_.t[:, 0:W], KT_g.t[:, hl, kt * 128:(kt + 1) * 128], qT_g.t[:, hl, n0:n0 + W], True, True, [KT_g, qT_g])
            pT = pT_r.next()
            ACT(pT.t[:, 0:W], bs_.t[:, 0:W], AF.Exp, [bs_], [pT], scale=SCALE_A)
            if kt * 128 >= qc * 512:
                TT("pool", pT.t[:, 0:128], pT.t[:, 0:128], maskD, ALU.mult, [pT, cb], [pT])
            it["pT"] = pT; it["W"] = W; it["o0"] = n0 - qc * 512

        cur = {}

        def emitPV(it):
            hl, qc, kt, nkt = it["hl"], it["qc"], it["kt"], it["nkt"]
            if kt == 0:
                cur["bo"] = ringO.next()
            bo = cur["bo"]
            MM(bo, bo.t[:, it["o0"]:512], V_g.t[:, kt, hl, :], it["pT"].t[:, 0:it["W"]], kt == 0, kt == nkt - 1, [V_g, it["pT"]])
            if kt == nkt - 1:
                rl = rl_r.next()
                oL, oH = (0, 64) if hl == 0 else (64, 128)
                lL, lH = (64, 128) if hl == 0 else (0, 64)
                f.op("dve", lambda e, rl=rl, bo=bo, lL=lL, lH=lH: e.reciprocal(out=rl.t[lL:lH, :], in_=bo.t[lL:lH, :]), reads=[bo], writes=[rl])
                TT("dve", OT_g.t[oL:oH, qc * 512:(qc + 1) * 512], bo.t[oL:oH, :], rl.t[lL:lH, :], ALU.mult, [bo, rl], [OT_g])
        DEPTH = 5
        for idx in range(len(iters) + DEPTH):
            if idx < len(iters):
                emitS(iters[idx])
            if FILLER:
                MM(PS[7], PS[7].t[:, 0:FILLER], KT_g.t[0:96, 0, 0:128], qT_g.t[0:96, 0, 0:FILLER], True, True, [KT_g, qT_g])
            if idx - DEPTH >= 0:
                emitPV(iters[idx - DEPTH])
        if g % 2 == 1:
            for qc in range(4):
                c0 = qc * 512
                for j in range(8):
                    pb = ringP.next()
                    for ip, (wo_, OT_) in enumerate(pend):
                        MM(pb, pb.t[:, 0:512], wo_.t[:, j * 128:(j + 1) * 128], OT_.t[:, c0:c0 + 512], ip == 0, ip == len(pend) - 1, [wo_, OT_])
                    TT("dve", hT.t[:, j, c0:c0 + 512], pb.t[:, 0:512], hT.t[:, j, c0:c0 + 512], ALU.add, [pb, hT], [hT])
    f.release(mA2)
    if stop_after == "A2":
        f.release(mA)
        emit_output()
        f.finish()
        return nc

    mA3 = f.mark()
    wuk = f.sbuf("wuk", [128, 2, 1024], BF16)
    wuv = f.sbuf("wuv", [128, 2, 1024], BF16)
    load_w(wuk, w_uk, 2, 0, 1024)
    load_w(wuv, w_uv, 2, 0, 1024)
    OTs = f.sbuf("OTs", [64, H, NS], BF16)
    wukT = f.sbuf("wukT", [64, H, 256], BF16)
    qabsT = f.sbuf("qabsT", [128, 2, 4, 128], BF16)
    qpeT = f.sbuf("qpeT", [96, 4, 128], BF16)
    MSET("pool", qabsT.t[:, :, :, :], 0.0, [qabsT])
    MSET("pool", qpeT.t[:, :, :], 0.0, [qpeT])
    maskS = f.sbuf("maskS", [64, 4, 16], F32)
    LD("sp", maskS.t[:, :, :], maskS_d.rearrange("p (b k) -> p b k", b=4), [maskS])
    pti = f.sbuf("pti", [128, 512], I32)
    ptf = f.sbuf("ptf", [128, 512], F32)
    iot = f.sbuf("iot", [128, 1], I32)
    iof = f.sbuf("iof", [128, 1], F32)
    ridx = f.sbuf("ridx", [128, 512], I32)
    LD("sp", pti.t[:, :], ptab[0:1, :].partition_broadcast(128), [pti])
    f.op("pool", lambda e: e.iota(iot.t[:, :], pattern=[[0, 1]], base=0, channel_multiplier=1), writes=[iot])
    CP("dve", iof.t[:, :], iot.t[:, :], [iot], [iof])
    CP("dve", ptf.t[:, :], pti.t[:, :], [pti], [ptf])
    f.op("dve", lambda e: e.tensor_scalar(out=ptf.t[:, :], in0=ptf.t[:, :], scalar1=128.0, scalar2=iof.t[:, 0:1], op0=ALU.mult, op1=ALU.add), reads=[ptf, iof], writes=[ptf])
    CP("dve", ridx.t[:, :], ptf.t[:, :], [ptf], [ridx])
    for hb in range(4):
        pb = PS[hb]
        for hq in range(4):
            for m in range(2):
                slot = hq * 2 + m
                TR(pb, bfv(pb)[0:64, slot * 128:(slot + 1) * 128], wuk.t[:, m, (hb * 4 + hq) * 64:(hb * 4 + hq + 1) * 64], idb.t[:, :], [wuk, idb])
        CP("dve" if hb % 2 else "act", wukT.t[0:64, hb * 4:(hb + 1) * 4, :], bfv(pb)[0:64, 0:1024].rearrange("p (h l) -> p h l", h=4), [pb], [wukT])
    qsg = f.sbuf("qsg", [64, H, NS], BF16)
    TS("dve", qsg.t[0:64, :, :], qs_all.t[0:64, :, :], vcol(V_GK96, 0, 64), ALU.mult, [qs_all, vecs], [qsg])
    pb = PS[4]
    for m in range(2):
        for hh in range(H):
            col = (m * H + hh) * NS
            MM(pb, pb.t[:, col:col + NS], wukT.t[0:64, hh, m * 128:(m + 1) * 128], qsg.t[0:64, hh, :], True, True, [wukT, qsg])
    for m in range(2):
        CP("dve", qabsT.t[:, m, :, 0:64].rearrange("p b (h t) -> p b h t", h=H),
           pb.t[:, m * 256:(m + 1) * 256].rearrange("p (h b t) -> p b h t", h=H, b=4), [pb], [qabsT])
    CP("pool", qpeT.t[64:96, :, 0:64].rearrange("p b (h t) -> p b h t", h=H),
       qs_all.t[64:96, :, :].rearrange("p h (b t) -> p b h t", b=4), [qs_all], [qpeT])
    BS = cb.t[:, CB_BS:CB_BS + 1024].rearrange("p (j c) -> p j c", j=8)
    mA3b = f.mark()
    rowsb_r = Ring([f.sbuf(f"rowsb{i}", [128, 4, D_CKV], BF16) for i in range(8)])
    cT_r = Ring([f.sbuf(f"cT{i}", [128, 2, 512], BF16) for i in range(3)])
    kpT_r = Ring([f.sbuf(f"kpT{i}", [96, 512], BF16) for i in range(3)])
    sqk_r = Ring([f.sbuf(f"sqk{i}", [128, 512], BF16) for i in range(3)])
    rs_r3 = Ring([f.sbuf(f"rsS{i}", [64, 512], F32) for i in range(2)])
    sc_r3 = Ring([f.sbuf(f"scS{i}", [64, 512], F32) for i in range(2)])
    pS_r3 = Ring([f.sbuf(f"pS{i}", [64, 512], BF16) for i in range(2)])
    pTs_r3 = Ring([f.sbuf(f"pTs{i}", [128, 4, 64], BF16) for i in range(2)])
    tmp_r3 = Ring([f.sbuf(f"tmpS{i}", [64, 8], F32) for i in range(2)])
    m_run = f.sbuf("m_run", [64, 1], F32)
    l_run = f.sbuf("l_run", [64, 2], F32)
    acc = f.sbuf("accS", [64, 256], F32)
    accn = f.sbuf("accn", [64, 256], BF16)
    olT = f.sbuf("olT", [128, 2, 64], BF16)
    bX, bY, bK0, bK1, bSS, bN, bP, bT2 = PS
    ringK = Ring([bK0, bK1])

    units = []
    for b in range(4):
        for gi in range(32):
            units.append(dict(b=b, gi=gi, N=512, first=(gi == 0), last=False))
        units.append(dict(b=b, gi=None, N=NS, first=False, last=True))

    def stageG(u):
        if u["gi"] is None:
            return
        rb = rowsb_r.next()
        u["rb"] = rb
        for pg in range(4):
            col = u["b"] * 128 + u["gi"] * 4 + pg
            f.dma("pool", lambda e, rb=rb, pg=pg, col=col: e.indirect_dma_start(
                out=rb.t[:, pg, :], out_offset=None, in_=cache[:, :],
                in_offset=bass.IndirectOffsetOnAxis(ap=ridx.t[:, col:col + 1], axis=0)), reads=[ridx], writes=[rb])

    def stageA_tr(u, pg):
        if u["gi"] is None:
            return
        rb = u["rb"]
        for m in range(2):
            slot = m * 4 + pg
            TR(bX, bfv(bX)[:, slot * 128:(slot + 1) * 128], rb.t[:, pg, m * 128:(m + 1) * 128], idb.t[:, :], [rb, idb])
        TR(bY, bfv(bY)[0:96, pg * 128:(pg + 1) * 128], rb.t[:, pg, 192:288], idb.t[:, :], [rb, idb])

    def stageA_cp(u):
        if u["gi"] is None:
            u["cT"] = lambda m: ckvT.t[:, m, SEQ:NT]
            u["kpT"] = kpe_b.t[64:96, SEQ:NT]
            u["nat"] = [(rown_b.t[0:NS, :], NS, 0)]
            u["cbufs"] = [ckvT, kpe_b, rown_b]
            u["mask"] = maskS.t[:, u["b"], :]
            return
        rb = u["rb"]
        cT = cT_r.next(); kpT = kpT_r.next()
        CP("dve", cT.t[:, :, :].rearrange("p m n -> p (m n)"), bfv(bX)[:, 0:1024], [bX], [cT])
        CP("dve", kpT.t[64:96, :], bfv(bY)[64:96, 0:512], [bY], [kpT])
        u["cT"] = lambda m, cT=cT: cT.t[:, m, :]
        u["kpT"] = kpT.t[64:96, :]
        u["nat"] = [(rb.t[:, pg, 0:256], 128, pg * 128) for pg in range(4)]
        u["cbufs"] = [cT, kpT, rb]
        u["mask"] = None

    def stageA(u):
        for pg in range(4):
            stageA_tr(u, pg)
        stageA_cp(u)

    def stageB_step(u, k):
        N = u["N"]; cT_ap = u["cT"]; cbufs = u["cbufs"]; b = u["b"]

        def kr(jc):
            bk = ringK.next()
            for m in range(2):
                MM(bk, bk.t[:, 0:N], wuk.t[:, m, jc * 128:(jc + 1) * 128], cT_ap(m), m == 0, m == 1, [wuk] + cbufs)
            sqk = sqk_r.next()
            ACT(sqk.t[:, 0:N], bk.t[:, 0:N], AF.Square, [bk], [sqk])
            u.setdefault("sq", {})[jc] = sqk

        def ss(jc):
            sqk = u["sq"][jc]
            MM(bSS, bSS.t[0:128, 0:N], BS[:, jc, :], sqk.t[:, 0:N], jc == 0, jc == 7, [cb, sqk])
        if k == 0:
            kr(0); kr(1)
        elif k < 7:
            ss(k - 1); kr(k + 1)
            if k == 1:
                for m in range(2):
                    MM(bN, bN.t[0:128, 0:N], qabsT.t[:, m, b, :], cT_ap(m), m == 0, m == 1, [qabsT] + cbufs)
                MM(bP, bP.t[0:128, 0:N], qpeT.t[64:96, b, :], u["kpT"], True, True, [qpeT] + cbufs)
        else:
            ss(6); ss(7)

    def stageC1(u):
        N = u["N"]
        rs = rs_r3.next(); sc = sc_r3.next()
        rstd_chain(bSS, 64, N, rs, 1.0)
        TT("dve", sc.t[0:64, 0:N], bN.t[0:64, 0:N], rs.t[0:64, 0:N], ALU.mult, [bN, rs], [sc])
        TT("dve", sc.t[0:64, 0:N], bP.t[0:64, 0:N], sc.t[0:64, 0:N], ALU.add, [bP, sc], [sc])
        if u["mask"] is not None:
            TT("dve", sc.t[0:64, 0:N], sc.t[0:64, 0:N], u["mask"], ALU.add, [sc, maskS], [sc])
        u["sc"] = sc

    def stageC2a(u):
        N = u["N"]; sc = u["sc"]; nat = u["nat"]; cbufs = u["cbufs"]; b = u["b"]
        if u["first"]:
            MSET("pool", m_run.t[:, :], NEG, [m_run])
            MSET("pool", l_run.t[:, 0:1], 0.0, [l_run])
            MSET("pool", acc.t[:, :], 0.0, [acc])
        tmp = tmp_r3.next(); pS = pS_r3.next(); pTs = pTs_r3.next()

        def tc(i):
            return tmp.t[0:64, i:i + 1]
        MSET("dve", tc(4), 0.0, [tmp])
        f.op("dve", lambda e: e.tensor_reduce(out=tc(0), in_=sc.t[0:64, 0:N], axis=AX.X, op=ALU.max), reads=[sc], writes=[tmp])
        TT("dve", tc(1), m_run.t[:, 0:1], tc(0), ALU.max, [m_run, tmp], [tmp])
        TS("dve", tc(2), tc(1), -SCALE_A, ALU.mult, [tmp], [tmp])
        u["tmp"] = tmp; u["pS"] = pS; u["pTs"] = pTs

    def stageC2a2(u):
        N = u["N"]; sc = u["sc"]
        tmp = u["tmp"]; pS = u["pS"]

        def tc(i):
            return tmp.t[0:64, i:i + 1]
        ACT(tc(3), m_run.t[:, 0:1], AF.Exp, [m_run, tmp], [tmp], scale=SCALE_A, bias=tc(2))
        ACT(pS.t[0:64, 0:N], sc.t[0:64, 0:N], AF.Exp, [sc, tmp], [pS, tmp], scale=SCALE_A, bias=tc(2), accum_out=tc(4))
        STT("dve", l_run.t[:, 0:1], l_run.t[:, 0:1], tc(3), tc(4), ALU.mult, ALU.add, [l_run, tmp], [l_run])
        CP("dve", m_run.t[:, 0:1], tc(1), [tmp], [m_run])

    def stageC2b(u):
        N = u["N"]; nat = u["nat"]; cbufs = u["cbufs"]; b = u["b"]
        tmp = u["tmp"]; pS = u["pS"]; pTs = u["pTs"]

        def tc(i):
            return tmp.t[0:64, i:i + 1]
        npg = len(nat)
        for pg, (rows_ap, K, col0) in enumerate(nat):
            TR(bT2, bfv(bT2)[0:K, pg * 64:(pg + 1) * 64], pS.t[0:64, col0:col0 + K], idb.t[0:64, 0:64], [pS, idb])
        Kmax = max(K for (_, K, _) in nat)
        CP("dve", pTs.t[0:Kmax, 0:npg, :], bfv(bT2)[0:Kmax, 0:npg * 64].rearrange("p (g c) -> p g c", g=npg), [bT2], [pTs])

    def stageC2c(u):
        N = u["N"]; nat = u["nat"]; cbufs = u["cbufs"]; b = u["b"]
        tmp = u["tmp"]; pS = u["pS"]; pTs = u["pTs"]

        def tc(i):
            return tmp.t[0:64, i:i + 1]
        npg = len(nat)
        for pg, (rows_ap, K, col0) in enumerate(nat):
            MM(bT2, bT2.t[0:64, 256:512], pTs.t[0:K, pg, :], rows_ap, pg == 0, pg == npg - 1, [pTs] + cbufs)
        STT("dve", acc.t[:, :], acc.t[:, :], tc(3), bT2.t[0:64, 256:512], ALU.mult, ALU.add, [acc, tmp, bT2], [acc])
        if u["last"]:
            f.op("dve", lambda e: e.reciprocal(out=l_run.t[:, 1:2], in_=l_run.t[:, 0:1]), reads=[l_run], writes=[l_run])
            TS("dve", accn.t[:, :], acc.t[:, :], l_run.t[:, 1:2], ALU.mult, [acc, l_run], [accn])
            for m in range(2):
                TR(bT2, bfv(bT2)[:, m * 64:(m + 1) * 64], accn.t[0:64, m * 128:(m + 1) * 128], idb.t[0:64, 0:64], [accn, idb])
            CP("act", olT.t[:, :, :], bfv(bT2)[:, 0:128].rearrange("p (m c) -> p m c", m=2), [bT2], [olT])
            for hh in range(H):
                for m in range(2):
                    MM(bT2, bT2.t[0:64, 256 + hh * 4:256 + (hh + 1) * 4], wuv.t[:, m, hh * 64:(hh + 1) * 64], olT.t[:, m, hh * 4:(hh + 1) * 4], m == 0, m == 1, [wuv, olT])
            CP("dve", OTs.t[0:64, :, b * 4:(b + 1) * 4], bT2.t[0:64, 256:320].rearrange("p (h t) -> p h t", h=H), [bT2], [OTs])

    nu = len(units)
    for i in range(min(4, nu)):
        stageG(units[i])
    stageA(units[0]); stageA(units[1])
    for k in range(8):
        stageB_step(units[0], k)
    stageC1(units[0])
    for i in range(nu):
        u0 = units[i]
        u1 = units[i + 1] if i + 1 < nu else None
        u2 = units[i + 2] if i + 2 < nu else None
        if i + 4 < nu:
            stageG(units[i + 4])
        for k in range(8):
            if u1 is not None:
                stageB_step(u1, k)
            if k < 4 and u2 is not None:
                stageA_tr(u2, k)
            if k == 0:
                stageC2a(u0)
            if k == 3 and u2 is not None:
                stageA_cp(u2)
            if k == 4:
                stageC2a2(u0)
            if k == 6:
                stageC2b(u0)
            if k == 7:
                stageC2c(u0)
        if u1 is not None:
            stageC1(u1)
    f.release(mA3b)
    wao = f.sbuf("wao", [64, H, D], BF16)
    for hh in range(H):
        LD("pool", wao.t[0:64, hh, :], w_a_out[hh * 64:(hh + 1) * 64, :], [wao])
    for j in range(8):
        pb = ringK.next()
        for hh in range(H):
            MM(pb, pb.t[:, 0:NS], wao.t[0:64, hh, j * 128:(j + 1) * 128], OTs.t[0:64, hh, :], hh == 0, hh == H - 1, [wao, OTs])
        TT("dve", hT.t[:, j, SEQ:NT], pb.t[:, 0:NS], hT.t[:, j, SEQ:NT], ALU.add, [pb, hT], [hT])
    f.release(mA3)
    f.release(mA)
    if stop_after == "A3":
        emit_output()
        f.finish()
        return nc

    def ffn(l):
        mF = f.mark()
        xnT = f.sbuf("xnT", [128, 8, NT], BF16)
        sq8 = f.sbuf("sq8F", [128, 8, 512], BF16)
        rs = f.sbuf("rsF", [128, 512], F32)
        gcol = V_GFA if l == 0 else V_GFB
        for (c0, N) in CHUNKS:
            ACT(sq8.t[:, :, 0:N], hT.t[:, :, c0:c0 + N], AF.Square, [hT], [sq8])
            pss = PS[7]
            for j in range(8):
                MM(pss, pss.t[:, 0:N], ones_b, sq8.t[:, j, 0:N], j == 0, j == 7, [cb, sq8])
            rstd_chain(pss, 128, N, rs, 1.0 / D)
            for j in range(8):
                STT("dve", xnT.t[:, j, c0:c0 + N], hT.t[:, j, c0:c0 + N], vcol(gcol + j), rs.t[:, 0:N], ALU.mult, ALU.mult, [hT, vecs, rs], [xnT])
        wg_r = Ring([f.sbuf(f"wg{i}", [128, 8, 512], BF16) for i in range(2)])
        wu_r = Ring([f.sbuf(f"wu{i}", [128, 8, 512], BF16) for i in range(2)])
        wo_r2 = Ring([f.sbuf(f"wo{i}", [128, 4, D], BF16) for i in range(2)])
        uT_r = Ring([f.sbuf(f"uT{i}", [128, 4, 512], BF16) for i in range(2)])
        sg_r = Ring([f.sbuf(f"sg{i}", [128, 512], F32) for i in range(2)])
        ring = Ring(PS[0:8])
        nblk = (D_FF + 511) // 512
        for hb in range(nblk):
            h0 = hb * 512
            HW = min(512, D_FF - h0)
            nhc = HW // 128
            wg = wg_r.next(); wu = wu_r.next(); wo = wo_r2.next()
            load_w(wg, w_ffn_in[l], 8, h0, HW)
            load_w(wu, w_ffn_in[l], 8, D_FF + h0, HW)
            for hc in range(nhc):
                LD("pool", wo.t[:, hc, :], w_ffn_out[l, h0 + hc * 128:h0 + (hc + 1) * 128, :], [wo])
            for (c0, N) in CHUNKS:
                uT = uT_r.next()
                for hc in range(nhc):
                    pg_ = ring.next()
                    for j in range(8):
                        MM(pg_, pg_.t[:, 0:N], wg.t[:, j, hc * 128:(hc + 1) * 128], xnT.t[:, j, c0:c0 + N], j == 0, j == 7, [wg, xnT])
                    pu_ = ring.next()
                    for j in range(8):
                        MM(pu_, pu_.t[:, 0:N], wu.t[:, j, hc * 128:(hc + 1) * 128], xnT.t[:, j, c0:c0 + N], j == 0, j == 7, [wu, xnT])
                    sg = sg_r.next()
                    ACT(sg.t[:, 0:N], pg_.t[:, 0:N], AF.Silu, [pg_], [sg])
                    TT("dve", uT.t[:, hc, 0:N], pu_.t[:, 0:N], sg.t[:, 0:N], ALU.mult, [pu_, sg], [uT])
                for j in range(8):
                    po = ring.next()
                    for hc in range(nhc):
                        MM(po, po.t[:, 0:N], wo.t[:, hc, j * 128:(j + 1) * 128], uT.t[:, hc, 0:N], hc == 0, hc == nhc - 1, [wo, uT])
                    TT("dve", hT.t[:, j, c0:c0 + N], po.t[:, 0:N], hT.t[:, j, c0:c0 + N], ALU.add, [po, hT], [hT])
        f.release(mF)

    ffn(0)
    if stop_after == "FA":
        emit_output()
        f.finish()
        return nc

    mB = f.mark()
    wkv = f.sbuf("wkv", [128, 8, 512], BF16)
    load_w(wkv, w_kv, 8, 0, 512)
    wq_r = Ring([f.sbuf("wqh0", [128, 8, 512], BF16)])
    wbo_r = Ring([f.sbuf("wboh0", [128, 4, D], BF16)])
    esk = f.sbuf("esk", [128, H], F32)
    ACT(esk.t[:, :], vecs.t[:, V_SINKR:V_SINKR + H], AF.Exp, [vecs], [esk])
    maskB = f.sbuf("maskB", [16, 4, 144], F32)
    LD("sp", maskB.t[:, :, :], maskB_d.rearrange("p (b k) -> p b k", b=4), [maskB])
    KB_c = f.sbuf("KB_c", [64, 4, 640], BF16)
    VB_c = f.sbuf("VB_c", [128, 5, 4, 192], BF16)
    MSET("pool", VB_c.t[:, :, :, 0:64], 1.0, [VB_c])
    MSET("pool", VB_c.t[:, :, :, 128:192], 1.0, [VB_c])
    QB_c = f.sbuf("QB_c", [64, 4, 8, 128], BF16)
    OB_c = f.sbuf("OB_c", [128, 4, 512], BF16)
    QBs = f.sbuf("QBs", [64, 4, 8, 4], BF16)
    OBs = f.sbuf("OBs", [128, 8, NS], BF16)
    kn_s = f.sbuf("kn_s", [64, 4, NS], F32)
    kn_sb = f.sbuf("kn_sb", [64, 4, NS], BF16)
    VBs_f = f.sbuf("VBs_f", [NS, 256], F32)
    VBs_b = f.sbuf("VBs_b", [NS, 256], BF16)
    sqx = f.sbuf("sqxB", [128, 8, 512], BF16)
    rs0 = f.sbuf("rs0B", [128, 512], F32)
    xkv = f.sbuf("xkv", [128, 8, 512], BF16)
    sq_r = Ring([f.sbuf(f"sqC{i}", [64, 512], BF16) for i in range(4)])
    rs_r = Ring([f.sbuf(f"rsC{i}", [64, 512], F32) for i in range(4)])
    kn_r = Ring([f.sbuf(f"knC{i}", [64, 512], F32) for i in range(4)])
    t1_r = Ring([f.sbuf(f"t1C{i}", [16, 512], F32) for i in range(2)])
    t2_r = Ring([f.sbuf(f"t2C{i}", [16, 512], F32) for i in range(2)])
    pT_r = Ring([f.sbuf(f"pTC{i}", [128, 512], BF16) for i in range(6)])
    lt_r = Ring([f.sbuf(f"ltC{i}", [128, 512], F32) for i in range(3)])
    tabc = Ring([f.sbuf(f"tabcC{i}", [16, 2, 512], F32) for i in range(2)])
    kvst = f.sbuf("kvst", [128, 2, 256], F32)
    ringP = Ring(PS[6:8])
    ringS = Ring(PS[0:3])
    ringO = Ring(PS[3:6])
    B64 = cb.t[0:64, CB_B96:CB_B96 + 64]
    P16 = cf.t[0:16, CF_P16:CF_P16 + 16]
    maskD4 = cb.t[:, CB_MD:CB_MD + 512]
    maskP4 = cb.t[:, CB_MP:CB_MP + 512]

    def pj_S1(specs, N, A):
        for i, sp in enumerate(specs):
            for j in range(8):
                MM(A[i], A[i].t[0:64, 0:N], sp["lhsT"](j), sp["rhs"](j), j == 0, j == 7, sp["rd"])

    def pj_mid(specs, N, A, Bs):
        n = len(specs)
        sqs, rss = [], []
        for i in range(n):
            sq = sq_r.next(); sqs.append(sq)
            ACT(sq.t[0:64, 0:N], A[i].t[0:64, 0:N], AF.Square, [A[i]], [sq])
        for i in range(n):
            MM(Bs[i], Bs[i].t[0:64, 0:N], B64, sqs[i].t[0:64, 0:N], True, True, [cb, sqs[i]])
        for i in range(n):
            rs = rs_r.next(); rss.append(rs)
            rstd_chain(Bs[i], 64, N, rs, 1.0)
        for i, sp in enumerate(specs):
            kn = kn_r.next(); sp["kn"] = kn
            if sp.get("dst") is not None:
                dap, dbuf = sp["dst"]
                vw = lambda ap: ap.rearrange("p (t q) -> p t q", t=4)
                STT("dve", dap(0, 64), vw(A[i].t[0:64, 0:N]), vcol(sp["gcol"], 0, 64), vw(rss[i].t[0:64, 0:N]), ALU.mult, ALU.mult, [A[i], vecs, rss[i]], [dbuf])
                STT("dve", kn.t[0:16, 0:N], A[i].t[0:16, 0:N], vcol(sp["gcol"], 0, 16), rss[i].t[0:16, 0:N], ALU.mult, ALU.mult, [A[i], vecs, rss[i]], [kn])
            else:
                STT("dve", kn.t[0:64, 0:N], A[i].t[0:64, 0:N], vcol(sp["gcol"], 0, 64), rss[i].t[0:64, 0:N], ALU.mult, ALU.mult, [A[i], vecs, rss[i]], [kn])

    def pj_tail(specs, N, tb, A):
        for i, sp in enumerate(specs):
            MM(A[i], A[i].t[0:16, 0:N], P16, sp["kn"].t[0:16, 0:N], True, True, [cf, sp["kn"]])
        for i, sp in enumerate(specs):
            kn = sp["kn"]
            t1 = t1_r.next(); t2 = t2_r.next()
            TT("pool", t1.t[0:16, 0:N], kn.t[0:16, 0:N], tb.t[0:16, 0, 0:N], ALU.mult, [kn, tb], [t1])
            TT("dve", t2.t[0:16, 0:N], A[i].t[0:16, 0:N], tb.t[0:16, 1, 0:N], ALU.mult, [A[i], tb], [t2])
            if sp.get("dst") is not None:
                dap, dbuf = sp["dst"]
                vw = lambda ap: ap.rearrange("p (t q) -> p t q", t=4)
                TT("dve", dap(0, 16), vw(t1.t[0:16, 0:N]), vw(t2.t[0:16, 0:N]), ALU.add, [t1, t2], [dbuf])
            else:
                TT("dve", kn.t[0:16, 0:N], t1.t[0:16, 0:N], t2.t[0:16, 0:N], ALU.add, [t1, t2], [kn])
                sp["consume"](kn)

    def proj_pipeline(batches, N, tb):
        sets = [PS[0:4], PS[4:8]]
        pj_S1(batches[0]["specs"], N, sets[0])
        for k, bt in enumerate(batches):
            A = sets[k % 2]; Bs = sets[(k + 1) % 2]
            pj_mid(bt["specs"], N, A, Bs)
            if k + 1 < len(batches):
                pj_S1(batches[k + 1]["specs"], N, Bs)
            pj_tail(bt["specs"], N, tb, A)
            if bt.get("after") is not None:
                bt["after"](A)

    for ci, (c0, N) in enumerate(CHUNKS):
        samp = c0 >= SEQ
        tb = tabc.next()
        LD("sp", tb.t[0:16, 0, 0:N], tabB_d[0, :, c0:c0 + N], [tb])
        LD("sp", tb.t[0:16, 1, 0:N], tabB_d[1, :, c0:c0 + N], [tb])
        ACT(sqx.t[:, :, 0:N], hT.t[:, :, c0:c0 + N], AF.Square, [hT], [sqx])
        pss = ringP.next()
        for j in range(8):
            MM(pss, pss.t[:, 0:N], ones_b, sqx.t[:, j, 0:N], j == 0, j == 7, [cb, sqx])
        rstd_chain(pss, 128, N, rs0, 1.0 / D)
        xnB = sqx
        for j in range(8):
            STT("dve", xkv.t[:, j, 0:N], hT.t[:, j, c0:c0 + N], vcol(V_GKV + j), rs0.t[:, 0:N], ALU.mult, ALU.mult, [hT, vecs, rs0], [xkv])
            STT("dve", xnB.t[:, j, 0:N], hT.t[:, j, c0:c0 + N], vcol(V_GB + j), rs0.t[:, 0:N], ALU.mult, ALU.mult, [hT, vecs, rs0], [xnB])
        kn_keep = {}

        def k_consume(kvh):
            def fn(kn):
                if not samp:
                    CP("pool", KB_c.t[0:64, kvh, 128:128 + N], kn.t[0:64, 0:N], [kn], [KB_c])
                    kn_keep[kvh] = kn
                else:
                    CP("pool", kn_s.t[0:64, kvh, :], kn.t[0:64, 0:NS], [kn], [kn_s])
                    CP("pool", kn_sb.t[0:64, kvh, :], kn.t[0:64, 0:NS], [kn], [kn_sb])
            return fn
        k_batch = dict(specs=[dict(lhsT=(lambda j, kvh=kvh: wkv.t[:, j, kvh * 64:(kvh + 1) * 64]), rhs=(lambda j: xkv.t[:, j, 0:N]),
                                    rd=[wkv, xkv], gcol=V_GKB, consume=k_consume(kvh)) for kvh in range(N_KV)], after=None)
        if ci == 3:
            def k_after(A):
                for kvh in range(N_KV):
                    pbT = A[3]
                    TR(pbT, pbT.t[:, 0:64], kn_keep[kvh].t[0:64, 384:512], cf.t[0:64, CF_ID:CF_ID + 64], [kn_keep[kvh], cf])
                    CP("act", kvst.t[:, 0, kvh * 64:(kvh + 1) * 64], pbT.t[:, 0:64], [pbT], [kvst])
                ST("sp", wk_p[:, :], kvst.t[:, 0, :], [kvst])
            k_batch["after"] = k_after
        if samp:
            proj_pipeline([k_batch], N, tb)
        if not samp:
            for ti in range(4):
                bv = ringP.next()
                for j in range(8):
                    MM(bv, bv.t[:, 0:256], xkv.t[:, j, ti * 128:(ti + 1) * 128], wkv.t[:, j, 256:512], j == 0, j == 7, [xkv, wkv])
                CP("act", VB_c.t[:, 1 + ti, :, 64:128], bv.t[:, 0:256].rearrange("p (k d) -> p k d", k=4), [bv], [VB_c])
                if ci == 3 and ti == 3:
                    CP("dve", kvst.t[:, 1, :], bv.t[:, 0:256], [bv], [kvst])
                    ST("sp", wv_p[:, :], kvst.t[:, 1, :], [kvst])
        else:
            bv = ringP.next()
            for j in range(8):
                MM(bv, bv.t[0:NS, 0:256], xkv.t[:, j, 0:NS], wkv.t[:, j, 256:512], j == 0, j == 7, [xkv, wkv])
            CP("act", VBs_f.t[:, :], bv.t[0:NS, 0:256], [bv], [VBs_f])
            CP("dve", VBs_b.t[:, :], bv.t[0:NS, 0:256], [bv], [VBs_b])
            pbT = ringP.next()
            for kvh in range(N_KV):
                TR(pbT, pbT.t[0:NS, kvh * 64:(kvh + 1) * 64], kn_s.t[0:64, kvh, :], cf.t[0:64, CF_ID:CF_ID + 64], [kn_s, cf])
            ktok = f.sbuf("ktok", [NS, 256], F32)
            CP("act", ktok.t[:, :], pbT.t[0:NS, 0:256], [pbT], [ktok])
            kw_all = f.sbuf("kw_all", [128, 4, 256], F32)
            vw_all = f.sbuf("vw_all", [128, 4, 256], F32)
            vwb_all = f.sbuf("vwb_all", [128, 4, 256], BF16)
            for b in range(4):
                LD("sp", kw_all.t[:, b, :], swk[b, :, :], [kw_all])
                LD("sp", vw_all.t[:, b, :], swv[b, :, :], [vw_all])
            CP("pool", vwb_all.t[:, :, :], vw_all.t[:, :, :], [vw_all], [vwb_all])
            for b in range(4):
                ST("sp", wk_s[b, 0:124, :], kw_all.t[4:128, b, :], [kw_all])
                ST("sp", wv_s[b, 0:124, :], vw_all.t[4:128, b, :], [vw_all])
                ST("sp", wk_s[b, 124:128, :], ktok.t[b * 4:(b + 1) * 4, :], [ktok])
                ST("sp", wv_s[b, 124:128, :], VBs_f.t[b * 4:(b + 1) * 4, :], [VBs_f])
            Kcat = f.sbuf("Kcat", [64, 144], BF16)
            scb = f.sbuf("scb", [16, 144], F32)
            pb_ = f.sbuf("pbB", [16, 144], BF16)
            pTw = f.sbuf("pTw", [128, 32], BF16)
            st2 = f.sbuf("st2", [16, 8], F32)
            onb = f.sbuf("onb", [16, 128], BF16)

            def s2(i):
                return st2.t[0:16, i:i + 1]
        for half in range(2):
            wqh = wq_r.next(); wboh = wbo_r.next()
            load_w(wqh, w_q_b, 8, half * 512, 512)
            for pr in range(4):
                LD("pool", wboh.t[:, pr, :], w_b_out[(half * 4 + pr) * 128:(half * 4 + pr + 1) * 128, :], [wboh])
            def q_consume(h8):
                def fn(kn):
                    if not samp:
                        CP("pool", QB_c.t[0:64, :, h8, :], kn.t[0:64, 0:512].rearrange("p (t q) -> p t q", t=4), [kn], [QB_c])
                    else:
                        CP("pool", QBs.t[0:64, :, h8, :], kn.t[0:64, 0:NS].rearrange("p (b t) -> p b t", b=4), [kn], [QBs])
                return fn
            def q_dst(h8):
                if samp:
                    return None
                slot8 = (h8 // 4) * 4 + [0, 2, 1, 3][h8 % 4]
                return ((lambda p0, p1, slot8=slot8: QB_c.t[p0:p1, :, slot8, :]), QB_c)
            batches = [dict(specs=[dict(lhsT=(lambda j, h8=h8: wqh.t[:, j, h8 * 64:(h8 + 1) * 64]), rhs=(lambda j: xnB.t[:, j, 0:N]),
                                        rd=[wqh, xnB], gcol=V_GQB, consume=q_consume(h8), dst=q_dst(h8)) for h8 in range(qb * 4, qb * 4 + 4)], after=None)
                       for qb in range(2)]
            if half == 0 and not samp:
                batches = [k_batch] + batches
            proj_pipeline(batches, N, tb)
            if not samp:
                iters = []
                for kk in range(2):
                    for ti in range(4):
                        gi = ci * 4 + ti
                        kts = [kt for kt in (gi - 1, gi) if kt >= 0]
                        for n_, kt in enumerate(kts):
                            iters.append(dict(kk=kk, ti=ti, gi=gi, kt=kt, first=(n_ == 0), last=(n_ == len(kts) - 1), n=len(iters)))

                def emitS(it):
                    kk, ti, kt, gi = it["kk"], it["ti"], it["kt"], it["gi"]
                    kvh = half * 2 + kk
                    slot = kt - (ci * 4 - 1)
                    bs_ = ringS.next()
                    MM(bs_, bs_.t[:, 0:512], KB_c.t[0:64, kvh, slot * 128:(slot + 1) * 128],
                       QB_c.t[0:64, ti, kk * 4:(kk + 1) * 4, :].rearrange("p h q -> p (h q)"), True, True, [KB_c, QB_c])
                    pT = pT_r.next()
                    ACT(pT.t[:, :], bs_.t[:, :], AF.Exp, [bs_], [pT], scale=SCALE_B)
                    TT("pool" if it["n"] % 4 == 3 else "dve", pT.t[:, :], pT.t[:, :], maskD4 if kt == gi else maskP4, ALU.mult, [pT, cb], [pT])
                    it["pT"] = pT; it["slot"] = slot; it["kvh"] = kvh
                curB = {}

                def emitPV(it):
                    kk, ti, kvh = it["kk"], it["ti"], it["kvh"]
                    if it["first"]:
                        curB["bo"] = ringO.next()
                    bo = curB["bo"]
                    MM(bo, bo.t[:, 0:256], VB_c.t[:, it["slot"], kvh, 64:192], it["pT"].t[:, 0:256], it["first"], it["last"], [VB_c, it["pT"]], sgc=True)
                    MM(bo, bo.t[:, 256:512], VB_c.t[:, it["slot"], kvh, 0:128], it["pT"].t[:, 256:512], False, it["last"], [VB_c, it["pT"]], sgc=True)
                    if it["last"]:
                        pend_norm.append((it["idx_emit"], bo, kvh, kk, ti))

                def emitNormA(bo, kvh, kk, ti):
                    lt = lt_r.next()
                    v3 = lambda ap: ap.rearrange("p (h q) -> p h q", h=2)
                    eskv = esk.t[:, kvh * 4:(kvh + 1) * 4].rearrange("p (h2 two) -> p h2 two", two=2)
                    TT("dve", v3(lt.t[64:128, 0:256]), v3(bo.t[64:128, 0:256]), eskv[64:128, :, 0].unsqueeze(2).broadcast_to([64, 2, 128]), ALU.add, [bo, esk], [lt])
                    TT("dve", v3(lt.t[0:64, 256:512]), v3(bo.t[0:64, 256:512]), eskv[0:64, :, 1].unsqueeze(2).broadcast_to([64, 2, 128]), ALU.add, [bo, esk], [lt])
                    ACT(lt.t[64:128, 0:256], lt.t[64:128, 0:256], AF.Ln, [lt], [lt])
                    ACT(lt.t[0:64, 256:512], lt.t[0:64, 256:512], AF.Ln, [lt], [lt])
                    ACT(lt.t[64:128, 0:256], lt.t[64:128, 0:256], AF.Exp, [lt], [lt], scale=-1.0)
                    ACT(lt.t[0:64, 256:512], lt.t[0:64, 256:512], AF.Exp, [lt], [lt], scale=-1.0)
                    return lt

                def emitNormB(bo, lt, kk, ti):
                    v3 = lambda ap: ap.rearrange("p (h q) -> p h q", h=2)
                    TT("dve", OB_c.t[0:64, kk * 2:(kk + 1) * 2, ti * 128:(ti + 1) * 128], v3(bo.t[0:64, 0:256]), v3(lt.t[64:128, 0:256]), ALU.mult, [bo, lt], [OB_c])
                    TT("dve", OB_c.t[64:128, kk * 2:(kk + 1) * 2, ti * 128:(ti + 1) * 128], v3(bo.t[64:128, 256:512]), v3(lt.t[0:64, 256:512]), ALU.mult, [bo, lt], [OB_c])
                DEPTH = 4
                NDA = 1
                NDB = 2
                pend_norm = []
                pend_b = []
                for idx in range(len(iters) + DEPTH + NDA + NDB):
                    if idx < len(iters):
                        emitS(iters[idx])
                    if 0 <= idx - DEPTH < len(iters):
                        iters[idx - DEPTH]["idx_emit"] = idx
                        emitPV(iters[idx - DEPTH])
                    while pend_norm and pend_norm[0][0] + NDA <= idx:
                        _, bo_, kvh_, kk_, ti_ = pend_norm.pop(0)
                        lt_ = emitNormA(bo_, kvh_, kk_, ti_)
                        pend_b.append((idx, bo_, lt_, kk_, ti_))
                    while pend_b and pend_b[0][0] + NDB <= idx:
                        _, bo_, lt_, kk_, ti_ = pend_b.pop(0)
                        emitNormB(bo_, lt_, kk_, ti_)
                assert not pend_norm and not pend_b
                for j in range(8):
                    pb = ringP.next()
                    for pr in range(4):
                        MM(pb, pb.t[:, 0:512], wboh.t[:, pr, j * 128:(j + 1) * 128], OB_c.t[:, pr, :], pr == 0, pr == 3, [wboh, OB_c])
                    TT("dve", hT.t[:, j, c0:c0 + 512], pb.t[:, 0:512], hT.t[:, j, c0:c0 + 512], ALU.add, [pb, hT], [hT])
            else:
                for b in range(4):
                    for kk in range(2):
                        kvh = half * 2 + kk
                        pbK = ringP.next()
                        TR(pbK, pbK.t[0:64, 0:128], kw_all.t[:, b, kvh * 64:(kvh + 1) * 64], identf, [kw_all, cf])
                        CP("act", Kcat.t[0:64, 0:128], pbK.t[0:64, 0:128], [pbK], [Kcat])
                        CP("pool", Kcat.t[0:64, 128:144], kn_sb.t[0:64, kvh, :], [kn_sb], [Kcat])
                        bs_ = ringS.next()
                        MM(bs_, bs_.t[0:16, 0:144], QBs.t[0:64, b, kk * 4:(kk + 1) * 4, :].rearrange("p h t -> p (h t)"), Kcat.t[0:64, 0:144], True, True, [QBs, Kcat])
                        TT("dve", scb.t[:, :], bs_.t[0:16, 0:144], maskB.t[:, b, :], ALU.add, [bs_, maskB], [scb])
                        f.op("dve", lambda e: e.tensor_reduce(out=s2(0), in_=scb.t[:, :], axis=AX.X, op=ALU.max), reads=[scb], writes=[st2])
                        TS("dve", s2(1), s2(0), -SCALE_B, ALU.mult, [st2], [st2])
                        MSET("pool", s2(2), 0.0, [st2])
                        ACT(pb_.t[:, :], scb.t[:, :], AF.Exp, [scb, st2], [pb_, st2], scale=SCALE_B, bias=s2(1), accum_out=s2(2))
                        ACT(s2(3), vecs.t[0:16, V_SINKC + kvh:V_SINKC + kvh + 1], AF.Exp, [vecs, st2], [st2], scale=1.0, bias=s2(1))
                        TT("dve", s2(4), s2(2), s2(3), ALU.add, [st2], [st2])
                        f.op("dve", lambda e: e.reciprocal(out=s2(5), in_=s2(4)), reads=[st2], writes=[st2])
                        pbP = ringP.next()
                        TR(pbP, bfv(pbP)[:, 0:16], pb_.t[0:16, 0:128], idb.t[0:16, 0:16], [pb_, idb])
                        TR(pbP, bfv(pbP)[0:16, 16:32], pb_.t[0:16, 128:144], idb.t[0:16, 0:16], [pb_, idb])
                        CP("act", pTw.t[:, 0:32], bfv(pbP)[:, 0:32], [pbP], [pTw])
                        bo = ringO.next()
                        MM(bo, bo.t[0:16, 0:64], pTw.t[:, 0:16], vwb_all.t[:, b, kvh * 64:(kvh + 1) * 64], True, False, [pTw, vwb_all])
                        MM(bo, bo.t[0:16, 0:64], pTw.t[0:16, 16:32], VBs_b.t[0:NS, kvh * 64:(kvh + 1) * 64], False, True, [pTw, VBs_b])
                        TS("dve", onb.t[:, 0:64], bo.t[0:16, 0:64], s2(5), ALU.mult, [bo, st2], [onb])
                        TS("dve", onb.t[:, 64:128], bo.t[0:16, 0:64], s2(5), ALU.mult, [bo, st2], [onb])
                        pbO = ringP.next()
                        TR(pbO, bfv(pbO)[0:128, 0:16], onb.t[0:16, 0:128], idb.t[0:16, 0:16], [onb, idb])
                        for par in range(2):
                            CP("act", OBs.t[par * 64:(par + 1) * 64, kk * 4:(kk + 1) * 4, b * 4:(b + 1) * 4].rearrange("p (h2 two) t -> p h2 two t", two=2)[:, :, par, :],
                               bfv(pbO)[par * 64:(par + 1) * 64, 0:16].rearrange("p (h2 two t) -> p h2 two t", two=2, t=4)[:, :, par, :], [pbO], [OBs])
                for j in range(8):
                    for par in range(2):
                        pb = ringP.next()
                        for pr in range(4):
                            h8 = pr * 2 + par
                            MM(pb, pb.t[:, 0:NS], wboh.t[par * 64:(par + 1) * 64, pr, j * 128:(j + 1) * 128], OBs.t[par * 64:(par + 1) * 64, h8, :], pr == 0, pr == 3, [wboh, OBs])
                        TT("dve", hT.t[:, j, SEQ:NT], pb.t[:, 0:NS], hT.t[:, j, SEQ:NT], ALU.add, [pb, hT], [hT])
        if ci < 3:
            CP("pool", KB_c.t[0:64, :, 0:128], KB_c.t[0:64, :, 512:640], [KB_c], [KB_c])
            CP("pool", VB_c.t[:, 0, :, 64:128], VB_c.t[:, 4, :, 64:128], [VB_c], [VB_c])
    f.release(mB)
    if stop_after == "B":
        emit_output()
        f.finish()
        return nc

    ffn(1)

    emit_output()
    f.finish()
    return nc


def _rope_tab(n_rot, pos):
    inv = np.power(np.float32(THETA), (-np.arange(0, n_rot, 2, dtype=np.float32) / np.float32(n_rot)).astype(np.float32)).astype(np.float32)
    ang = (pos.astype(np.float32)[:, None] * inv[None, :]).astype(np.float32)
    return np.cos(ang.astype(np.float64)).astype(np.float32), np.sin(ang.astype(np.float64)).astype(np.float32)


def _constants():
    pos = np.concatenate([np.arange(SEQ), np.tile(PAST + np.arange(4), 4)]).astype(np.int64)
    cA, sA = _rope_tab(D_ROPE, pos)
    tabA = np.zeros((2, 96, NT), np.float32)
    tabA[0, :64] = 1.0
    for d in range(32):
        tabA[0, 64 + d] = cA[:, d % 16]
        tabA[1, 64 + d] = sA[:, d % 16]
    cB, sB = _rope_tab(16, pos)
    tabB = np.zeros((2, 16, NT), np.float32)
    for d in range(16):
        tabB[0, d] = cB[:, d % 8]
        tabB[1, d] = sB[:, d % 8]
    cf = np.zeros((128, CF_W), np.float32)
    cf[:, CF_ID:CF_ID + 128] = np.eye(128, dtype=np.float32)
    for d in range(16):
        cf[64 + d + 16, CF_P96 + 64 + d] = -1.0
        cf[64 + d, CF_P96 + 64 + d + 16] = 1.0
    for d in range(8):
        cf[d + 8, CF_P16 + d] = -1.0
        cf[d, CF_P16 + d + 8] = 1.0
    cb = np.zeros((128, CB_W), np.float32)
    cb[:, CB_ONES:CB_ONES + 128] = 1.0
    cb[0:64, CB_B96:CB_B96 + 64] = 1.0 / 64
    cb[64:96, CB_B96 + 64:CB_B96 + 96] = 1.0 / 32
    for jc in range(8):
        for p in range(128):
            hh = 2 * jc + p // 64
            cb[p, CB_BS + jc * 128 + hh * 4:CB_BS + jc * 128 + hh * 4 + 4] = 1.0 / 64
    pp = np.arange(128)[:, None]; cc = np.arange(128)[None, :]
    mD = (cc >= pp).astype(np.float32); mP = (cc < pp).astype(np.float32)
    cb[:, CB_MD:CB_MD + 512] = np.tile(mD, (1, 4))
    cb[:, CB_MP:CB_MP + 512] = np.tile(mP, (1, 4))
    maskS = np.full((64, 4, 16), NEG, np.float32)
    for hh in range(H):
        for t in range(4):
            for b in range(4):
                for t2 in range(t + 1):
                    maskS[hh * 4 + t, b, b * 4 + t2] = 0.0
    maskB = np.full((16, 4, 144), NEG, np.float32)
    for hq in range(4):
        for t in range(4):
            for b in range(4):
                maskB[hq * 4 + t, b, t + 1:128] = 0.0
                for t2 in range(t + 1):
                    maskB[hq * 4 + t, b, 128 + b * 4 + t2] = 0.0
    return dict(tabA=tabA, tabB=tabB, cf32=cf, cb32=cb, maskS=maskS.reshape(64, 64), maskB=maskB.reshape(16, 576))


def _vecs(inp):
    v = np.zeros((128, NV), np.float32)

    def colmaj(g, c0):
        n = g.shape[0] // 128
        v[:, c0:c0 + n] = g.reshape(n, 128).T
    colmaj(inp["norm_attn"][0], V_GA); colmaj(inp["norm_ffn"][0], V_GFA); colmaj(inp["g_kv_shared"], V_GKV)
    colmaj(inp["norm_attn"][1], V_GB); colmaj(inp["norm_ffn"][1], V_GFB)
    colmaj(inp["g_qc"][0], V_GQC); colmaj(inp["g_ckv"][0], V_GCKV)
    v[0:64, V_GQ96] = inp["g_qn_a"][0]; v[64:96, V_GQ96] = inp["g_qr_a"][0]
    v[0:64, V_GK96] = inp["g_kn_a"][0]; v[64:96, V_GK96] = inp["g_kr_a"][0]
    v[0:64, V_GKB] = inp["g_k_b"]; v[0:64, V_GQB] = inp["g_q_b"][0]
    sk = inp["sinks"][0]
    for kvh in range(4):
        for hq in range(4):
            v[hq * 4:hq * 4 + 4, V_SINKC + kvh] = sk[kvh * 4 + hq]
    v[:, V_SINKR:V_SINKR + H] = sk[None, :]
    return v


_PROG = {}


def kernel(_ncores=8, _stop_after=None, **inp):
    inp = {k: np.asarray(v) for k, v in inp.items()}
    key = _stop_after
    if key not in _PROG:
        _PROG[key] = build_program(_stop_after)
    nc = _PROG[key]
    consts = _constants()
    vecs = _vecs(inp)
    cache2d = np.ascontiguousarray(inp["cache_mla"][0].reshape(NPOOL * 128, D_CKV))
    shared = dict(
        cache=cache2d,
        w_a_in=np.ascontiguousarray(inp["w_a_in"][0]), w_uq=np.ascontiguousarray(inp["w_uq"][0]),
        w_uk=np.ascontiguousarray(inp["w_uk"][0]), w_uv=np.ascontiguousarray(inp["w_uv"][0]),
        w_a_out=np.ascontiguousarray(inp["w_a_out"][0]), w_kv=np.ascontiguousarray(inp["w_kv_shared"]),
        w_q_b=np.ascontiguousarray(inp["w_q_b"][0]), w_b_out=np.ascontiguousarray(inp["w_b_out"][0]),
        w_ffn_in=np.ascontiguousarray(inp["w_ffn_in"]), w_ffn_out=np.ascontiguousarray(inp["w_ffn_out"]),
        vecs=vecs, **consts)
    in_maps = []
    for c in range(_ncores):
        m = dict(shared)
        m["x_p"] = np.ascontiguousarray(inp["x_prompt"][c])
        m["x_s"] = np.ascontiguousarray(inp["x_sample"][4 * c:4 * c + 4].reshape(NS, D))
        m["ptab"] = np.ascontiguousarray(inp["page_table"][4 * c:4 * c + 4].reshape(1, 512).astype(np.int32))
        m["swk"] = np.ascontiguousarray(inp["state_win_k"][4 * c:4 * c + 4].reshape(4, 128, 256))
        m["swv"] = np.ascontiguousarray(inp["state_win_v"][4 * c:4 * c + 4].reshape(4, 128, 256))
        in_maps.append(m)
    res = run_bass_kernel_spmd(nc, in_maps, core_ids=list(range(_ncores)))
    R = res.results
    n = _ncores
    y_prompt = np.stack([R[c]["y_p"] for c in range(n)])
    y_sample = np.concatenate([R[c]["y_s"].reshape(4, 4, D) for c in range(n)])
    rows_pr = np.stack([R[c]["rows_p"] for c in range(n)])[None]
    rows_sa = np.concatenate([R[c]["rows_s"].reshape(4, 4, D_CKV) for c in range(n)])[None]
    wkp = np.stack([R[c]["wk_p"].reshape(128, 4, 64) for c in range(n)])
    wvp = np.stack([R[c]["wv_p"].reshape(128, 4, 64) for c in range(n)])
    wks = np.concatenate([R[c]["wk_s"].reshape(4, 128, 4, 64) for c in range(n)])
    wvs = np.concatenate([R[c]["wv_s"].reshape(4, 128, 4, 64) for c in range(n)])
    f32 = np.float32
    return (y_prompt.astype(f32), y_sample.astype(f32), rows_pr.astype(f32), rows_sa.astype(f32),
            wkp.astype(f32), wvp.astype(f32), wks.astype(f32), wvs.astype(f32))
```
